# Optimizing a Trainium2 kernel written in Bass

```python
import math
import jax
import jax.numpy as jnp
from jax import lax
import numpy as np

D_MODEL = 1024
BATCH = 2
SEQ = 8192
DEPTH = 1
DEC_BATCH = 32
DEC_SEQ = 4
PAST_LEN = 8192
PAGE_SIZE = 128

HEAD_DIM = 64
N_HEADS = D_MODEL // HEAD_DIM
H_MOBA = N_HEADS // 2
H_NSA = N_HEADS - H_MOBA
H_NSA_KV = 2
NSA_GROUP = H_NSA // H_NSA_KV
W_MOBA = H_MOBA * HEAD_DIM
W_NSA = H_NSA * HEAD_DIM
W_NSA_KV = H_NSA_KV * HEAD_DIM
D_IN = 4 * W_MOBA + 2 * W_NSA + 6 * W_NSA_KV + 3 * H_NSA
MOBA_BLOCK = 256
MOBA_TOPK = 3
CMP_LEN = 32
CMP_STRIDE = 16
CMP_HIDDEN = 2 * HEAD_DIM
SLC_BLOCK = 64
SLC_TOPN = 16
WINDOW = 512
N_BUCKETS = 32
MAX_DISTANCE = 128
Q_BLOCK = 64
RMS_EPS = 1e-6
NEG = -1e30
FORCE = 1e9
TINY = 1e-30

kernel_name = 'hymba_moba_nsa_decoder_step'


def proj_splits():
    sizes = [W_MOBA] * 4 + [W_NSA] + [W_NSA_KV] * 6 + [3 * H_NSA, W_NSA]
    return [int(s) for s in np.cumsum(sizes)[:-1]]


def rmsnorm(x, gain):
    xf = x.astype(jnp.float32)
    inv = lax.rsqrt(jnp.mean(xf * xf, axis=-1, keepdims=True) + RMS_EPS)
    return (xf * inv).astype(x.dtype) * gain


def masked_softmax(logits, mask):
    logits = jnp.where(mask, logits.astype(jnp.float32), NEG)
    p = jnp.exp(logits - jnp.max(logits, axis=-1, keepdims=True)) * mask
    return p / jnp.maximum(jnp.sum(p, axis=-1, keepdims=True), TINY)


def t5_bucket(rel):
    n = jnp.maximum(rel, 0)
    exact = N_BUCKETS // 2
    nf = jnp.maximum(n, 1).astype(jnp.float32)
    large = exact + (jnp.log(nf / exact) / math.log(MAX_DISTANCE / exact) * (N_BUCKETS - exact)).astype(jnp.int32)
    return jnp.where(n < exact, n, jnp.minimum(large, N_BUCKETS - 1))


def project(x, c, w_ada_l, b_ada_l, gain_l, w_in_l):
    B, T, _ = x.shape
    shift, scale, gate = jnp.split(c @ w_ada_l + b_ada_l, 3, axis=-1)
    h = rmsnorm(x, gain_l) * (1.0 + scale[:, None]) + shift[:, None]
    (q_m, k_m, v_m, z_m, q_n, kc, vc, ks, vs, kw, vw, g_n, z_n) = jnp.split(h @ w_in_l, proj_splits(), axis=-1)
    heads = lambda a: a.reshape(B, T, -1, HEAD_DIM)
    gates = jax.nn.sigmoid(g_n).reshape(B, T, H_NSA, 3)
    return (gate, z_m, z_n, heads(q_m), heads(k_m), heads(v_m), heads(q_n),
            heads(kc), heads(vc), heads(ks), heads(vs), heads(kw), heads(vw), gates)


def moba_keys(k, v):
    B, L, H, Dh = k.shape
    nb = -(-L // MOBA_BLOCK)
    pad = ((0, 0), (0, nb * MOBA_BLOCK - L), (0, 0), (0, 0))
    kb = jnp.pad(k, pad).reshape(B, nb, MOBA_BLOCK, H, Dh)
    vb = jnp.pad(v, pad).reshape(B, nb, MOBA_BLOCK, H, Dh)
    kmean = jnp.mean(kb.astype(jnp.float32), axis=2).astype(k.dtype)
    return kb, vb, kmean


def moba_attend(q, qpos, kb, vb, kmean, bias_tab):
    B, Q, H, Dh = q.shape
    nb = kb.shape[1]
    cur = qpos // MOBA_BLOCK
    s = jnp.einsum('bqhd,bnhd->bhqn', q, kmean).astype(jnp.float32)
    full = jnp.arange(nb, dtype=jnp.int32)[None, :] < cur[:, None]
    top = min(MOBA_TOPK, nb)
    _, idx = lax.top_k(jnp.where(full, s, NEG), top)
    own = jnp.broadcast_to(cur[None, None, :, None], (B, H, Q, 1))
    blocks = jnp.concatenate([idx, own], axis=-1)
    ok = jnp.concatenate([idx < cur[None, None, :, None], jnp.ones_like(own, dtype=bool)], axis=-1)
    bi = jnp.arange(B)[:, None, None, None]
    hi = jnp.arange(H)[None, :, None, None]
    kg = kb[bi, blocks, :, hi]
    vg = vb[bi, blocks, :, hi]
    n = (top + 1) * MOBA_BLOCK
    logits = jnp.einsum('bqhd,bhqnkd->bhqnk', q, kg) * (HEAD_DIM ** -0.5)
    rel = qpos[None, None, :, None, None] - (blocks[..., None] * MOBA_BLOCK + jnp.arange(MOBA_BLOCK, dtype=jnp.int32))
    mask = ok[..., None] & (rel >= 0)
    hh = jnp.arange(H)[None, :, None, None, None]
    logits = logits.astype(jnp.float32) + bias_tab[hh, t5_bucket(rel)].astype(jnp.float32)
    p = masked_softmax(logits.reshape(B, H, Q, n), mask.reshape(B, H, Q, n))
    o = jnp.einsum('bhqn,bhqnd->bqhd', p.astype(vg.dtype), vg.reshape(B, H, Q, n, Dh))
    return o.reshape(B, Q, H * Dh)


def nsa_compress(rows, pe, w1, w2):
    B, L, Hk, Dh = rows.shape
    n_cmp = (L - CMP_LEN) // CMP_STRIDE + 1
    idx = jnp.arange(n_cmp)[:, None] * CMP_STRIDE + jnp.arange(CMP_LEN)[None, :]
    blk = rows[:, idx] + pe[:, None, :]
    blk = jnp.moveaxis(blk, 3, 2).reshape(B, n_cmp, Hk, CMP_LEN * Dh)
    return jax.nn.silu(blk @ w1) @ w2


def nsa_keys(kc_r, vc_r, ks_r, vs_r, pe, kw1, kw2, vw1, vw2):
    B, L, Hk, Dh = ks_r.shape
    kc = nsa_compress(kc_r, pe[0], kw1, kw2)
    vc = nsa_compress(vc_r, pe[1], vw1, vw2)
    n_cmp = kc.shape[1]
    start = jnp.arange(n_cmp, dtype=jnp.int32) * CMP_STRIDE
    cmp_end = start + (CMP_LEN - 1)
    n_slc = -(-L // SLC_BLOCK)
    bstart = jnp.arange(n_slc, dtype=jnp.int32) * SLC_BLOCK
    ov = ((start[:, None] < bstart[None, :] + SLC_BLOCK) & (cmp_end[:, None] >= bstart[None, :])).astype(jnp.float32)
    pad = ((0, 0), (0, n_slc * SLC_BLOCK - L), (0, 0), (0, 0))
    ksb = jnp.pad(ks_r, pad).reshape(B, n_slc, SLC_BLOCK, Hk, Dh)
    vsb = jnp.pad(vs_r, pad).reshape(B, n_slc, SLC_BLOCK, Hk, Dh)
    return kc, vc, cmp_end, ov, ksb, vsb


def nsa_attend(q, gates, qpos, kc, vc, cmp_end, ov, ksb, vsb, kw, vw, kw_pos, bias_tab):
    B, Q = q.shape[:2]
    G = NSA_GROUP
    sc = HEAD_DIM ** -0.5
    qg = q.reshape(B, Q, H_NSA_KV, G, HEAD_DIM)
    bt = bias_tab.reshape(H_NSA_KV, G, N_BUCKETS)
    lc = jnp.einsum('bqkgd,bnkd->bkgqn', qg, kc) * sc
    pc = masked_softmax(lc, (cmp_end[None, :] <= qpos[:, None])[None, None, None])
    o_cmp = jnp.einsum('bkgqn,bnkd->bqkgd', pc.astype(vc.dtype), vc)
    n_slc = ov.shape[1]
    imp = jnp.einsum('bkgqn,ns->bkqs', pc, ov)
    cur = qpos // SLC_BLOCK
    j = jnp.arange(n_slc, dtype=jnp.int32)[None, :]
    avail = j <= cur[:, None]
    forced = (j == 0) | (j == cur[:, None]) | (j == cur[:, None] - 1)
    imp = jnp.where(avail, jnp.where(forced, FORCE, imp), NEG)
    top = min(SLC_TOPN, n_slc)
    _, idx = lax.top_k(imp, top)
    bi = jnp.arange(B)[:, None, None, None]
    ki = jnp.arange(H_NSA_KV)[None, :, None, None]
    ks = ksb[bi, idx, :, ki]
    vs = vsb[bi, idx, :, ki]
    ls = jnp.einsum('bqkgd,bkqnsd->bkgqns', qg, ks) * sc
    rel_s = qpos[None, None, :, None, None] - (idx[..., None] * SLC_BLOCK + jnp.arange(SLC_BLOCK, dtype=jnp.int32))
    mask_s = (idx <= cur[None, None, :, None])[..., None] & (rel_s >= 0)
    kk = jnp.arange(H_NSA_KV)[None, :, None, None, None, None]
    gg = jnp.arange(G)[None, None, :, None, None, None]
    ls = ls.astype(jnp.float32) + bt[kk, gg, t5_bucket(rel_s)[:, :, None]].astype(jnp.float32)
    n = top * SLC_BLOCK
    ps = masked_softmax(ls.reshape(B, H_NSA_KV, G, Q, n), mask_s.reshape(B, H_NSA_KV, 1, Q, n))
    o_slc = jnp.einsum('bkgqn,bkqnd->bqkgd', ps.astype(vs.dtype), vs.reshape(B, H_NSA_KV, Q, n, HEAD_DIM))
    lw = jnp.einsum('bqkgd,bwkd->bkgqw', qg, kw) * sc
    rel_w = qpos[:, None] - kw_pos[None, :]
    mask_w = (rel_w >= 0) & (rel_w < WINDOW) & (kw_pos[None, :] >= 0)
    lw = lw.astype(jnp.float32) + bt[:, :, t5_bucket(rel_w)].astype(jnp.float32)
    pw = masked_softmax(lw, mask_w)
    o_win = jnp.einsum('bkgqw,bwkd->bqkgd', pw.astype(vw.dtype), vw)
    g = gates.reshape(B, Q, H_NSA_KV, G, 3)
    o = g[..., 0:1] * o_cmp + g[..., 1:2] * o_slc + g[..., 2:3] * o_win
    return o.reshape(B, Q, W_NSA)


def mix_out(x, gate, o_m, z_m, o_n, z_n, w_out_l):
    mixed = jnp.concatenate([o_m * jax.nn.silu(z_m), o_n * jax.nn.silu(z_n)], axis=-1)
    return x + gate[:, None, :] * (mixed @ w_out_l)


def prompt_layer(x, c, lp, bias_m, bias_n):
    w_ada_l, b_ada_l, gain_l, w_in_l, pe, kw1, kw2, vw1, vw2, w_out_l = lp
    B, T, _ = x.shape
    (gate, z_m, z_n, q_m, k_m, v_m, q_n, kc_r, vc_r, ks_r, vs_r, kw_r, vw_r, gates) = project(x, c, w_ada_l, b_ada_l, gain_l, w_in_l)
    kb, vb, kmean = moba_keys(k_m, v_m)
    nk = nsa_keys(kc_r, vc_r, ks_r, vs_r, pe, kw1, kw2, vw1, vw2)
    wpad = ((0, 0), (WINDOW, 0), (0, 0), (0, 0))
    kw_pad = jnp.pad(kw_r, wpad)
    vw_pad = jnp.pad(vw_r, wpad)
    nc = T // Q_BLOCK
    chunk = lambda a: jnp.moveaxis(a.reshape(B, nc, Q_BLOCK, *a.shape[2:]), 1, 0)
    pos = jnp.arange(T, dtype=jnp.int32).reshape(nc, Q_BLOCK)

    def step(args):
        qm, qn, g, p = args
        s0 = p[0]
        kw = lax.dynamic_slice_in_dim(kw_pad, s0, WINDOW + Q_BLOCK, axis=1)
        vw = lax.dynamic_slice_in_dim(vw_pad, s0, WINDOW + Q_BLOCK, axis=1)
        kw_pos = s0 - WINDOW + jnp.arange(WINDOW + Q_BLOCK, dtype=jnp.int32)
        o_m = moba_attend(qm, p, kb, vb, kmean, bias_m)
        o_n = nsa_attend(qn, g, p, *nk, kw, vw, kw_pos, bias_n)
        return o_m, o_n

    o_m, o_n = lax.map(step, (chunk(q_m), chunk(q_n), chunk(gates), pos))
    unchunk = lambda o: jnp.moveaxis(o, 0, 1).reshape(B, T, -1)
    x_new = mix_out(x, gate, unchunk(o_m), z_m, unchunk(o_n), z_n, w_out_l)
    wb = min(WINDOW, T)
    return (x_new, jnp.stack([k_m, v_m], axis=2), jnp.stack([kc_r, vc_r, ks_r, vs_r], axis=2),
            jnp.stack([kw_r, vw_r], axis=2)[:, T - wb:])


def sample_layer(x, c, cache_m, cache_n, win_state, page_table, lp, bias_m, bias_n):
    w_ada_l, b_ada_l, gain_l, w_in_l, pe, kw1, kw2, vw1, vw2, w_out_l = lp
    B, S, _ = x.shape
    P = page_table.shape[1] * cache_m.shape[1]
    (gate, z_m, z_n, q_m, k_m, v_m, q_n, kc_r, vc_r, ks_r, vs_r, kw_r, vw_r, gates) = project(x, c, w_ada_l, b_ada_l, gain_l, w_in_l)
    past_m = cache_m[page_table].reshape(B, P, 2, H_MOBA, HEAD_DIM)
    past_n = cache_n[page_table].reshape(B, P, 4, H_NSA_KV, HEAD_DIM)
    cat = lambda past, new: jnp.concatenate([past, new], axis=1)
    qpos = P + jnp.arange(S, dtype=jnp.int32)
    o_m = moba_attend(q_m, qpos, *moba_keys(cat(past_m[:, :, 0], k_m), cat(past_m[:, :, 1], v_m)), bias_m)
    nk = nsa_keys(cat(past_n[:, :, 0], kc_r), cat(past_n[:, :, 1], vc_r), cat(past_n[:, :, 2], ks_r),
                  cat(past_n[:, :, 3], vs_r), pe, kw1, kw2, vw1, vw2)
    win_all = cat(win_state, jnp.stack([kw_r, vw_r], axis=2))
    wb = win_state.shape[1]
    kw_pos = P - wb + jnp.arange(wb + S, dtype=jnp.int32)
    o_n = nsa_attend(q_n, gates, qpos, *nk, win_all[:, :, 0], win_all[:, :, 1], kw_pos, bias_n)
    x_new = mix_out(x, gate, o_m, z_m, o_n, z_n, w_out_l)
    return (x_new, jnp.stack([k_m, v_m], axis=2), jnp.stack([kc_r, vc_r, ks_r, vs_r], axis=2), win_all[:, S:])


def setup_inputs(seed: int = 0) -> dict:
    key = jax.random.key(seed)
    ks = jax.random.split(key, 24)
    n_pages = PAST_LEN // PAGE_SIZE
    used = DEC_BATCH * n_pages
    n_phys = used + max(1, used // 4)
    wb = min(WINDOW, PAST_LEN)
    nrm = lambda k, shape, s=1.0: jax.random.normal(k, shape, jnp.float32) * s
    page_table = jax.random.permutation(ks[7], n_phys)[:used].reshape(DEC_BATCH, n_pages).astype(jnp.int32)
    return {
        'x_prompt': nrm(ks[0], (BATCH, SEQ, D_MODEL)),
        'x_sample': nrm(ks[1], (DEC_BATCH, DEC_SEQ, D_MODEL)),
        'c_prompt': nrm(ks[2], (BATCH, D_MODEL)),
        'c_sample': nrm(ks[3], (DEC_BATCH, D_MODEL)),
        'cache_moba_kv': nrm(ks[4], (DEPTH, n_phys, PAGE_SIZE, 2, H_MOBA, HEAD_DIM)),
        'cache_nsa_kv': nrm(ks[5], (DEPTH, n_phys, PAGE_SIZE, 4, H_NSA_KV, HEAD_DIM)),
        'state_nsa_win': nrm(ks[6], (DEPTH, DEC_BATCH, wb, 2, H_NSA_KV, HEAD_DIM)),
        'page_table': page_table,
        'w_ada': nrm(ks[8], (DEPTH, D_MODEL, 3 * D_MODEL), 0.3 * D_MODEL ** -0.5),
        'b_ada': nrm(ks[9], (DEPTH, 3 * D_MODEL), 0.01),
        'norm_gain': 1.0 + nrm(ks[10], (DEPTH, D_MODEL), 0.01),
        'w_in': nrm(ks[11], (DEPTH, D_MODEL, D_IN), D_MODEL ** -0.5),
        'cmp_pe': nrm(ks[12], (DEPTH, 2, CMP_LEN, HEAD_DIM), 0.5),
        'cmp_k_w1': nrm(ks[13], (DEPTH, CMP_LEN * HEAD_DIM, CMP_HIDDEN), (CMP_LEN * HEAD_DIM) ** -0.5),
        'cmp_k_w2': nrm(ks[14], (DEPTH, CMP_HIDDEN, HEAD_DIM), CMP_HIDDEN ** -0.5),
        'cmp_v_w1': nrm(ks[15], (DEPTH, CMP_LEN * HEAD_DIM, CMP_HIDDEN), (CMP_LEN * HEAD_DIM) ** -0.5),
        'cmp_v_w2': nrm(ks[16], (DEPTH, CMP_HIDDEN, HEAD_DIM), CMP_HIDDEN ** -0.5),
        'w_out': nrm(ks[17], (DEPTH, D_MODEL, D_MODEL), D_MODEL ** -0.5),
        'rel_bias': nrm(ks[18], (N_BUCKETS, N_HEADS), 0.5),
        'final_gain': 1.0 + nrm(ks[19], (D_MODEL,), 0.01),
    }


def reference(x_prompt, x_sample, c_prompt, c_sample, cache_moba_kv, cache_nsa_kv, state_nsa_win, page_table,
              w_ada, b_ada, norm_gain, w_in, cmp_pe, cmp_k_w1, cmp_k_w2, cmp_v_w1, cmp_v_w2, w_out,
              rel_bias, final_gain):
    bias_m = rel_bias[:, :H_MOBA].T
    bias_n = rel_bias[:, H_MOBA:].T
    xp, xs = x_prompt, x_sample
    mkv_p, mkv_s, nkv_p, nkv_s, win_p, win_s = [], [], [], [], [], []
    for l in range(DEPTH):
        lp = (w_ada[l], b_ada[l], norm_gain[l], w_in[l], cmp_pe[l], cmp_k_w1[l], cmp_k_w2[l],
              cmp_v_w1[l], cmp_v_w2[l], w_out[l])
        xp, a, b, cw = prompt_layer(xp, c_prompt, lp, bias_m, bias_n)
        mkv_p.append(a)
        nkv_p.append(b)
        win_p.append(cw)
        xs, a, b, cw = sample_layer(xs, c_sample, cache_moba_kv[l], cache_nsa_kv[l], state_nsa_win[l],
                                    page_table, lp, bias_m, bias_n)
        mkv_s.append(a)
        nkv_s.append(b)
        win_s.append(cw)
    y_prompt = rmsnorm(xp, final_gain)
    y_sample = rmsnorm(xs, final_gain)
    moba_kv_prompt = jnp.stack(mkv_p)
    moba_kv_sample = jnp.stack(mkv_s)
    nsa_kv_prompt = jnp.stack(nkv_p)
    nsa_kv_sample = jnp.stack(nkv_s)
    win_prompt = jnp.stack(win_p)
    win_sample = jnp.stack(win_s)
    return (y_prompt, y_sample, moba_kv_prompt, moba_kv_sample, nsa_kv_prompt, nsa_kv_sample, win_prompt, win_sample)
```

```python
import contextlib
import math
import numpy as np
import ml_dtypes
import concourse.bass as bass
import concourse.mybir as mybir
from concourse.bass_utils import run_bass_kernel_spmd

F32 = mybir.dt.float32
BF16 = mybir.dt.bfloat16
I32 = mybir.dt.int32
AF = mybir.ActivationFunctionType
ALU = mybir.AluOpType
AX = mybir.AxisListType
NPBF = ml_dtypes.bfloat16

BIG = 30000.0
NEGS = -1e30
GL = 1664
HW = 1408
D = 1024


class G:
    __slots__ = ("w", "r", "excl")

    def __init__(self, excl=False):
        self.w = None
        self.r = []
        self.excl = excl


class Op:
    __slots__ = ("eng", "emit", "deps", "inc", "val", "dma", "slot", "dval", "prev")

    def __init__(self, eng, emit, dma):
        self.eng = eng
        self.emit = emit
        self.dma = dma
        self.deps = []
        self.inc = False
        self.val = 0
        self.slot = None
        self.dval = 0
        self.prev = 0


SERIAL = [True]
NSLOT = {"sp": 12, "act": 2, "pool": 12}
ENGS = ("pe", "act", "dve", "pool", "sp")


class Prog:
    def __init__(self, nc):
        self.nc = nc
        self.ops = {e: [] for e in ENGS}
        self.rr = {e: 0 for e in NSLOT}
        self.sv = {e: [0] * n for e, n in NSLOT.items()}
        self.pend = {e: [] for e in ENGS}
        self.lastdma = {}

    def barrier(self):
        deps = []
        for e in ("pe", "act", "dve", "pool"):
            for op in reversed(self.ops[e]):
                if not op.dma:
                    deps.append(op)
                    break
        deps += list(self.lastdma.values())
        for e in ENGS:
            self.pend[e] = list(deps)

    def add(self, eng, emit, reads=(), writes=(), dma=False):
        op = Op(eng, emit, dma)
        deps = []
        seen = set()

        def push(d):
            if d is None or id(d) in seen:
                return
            seen.add(id(d))
            if d.eng == "pe" and eng == "pe" and not d.dma and not dma:
                return
            deps.append(d)

        for g in reads:
            push(g.w)
            if g.excl:
                for r in g.r:
                    push(r)
        for g in writes:
            push(g.w)
            for r in g.r:
                push(r)
        if eng in ("act", "dve", "pool") and not dma and SERIAL[0]:
            for prev in reversed(self.ops[eng]):
                if not prev.dma:
                    if id(prev) not in seen:
                        seen.add(id(prev))
                        deps.append(prev)
                    break
        if self.pend[eng]:
            for d in self.pend[eng]:
                if d is not None and id(d) not in seen:
                    seen.add(id(d))
                    deps.append(d)
            self.pend[eng] = []
        op.deps = deps
        for d in deps:
            d.inc = True
        for g in reads:
            g.r.append(op)
        for g in writes:
            g.w = op
            g.r = []
        if dma:
            i = self.rr[eng]
            self.rr[eng] = (i + 1) % NSLOT[eng]
            op.slot = i
            op.prev = self.sv[eng][i]
            self.sv[eng][i] += 16
            op.dval = self.sv[eng][i]
            self.lastdma[(eng, i)] = op
        self.ops[eng].append(op)
        return op

    def setup(self, stack):
        nc = self.nc
        self.stack = stack
        self.csem = {e: [] for e in ("pe", "act", "dve", "pool")}
        self.dsem = {e: [stack.enter_context(nc.semaphore("d_%s%d" % (e, i))) for i in range(n)]
                     for e, n in NSLOT.items()}
        self.cursor = {e: 0 for e in ENGS}
        self.cval = {e: 0 for e in ENGS}
        self.waited = {e: {} for e in ENGS}

    def flush(self, final=False):
        nc = self.nc
        csem, dsem = self.csem, self.dsem
        ops = self.ops
        sv = self.sv
        EP = 12000
        for e in ("pe", "act", "dve", "pool"):
            for op in ops[e][self.cursor[e]:]:
                if not op.dma:
                    self.cval[e] += 1
                    op.val = self.cval[e]

        def csem_of(eng, val):
            ep = (val - 1) // EP
            lst = csem[eng]
            while len(lst) <= ep:
                lst.append(self.stack.enter_context(nc.semaphore("c_%s%d" % (eng, len(lst)))))
            return lst[ep], val - ep * EP, ("c", eng, ep)

        def sig(d):
            if d.dma:
                return dsem[d.eng][d.slot], d.dval, ("d", d.eng, d.slot)
            return csem_of(d.eng, d.val)

        def run(engname, e, fin=False):
            waited = self.waited[engname]

            def w(sem, val, key):
                if waited.get(key, 0) < val:
                    e.wait_ge(sem, val)
                    waited[key] = val

            for op in ops[engname][self.cursor[engname]:]:
                for d in op.deps:
                    s, v, k = sig(d)
                    w(s, v, k)
                if op.dma and op.prev > 0:
                    w(dsem[engname][op.slot], op.prev, ("d", engname, op.slot))
                ins = op.emit(e)
                if op.dma:
                    ins.then_inc(dsem[engname][op.slot], 16)
                else:
                    ins.then_inc(csem_of(engname, op.val)[0], 1)
            self.cursor[engname] = len(ops[engname])
            if fin:
                for qe, n in NSLOT.items():
                    for i in range(n):
                        if sv[qe][i] > 0:
                            w(dsem[qe][i], sv[qe][i], ("d", qe, i))

        with nc.Block() as block:
            @block.sync
            def _(e):
                run("sp", e, fin=final)

            @block.scalar
            def _(e):
                run("act", e)

            @block.vector
            def _(e):
                run("dve", e)

            @block.gpsimd
            def _(e):
                run("pool", e)

            @block.tensor
            def _(e):
                run("pe", e)

    def emit_all(self, stack):
        self.flush(final=True)


def t5_bucket_np(rel):
    n = np.maximum(rel, 0)
    nf = np.maximum(n, 1).astype(np.float32)
    large = 16 + (np.log(nf / np.float32(16)) / np.float32(math.log(8.0)) * np.float32(16)).astype(np.int32)
    return np.where(n < 16, n, np.minimum(large, 31))


def host_consts(cfg):
    T, PAST = cfg["T"], cfg["PAST"]
    TK = cfg["TK"]
    c = {}
    c["idb"] = np.eye(128, dtype=np.float32).astype(NPBF)
    c["jb"] = np.eye(128, dtype=np.float32)[::-1].copy().astype(NPBF)
    c["ones"] = np.ones((128, 64), np.float32)
    sel6 = np.zeros((6, 6 * 64), np.float32)
    for r in range(6):
        sel6[r, r * 64:(r + 1) * 64] = 1.0
    c["sel6"] = sel6
    k = np.arange(TK)
    kmax = max(T, PAST)
    indm = np.zeros((64, TK), np.float32)
    inds = np.zeros((64, TK), np.float32)
    for r in range(32):
        indm[r] = ((k // 256) == r) & (k < kmax)
    for r in range(64):
        inds[r] = (((k // 64) % 64) == r) & (k < kmax)
    c["indm"] = indm.astype(NPBF)
    c["inds"] = inds.astype(NPBF)
    ni = np.arange(128)[:, None]
    qi = np.arange(512)[None, :]
    cm = np.zeros((128, 5, 512), np.float32)
    for j, dl in enumerate([0, -1, -2, -3, -4]):
        cm[:, j, :] = np.where(qi - 16 * ni >= 512 * dl + 31, 0.0, -BIG)
    c["cmask"] = cm.astype(NPBF)
    ovx = np.zeros((128, 4, 129), np.float32)
    for cc in range(4):
        for n_ in range(128):
            n = 128 * cc + n_
            for s in (n // 4, (n + 1) // 4 if n % 4 == 3 else -1):
                if 0 <= s < 128:
                    ovx[n_, cc, s] = 1.0
        ovx[:, cc, 128] = 1.0
    c["ovx"] = ovx.astype(NPBF)
    pm = np.ones((128, 256), np.float32)
    pa = np.zeros((128, 256), np.float32)
    for p in range(128):
        cur = 0 if p < 64 else 1
        for x in range(256):
            j = x - 128
            if j > cur:
                pm[p, x] = 0.0
                pa[p, x] = NEGS
            elif j == cur or j == cur - 1:
                pm[p, x] = 0.0
                pa[p, x] = 1e9
    c["patm"] = pm
    c["pata"] = pa
    m = np.arange(GL)
    rel = m - 511
    oh = np.zeros((32, GL), np.float32)
    b = t5_bucket_np(rel)
    for i in range(GL):
        if rel[i] >= 0:
            oh[b[i], i] = 1.0
    c["oh"] = oh
    addm = np.zeros((6, GL), np.float32)
    addm[:, rel < 0] = -BIG
    addm[4:6, rel >= 512] = -BIG
    c["addm"] = addm
    c["iop"] = np.arange(128, dtype=np.int32).reshape(128, 1)
    return c


FMCH = {}
_o = 0
for _n, _w in [("qmA", 64), ("qmB", 64), ("kmA", 64), ("kmB", 64), ("zmA", 64), ("zmB", 64),
               ("qnA", 64), ("qnB", 64), ("znA", 64), ("znB", 64), ("ks", 64), ("kw", 64),
               ("kcvc", 128), ("g", 8), ("qnC", 64), ("qnD", 64)]:
    FMCH[_n] = (_o, _w)
    _o += _w
FMC = _o
TMC = 640


def proj_offsets():
    sizes = [512] * 4 + [512] + [128] * 6 + [24, 512]
    offs = np.concatenate([[0], np.cumsum(sizes)])
    names = ["q_m", "k_m", "v_m", "z_m", "q_n", "kc", "vc", "ks", "vs", "kw", "vw", "g_n", "z_n"]
    return {n: int(o) for n, o in zip(names, offs)}


def fm_cols(hp):
    po = proj_offsets()
    A, B = 2 * hp, 2 * hp + 1
    kv = hp // 2
    sib = hp ^ 1
    C, Dh = 2 * sib, 2 * sib + 1
    r64 = lambda base, h: list(range(base + 64 * h, base + 64 * h + 64))
    cols = []
    cols += r64(po["q_m"], A) + r64(po["q_m"], B) + r64(po["k_m"], A) + r64(po["k_m"], B)
    cols += r64(po["z_m"], A) + r64(po["z_m"], B)
    cols += r64(po["q_n"], A) + r64(po["q_n"], B) + r64(po["z_n"], A) + r64(po["z_n"], B)
    cols += r64(po["ks"], kv) + r64(po["kw"], kv)
    cols += r64(po["kc"], kv) + r64(po["vc"], kv)
    cols += list(range(po["g_n"] + 6 * hp, po["g_n"] + 6 * hp + 6)) + [po["g_n"], po["g_n"]]
    cols += r64(po["q_n"], C) + r64(po["q_n"], Dh)
    assert len(cols) == FMC
    return cols


def tm_cols(hp):
    po = proj_offsets()
    A, B = 2 * hp, 2 * hp + 1
    kv = hp // 2
    r64 = lambda base, h: list(range(base + 64 * h, base + 64 * h + 64))
    cols = r64(po["v_m"], A) + r64(po["v_m"], B) + r64(po["k_m"], A) + r64(po["k_m"], B)
    cols += r64(po["kc"], kv) + r64(po["vc"], kv) + r64(po["ks"], kv) + r64(po["vs"], kv)
    cols += r64(po["kw"], kv) + r64(po["vw"], kv)
    assert len(cols) == TMC
    return cols


def build(cfg):
    T, PAST, NS, NHP = cfg["T"], cfg["PAST"], cfg["NS"], cfg["NHP"]
    TK = cfg["TK"]
    NTK = TK // 128
    NPHYS = cfg["NPHYS"]
    NPG = PAST // 128
    NT = T // 512
    WB = cfg["WB"]
    nc = bass.Bass("TRN2", target_bir_lowering=False)
    P = Prog(nc)

    def din(name, shape, dt=F32):
        return nc.dram_tensor(name, list(shape), dt, kind="ExternalInput")

    def dout(name, shape, dt=F32):
        return nc.dram_tensor(name, list(shape), dt, kind="ExternalOutput")

    x_d = din("x", [T, D]).ap()
    xs_d = din("xs", [NS * 4, D]).ap()
    cT_d = din("cT", [D, 144]).ap()
    wada_d = din("w_ada", [D, 3 * D]).ap()
    bada_d = din("b_ada", [3 * D]).ap()
    gain_d = din("gain", [D]).ap()
    fgain_d = din("fgain", [D]).ap()
    wfm_d = din("wfm", [NHP, D, FMC]).ap()
    wtm_d = din("wtm", [NHP, D, TMC]).ap()
    w1_d = din("w1", [128, 32, 128]).ap()
    w2_d = din("w2", [128, 128]).ap()
    pet_d = din("pet", [128, 32]).ap()
    wout_d = din("wout", [64, 16, D]).ap()
    tabsel_d = din("tabsel", [32, NHP * 6]).ap()
    tab31_d = din("tab31", [NHP * 4]).ap()
    cm_d = [din("cache_m%d" % i, [NPHYS * 128, 256]).ap() for i in range(4)]
    cn_d = [din("cache_n%d" % i, [NPHYS * 128, 256]).ap() for i in range(2)]
    win_d = din("win", [NS, WB, 256]).ap()
    ptab_d = din("ptab", [NS, NPG], I32).ap()
    hc = host_consts(cfg)
    cdram = {}
    for k_, v_ in hc.items():
        dt_ = BF16 if v_.dtype == NPBF else (I32 if v_.dtype == np.int32 else F32)
        cdram[k_] = din("c_" + k_, v_.shape, dt_).ap()

    y_d = dout("y", [T, D]).ap()
    ys_d = dout("ys", [NS * 4, D]).ap()
    om_d = dout("om", [NHP, T, 256]).ap()
    on_d = dout("on", [NHP, T, 256]).ap()
    ow_d = dout("ow", [NHP, T, 128]).ap()
    oms_d = dout("oms", [NHP, NS * 4, 256]).ap()
    ons_d = dout("ons", [NHP, NS * 4, 256]).ap()
    ows_d = dout("ows", [NHP, NS * 4, 128]).ap()
    owin_d = dout("owin", [NS, WB - 4, 256]).ap()
    gd_h = nc.dram_tensor("gd", [NHP * 6, GL], BF16, kind="Internal")
    gd_d = gd_h.ap()
    mixd_d = nc.dram_tensor("mixd", [16, 64, T], BF16, kind="Internal").ap()

    with contextlib.ExitStack() as st:
        cur = [st]
        P.setup(st)

        def sb(name, shape, dt):
            return cur[0].enter_context(nc.sbuf_tensor("s_" + name, list(shape), dt)), G()

        def psum(name, shape, dt):
            return st.enter_context(nc.psum_tensor(name, list(shape), dt)), G(excl=True)

        PS_S = [psum("ps_s%d" % i, [128, 512], F32) for i in range(2)]
        PS_O = [psum("ps_o%d" % i, [128, 512], F32) for i in range(2)]
        PS_X = [psum("ps_x%d" % i, [128, 512], F32) for i in range(2)]
        PS_T = [psum("ps_t%d" % i, [128, 1024], BF16) for i in range(2)]
        cnt = {"s": 0, "o": 0, "x": 0, "t": 0, "pt": 0}

        def nxt(kind, lst):
            i = cnt[kind]
            cnt[kind] = (i + 1) % len(lst)
            return lst[i]

        def cload(name, shape, dt, eng="sp"):
            t, g = sb("k_" + name, shape, dt)
            P.add(eng, lambda e: e.dma_start(out=t[:], in_=cdram[name]), writes=[g], dma=True)
            return t, g

        IDB, gIDB = cload("idb", [128, 128], BF16)
        JB, gJB = cload("jb", [128, 128], BF16)
        ONES, gONES = cload("ones", [128, 64], F32)
        SEL6, gSEL6 = cload("sel6", [6, 384], F32)
        CMASK, gCMASK = cload("cmask", [128, 5, 512], BF16)
        OVX, gOVX = cload("ovx", [128, 4, 129], BF16)
        PATM, gPATM = cload("patm", [128, 256], F32)
        PATA, gPATA = cload("pata", [128, 256], F32)
        IOP, gIOP = cload("iop", [128, 1], I32)
        EPS, gEPS = sb("eps", [128, 1], F32)
        P.add("dve", lambda e: e.memset(EPS[:], 1e-6), writes=[gEPS])

        MIXS = sb("mixs", [64, 16, 16], BF16)
        stA = contextlib.ExitStack()
        cur[0] = stA
        KA_M = [sb("ka_m%d" % i, [128, TK], BF16) for i in range(2)]
        VA_M = [sb("va_m%d" % i, [128, NTK, 65], BF16) for i in range(2)]
        KA_S = sb("ka_s", [128, TK], BF16)
        VA_S = sb("va_s", [128, NTK, 65], BF16)
        KWR = sb("kwr", [128, 9 * 128], BF16)
        VWR = sb("vwr", [128, 9, 65], BF16)
        KCVC = sb("kcvc", [128, TK], BF16)
        NCT = max(1, (max(T, PAST) + 2047) // 2048)
        KCMP = sb("kcmp", [64, NCT * 128], BF16)
        VC = sb("vc", [128, NCT, 65], BF16)

        def init_resident():
            for (t, g) in KA_M:
                P.add("pool", lambda e, t=t: e.memset(t[:], 0.0), writes=[g])
                P.add("sp", lambda e, t=t: e.dma_start(out=t[64:128, :], in_=cdram["indm"]), writes=[g], dma=True)
            for (t, g) in VA_M + [VA_S, VWR, VC]:
                P.add("pool", lambda e, t=t: e.memset(t[:], 0.0), writes=[g])
                P.add("pool", lambda e, t=t: e.memset(t[:, :, 64:65], 1.0), writes=[g])
            t, g = KA_S
            P.add("pool", lambda e: e.memset(KA_S[0][:], 0.0), writes=[g])
            P.add("sp", lambda e: e.dma_start(out=KA_S[0][64:128, :], in_=cdram["inds"]), writes=[g], dma=True)
            for (t, g) in (KWR, KCVC, KCMP):
                P.add("pool", lambda e, t=t: e.memset(t[:], 0.0), writes=[g])

        init_resident()

        WFM = sb("wfm", [128, 8, FMC], BF16)
        WTM = sb("wtm", [128, 8, TMC], BF16)
        W1 = sb("w1", [128, 32, 128], BF16)
        W2 = sb("w2", [128, 128], BF16)
        PET = sb("pet", [128, 32], BF16)
        PEB = sb("peb", [128, 2], F32)
        HKW = [1024, 1024, 1024, 1024, HW, HW]
        HK = [sb("hk%d" % i, [128, HKW[i]], BF16) for i in range(6)]
        FARB = sb("farb", [128, NHP * 4], F32)
        TAB = sb("tab", [32, NHP * 6], F32)
        P.add("pool", lambda e: e.dma_start(out=W1[0][:], in_=w1_d), writes=[W1[1]], dma=True)
        P.add("pool", lambda e: e.dma_start(out=W2[0][:], in_=w2_d), writes=[W2[1]], dma=True)
        P.add("pool", lambda e: e.dma_start(out=PET[0][:], in_=pet_d), writes=[PET[1]], dma=True)
        P.add("sp", lambda e: e.dma_start(out=FARB[0][:], in_=tab31_d.partition_broadcast(128)), writes=[FARB[1]], dma=True)
        P.add("sp", lambda e: e.dma_start(out=TAB[0][:], in_=tabsel_d), writes=[TAB[1]], dma=True)

        with contextlib.ExitStack() as st2:
            OH = st2.enter_context(nc.sbuf_tensor("oh", [32, GL], F32)); gOH = G()
            ADDM = st2.enter_context(nc.sbuf_tensor("addm", [6, GL], F32)); gADDM = G()
            GV = st2.enter_context(nc.sbuf_tensor("gv", [6, GL], BF16)); gGV = G()
            P.add("sp", lambda e: e.dma_start(out=OH[:], in_=cdram["oh"]), writes=[gOH], dma=True)
            P.add("sp", lambda e: e.dma_start(out=ADDM[:], in_=cdram["addm"]), writes=[gADDM], dma=True)
            for hp in range(NHP):
                for c0 in range(0, GL, 512):
                    w_ = min(512, GL - c0)
                    px, gpx = nxt("x", PS_X)
                    P.add("pe", lambda e, px=px, c0=c0, w_=w_, hp=hp: e.matmul(
                        px[0:6, 0:w_], lhsT=TAB[0][:, hp * 6:hp * 6 + 6], rhs=OH[:, c0:c0 + w_], start=True, stop=True),
                        reads=[TAB[1], gOH], writes=[gpx])
                    P.add("dve", lambda e, px=px, c0=c0, w_=w_: e.tensor_tensor(
                        out=GV[:, c0:c0 + w_], in0=px[0:6, 0:w_], in1=ADDM[:, c0:c0 + w_], op=ALU.add),
                        reads=[gpx, gADDM], writes=[gGV])
                gGD = G()
                P.add("sp", lambda e, hp=hp: e.dma_start(out=gd_d[hp * 6:hp * 6 + 6, :], in_=GV[:]),
                      reads=[gGV], writes=[gGD], dma=True)
                cfg.setdefault("_ggd", []).append(gGD)
            P.flush()
        gGDs = cfg.pop("_ggd")
        P.barrier()

        SHT = sb("sht", [128, 8, 8], F32)
        SCT = sb("sct", [128, 8, 8], F32)
        gate_tiles = {}

        def adaln(part):
            with contextlib.ExitStack() as st2:
                CT = st2.enter_context(nc.sbuf_tensor("ct_" + part, [128, 8, 144], F32)); gCT = G()
                WA = st2.enter_context(nc.sbuf_tensor("wa_" + part, [128, 8, 512], F32)); gWA = G()
                BAT = st2.enter_context(nc.sbuf_tensor("bat_" + part, [128, 24], F32)); gBAT = G()
                GNT = st2.enter_context(nc.sbuf_tensor("gnt_" + part, [128, 8], F32)); gGNT = G()
                P.add("sp", lambda e: e.dma_start(out=CT[:], in_=cT_d.rearrange("(k p) n -> p k n", p=128)), writes=[gCT], dma=True)
                P.add("sp", lambda e: e.dma_start(out=BAT[:], in_=bada_d.rearrange("(k p) -> p k", p=128), allow_slow_non_contiguous=True), writes=[gBAT], dma=True)
                P.add("sp", lambda e: e.dma_start(out=GNT[:], in_=gain_d.rearrange("(k p) -> p k", p=128), allow_slow_non_contiguous=True), writes=[gGNT], dma=True)
                if part == "gate":
                    BAB = st2.enter_context(nc.sbuf_tensor("bab", [128, D], F32)); gBAB = G()
                    GATEP, GATES, FG = gate_tiles["p"], gate_tiles["s"], gate_tiles["f"]
                    P.add("sp", lambda e: e.dma_start(out=BAB[:], in_=bada_d[2 * D:3 * D].partition_broadcast(128)), writes=[gBAB], dma=True)
                    P.add("sp", lambda e: e.dma_start(out=FG[0][:], in_=fgain_d.partition_broadcast(128)), writes=[FG[1]], dma=True)
                for j in (range(4) if part == "fm" else range(4, 6)):
                    P.add("sp", lambda e, j=j: e.dma_start(
                        out=WA[:], in_=wada_d[:, j * 512:(j + 1) * 512].rearrange("(k p) n -> p k n", p=128)),
                        writes=[gWA], dma=True)
                    if j < 4:
                        for f in range(4):
                            fc = j * 4 + f
                            px, gpx = nxt("x", PS_X)
                            for k in range(8):
                                P.add("pe", lambda e, px=px, k=k, f=f: e.matmul(
                                    px[:, 0:144], lhsT=WA[:, k, f * 128:(f + 1) * 128], rhs=CT[:, k, :],
                                    start=(k == 0), stop=(k == 7)), reads=[gWA, gCT], writes=[gpx])
                            dst = SHT if fc < 8 else SCT
                            fcc = fc % 8
                            for (dc, sc0, sc1, stp) in ((0, 0, 1, 1), (1, 128, 144, 4)):
                                ncol = 1 if dc == 0 else 4
                                if fc < 8:
                                    P.add("dve", lambda e, px=px, fc=fc, fcc=fcc, dc=dc, sc0=sc0, sc1=sc1, stp=stp, ncol=ncol: e.tensor_scalar(
                                        out=SHT[0][:, fcc, dc:dc + ncol], in0=px[:, sc0:sc1:stp], scalar1=BAT[:, fc:fc + 1], scalar2=None, op0=ALU.add),
                                        reads=[gpx, gBAT], writes=[SHT[1]])
                                else:
                                    P.add("dve", lambda e, px=px, fc=fc, fcc=fcc, dc=dc, sc0=sc0, sc1=sc1, stp=stp, ncol=ncol: e.tensor_scalar(
                                        out=SCT[0][:, fcc, dc:dc + ncol], in0=px[:, sc0:sc1:stp], scalar1=BAT[:, fc:fc + 1], scalar2=1.0,
                                        op0=ALU.add, op1=ALU.add), reads=[gpx, gBAT], writes=[SCT[1]])
                            if fc >= 8:
                                P.add("dve", lambda e, fcc=fcc: e.tensor_scalar(
                                    out=SCT[0][:, fcc, 0:5], in0=SCT[0][:, fcc, 0:5], scalar1=GNT[:, fcc:fcc + 1], scalar2=None,
                                    op0=ALU.mult), reads=[gGNT, SCT[1]], writes=[SCT[1]])
                    else:
                        oc = j - 4
                        px, gpx = nxt("x", PS_X)
                        for k in range(8):
                            P.add("pe", lambda e, px=px, k=k: e.matmul(
                                px[:, 0:512], lhsT=CT[:, k, 0:128], rhs=WA[:, k, :], start=(k == 0), stop=(k == 7)),
                                reads=[gWA, gCT], writes=[gpx])
                        P.add("dve", lambda e, px=px, oc=oc: e.tensor_tensor(
                            out=GATEP[0][:, oc * 512:(oc + 1) * 512], in0=px[:, 0:512], in1=BAB[:, oc * 512:(oc + 1) * 512], op=ALU.add),
                            reads=[gpx, gBAB], writes=[GATEP[1]])
                        px, gpx = nxt("x", PS_X)
                        for k in range(8):
                            P.add("pe", lambda e, px=px, k=k: e.matmul(
                                px[0:16, 0:512], lhsT=CT[:, k, 128:144], rhs=WA[:, k, :], start=(k == 0), stop=(k == 7)),
                                reads=[gWA, gCT], writes=[gpx])
                        P.add("dve", lambda e, px=px, oc=oc: e.tensor_tensor(
                            out=GATES[0][:, oc * 512:(oc + 1) * 512], in0=px[0:16, 0:512], in1=BAB[0:16, oc * 512:(oc + 1) * 512], op=ALU.add),
                            reads=[gpx, gBAB], writes=[GATES[1]])
                P.flush()
            P.barrier()

        adaln("fm")

        XT_ = sb("xt", [128, D], F32)
        XN = sb("xn", [128, D], BF16)
        SSQ = sb("ssq", [128, 4], F32)
        HT = sb("ht", [128, 8, 512], BF16)
        QA_M = [sb("qa_m%d" % i, [128, 512], BF16) for i in range(2)]
        QA_S = [[sb("qa_s%d%d" % (i, v), [128, 512], BF16) for v in range(2)] for i in range(2)]
        QSIB = [sb("qsib%d" % i, [64, 512], BF16) for i in range(2)]
        ZT = [sb("zt%d" % i, [64, 512], BF16) for i in range(4)]
        GT = sb("gt", [8, 512], F32)
        KMF = [sb("kmf%d" % i, [64, 40], BF16) for i in range(2)]
        KMFF = sb("kmff", [64, 40], F32)
        PTB = [sb("ptb%d" % i, [128, 512], BF16) for i in range(2)]
        OTM = sb("otm", [128, TMC], F32)
        IMPACC = sb("impacc", [128, 4, 128], F32)
        SC = sb("sc", [128, 40], F32)
        M8 = sb("m8", [128, 16], F32)
        THR = sb("thr", [128, 2], F32)
        IMPM = sb("impm", [128, 128], F32)
        IMP2 = sb("imp2", [128, 128], F32)
        MBP = sb("mbp", [128, 256], BF16)
        RS = sb("rs", [128, 512], F32)
        RSI = sb("rsi", [128, 2], F32)
        BCZ = sb("bcz", [64, 512], F32)
        TMP = sb("tmp", [64, 512], F32)
        ACCS = [sb("acc%d" % i, [64, 512], F32) for i in range(2)]
        MIXB = sb("mixb", [64, 512], BF16)
        AKV = sb("akv", [128, 256], BF16)
        SG = sb("sg", [128, 512], F32)
        for (t, g) in (MBP,):
            P.add("pool", lambda e, t=t: e.memset(t[:], 0.0), writes=[g])
        for lst in (QA_M, QA_S[0], QA_S[1]):
            for (t, g) in lst:
                P.add("pool", lambda e, t=t: e.memset(t[:], 0.0), writes=[g])

        def load_hp(hp):
            P.add("pool", lambda e: e.dma_start(out=WFM[0][:], in_=wfm_d[hp].rearrange("(k p) n -> p k n", p=128)),
                  writes=[WFM[1]], dma=True)
            P.add("pool", lambda e: e.dma_start(out=WTM[0][:], in_=wtm_d[hp].rearrange("(k p) n -> p k n", p=128)),
                  writes=[WTM[1]], dma=True)
            for v in range(6):
                src = bass.AP(gd_h, (hp * 6 + v) * GL, [[1, 128], [1, HKW[v]]])
                P.add("sp", lambda e, v=v, src=src: e.dma_start(out=HK[v][0][:], in_=src),
                      reads=[gGDs[hp]], writes=[HK[v][1]], dma=True)

        def peb_compute():
            for kvi in range(2):
                px, gpx = nxt("x", PS_X)
                lo = 64 * kvi
                for l in range(32):
                    P.add("pe", lambda e, px=px, l=l, lo=lo: e.matmul(
                        px[:, 0:1], lhsT=W1[0][lo:lo + 64, l, :], rhs=PET[0][lo:lo + 64, l:l + 1],
                        start=(l == 0), stop=(l == 31)), reads=[W1[1], PET[1]], writes=[gpx])
                P.add("dve", lambda e, px=px, kvi=kvi: e.tensor_copy(out=PEB[0][:, kvi:kvi + 1], in_=px[:, 0:1]),
                      reads=[gpx], writes=[PEB[1]])

        peb_compute()

        def project_tile(hp, N, q0, ht_ready, subs, om_dst, on_dst, ow_dst, ring_kt):
            HTt, gHT = HT

            def fm(name, evac):
                off, w_ = FMCH[name]
                w_ = max(w_, 64)
                px, gpx = nxt("x", PS_X)
                for k in range(8):
                    P.add("pe", lambda e, px=px, k=k, off=off, w_=w_: e.matmul(
                        px[0:w_, 0:N], lhsT=WFM[0][:, k, off:off + w_], rhs=HTt[:, k, 0:N],
                        start=(k == 0), stop=(k == 7)), reads=[WFM[1], gHT], writes=[gpx])
                evac(px, gpx)

            kc0 = q0
            for i, nm in enumerate(("qmA", "qmB")):
                def ev(px, gpx, i=i):
                    P.add("act", lambda e: e.activation(out=QA_M[i][0][0:64, 0:N], in_=px[0:64, 0:N], func=AF.Copy, scale=0.125),
                          reads=[gpx], writes=[QA_M[i][1]])
                fm(nm, ev)
            for i, nm in enumerate(("kmA", "kmB")):
                def ev(px, gpx, i=i):
                    P.add("act", lambda e: e.activation(out=KA_M[i][0][0:64, kc0:kc0 + N], in_=px[0:64, 0:N], func=AF.Copy),
                          reads=[gpx], writes=[KA_M[i][1]])
                fm(nm, ev)
            if cfg.get("STAGE", 99) < 2.45:
                return
            for i, nm in enumerate(("zmA", "zmB", "znA", "znB")):
                def ev(px, gpx, i=i):
                    if cfg.get("STAGE", 99) >= 2.47:
                        P.add("act", lambda e: e.activation(out=SG[0][0:64, 0:N], in_=px[0:64, 0:N], func=AF.Sigmoid),
                              reads=[gpx], writes=[SG[1]])
                    if cfg.get("STAGE", 99) >= 2.49:
                        if cfg.get("VAR", 0) == 1:
                            P.add("dve", lambda e: e.tensor_tensor(out=TMP[0][:, 0:N], in0=px[0:64, 0:N], in1=SG[0][0:64, 0:N], op=ALU.mult),
                                  reads=[gpx, SG[1]], writes=[TMP[1]])
                        elif cfg.get("VAR", 0) == 2:
                            P.add("dve", lambda e: e.tensor_copy(out=TMP[0][:, 0:N], in_=px[0:64, 0:N]), reads=[gpx], writes=[TMP[1]])
                            P.add("dve", lambda e: e.tensor_tensor(out=ZT[i][0][:, 0:N], in0=TMP[0][:, 0:N], in1=SG[0][0:64, 0:N], op=ALU.mult),
                                  reads=[TMP[1], SG[1]], writes=[ZT[i][1]])
                        elif cfg.get("VAR", 0) == 3:
                            P.add("dve", lambda e: e.tensor_tensor(out=ZT[i][0][:, 0:N], in0=px[0:64, 0:N], in1=SG[0][0:64, 0:N], op=ALU.mult),
                                  reads=[gpx, SG[1]], writes=[ZT[i][1], TMP[1]])
                        elif cfg.get("VAR", 0) == 4:
                            P.add("dve", lambda e: e.tensor_tensor(out=ZT[0][0][:, 0:N], in0=px[0:64, 0:N], in1=SG[0][0:64, 0:N], op=ALU.mult),
                                  reads=[gpx, SG[1]], writes=[ZT[0][1]])
                        else:
                            P.add("dve", lambda e: e.tensor_tensor(out=ZT[i][0][:, 0:N], in0=px[0:64, 0:N], in1=SG[0][0:64, 0:N], op=ALU.mult),
                                  reads=[gpx, SG[1]], writes=[ZT[i][1]])
                fm(nm, ev)
            if cfg.get("STAGE", 99) < 2.6:
                return
            for i, nm in enumerate(("qnA", "qnB")):
                def ev(px, gpx, i=i):
                    for v in range(2):
                        P.add("act" if v == 0 else "dve", (lambda e, v=v: e.activation(
                            out=QA_S[i][v][0][0:64, 0:N], in_=px[0:64, 0:N], func=AF.Copy, scale=0.125)) if v == 0 else
                            (lambda e, v=v: e.tensor_scalar(out=QA_S[i][v][0][0:64, 0:N], in0=px[0:64, 0:N], scalar1=0.125,
                                                            scalar2=None, op0=ALU.mult)),
                            reads=[gpx], writes=[QA_S[i][v][1]])
                fm(nm, ev)
            if cfg.get("STAGE", 99) < 2.61:
                return
            for i, nm in enumerate(("qnC", "qnD")):
                def ev(px, gpx, i=i):
                    P.add("act", lambda e: e.activation(out=QSIB[i][0][:, 0:N], in_=px[0:64, 0:N], func=AF.Copy, scale=0.125),
                          reads=[gpx], writes=[QSIB[i][1]])
                fm(nm, ev)

            if cfg.get("STAGE", 99) < 2.62:
                return

            def ev(px, gpx):
                P.add("act", lambda e: e.activation(out=KA_S[0][0:64, kc0:kc0 + N], in_=px[0:64, 0:N], func=AF.Copy),
                      reads=[gpx], writes=[KA_S[1]])
            fm("ks", ev)
            if cfg.get("STAGE", 99) < 2.63:
                return

            def ev(px, gpx):
                for j in range(0, N, 128):
                    n_ = min(128, N - j)
                    r0 = ((ring_kt + j // 128) % 9) * 128
                    P.add("dve", lambda e, j=j, n_=n_, r0=r0: e.tensor_copy(out=KWR[0][0:64, r0:r0 + n_], in_=px[0:64, j:j + n_]),
                          reads=[gpx], writes=[KWR[1]])
            fm("kw", ev)

            if cfg.get("STAGE", 99) < 2.64:
                return

            def ev(px, gpx):
                P.add("act", lambda e: e.activation(out=KCVC[0][:, kc0:kc0 + N], in_=px[:, 0:N], func=AF.Copy),
                      reads=[gpx], writes=[KCVC[1]])
            fm("kcvc", ev)

            if cfg.get("STAGE", 99) < 2.645:
                return

            def ev(px, gpx):
                P.add("act", lambda e: e.activation(out=GT[0][:, 0:N], in_=px[0:8, 0:N], func=AF.Sigmoid),
                      reads=[gpx], writes=[GT[1]])
            fm("g", ev)

            if cfg.get("STAGE", 99) < 2.7:
                return
            for si, (rows, c0) in enumerate(subs):
                pa, gpa = nxt("x", PS_X)
                pb, gpb = nxt("x", PS_X)
                for k in range(8):
                    P.add("pe", lambda e, pa=pa, k=k, rows=rows, c0=c0: e.matmul(
                        pa[0:rows, 0:512], lhsT=HTt[:, k, c0:c0 + rows], rhs=WTM[0][:, k, 0:512],
                        start=(k == 0), stop=(k == 7)), reads=[WTM[1], gHT], writes=[gpa])
                for k in range(8):
                    P.add("pe", lambda e, pb=pb, k=k, rows=rows, c0=c0: e.matmul(
                        pb[0:rows, 0:128], lhsT=HTt[:, k, c0:c0 + rows], rhs=WTM[0][:, k, 512:640],
                        start=(k == 0), stop=(k == 7)), reads=[WTM[1], gHT], writes=[gpb])
                P.add("act", lambda e, pa=pa, rows=rows: e.activation(out=OTM[0][0:rows, 0:512], in_=pa[0:rows, 0:512], func=AF.Copy),
                      reads=[gpa], writes=[OTM[1]])
                P.add("dve", lambda e, pb=pb, rows=rows: e.tensor_copy(out=OTM[0][0:rows, 512:640], in_=pb[0:rows, 0:128]),
                      reads=[gpb], writes=[OTM[1]])
                if cfg.get("STAGE", 99) < 2.8:
                    continue
                kt = (q0 + c0) // 128
                r_ = (q0 + c0) % 128
                assert r_ == 0
                for i in range(2):
                    P.add("pool", lambda e, i=i, kt=kt, rows=rows: e.tensor_copy(
                        out=VA_M[i][0][0:rows, kt, 0:64], in_=OTM[0][0:rows, 64 * i:64 * i + 64]),
                        reads=[OTM[1]], writes=[VA_M[i][1]])
                P.add("pool", lambda e, kt=kt, rows=rows: e.tensor_copy(
                    out=VA_S[0][0:rows, kt, 0:64], in_=OTM[0][0:rows, 448:512]), reads=[OTM[1]], writes=[VA_S[1]])
                rk = (ring_kt + c0 // 128) % 9
                P.add("pool", lambda e, rk=rk, rows=rows: e.tensor_copy(
                    out=VWR[0][0:rows, rk, 0:64], in_=OTM[0][0:rows, 576:640]), reads=[OTM[1]], writes=[VWR[1]])
                if cfg.get("STAGE", 99) >= 2.9:
                    om_dst(si, rows, c0)

        def compress(c):
            for kvi in range(2):
                lo = 64 * kvi
                px, gpx = nxt("x", PS_X)
                for l in range(32):
                    s0 = 2048 * c + l
                    P.add("pe", lambda e, px=px, l=l, lo=lo, s0=s0: e.matmul(
                        px[:, 0:128], lhsT=W1[0][lo:lo + 64, l, :], rhs=KCVC[0][lo:lo + 64, s0:s0 + 2033:16],
                        start=(l == 0), stop=(l == 31)), reads=[W1[1], KCVC[1]], writes=[gpx])
                P.add("act", lambda e, px=px, kvi=kvi: e.activation(
                    out=SG[0][:, 0:128], in_=px[:, 0:128], func=AF.Sigmoid, bias=PEB[0][:, kvi:kvi + 1]),
                    reads=[gpx, PEB[1]], writes=[SG[1]])
                P.add("dve", lambda e, px=px, kvi=kvi: e.scalar_tensor_tensor(
                    out=AKV[0][:, 128 * kvi:128 * kvi + 128], in0=px[:, 0:128], scalar=PEB[0][:, kvi:kvi + 1], in1=SG[0][:, 0:128],
                    op0=ALU.add, op1=ALU.mult), reads=[gpx, PEB[1], SG[1]], writes=[AKV[1]])
            px, gpx = nxt("x", PS_X)
            P.add("pe", lambda e, px=px: e.matmul(px[0:64, 0:128], lhsT=W2[0][:, 0:64], rhs=AKV[0][:, 0:128], start=True, stop=True),
                  reads=[W2[1], AKV[1]], writes=[gpx])
            P.add("dve", lambda e, px=px: e.tensor_copy(out=KCMP[0][:, 128 * c:128 * c + 128], in_=px[0:64, 0:128]),
                  reads=[gpx], writes=[KCMP[1]])
            px, gpx = nxt("x", PS_X)
            P.add("pe", lambda e, px=px: e.matmul(px[:, 0:64], lhsT=AKV[0][:, 128:256], rhs=W2[0][:, 64:128], start=True, stop=True),
                  reads=[W2[1], AKV[1]], writes=[gpx])
            P.add("dve", lambda e, px=px: e.tensor_copy(out=VC[0][:, c, 0:64], in_=px[:, 0:64]), reads=[gpx], writes=[VC[1]])

        def attend(QT, gQ, qrows, KT, gK, krows, VT, gV, N, tiles, hk, farcol, on_pt=None, first=True, last=True, po=None):
            if po is None:
                po = nxt("o", PS_O)
            pO, gO = po
            nt = len(tiles)
            for ti, (kc, vs, kind, arg, qlo) in enumerate(tiles):
                pS, gS = nxt("s", PS_S)
                two = kind in ("hk", "cm")
                P.add("pe", lambda e, pS=pS, kc=kc, qlo=qlo, two=two: e.matmul(
                    pS[:, qlo:N], lhsT=KT[0:krows, kc:kc + 128], rhs=QT[0:qrows, qlo:N], start=True, stop=not two),
                    reads=[gK, gQ], writes=[gS])
                if kind == "hk":
                    c0 = arg + 384 + qlo
                    P.add("pe", lambda e, pS=pS, qlo=qlo, c0=c0: e.matmul(
                        pS[:, qlo:N], lhsT=JB[:], rhs=HK[hk][0][:, c0:c0 + N - qlo], start=False, stop=True),
                        reads=[gJB, HK[hk][1]], writes=[gS])
                elif kind == "cm":
                    P.add("pe", lambda e, pS=pS, qlo=qlo, arg=arg: e.matmul(
                        pS[:, qlo:N], lhsT=IDB[:], rhs=CMASK[:, arg, qlo:N], start=False, stop=True),
                        reads=[gIDB, gCMASK], writes=[gS])
                pt, gpt = nxt("pt", PTB)
                if kind == "far":
                    P.add("act", lambda e, pS=pS, pt=pt, qlo=qlo: e.activation(
                        out=pt[:, qlo:N], in_=pS[:, qlo:N], func=AF.Exp, bias=FARB[0][:, farcol:farcol + 1]),
                        reads=[gS, FARB[1]], writes=[gpt])
                else:
                    P.add("act", lambda e, pS=pS, pt=pt, qlo=qlo: e.activation(
                        out=pt[:, qlo:N], in_=pS[:, qlo:N], func=AF.Exp), reads=[gS], writes=[gpt])
                if VT is not None:
                    P.add("pe", lambda e, pt=pt, vs=vs, qlo=qlo, ti=ti: e.matmul(
                        pO[0:65, qlo:N], lhsT=VT[:, vs, 0:65], rhs=pt[:, qlo:N], start=(first and ti == 0), stop=(last and ti == nt - 1)),
                        reads=[gV, gpt], writes=[gO])
                if on_pt is not None:
                    on_pt(ti, pt, gpt)
            return po

        def finish(po, N, zi, coef_row, acc_first, acc_last, head_slot, q0, ACC=None):
            ACC = ACC or ACCS[0]
            pO, gO = po
            P.add("dve", lambda e: e.tensor_scalar(out=RS[0][64:65, 0:N], in0=pO[64:65, 0:N], scalar1=1e-30, scalar2=None, op0=ALU.max),
                  reads=[gO], writes=[RS[1]])
            P.add("dve", lambda e: e.reciprocal(out=RS[0][64:65, 0:N], in_=RS[0][64:65, 0:N]), reads=[RS[1]], writes=[RS[1]])
            pb, gpb = nxt("x", PS_X)
            P.add("pe", lambda e: e.matmul(pb[0:64, 0:N], lhsT=ONES[64:65, 0:64], rhs=RS[0][64:65, 0:N], start=True, stop=True),
                  reads=[gONES, RS[1]], writes=[gpb])
            P.add("dve", lambda e: e.tensor_tensor(out=BCZ[0][:, 0:N], in0=pb[0:64, 0:N], in1=ZT[zi][0][:, 0:N], op=ALU.mult),
                  reads=[gpb, ZT[zi][1]], writes=[BCZ[1]])
            if coef_row is not None:
                pg, gpg = nxt("x", PS_X)
                P.add("pe", lambda e: e.matmul(pg[0:64, 0:N], lhsT=SEL6[0:6, coef_row * 64:coef_row * 64 + 64], rhs=GT[0][0:6, 0:N],
                                               start=True, stop=True), reads=[gSEL6, GT[1]], writes=[gpg])
                P.add("dve", lambda e: e.tensor_tensor(out=BCZ[0][:, 0:N], in0=BCZ[0][:, 0:N], in1=pg[0:64, 0:N], op=ALU.mult),
                      reads=[gpg, BCZ[1]], writes=[BCZ[1]])
            if acc_first and acc_last:
                P.add("dve", lambda e: e.tensor_tensor(out=MIXB[0][:, 0:N], in0=pO[0:64, 0:N], in1=BCZ[0][:, 0:N], op=ALU.mult),
                      reads=[gO, BCZ[1]], writes=[MIXB[1]])
            elif acc_first:
                P.add("dve", lambda e: e.tensor_tensor(out=ACC[0][:, 0:N], in0=pO[0:64, 0:N], in1=BCZ[0][:, 0:N], op=ALU.mult),
                      reads=[gO, BCZ[1]], writes=[ACC[1]])
            else:
                P.add("dve", lambda e: e.tensor_tensor(out=TMP[0][:, 0:N], in0=pO[0:64, 0:N], in1=BCZ[0][:, 0:N], op=ALU.mult),
                      reads=[gO, BCZ[1]], writes=[TMP[1]])
                if acc_last:
                    P.add("pool", lambda e: e.tensor_tensor(out=MIXB[0][:, 0:N], in0=ACC[0][:, 0:N], in1=TMP[0][:, 0:N], op=ALU.add),
                          reads=[ACC[1], TMP[1]], writes=[MIXB[1]])
                else:
                    P.add("pool", lambda e: e.tensor_tensor(out=ACC[0][:, 0:N], in0=ACC[0][:, 0:N], in1=TMP[0][:, 0:N], op=ALU.add),
                          reads=[ACC[1], TMP[1]], writes=[ACC[1]])
            if acc_last:
                head_slot(MIXB)

        def key_tiles(N, q0, nkeys_tiles, vslot_fn, win=False):
            tl = []
            for kt in range(nkeys_tiles):
                off = q0 - 128 * kt
                qlo = max(0, -off)
                if qlo >= N:
                    continue
                if win:
                    if off > 512:
                        continue
                    tl.append((None, vslot_fn(kt), "hk", off, qlo, kt))
                else:
                    kind = "hk" if off <= 128 else "far"
                    tl.append((128 * kt, vslot_fn(kt), kind, off, qlo, kt))
            return tl

        def attend_tile(hp, N, q0, subs, mix_dst):
            nkt = (q0 + N + 127) // 128
            for i in range(2):
                KT, gK = KA_M[i]
                nblk = (q0 + N) // 256
                curs = [(q0 + c0) // 256 for (_, c0) in subs]
                nb = max(curs)
                if nb > 0:
                    P.add("dve", lambda e, KT=KT, nb=nb, i=i: e.tensor_reduce(
                        out=KMFF[0][:, 0:nb], in_=KT[0:64, 0:nb * 256].rearrange("p (n k) -> p n k", k=256), axis=AX.X, op=ALU.add),
                        reads=[gK], writes=[KMFF[1]])
                    P.add("dve", lambda e, nb=nb, i=i: e.tensor_copy(out=KMF[i][0][:, 0:nb], in_=KMFF[0][:, 0:nb]),
                          reads=[KMFF[1]], writes=[KMF[i][1]])
                for si, (rows, c0) in enumerate(subs):
                    cur = curs[si]
                    P.add("pool", lambda e, rows=rows: e.memset(SC[0][0:rows, :], NEGS), writes=[SC[1]])
                    if cur > 0:
                        px, gpx = nxt("x", PS_X)
                        P.add("pe", lambda e, px=px, rows=rows, c0=c0, cur=cur, i=i: e.matmul(
                            px[0:rows, 0:cur], lhsT=QA_M[i][0][0:64, c0:c0 + rows], rhs=KMF[i][0][:, 0:cur], start=True, stop=True),
                            reads=[QA_M[i][1], KMF[i][1]], writes=[gpx])
                        P.add("dve", lambda e, px=px, rows=rows, cur=cur: e.tensor_copy(out=SC[0][0:rows, 0:cur], in_=px[0:rows, 0:cur]),
                              reads=[gpx], writes=[SC[1]])
                    P.add("dve", lambda e, rows=rows: e.max(out=M8[0][0:rows, 0:8], in_=SC[0][0:rows, 0:32]), reads=[SC[1]], writes=[M8[1]])
                    P.add("dve", lambda e, rows=rows: e.tensor_scalar(out=THR[0][0:rows, 0:1], in0=M8[0][0:rows, 2:3], scalar1=-1e29,
                                                                      scalar2=None, op0=ALU.max), reads=[M8[1]], writes=[THR[1]])
                    P.add("dve", lambda e, rows=rows: e.tensor_scalar(out=MBP[0][0:rows, 64:96], in0=SC[0][0:rows, 0:32],
                                                                      scalar1=THR[0][0:rows, 0:1], scalar2=-BIG, op0=ALU.is_lt, op1=ALU.mult),
                          reads=[SC[1], THR[1]], writes=[MBP[1]])
                    if cur < 32:
                        P.add("dve", lambda e, rows=rows, cur=cur: e.memset(MBP[0][0:rows, 64 + cur:65 + cur], 0.0), writes=[MBP[1]])
                    pt_, gpt_ = nxt("t", PS_T)
                    P.add("pe", lambda e, pt_=pt_, rows=rows: e.transpose(out=pt_[:, 0:rows], in_=MBP[0][0:rows, 0:128], identity=IDB[0:rows, 0:rows]),
                          reads=[MBP[1], gIDB], writes=[gpt_])
                    P.add("act", lambda e, pt_=pt_, rows=rows, c0=c0, i=i: e.activation(
                        out=QA_M[i][0][64:128, c0:c0 + rows], in_=pt_[64:128, 0:rows], func=AF.Copy), reads=[gpt_], writes=[QA_M[i][1]])
                tl = [(a, b, c_, d, e_) for (a, b, c_, d, e_, _) in key_tiles(N, q0, nkt, lambda kt: kt)]
                po = attend(QA_M[i][0], QA_M[i][1], 128, KT, gK, 128, VA_M[i][0], VA_M[i][1], N, tl, hk=i, farcol=hp * 4 + i)
                finish(po, N, i, None, True, True, lambda mb, i=i: mix_dst(2 * hp + i, mb), q0)

            if N == 512:
                t = q0 // 512
                c = t // 4
                if t % 4 == 0 and t > 0:
                    compress(c - 1)
                compress(c)
                cts = list(range(0, c + 1))
            else:
                for c in range(NCT):
                    compress(c)
                cts = list(range(NCT))
            tq = q0 // 512
            ctl = []
            for c in cts:
                dl = 4 * c - tq
                if dl > 0:
                    continue
                kind, arg = ("cm", -dl) if dl >= -4 else ("none", 0)
                ctl.append((128 * c, c, kind, arg, 0))
            nsub = len(subs)
            heads = [(QA_S[0][0][0], QA_S[0][0][1], True, 0), (QA_S[1][0][0], QA_S[1][0][1], True, 1),
                     (QSIB[0][0], QSIB[0][1], False, 0), (QSIB[1][0], QSIB[1][1], False, 1)]
            ocmp = [None, None]
            for hi, (QT, gQ, own, oi) in enumerate(heads):
                pI = [nxt("x", PS_X), nxt("x", PS_X)]
                state = {"first": [True, True]}

                def on_pt(ti, pt, gpt, pI=pI, state=state):
                    cc = ctl[ti][1]
                    for si, (rows, c0) in enumerate(subs):
                        b = si // 2
                        cb = (si % 2) * 129
                        fst = state["first"][b]
                        state["first"][b] = False
                        P.add("pe", lambda e, b=b, cb=cb, rows=rows, c0=c0, cc=cc, fst=fst, pt=pt: e.matmul(
                            pI[b][0][0:rows, cb:cb + 129], lhsT=pt[:, c0:c0 + rows], rhs=OVX[:, cc, :], start=fst, stop=True,
                            skip_group_check=True),
                            reads=[gpt, gOVX], writes=[pI[b][1]])
                po = attend(QT, gQ, 64, KCMP[0], KCMP[1], 64, VC[0] if own else None, VC[1], N, ctl, hk=0, farcol=0, on_pt=on_pt)
                if own:
                    ocmp[oi] = po
                for si, (rows, c0) in enumerate(subs):
                    b = si // 2
                    cb = (si % 2) * 129
                    P.add("dve", lambda e, b=b, cb=cb, rows=rows: e.tensor_scalar(
                        out=RSI[0][0:rows, 0:1], in0=pI[b][0][0:rows, cb + 128:cb + 129], scalar1=1e-30, scalar2=None, op0=ALU.max),
                        reads=[pI[b][1]], writes=[RSI[1]])
                    P.add("dve", lambda e, rows=rows: e.reciprocal(out=RSI[0][0:rows, 0:1], in_=RSI[0][0:rows, 0:1]),
                          reads=[RSI[1]], writes=[RSI[1]])
                    if hi == 0:
                        P.add("dve", lambda e, b=b, cb=cb, rows=rows, si=si: e.tensor_scalar(
                            out=IMPACC[0][0:rows, si, :], in0=pI[b][0][0:rows, cb:cb + 128], scalar1=RSI[0][0:rows, 0:1], scalar2=None,
                            op0=ALU.mult), reads=[pI[b][1], RSI[1]], writes=[IMPACC[1]])
                    else:
                        P.add("dve", lambda e, b=b, cb=cb, rows=rows, si=si: e.scalar_tensor_tensor(
                            out=IMPACC[0][0:rows, si, :], in0=pI[b][0][0:rows, cb:cb + 128], scalar=RSI[0][0:rows, 0:1],
                            in1=IMPACC[0][0:rows, si, :], op0=ALU.mult, op1=ALU.add), reads=[pI[b][1], RSI[1], IMPACC[1]], writes=[IMPACC[1]])
            rank = 16 if (q0 // 64) < 128 else 15
            for si, (rows, c0) in enumerate(subs):
                cur0 = (q0 + c0) // 64
                p0 = 128 - cur0
                P.add("dve", lambda e, rows=rows, si=si, p0=p0: e.tensor_tensor(
                    out=IMPM[0][0:rows, :], in0=IMPACC[0][0:rows, si, :], in1=PATM[0:rows, p0:p0 + 128], op=ALU.mult),
                    reads=[IMPACC[1], gPATM], writes=[IMPM[1]])
                P.add("dve", lambda e, rows=rows, p0=p0: e.tensor_tensor(
                    out=IMPM[0][0:rows, :], in0=IMPM[0][0:rows, :], in1=PATA[0:rows, p0:p0 + 128], op=ALU.add),
                    reads=[IMPM[1], gPATA], writes=[IMPM[1]])
                P.add("dve", lambda e, rows=rows: e.memset(IMPM[0][0:rows, 0:1], 1e9), writes=[IMPM[1]])
                P.add("dve", lambda e, rows=rows: e.max(out=M8[0][0:rows, 0:8], in_=IMPM[0][0:rows, :]), reads=[IMPM[1]], writes=[M8[1]])
                P.add("dve", lambda e, rows=rows: e.match_replace(out=IMP2[0][0:rows, :], in_to_replace=M8[0][0:rows, 0:8],
                                                                  in_values=IMPM[0][0:rows, :], imm_value=-3e38),
                      reads=[IMPM[1], M8[1]], writes=[IMP2[1]])
                P.add("dve", lambda e, rows=rows: e.max(out=M8[0][0:rows, 8:16], in_=IMP2[0][0:rows, :]), reads=[IMP2[1]], writes=[M8[1]])
                P.add("dve", lambda e, rows=rows: e.tensor_scalar(out=THR[0][0:rows, 1:2], in0=M8[0][0:rows, rank - 1:rank], scalar1=-1e29,
                                                                  scalar2=None, op0=ALU.max), reads=[M8[1]], writes=[THR[1]])
                for v in range(2):
                    P.add("dve", lambda e, rows=rows, v=v: e.tensor_scalar(
                        out=MBP[0][0:rows, 64:128], in0=IMPM[0][0:rows, 64 * v:64 * v + 64],
                        scalar1=THR[0][0:rows, 1:2], scalar2=-BIG, op0=ALU.is_lt, op1=ALU.mult),
                        reads=[IMPM[1], THR[1]], writes=[MBP[1]])
                    pt_, gpt_ = nxt("t", PS_T)
                    P.add("pe", lambda e, pt_=pt_, rows=rows: e.transpose(out=pt_[:, 0:rows], in_=MBP[0][0:rows, 0:128], identity=IDB[0:rows, 0:rows]),
                          reads=[MBP[1], gIDB], writes=[gpt_])
                    for i in range(2):
                        P.add("act" if i == 0 else "dve", (lambda e, pt_=pt_, rows=rows, c0=c0, i=i, v=v: e.activation(
                            out=QA_S[i][v][0][64:128, c0:c0 + rows], in_=pt_[64:128, 0:rows], func=AF.Copy)) if i == 0 else
                            (lambda e, pt_=pt_, rows=rows, c0=c0, i=i, v=v: e.tensor_copy(
                                out=QA_S[i][v][0][64:128, c0:c0 + rows], in_=pt_[64:128, 0:rows])),
                            reads=[gpt_], writes=[QA_S[i][v][1]])
            for i in range(2):
                finish(ocmp[i], N, 2 + i, 3 * i + 0, True, False, None, q0, ACC=ACCS[i])
            for i in range(2):
                ktl = key_tiles(N, q0, nkt, lambda kt: kt)
                lo = [(a, b, c_, d, e_) for (a, b, c_, d, e_, kt) in ktl if kt < 32 or kt * 128 >= max(T, PAST)]
                hi_ = [(a, b, c_, d, e_) for (a, b, c_, d, e_, kt) in ktl if not (kt < 32 or kt * 128 >= max(T, PAST))]
                po = attend(QA_S[i][0][0], QA_S[i][0][1], 128, KA_S[0], KA_S[1], 128, VA_S[0], VA_S[1], N, lo, hk=2 + i,
                            farcol=hp * 4 + 2 + i, first=True, last=(len(hi_) == 0))
                if hi_:
                    attend(QA_S[i][1][0], QA_S[i][1][1], 128, KA_S[0], KA_S[1], 128, VA_S[0], VA_S[1], N, hi_, hk=2 + i,
                           farcol=hp * 4 + 2 + i, first=False, last=True, po=po)
                finish(po, N, 2 + i, 3 * i + 1, False, False, None, q0, ACC=ACCS[i])
                wtl = [((kt % 9) * 128, kt % 9, c_, d, e_) for (a, b, c_, d, e_, kt) in key_tiles(N, q0, nkt, lambda kt: kt, win=True)]
                po = attend(QA_S[i][0][0], QA_S[i][0][1], 64, KWR[0], KWR[1], 64, VWR[0], VWR[1], N, wtl, hk=4 + i, farcol=0)
                finish(po, N, 2 + i, 3 * i + 2, False, True, lambda mb, i=i: mix_dst(8 + 2 * hp + i, mb), q0, ACC=ACCS[i])

        def norm_rows(src_ap, rows, HTcol0, rowsel):
            P.add("sp", lambda e: e.dma_start(out=XT_[0][0:rows, :], in_=src_ap), writes=[XT_[1]], dma=True)
            P.add("act", lambda e: e.activation(out=XN[0][0:rows, :], in_=XT_[0][0:rows, :], func=AF.Square, accum_out=SSQ[0][0:rows, 0:1]),
                  reads=[XT_[1]], writes=[XN[1], SSQ[1]])
            P.add("act", lambda e: e.activation(out=SSQ[0][0:rows, 1:2], in_=SSQ[0][0:rows, 0:1], func=AF.Sqrt, scale=1.0 / D, bias=EPS[0:rows, :]),
                  reads=[SSQ[1], gEPS], writes=[SSQ[1]])
            P.add("dve", lambda e: e.reciprocal(out=SSQ[0][0:rows, 2:3], in_=SSQ[0][0:rows, 1:2]), reads=[SSQ[1]], writes=[SSQ[1]])
            P.add("dve", lambda e: e.tensor_scalar(out=XN[0][0:rows, :], in0=XT_[0][0:rows, :], scalar1=SSQ[0][0:rows, 2:3], scalar2=None, op0=ALU.mult),
                  reads=[XT_[1], SSQ[1]], writes=[XN[1]])

        def transpose_rows(rows, HTcol0, groups):
            for k in range(8):
                pt_, gpt_ = nxt("t", PS_T)
                P.add("pe", lambda e, pt_=pt_, k=k: e.transpose(out=pt_[:, 0:rows], in_=XN[0][0:rows, k * 128:(k + 1) * 128],
                                                                identity=IDB[0:rows, 0:rows]), reads=[XN[1], gIDB], writes=[gpt_])
                for (r0, n_, ar) in groups:
                    P.add("act", lambda e, pt_=pt_, k=k, r0=r0, n_=n_, ar=ar: e.activation(
                        out=HT[0][:, k, HTcol0 + r0:HTcol0 + r0 + n_], in_=pt_[:, r0:r0 + n_], func=AF.Identity,
                        scale=SCT[0][:, k, ar:ar + 1], bias=SHT[0][:, k, ar:ar + 1]), reads=[gpt_, SCT[1], SHT[1]], writes=[HT[1]])

        subsP = [(128, 128 * s) for s in range(4)]

        def mix_dst_prompt(q0, N):
            def f(head, mb):
                P.add("sp", lambda e: e.dma_start(out=mixd_d[head, :, q0:q0 + N], in_=mb[0][:, 0:N]), reads=[mb[1]], writes=[gMIXD], dma=True)
            return f

        gMIXD = G()
        STAGE = cfg.get("STAGE", 99)
        for hp in range(NHP if STAGE >= 4 else (1 if STAGE >= 2 else 0)):
            load_hp(hp)
            for t in range(NT):
                q0 = 512 * t
                for s in range(4):
                    if STAGE >= 2.2:
                        norm_rows(x_d[q0 + 128 * s:q0 + 128 * s + 128, :], 128, 128 * s, None)
                    if STAGE >= 2.3:
                        transpose_rows(128, 128 * s, [(0, 128, 0)])
                if STAGE < 2.4:
                    continue

                def om_dst(si, rows, c0, hp=hp, q0=q0):
                    P.add("sp", lambda e: e.dma_start(out=om_d[hp, q0 + c0:q0 + c0 + rows, :], in_=OTM[0][0:rows, 0:256]), reads=[OTM[1]], dma=True)
                    P.add("sp", lambda e: e.dma_start(out=on_d[hp, q0 + c0:q0 + c0 + rows, :], in_=OTM[0][0:rows, 256:512]), reads=[OTM[1]], dma=True)
                    P.add("sp", lambda e: e.dma_start(out=ow_d[hp, q0 + c0:q0 + c0 + rows, :], in_=OTM[0][0:rows, 512:640]), reads=[OTM[1]], dma=True)
                project_tile(hp, 512, q0, None, subsP, om_dst, None, None, ring_kt=4 * t)
                if STAGE >= 3:
                    attend_tile(hp, 512, q0, subsP, mix_dst_prompt(q0, 512))

        if NS > 0 and STAGE >= 5:
            NTOK = NS * 4
            PTI = sb("pti", [128, NPG], I32)
            IDX = sb("idx", [128, NPG], I32)
            STG = [sb("stg%d" % i, [128, 512], BF16) for i in range(2)]
            norm_rows(xs_d, NTOK, 0, None)
            transpose_rows(NTOK, 0, [(4 * s, 4, 1 + s) for s in range(NS)])
            for s in range(NS):
                P.add("sp", lambda e, s=s: e.dma_start(out=owin_d[s], in_=win_d[s, 4:WB, :]), dma=True)
            HTS = sb("hts", [128, 8, 16], BF16)
            P.add("dve", lambda e: e.tensor_copy(out=HTS[0][:, :, 0:NTOK], in_=HT[0][:, :, 0:NTOK]), reads=[HT[1]], writes=[HTS[1]])
            for s in range(NS):
                P.add("sp", lambda e, s=s: e.dma_start(out=PTI[0][:], in_=ptab_d[s].partition_broadcast(128)), writes=[PTI[1]], dma=True)
                P.add("dve", lambda e: e.tensor_scalar(out=IDX[0][:], in0=PTI[0][:], scalar1=128, scalar2=IOP[:, 0:1], op0=ALU.mult, op1=ALU.add),
                      reads=[PTI[1], gIOP], writes=[IDX[1]])
                for hp in range(NHP):
                    load_hp(hp)
                    kv = hp // 2
                    for pg in range(NPG):
                        sg, gsg = STG[pg % 2]
                        P.add("pool", lambda e, sg=sg, pg=pg, hp=hp: e.indirect_dma_start(
                            out=sg[:, 0:256], out_offset=None,
                            in_=cm_d[hp],
                            in_offset=bass.IndirectOffsetOnAxis(ap=IDX[0][:, pg:pg + 1], axis=0)),
                            reads=[IDX[1]], writes=[gsg], dma=True)
                        P.add("pool", lambda e, sg=sg, pg=pg, kv=kv: e.indirect_dma_start(
                            out=sg[:, 256:512], out_offset=None,
                            in_=cn_d[kv],
                            in_offset=bass.IndirectOffsetOnAxis(ap=IDX[0][:, pg:pg + 1], axis=0)),
                            reads=[IDX[1]], writes=[gsg], dma=True)
                        kc0 = 128 * pg
                        for i in range(2):
                            pt_, gpt_ = nxt("t", PS_T)
                            P.add("pe", lambda e, pt_=pt_, sg=sg, i=i: e.transpose(out=pt_[0:64, 0:128], in_=sg[:, 64 * i:64 * i + 64], identity=IDB[:]),
                                  reads=[gsg, gIDB], writes=[gpt_])
                            P.add("act" if i == 0 else "dve", (lambda e, pt_=pt_, i=i, kc0=kc0: e.activation(
                                out=KA_M[i][0][0:64, kc0:kc0 + 128], in_=pt_[0:64, 0:128], func=AF.Copy)) if i == 0 else
                                (lambda e, pt_=pt_, i=i, kc0=kc0: e.tensor_copy(out=KA_M[i][0][0:64, kc0:kc0 + 128], in_=pt_[0:64, 0:128])),
                                reads=[gpt_], writes=[KA_M[i][1]])
                            P.add("pool", lambda e, sg=sg, i=i, pg=pg: e.tensor_copy(out=VA_M[i][0][:, pg, 0:64], in_=sg[:, 128 + 64 * i:192 + 64 * i]),
                                  reads=[gsg], writes=[VA_M[i][1]])
                        pt_, gpt_ = nxt("t", PS_T)
                        P.add("pe", lambda e, pt_=pt_, sg=sg: e.transpose(out=pt_[:, 0:128], in_=sg[:, 256:384], identity=IDB[:]),
                              reads=[gsg, gIDB], writes=[gpt_])
                        P.add("act", lambda e, pt_=pt_, kc0=kc0: e.activation(out=KCVC[0][:, kc0:kc0 + 128], in_=pt_[:, 0:128], func=AF.Copy),
                              reads=[gpt_], writes=[KCVC[1]])
                        pt_, gpt_ = nxt("t", PS_T)
                        P.add("pe", lambda e, pt_=pt_, sg=sg: e.transpose(out=pt_[0:64, 0:128], in_=sg[:, 384:448], identity=IDB[:]),
                              reads=[gsg, gIDB], writes=[gpt_])
                        P.add("dve", lambda e, pt_=pt_, kc0=kc0: e.tensor_copy(out=KA_S[0][0:64, kc0:kc0 + 128], in_=pt_[0:64, 0:128]),
                              reads=[gpt_], writes=[KA_S[1]])
                        P.add("pool", lambda e, sg=sg, pg=pg: e.tensor_copy(out=VA_S[0][:, pg, 0:64], in_=sg[:, 448:512]),
                              reads=[gsg], writes=[VA_S[1]])
                    for wt in range(WB // 128):
                        sg, gsg = STG[wt % 2]
                        kt = (PAST - WB) // 128 + wt
                        P.add("pool", lambda e, sg=sg, s=s, wt=wt, kv=kv: e.dma_start(
                            out=sg[:, 0:128].rearrange("p (f c) -> p f c", f=2),
                            in_=win_d[s, 128 * wt:128 * wt + 128, :].rearrange("r (f h d) -> r f h d", f=2, h=2)[:, :, kv, :]),
                            writes=[gsg], dma=True)
                        pt_, gpt_ = nxt("t", PS_T)
                        P.add("pe", lambda e, pt_=pt_, sg=sg: e.transpose(out=pt_[0:64, 0:128], in_=sg[:, 0:64], identity=IDB[:]),
                              reads=[gsg, gIDB], writes=[gpt_])
                        r0 = (kt % 9) * 128
                        P.add("dve", lambda e, pt_=pt_, r0=r0: e.tensor_copy(out=KWR[0][0:64, r0:r0 + 128], in_=pt_[0:64, 0:128]),
                              reads=[gpt_], writes=[KWR[1]])
                        P.add("pool", lambda e, sg=sg, kt=kt: e.tensor_copy(out=VWR[0][:, kt % 9, 0:64], in_=sg[:, 64:128]),
                              reads=[gsg], writes=[VWR[1]])
                    ktn = PAST // 128
                    for (t_, g_) in (KA_M[0], KA_M[1], KA_S, KCVC):
                        P.add("pool", lambda e, t_=t_: e.memset(t_[0:64, PAST:PAST + 128], 0.0), writes=[g_])
                    P.add("pool", lambda e: e.memset(KCVC[0][64:128, PAST:PAST + 128], 0.0), writes=[KCVC[1]])
                    P.add("pool", lambda e: e.memset(KWR[0][0:64, (ktn % 9) * 128:(ktn % 9) * 128 + 128], 0.0), writes=[KWR[1]])
                    for (t_, g_) in (VA_M[0], VA_M[1], VA_S):
                        P.add("pool", lambda e, t_=t_: e.memset(t_[:, ktn, 0:64], 0.0), writes=[g_])
                    P.add("pool", lambda e: e.memset(VWR[0][:, ktn % 9, 0:64], 0.0), writes=[VWR[1]])
                    P.add("dve", lambda e, s=s: e.tensor_copy(out=HT[0][:, :, 0:4], in_=HTS[0][:, :, 4 * s:4 * s + 4]), reads=[HTS[1]], writes=[HT[1]])

                    def om_dst(si, rows, c0, hp=hp, s=s):
                        P.add("sp", lambda e: e.dma_start(out=oms_d[hp, 4 * s:4 * s + 4, :], in_=OTM[0][0:4, 0:256]), reads=[OTM[1]], dma=True)
                        P.add("sp", lambda e: e.dma_start(out=ons_d[hp, 4 * s:4 * s + 4, :], in_=OTM[0][0:4, 256:512]), reads=[OTM[1]], dma=True)
                        P.add("sp", lambda e: e.dma_start(out=ows_d[hp, 4 * s:4 * s + 4, :], in_=OTM[0][0:4, 512:640]), reads=[OTM[1]], dma=True)
                    project_tile(hp, 4, PAST, None, [(4, 0)], om_dst, None, None, ring_kt=PAST // 128)

                    def mix_dst(head, mb, s=s):
                        P.add("pool", lambda e: e.tensor_copy(out=MIXS[0][:, head, 4 * s:4 * s + 4], in_=mb[0][:, 0:4]), reads=[mb[1]], writes=[MIXS[1]])
                    attend_tile(hp, 4, PAST, [(4, 0)], mix_dst)

        P.flush()
        stA.close()
        cur[0] = st
        P.barrier()
        GATEP = sb("gatep", [128, D], F32)
        GATES = sb("gates", [16, D], F32)
        FG = sb("fg", [128, D], F32)
        gate_tiles.update({"p": GATEP, "s": GATES, "f": FG})
        if STAGE >= 6:
            adaln("gate")
        XT_ = sb("xt2", [128, D], F32)
        SSQ = sb("ssq2", [128, 4], F32)
        WOUT = sb("wout", [64, 16, D], BF16)
        MIXL = sb("mixl", [64, 16, 512], BF16)
        YP = sb("yp", [128, D], F32)
        if STAGE >= 6:
            P.add("pool", lambda e: e.dma_start(out=WOUT[0][:], in_=wout_d), writes=[WOUT[1]], dma=True)

        def outproj(rows, lhs_fn, x_ap, gate_t, y_ap, extra_reads):
            P.add("sp", lambda e: e.dma_start(out=XT_[0][0:rows, :], in_=x_ap), writes=[XT_[1]], dma=True)
            for oc in range(2):
                px, gpx = nxt("x", PS_X)
                for h in range(16):
                    P.add("pe", lambda e, px=px, h=h, oc=oc: e.matmul(
                        px[0:rows, 0:512], lhsT=lhs_fn(h), rhs=WOUT[0][:, h, oc * 512:(oc + 1) * 512], start=(h == 0), stop=(h == 15)),
                        reads=[WOUT[1]] + extra_reads, writes=[gpx])
                P.add("dve", lambda e, px=px, oc=oc: e.tensor_tensor(
                    out=YP[0][0:rows, oc * 512:(oc + 1) * 512], in0=px[0:rows, 0:512], in1=gate_t[0][0:rows, oc * 512:(oc + 1) * 512], op=ALU.mult),
                    reads=[gpx, gate_t[1]], writes=[YP[1]])
            P.add("pool", lambda e: e.tensor_tensor(out=YP[0][0:rows, :], in0=YP[0][0:rows, :], in1=XT_[0][0:rows, :], op=ALU.add),
                  reads=[YP[1], XT_[1]], writes=[YP[1]])
            P.add("act", lambda e: e.activation(out=XT_[0][0:rows, :], in_=YP[0][0:rows, :], func=AF.Square, accum_out=SSQ[0][0:rows, 0:1]),
                  reads=[YP[1]], writes=[XT_[1], SSQ[1]])
            P.add("act", lambda e: e.activation(out=SSQ[0][0:rows, 1:2], in_=SSQ[0][0:rows, 0:1], func=AF.Sqrt, scale=1.0 / D, bias=EPS[0:rows, :]),
                  reads=[SSQ[1], gEPS], writes=[SSQ[1]])
            P.add("dve", lambda e: e.reciprocal(out=SSQ[0][0:rows, 2:3], in_=SSQ[0][0:rows, 1:2]), reads=[SSQ[1]], writes=[SSQ[1]])
            P.add("dve", lambda e: e.scalar_tensor_tensor(out=YP[0][0:rows, :], in0=YP[0][0:rows, :], scalar=SSQ[0][0:rows, 2:3],
                                                          in1=FG[0][0:rows, :], op0=ALU.mult, op1=ALU.mult),
                  reads=[YP[1], SSQ[1], FG[1]], writes=[YP[1]])
            P.add("sp", lambda e: e.dma_start(out=y_ap, in_=YP[0][0:rows, :]), reads=[YP[1]], dma=True)

        if NHP == 4 and STAGE >= 6:
            for t in range(NT):
                q0 = 512 * t
                P.add("sp", lambda e, q0=q0: e.dma_start(out=MIXL[0][:], in_=mixd_d[:, :, q0:q0 + 512].rearrange("h p n -> p h n")),
                      reads=[gMIXD], writes=[MIXL[1]], dma=True)
                for s in range(4):
                    outproj(128, lambda h, s=s: MIXL[0][:, h, 128 * s:128 * s + 128], x_d[q0 + 128 * s:q0 + 128 * s + 128, :],
                            GATEP, y_d[q0 + 128 * s:q0 + 128 * s + 128, :], [MIXL[1]])

        if NS > 0 and NHP == 4 and STAGE >= 6:
            outproj(NS * 4, lambda h: MIXS[0][:, h, 0:NS * 4], xs_d, GATES, ys_d, [MIXS[1]])

        P.emit_all(st)
    return nc


def make_cfg(T, PAST, NS, NHP, NPHYS):
    return {"T": T, "PAST": PAST, "NS": NS, "NHP": NHP, "TK": ((max(T, PAST) + 2047) // 2048) * 2048 + 128, "NPHYS": NPHYS, "WB": min(512, PAST)}


def make_in_maps(inp, cfg, ncores):
    NS, NHP = cfg["NS"], cfg["NHP"]
    f32 = lambda a: np.ascontiguousarray(np.asarray(a, dtype=np.float32))
    w_in = f32(inp["w_in"])[0]
    rb = f32(inp["rel_bias"])
    shared = {}
    shared["w_ada"] = f32(inp["w_ada"])[0]
    shared["b_ada"] = f32(inp["b_ada"])[0]
    shared["gain"] = f32(inp["norm_gain"])[0]
    shared["fgain"] = f32(inp["final_gain"])
    shared["wfm"] = np.stack([w_in[:, fm_cols(hp)] for hp in range(NHP)])
    shared["wtm"] = np.stack([w_in[:, tm_cols(hp)] for hp in range(NHP)])
    w1k = f32(inp["cmp_k_w1"])[0].reshape(32, 64, 128).transpose(1, 0, 2)
    w1v = f32(inp["cmp_v_w1"])[0].reshape(32, 64, 128).transpose(1, 0, 2)
    shared["w1"] = np.ascontiguousarray(np.concatenate([w1k, w1v], 0))
    shared["w2"] = np.ascontiguousarray(np.concatenate([f32(inp["cmp_k_w2"])[0], f32(inp["cmp_v_w2"])[0]], 1))
    pe = f32(inp["cmp_pe"])[0]
    shared["pet"] = np.ascontiguousarray(np.concatenate([pe[0].T, pe[1].T], 0))
    shared["wout"] = np.ascontiguousarray(f32(inp["w_out"])[0].reshape(16, 64, D).transpose(1, 0, 2))
    ts, t31 = [], []
    for hp in range(NHP):
        hs = [2 * hp, 2 * hp + 1, 8 + 2 * hp, 9 + 2 * hp, 8 + 2 * hp, 9 + 2 * hp]
        ts.append(rb[:, hs])
        t31.append(rb[31, hs[:4]])
    shared["tabsel"] = np.ascontiguousarray(np.concatenate(ts, 1))
    shared["tab31"] = np.ascontiguousarray(np.concatenate(t31, 0))
    cmk = f32(inp["cache_moba_kv"])[0]
    nph = cmk.shape[0]
    for hp in range(4):
        shared["cache_m%d" % hp] = np.ascontiguousarray(cmk[:, :, :, 2 * hp:2 * hp + 2, :].reshape(nph * 128, 256))
    cnk = f32(inp["cache_nsa_kv"])[0]
    for kv in range(2):
        shared["cache_n%d" % kv] = np.ascontiguousarray(cnk[:, :, :, kv, :].reshape(nph * 128, 256))
    for k_, v_ in host_consts(cfg).items():
        shared["c_" + k_] = v_
    xp = f32(inp["x_prompt"])
    xs = f32(inp["x_sample"])
    cp = f32(inp["c_prompt"])
    cs = f32(inp["c_sample"])
    win = f32(inp["state_nsa_win"])[0]
    pt = np.asarray(inp["page_table"]).astype(np.int32)
    maps = []
    for c in range(ncores):
        b = c % xp.shape[0]
        m = dict(shared)
        m["x"] = xp[b]
        m["xs"] = np.ascontiguousarray(xs[NS * c:NS * c + NS].reshape(NS * 4, D))
        cT = np.zeros((D, 144), np.float32)
        cT[:, 0:128] = cp[b][:, None]
        for s in range(NS):
            cT[:, 128 + 4 * s:132 + 4 * s] = cs[NS * c + s][:, None]
        m["cT"] = cT
        m["win"] = np.ascontiguousarray(win[NS * c:NS * c + NS].reshape(NS, win.shape[1], 256))
        m["ptab"] = np.ascontiguousarray(pt[NS * c:NS * c + NS])
        maps.append(m)
    return maps


def assemble(res, cfg, B, ncores):
    T, NS, NHP, WB = cfg["T"], cfg["NS"], cfg["NHP"], cfg["WB"]
    DB = NS * ncores
    y_p = np.zeros((B, T, D), np.float32)
    y_s = np.zeros((DB, 4, D), np.float32)
    mkp = np.zeros((1, B, T, 2, 8, 64), np.float32)
    mks = np.zeros((1, DB, 4, 2, 8, 64), np.float32)
    nkp = np.zeros((1, B, T, 4, 2, 64), np.float32)
    nks = np.zeros((1, DB, 4, 4, 2, 64), np.float32)
    wp = np.zeros((1, B, min(512, T), 2, 2, 64), np.float32)
    ws = np.zeros((1, DB, WB, 2, 2, 64), np.float32)
    for c in range(ncores):
        r = res[c]
        if c < B:
            b = c
            y_p[b] = r["y"]
            for hp in range(NHP):
                om = np.asarray(r["om"][hp])
                mkp[0, b, :, 1, 2 * hp:2 * hp + 2, :] = om[:, 0:128].reshape(T, 2, 64)
                mkp[0, b, :, 0, 2 * hp:2 * hp + 2, :] = om[:, 128:256].reshape(T, 2, 64)
            for kv in range(2):
                on = np.asarray(r["on"][2 * kv])
                ow = np.asarray(r["ow"][2 * kv])
                for f in range(4):
                    nkp[0, b, :, f, kv, :] = on[:, 64 * f:64 * f + 64]
                for f in range(2):
                    wp[0, b, :, f, kv, :] = ow[T - wp.shape[2]:, 64 * f:64 * f + 64]
        sl = slice(NS * c, NS * c + NS)
        y_s[sl] = np.asarray(r["ys"]).reshape(NS, 4, D)
        for hp in range(NHP):
            om = np.asarray(r["oms"][hp]).reshape(NS, 4, 256)
            mks[0, sl, :, 1, 2 * hp:2 * hp + 2, :] = om[:, :, 0:128].reshape(NS, 4, 2, 64)
            mks[0, sl, :, 0, 2 * hp:2 * hp + 2, :] = om[:, :, 128:256].reshape(NS, 4, 2, 64)
        ws[0, sl, 0:WB - 4] = np.asarray(r["owin"]).reshape(NS, WB - 4, 2, 2, 64)
        for kv in range(2):
            on = np.asarray(r["ons"][2 * kv]).reshape(NS, 4, 256)
            ow = np.asarray(r["ows"][2 * kv]).reshape(NS, 4, 128)
            for f in range(4):
                nks[0, sl, :, f, kv, :] = on[:, :, 64 * f:64 * f + 64]
            for f in range(2):
                ws[0, sl, WB - 4:, f, kv, :] = ow[:, :, 64 * f:64 * f + 64]
    return (y_p, y_s, mkp, mks, nkp, nks, wp, ws)


def kernel(**inputs):
    cfg = make_cfg(8192, 8192, 4, 4, 2560)
    nc = build(dict(cfg))
    maps = make_in_maps(inputs, cfg, 8)
    res = run_bass_kernel_spmd(nc, maps, core_ids=list(range(8)))
    return assemble(res.results, cfg, 2, 8)
```

```python
import contextlib
import math
import numpy as np
import ml_dtypes
import concourse.bass as bass
import concourse.mybir as mybir
from concourse.bass_utils import run_bass_kernel_spmd

F32 = mybir.dt.float32
BF16 = mybir.dt.bfloat16
I32 = mybir.dt.int32
AF = mybir.ActivationFunctionType
ALU = mybir.AluOpType
AX = mybir.AxisListType
NPBF = ml_dtypes.bfloat16

BIG = 30000.0
NEGS = -1e30
GL = 1664
HW = 1408
D = 1024


class G:
    __slots__ = ("w", "r", "excl")

    def __init__(self, excl=False):
        self.w = None
        self.r = []
        self.excl = excl


class Op:
    __slots__ = ("eng", "emit", "deps", "inc", "val", "dma", "slot", "dval", "prev")

    def __init__(self, eng, emit, dma):
        self.eng = eng
        self.emit = emit
        self.dma = dma
        self.deps = []
        self.inc = False
        self.val = 0
        self.slot = None
        self.dval = 0
        self.prev = 0


SERIAL = [True]
NSLOT = {"sp": 12, "act": 2, "pool": 12}
ENGS = ("pe", "act", "dve", "pool", "sp")


class Prog:
    def __init__(self, nc):
        self.nc = nc
        self.ops = {e: [] for e in ENGS}
        self.rr = {e: 0 for e in NSLOT}
        self.sv = {e: [0] * n for e, n in NSLOT.items()}
        self.pend = {e: [] for e in ENGS}
        self.lastdma = {}

    def barrier(self):
        deps = []
        for e in ("pe", "act", "dve", "pool"):
            for op in reversed(self.ops[e]):
                if not op.dma:
                    deps.append(op)
                    break
        deps += list(self.lastdma.values())
        for e in ENGS:
            self.pend[e] = list(deps)

    def add(self, eng, emit, reads=(), writes=(), dma=False):
        op = Op(eng, emit, dma)
        deps = []
        seen = set()

        def push(d):
            if d is None or id(d) in seen:
                return
            seen.add(id(d))
            if d.eng == "pe" and eng == "pe" and not d.dma and not dma:
                return
            deps.append(d)

        for g in reads:
            push(g.w)
            if g.excl:
                for r in g.r:
                    push(r)
        for g in writes:
            push(g.w)
            for r in g.r:
                push(r)
        if eng in ("act", "dve", "pool") and not dma and SERIAL[0]:
            for prev in reversed(self.ops[eng]):
                if not prev.dma:
                    if id(prev) not in seen:
                        seen.add(id(prev))
                        deps.append(prev)
                    break
        if self.pend[eng]:
            for d in self.pend[eng]:
                if d is not None and id(d) not in seen:
                    seen.add(id(d))
                    deps.append(d)
            self.pend[eng] = []
        op.deps = deps
        for d in deps:
            d.inc = True
        for g in reads:
            g.r.append(op)
        for g in writes:
            g.w = op
            g.r = []
        if dma:
            i = self.rr[eng]
            self.rr[eng] = (i + 1) % NSLOT[eng]
            op.slot = i
            op.prev = self.sv[eng][i]
            self.sv[eng][i] += 16
            op.dval = self.sv[eng][i]
            self.lastdma[(eng, i)] = op
        self.ops[eng].append(op)
        return op

    def setup(self, stack):
        nc = self.nc
        self.stack = stack
        self.csem = {e: [] for e in ("pe", "act", "dve", "pool")}
        self.dsem = {e: [stack.enter_context(nc.semaphore("d_%s%d" % (e, i))) for i in range(n)]
                     for e, n in NSLOT.items()}
        self.cursor = {e: 0 for e in ENGS}
        self.cval = {e: 0 for e in ENGS}
        self.waited = {e: {} for e in ENGS}

    def flush(self, final=False):
        nc = self.nc
        csem, dsem = self.csem, self.dsem
        ops = self.ops
        sv = self.sv
        EP = 12000
        for e in ("pe", "act", "dve", "pool"):
            for op in ops[e][self.cursor[e]:]:
                if not op.dma:
                    self.cval[e] += 1
                    op.val = self.cval[e]

        def csem_of(eng, val):
            ep = (val - 1) // EP
            lst = csem[eng]
            while len(lst) <= ep:
                lst.append(self.stack.enter_context(nc.semaphore("c_%s%d" % (eng, len(lst)))))
            return lst[ep], val - ep * EP, ("c", eng, ep)

        def sig(d):
            if d.dma:
                return dsem[d.eng][d.slot], d.dval, ("d", d.eng, d.slot)
            return csem_of(d.eng, d.val)

        def run(engname, e, fin=False):
            waited = self.waited[engname]

            def w(sem, val, key):
                if waited.get(key, 0) < val:
                    e.wait_ge(sem, val)
                    waited[key] = val

            for op in ops[engname][self.cursor[engname]:]:
                for d in op.deps:
                    s, v, k = sig(d)
                    w(s, v, k)
                if op.dma and op.prev > 0:
                    w(dsem[engname][op.slot], op.prev, ("d", engname, op.slot))
                ins = op.emit(e)
                if op.dma:
                    ins.then_inc(dsem[engname][op.slot], 16)
                else:
                    ins.then_inc(csem_of(engname, op.val)[0], 1)
            self.cursor[engname] = len(ops[engname])
            if fin:
                for qe, n in NSLOT.items():
                    for i in range(n):
                        if sv[qe][i] > 0:
                            w(dsem[qe][i], sv[qe][i], ("d", qe, i))

        with nc.Block() as block:
            @block.sync
            def _(e):
                run("sp", e, fin=final)

            @block.scalar
            def _(e):
                run("act", e)

            @block.vector
            def _(e):
                run("dve", e)

            @block.gpsimd
            def _(e):
                run("pool", e)

            @block.tensor
            def _(e):
                run("pe", e)

    def emit_all(self, stack):
        self.flush(final=True)


def t5_bucket_np(rel):
    n = np.maximum(rel, 0)
    nf = np.maximum(n, 1).astype(np.float32)
    large = 16 + (np.log(nf / np.float32(16)) / np.float32(math.log(8.0)) * np.float32(16)).astype(np.int32)
    return np.where(n < 16, n, np.minimum(large, 31))


def host_consts(cfg):
    T, PAST = cfg["T"], cfg["PAST"]
    TK = cfg["TK"]
    c = {}
    c["idb"] = np.eye(128, dtype=np.float32).astype(NPBF)
    c["jb"] = np.eye(128, dtype=np.float32)[::-1].copy().astype(NPBF)
    c["ones"] = np.ones((128, 64), np.float32)
    sel6 = np.zeros((6, 6 * 64), np.float32)
    for r in range(6):
        sel6[r, r * 64:(r + 1) * 64] = 1.0
    c["sel6"] = sel6
    k = np.arange(TK)
    kmax = max(T, PAST)
    indm = np.zeros((64, TK), np.float32)
    inds = np.zeros((64, TK), np.float32)
    for r in range(32):
        indm[r] = ((k // 256) == r) & (k < kmax)
    for r in range(64):
        inds[r] = (((k // 64) % 64) == r) & (k < kmax)
    c["indm"] = indm.astype(NPBF)
    c["inds"] = inds.astype(NPBF)
    ni = np.arange(128)[:, None]
    qi = np.arange(512)[None, :]
    cm = np.zeros((128, 5, 512), np.float32)
    for j, dl in enumerate([0, -1, -2, -3, -4]):
        cm[:, j, :] = np.where(qi - 16 * ni >= 512 * dl + 31, 0.0, -BIG)
    c["cmask"] = cm.astype(NPBF)
    ovx = np.zeros((128, 4, 129), np.float32)
    for cc in range(4):
        for n_ in range(128):
            n = 128 * cc + n_
            for s in (n // 4, (n + 1) // 4 if n % 4 == 3 else -1):
                if 0 <= s < 128:
                    ovx[n_, cc, s] = 1.0
        ovx[:, cc, 128] = 1.0
    c["ovx"] = ovx.astype(NPBF)
    pm = np.ones((128, 256), np.float32)
    pa = np.zeros((128, 256), np.float32)
    for p in range(128):
        cur = 0 if p < 64 else 1
        for x in range(256):
            j = x - 128
            if j > cur:
                pm[p, x] = 0.0
                pa[p, x] = NEGS
            elif j == cur or j == cur - 1:
                pm[p, x] = 0.0
                pa[p, x] = 1e9
    c["patm"] = pm
    c["pata"] = pa
    m = np.arange(GL)
    rel = m - 511
    oh = np.zeros((32, GL), np.float32)
    b = t5_bucket_np(rel)
    for i in range(GL):
        if rel[i] >= 0:
            oh[b[i], i] = 1.0
    c["oh"] = oh
    addm = np.zeros((6, GL), np.float32)
    addm[:, rel < 0] = -BIG
    addm[4:6, rel >= 512] = -BIG
    c["addm"] = addm
    c["iop"] = np.arange(128, dtype=np.int32).reshape(128, 1)
    return c


FMCH = {}
_o = 0
for _n, _w in [("qmA", 64), ("qmB", 64), ("kmA", 64), ("kmB", 64), ("zmA", 64), ("zmB", 64),
               ("qnA", 64), ("qnB", 64), ("znA", 64), ("znB", 64), ("ks", 64), ("kw", 64),
               ("kcvc", 128), ("g", 8), ("qnC", 64), ("qnD", 64)]:
    FMCH[_n] = (_o, _w)
    _o += _w
FMC = _o
TMC = 640


def proj_offsets():
    sizes = [512] * 4 + [512] + [128] * 6 + [24, 512]
    offs = np.concatenate([[0], np.cumsum(sizes)])
    names = ["q_m", "k_m", "v_m", "z_m", "q_n", "kc", "vc", "ks", "vs", "kw", "vw", "g_n", "z_n"]
    return {n: int(o) for n, o in zip(names, offs)}


def fm_cols(hp):
    po = proj_offsets()
    A, B = 2 * hp, 2 * hp + 1
    kv = hp // 2
    sib = hp ^ 1
    C, Dh = 2 * sib, 2 * sib + 1
    r64 = lambda base, h: list(range(base + 64 * h, base + 64 * h + 64))
    cols = []
    cols += r64(po["q_m"], A) + r64(po["q_m"], B) + r64(po["k_m"], A) + r64(po["k_m"], B)
    cols += r64(po["z_m"], A) + r64(po["z_m"], B)
    cols += r64(po["q_n"], A) + r64(po["q_n"], B) + r64(po["z_n"], A) + r64(po["z_n"], B)
    cols += r64(po["ks"], kv) + r64(po["kw"], kv)
    cols += r64(po["kc"], kv) + r64(po["vc"], kv)
    cols += list(range(po["g_n"] + 6 * hp, po["g_n"] + 6 * hp + 6)) + [po["g_n"], po["g_n"]]
    cols += r64(po["q_n"], C) + r64(po["q_n"], Dh)
    assert len(cols) == FMC
    return cols


def tm_cols(hp):
    po = proj_offsets()
    A, B = 2 * hp, 2 * hp + 1
    kv = hp // 2
    r64 = lambda base, h: list(range(base + 64 * h, base + 64 * h + 64))
    cols = r64(po["v_m"], A) + r64(po["v_m"], B) + r64(po["k_m"], A) + r64(po["k_m"], B)
    cols += r64(po["kc"], kv) + r64(po["vc"], kv) + r64(po["ks"], kv) + r64(po["vs"], kv)
    cols += r64(po["kw"], kv) + r64(po["vw"], kv)
    assert len(cols) == TMC
    return cols


def build(cfg):
    T, PAST, NS, NHP = cfg["T"], cfg["PAST"], cfg["NS"], cfg["NHP"]
    TK = cfg["TK"]
    NTK = TK // 128
    NPHYS = cfg["NPHYS"]
    NPG = PAST // 128
    NT = T // 512
    WB = cfg["WB"]
    nc = bass.Bass("TRN2", target_bir_lowering=False)
    P = Prog(nc)

    def din(name, shape, dt=F32):
        return nc.dram_tensor(name, list(shape), dt, kind="ExternalInput")

    def dout(name, shape, dt=F32):
        return nc.dram_tensor(name, list(shape), dt, kind="ExternalOutput")

    x_d = din("x", [T, D]).ap()
    xs_d = din("xs", [NS * 4, D]).ap()
    cT_d = din("cT", [D, 144]).ap()
    wada_d = din("w_ada", [D, 3 * D]).ap()
    bada_d = din("b_ada", [3 * D]).ap()
    gain_d = din("gain", [D]).ap()
    fgain_d = din("fgain", [D]).ap()
    wfm_d = din("wfm", [NHP, D, FMC]).ap()
    wtm_d = din("wtm", [NHP, D, TMC]).ap()
    w1_d = din("w1", [128, 32, 128]).ap()
    w2_d = din("w2", [128, 128]).ap()
    pet_d = din("pet", [128, 32]).ap()
    wout_d = din("wout", [64, 16, D]).ap()
    tabsel_d = din("tabsel", [32, NHP * 6]).ap()
    tab31_d = din("tab31", [NHP * 4]).ap()
    cm_d = [din("cache_m%d" % i, [NPHYS * 128, 256]).ap() for i in range(4)]
    cn_d = [din("cache_n%d" % i, [NPHYS * 128, 256]).ap() for i in range(2)]
    win_d = din("win", [NS, WB, 256]).ap()
    ptab_d = din("ptab", [NS, NPG], I32).ap()
    hc = host_consts(cfg)
    cdram = {}
    for k_, v_ in hc.items():
        dt_ = BF16 if v_.dtype == NPBF else (I32 if v_.dtype == np.int32 else F32)
        cdram[k_] = din("c_" + k_, v_.shape, dt_).ap()

    y_d = dout("y", [T, D]).ap()
    ys_d = dout("ys", [NS * 4, D]).ap()
    om_d = dout("om", [NHP, T, 256]).ap()
    on_d = dout("on", [NHP, T, 256]).ap()
    ow_d = dout("ow", [NHP, T, 128]).ap()
    oms_d = dout("oms", [NHP, NS * 4, 256]).ap()
    ons_d = dout("ons", [NHP, NS * 4, 256]).ap()
    ows_d = dout("ows", [NHP, NS * 4, 128]).ap()
    owin_d = dout("owin", [NS, WB - 4, 256]).ap()
    gd_h = nc.dram_tensor("gd", [NHP * 6, GL], BF16, kind="Internal")
    gd_d = gd_h.ap()
    mixd_d = nc.dram_tensor("mixd", [16, 64, T], BF16, kind="Internal").ap()

    with contextlib.ExitStack() as st:
        cur = [st]
        P.setup(st)

        def sb(name, shape, dt):
            return cur[0].enter_context(nc.sbuf_tensor("s_" + name, list(shape), dt)), G()

        def psum(name, shape, dt):
            return st.enter_context(nc.psum_tensor(name, list(shape), dt)), G(excl=True)

        PS_S = [psum("ps_s%d" % i, [128, 512], F32) for i in range(2)]
        PS_O = [psum("ps_o%d" % i, [128, 512], F32) for i in range(2)]
        PS_X = [psum("ps_x%d" % i, [128, 512], F32) for i in range(2)]
        PS_T = [psum("ps_t%d" % i, [128, 1024], BF16) for i in range(2)]
        cnt = {"s": 0, "o": 0, "x": 0, "t": 0, "pt": 0}

        def nxt(kind, lst):
            i = cnt[kind]
            cnt[kind] = (i + 1) % len(lst)
            return lst[i]

        def cload(name, shape, dt, eng="sp"):
            t, g = sb("k_" + name, shape, dt)
            P.add(eng, lambda e: e.dma_start(out=t[:], in_=cdram[name]), writes=[g], dma=True)
            return t, g

        IDB, gIDB = cload("idb", [128, 128], BF16)
        JB, gJB = cload("jb", [128, 128], BF16)
        ONES, gONES = cload("ones", [128, 64], F32)
        SEL6, gSEL6 = cload("sel6", [6, 384], F32)
        CMASK, gCMASK = cload("cmask", [128, 5, 512], BF16)
        OVX, gOVX = cload("ovx", [128, 4, 129], BF16)
        PATM, gPATM = cload("patm", [128, 256], F32)
        PATA, gPATA = cload("pata", [128, 256], F32)
        IOP, gIOP = cload("iop", [128, 1], I32)
        EPS, gEPS = sb("eps", [128, 1], F32)
        P.add("dve", lambda e: e.memset(EPS[:], 1e-6), writes=[gEPS])

        MIXS = sb("mixs", [64, 16, 16], BF16)
        stA = contextlib.ExitStack()
        cur[0] = stA
        KA_M = [sb("ka_m%d" % i, [128, TK], BF16) for i in range(2)]
        VA_M = [sb("va_m%d" % i, [128, NTK, 65], BF16) for i in range(2)]
        KA_S = sb("ka_s", [128, TK], BF16)
        VA_S = sb("va_s", [128, NTK, 65], BF16)
        KWR = sb("kwr", [128, 9 * 128], BF16)
        VWR = sb("vwr", [128, 9, 65], BF16)
        KCVC = sb("kcvc", [128, TK], BF16)
        NCT = max(1, (max(T, PAST) + 2047) // 2048)
        KCMP = sb("kcmp", [64, NCT * 128], BF16)
        VC = sb("vc", [128, NCT, 65], BF16)

        def init_resident():
            for (t, g) in KA_M:
                P.add("pool", lambda e, t=t: e.memset(t[:], 0.0), writes=[g])
                P.add("sp", lambda e, t=t: e.dma_start(out=t[64:128, :], in_=cdram["indm"]), writes=[g], dma=True)
            for (t, g) in VA_M + [VA_S, VWR, VC]:
                P.add("pool", lambda e, t=t: e.memset(t[:], 0.0), writes=[g])
                P.add("pool", lambda e, t=t: e.memset(t[:, :, 64:65], 1.0), writes=[g])
            t, g = KA_S
            P.add("pool", lambda e: e.memset(KA_S[0][:], 0.0), writes=[g])
            P.add("sp", lambda e: e.dma_start(out=KA_S[0][64:128, :], in_=cdram["inds"]), writes=[g], dma=True)
            for (t, g) in (KWR, KCVC, KCMP):
                P.add("pool", lambda e, t=t: e.memset(t[:], 0.0), writes=[g])

        init_resident()

        WFM = sb("wfm", [128, 8, FMC], BF16)
        WTM = sb("wtm", [128, 8, TMC], BF16)
        W1 = sb("w1", [128, 32, 128], BF16)
        W2 = sb("w2", [128, 128], BF16)
        PET = sb("pet", [128, 32], BF16)
        PEB = sb("peb", [128, 2], F32)
        HKW = [1024, 1024, 1024, 1024, HW, HW]
        HK = [sb("hk%d" % i, [128, HKW[i]], BF16) for i in range(6)]
        FARB = sb("farb", [128, NHP * 4], F32)
        TAB = sb("tab", [32, NHP * 6], F32)
        P.add("pool", lambda e: e.dma_start(out=W1[0][:], in_=w1_d), writes=[W1[1]], dma=True)
        P.add("pool", lambda e: e.dma_start(out=W2[0][:], in_=w2_d), writes=[W2[1]], dma=True)
        P.add("pool", lambda e: e.dma_start(out=PET[0][:], in_=pet_d), writes=[PET[1]], dma=True)
        P.add("sp", lambda e: e.dma_start(out=FARB[0][:], in_=tab31_d.partition_broadcast(128)), writes=[FARB[1]], dma=True)
        P.add("sp", lambda e: e.dma_start(out=TAB[0][:], in_=tabsel_d), writes=[TAB[1]], dma=True)

        with contextlib.ExitStack() as st2:
            OH = st2.enter_context(nc.sbuf_tensor("oh", [32, GL], F32)); gOH = G()
            ADDM = st2.enter_context(nc.sbuf_tensor("addm", [6, GL], F32)); gADDM = G()
            GV = st2.enter_context(nc.sbuf_tensor("gv", [6, GL], BF16)); gGV = G()
            P.add("sp", lambda e: e.dma_start(out=OH[:], in_=cdram["oh"]), writes=[gOH], dma=True)
            P.add("sp", lambda e: e.dma_start(out=ADDM[:], in_=cdram["addm"]), writes=[gADDM], dma=True)
            for hp in range(NHP):
                for c0 in range(0, GL, 512):
                    w_ = min(512, GL - c0)
                    px, gpx = nxt("x", PS_X)
                    P.add("pe", lambda e, px=px, c0=c0, w_=w_, hp=hp: e.matmul(
                        px[0:6, 0:w_], lhsT=TAB[0][:, hp * 6:hp * 6 + 6], rhs=OH[:, c0:c0 + w_], start=True, stop=True),
                        reads=[TAB[1], gOH], writes=[gpx])
                    P.add("dve", lambda e, px=px, c0=c0, w_=w_: e.tensor_tensor(
                        out=GV[:, c0:c0 + w_], in0=px[0:6, 0:w_], in1=ADDM[:, c0:c0 + w_], op=ALU.add),
                        reads=[gpx, gADDM], writes=[gGV])
                gGD = G()
                P.add("sp", lambda e, hp=hp: e.dma_start(out=gd_d[hp * 6:hp * 6 + 6, :], in_=GV[:]),
                      reads=[gGV], writes=[gGD], dma=True)
                cfg.setdefault("_ggd", []).append(gGD)
            P.flush()
        gGDs = cfg.pop("_ggd")
        P.barrier()

        SHT = sb("sht", [128, 8, 8], F32)
        SCT = sb("sct", [128, 8, 8], F32)
        gate_tiles = {}

        def adaln(part):
            with contextlib.ExitStack() as st2:
                CT = st2.enter_context(nc.sbuf_tensor("ct_" + part, [128, 8, 144], F32)); gCT = G()
                WA = st2.enter_context(nc.sbuf_tensor("wa_" + part, [128, 8, 512], F32)); gWA = G()
                BAT = st2.enter_context(nc.sbuf_tensor("bat_" + part, [128, 24], F32)); gBAT = G()
                GNT = st2.enter_context(nc.sbuf_tensor("gnt_" + part, [128, 8], F32)); gGNT = G()
                P.add("sp", lambda e: e.dma_start(out=CT[:], in_=cT_d.rearrange("(k p) n -> p k n", p=128)), writes=[gCT], dma=True)
                P.add("sp", lambda e: e.dma_start(out=BAT[:], in_=bada_d.rearrange("(k p) -> p k", p=128), allow_slow_non_contiguous=True), writes=[gBAT], dma=True)
                P.add("sp", lambda e: e.dma_start(out=GNT[:], in_=gain_d.rearrange("(k p) -> p k", p=128), allow_slow_non_contiguous=True), writes=[gGNT], dma=True)
                if part == "gate":
                    BAB = st2.enter_context(nc.sbuf_tensor("bab", [128, D], F32)); gBAB = G()
                    GATEP, GATES, FG = gate_tiles["p"], gate_tiles["s"], gate_tiles["f"]
                    P.add("sp", lambda e: e.dma_start(out=BAB[:], in_=bada_d[2 * D:3 * D].partition_broadcast(128)), writes=[gBAB], dma=True)
                    P.add("sp", lambda e: e.dma_start(out=FG[0][:], in_=fgain_d.partition_broadcast(128)), writes=[FG[1]], dma=True)
                for j in (range(4) if part == "fm" else range(4, 6)):
                    P.add("sp", lambda e, j=j: e.dma_start(
                        out=WA[:], in_=wada_d[:, j * 512:(j + 1) * 512].rearrange("(k p) n -> p k n", p=128)),
                        writes=[gWA], dma=True)
                    if j < 4:
                        for f in range(4):
                            fc = j * 4 + f
                            px, gpx = nxt("x", PS_X)
                            for k in range(8):
                                P.add("pe", lambda e, px=px, k=k, f=f: e.matmul(
                                    px[:, 0:144], lhsT=WA[:, k, f * 128:(f + 1) * 128], rhs=CT[:, k, :],
                                    start=(k == 0), stop=(k == 7)), reads=[gWA, gCT], writes=[gpx])
                            dst = SHT if fc < 8 else SCT
                            fcc = fc % 8
                            for (dc, sc0, sc1, stp) in ((0, 0, 1, 1), (1, 128, 144, 4)):
                                ncol = 1 if dc == 0 else 4
                                if fc < 8:
                                    P.add("dve", lambda e, px=px, fc=fc, fcc=fcc, dc=dc, sc0=sc0, sc1=sc1, stp=stp, ncol=ncol: e.tensor_scalar(
                                        out=SHT[0][:, fcc, dc:dc + ncol], in0=px[:, sc0:sc1:stp], scalar1=BAT[:, fc:fc + 1], scalar2=None, op0=ALU.add),
                                        reads=[gpx, gBAT], writes=[SHT[1]])
                                else:
                                    P.add("dve", lambda e, px=px, fc=fc, fcc=fcc, dc=dc, sc0=sc0, sc1=sc1, stp=stp, ncol=ncol: e.tensor_scalar(
                                        out=SCT[0][:, fcc, dc:dc + ncol], in0=px[:, sc0:sc1:stp], scalar1=BAT[:, fc:fc + 1], scalar2=1.0,
                                        op0=ALU.add, op1=ALU.add), reads=[gpx, gBAT], writes=[SCT[1]])
                            if fc >= 8:
                                P.add("dve", lambda e, fcc=fcc: e.tensor_scalar(
                                    out=SCT[0][:, fcc, 0:5], in0=SCT[0][:, fcc, 0:5], scalar1=GNT[:, fcc:fcc + 1], scalar2=None,
                                    op0=ALU.mult), reads=[gGNT, SCT[1]], writes=[SCT[1]])
                    else:
                        oc = j - 4
                        px, gpx = nxt("x", PS_X)
                        for k in range(8):
                            P.add("pe", lambda e, px=px, k=k: e.matmul(
                                px[:, 0:512], lhsT=CT[:, k, 0:128], rhs=WA[:, k, :], start=(k == 0), stop=(k == 7)),
                                reads=[gWA, gCT], writes=[gpx])
                        P.add("dve", lambda e, px=px, oc=oc: e.tensor_tensor(
                            out=GATEP[0][:, oc * 512:(oc + 1) * 512], in0=px[:, 0:512], in1=BAB[:, oc * 512:(oc + 1) * 512], op=ALU.add),
                            reads=[gpx, gBAB], writes=[GATEP[1]])
                        px, gpx = nxt("x", PS_X)
                        for k in range(8):
                            P.add("pe", lambda e, px=px, k=k: e.matmul(
                                px[0:16, 0:512], lhsT=CT[:, k, 128:144], rhs=WA[:, k, :], start=(k == 0), stop=(k == 7)),
                                reads=[gWA, gCT], writes=[gpx])
                        P.add("dve", lambda e, px=px, oc=oc: e.tensor_tensor(
                            out=GATES[0][:, oc * 512:(oc + 1) * 512], in0=px[0:16, 0:512], in1=BAB[0:16, oc * 512:(oc + 1) * 512], op=ALU.add),
                            reads=[gpx, gBAB], writes=[GATES[1]])
                P.flush()
            P.barrier()

        adaln("fm")

        XT_ = sb("xt", [128, D], F32)
        XN = sb("xn", [128, D], BF16)
        SSQ = sb("ssq", [128, 4], F32)
        HT = sb("ht", [128, 8, 512], BF16)
        QA_M = [sb("qa_m%d" % i, [128, 512], BF16) for i in range(2)]
        QA_S = [[sb("qa_s%d%d" % (i, v), [128, 512], BF16) for v in range(2)] for i in range(2)]
        QSIB = [sb("qsib%d" % i, [64, 512], BF16) for i in range(2)]
        ZT = [sb("zt%d" % i, [64, 512], BF16) for i in range(4)]
        GT = sb("gt", [8, 512], F32)
        KMF = [sb("kmf%d" % i, [64, 40], BF16) for i in range(2)]
        KMFF = sb("kmff", [64, 40], F32)
        PTB = [sb("ptb%d" % i, [128, 512], BF16) for i in range(2)]
        OTM = sb("otm", [128, TMC], F32)
        IMPACC = sb("impacc", [128, 4, 128], F32)
        SC = sb("sc", [128, 40], F32)
        M8 = sb("m8", [128, 16], F32)
        THR = sb("thr", [128, 2], F32)
        IMPM = sb("impm", [128, 128], F32)
        IMP2 = sb("imp2", [128, 128], F32)
        MBP = sb("mbp", [128, 256], BF16)
        RS = sb("rs", [128, 512], F32)
        RSI = sb("rsi", [128, 2], F32)
        BCZ = sb("bcz", [64, 512], F32)
        TMP = sb("tmp", [64, 512], F32)
        ACCS = [sb("acc%d" % i, [64, 512], F32) for i in range(2)]
        MIXB = sb("mixb", [64, 512], BF16)
        AKV = sb("akv", [128, 256], BF16)
        SG = sb("sg", [128, 512], F32)
        for (t, g) in (MBP,):
            P.add("pool", lambda e, t=t: e.memset(t[:], 0.0), writes=[g])
        for lst in (QA_M, QA_S[0], QA_S[1]):
            for (t, g) in lst:
                P.add("pool", lambda e, t=t: e.memset(t[:], 0.0), writes=[g])

        def load_hp(hp):
            P.add("pool", lambda e: e.dma_start(out=WFM[0][:], in_=wfm_d[hp].rearrange("(k p) n -> p k n", p=128)),
                  writes=[WFM[1]], dma=True)
            P.add("pool", lambda e: e.dma_start(out=WTM[0][:], in_=wtm_d[hp].rearrange("(k p) n -> p k n", p=128)),
                  writes=[WTM[1]], dma=True)
            for v in range(6):
                src = bass.AP(gd_h, (hp * 6 + v) * GL, [[1, 128], [1, HKW[v]]])
                P.add("sp", lambda e, v=v, src=src: e.dma_start(out=HK[v][0][:], in_=src),
                      reads=[gGDs[hp]], writes=[HK[v][1]], dma=True)

        def peb_compute():
            for kvi in range(2):
                px, gpx = nxt("x", PS_X)
                lo = 64 * kvi
                for l in range(32):
                    P.add("pe", lambda e, px=px, l=l, lo=lo: e.matmul(
                        px[:, 0:1], lhsT=W1[0][lo:lo + 64, l, :], rhs=PET[0][lo:lo + 64, l:l + 1],
                        start=(l == 0), stop=(l == 31)), reads=[W1[1], PET[1]], writes=[gpx])
                P.add("dve", lambda e, px=px, kvi=kvi: e.tensor_copy(out=PEB[0][:, kvi:kvi + 1], in_=px[:, 0:1]),
                      reads=[gpx], writes=[PEB[1]])

        peb_compute()

        def project_tile(hp, N, q0, ht_ready, subs, om_dst, on_dst, ow_dst, ring_kt):
            HTt, gHT = HT

            def fm(name, evac):
                off, w_ = FMCH[name]
                w_ = max(w_, 64)
                px, gpx = nxt("x", PS_X)
                for k in range(8):
                    P.add("pe", lambda e, px=px, k=k, off=off, w_=w_: e.matmul(
                        px[0:w_, 0:N], lhsT=WFM[0][:, k, off:off + w_], rhs=HTt[:, k, 0:N],
                        start=(k == 0), stop=(k == 7)), reads=[WFM[1], gHT], writes=[gpx])
                evac(px, gpx)

            kc0 = q0
            for i, nm in enumerate(("qmA", "qmB")):
                def ev(px, gpx, i=i):
                    P.add("act", lambda e: e.activation(out=QA_M[i][0][0:64, 0:N], in_=px[0:64, 0:N], func=AF.Copy, scale=0.125),
                          reads=[gpx], writes=[QA_M[i][1]])
                fm(nm, ev)
            for i, nm in enumerate(("kmA", "kmB")):
                def ev(px, gpx, i=i):
                    P.add("act", lambda e: e.activation(out=KA_M[i][0][0:64, kc0:kc0 + N], in_=px[0:64, 0:N], func=AF.Copy),
                          reads=[gpx], writes=[KA_M[i][1]])
                fm(nm, ev)
            if cfg.get("STAGE", 99) < 2.45:
                return
            for i, nm in enumerate(("zmA", "zmB", "znA", "znB")):
                def ev(px, gpx, i=i):
                    if cfg.get("STAGE", 99) >= 2.47:
                        P.add("act", lambda e: e.activation(out=SG[0][0:64, 0:N], in_=px[0:64, 0:N], func=AF.Sigmoid),
                              reads=[gpx], writes=[SG[1]])
                    if cfg.get("STAGE", 99) >= 2.49:
                        if cfg.get("VAR", 0) == 1:
                            P.add("dve", lambda e: e.tensor_tensor(out=TMP[0][:, 0:N], in0=px[0:64, 0:N], in1=SG[0][0:64, 0:N], op=ALU.mult),
                                  reads=[gpx, SG[1]], writes=[TMP[1]])
                        elif cfg.get("VAR", 0) == 2:
                            P.add("dve", lambda e: e.tensor_copy(out=TMP[0][:, 0:N], in_=px[0:64, 0:N]), reads=[gpx], writes=[TMP[1]])
                            P.add("dve", lambda e: e.tensor_tensor(out=ZT[i][0][:, 0:N], in0=TMP[0][:, 0:N], in1=SG[0][0:64, 0:N], op=ALU.mult),
                                  reads=[TMP[1], SG[1]], writes=[ZT[i][1]])
                        elif cfg.get("VAR", 0) == 3:
                            P.add("dve", lambda e: e.tensor_tensor(out=ZT[i][0][:, 0:N], in0=px[0:64, 0:N], in1=SG[0][0:64, 0:N], op=ALU.mult),
                                  reads=[gpx, SG[1]], writes=[ZT[i][1], TMP[1]])
                        elif cfg.get("VAR", 0) == 4:
                            P.add("dve", lambda e: e.tensor_tensor(out=ZT[0][0][:, 0:N], in0=px[0:64, 0:N], in1=SG[0][0:64, 0:N], op=ALU.mult),
                                  reads=[gpx, SG[1]], writes=[ZT[0][1]])
                        else:
                            P.add("dve", lambda e: e.tensor_tensor(out=ZT[i][0][:, 0:N], in0=px[0:64, 0:N], in1=SG[0][0:64, 0:N], op=ALU.mult),
                                  reads=[gpx, SG[1]], writes=[ZT[i][1]])
                fm(nm, ev)
            if cfg.get("STAGE", 99) < 2.6:
                return
            for i, nm in enumerate(("qnA", "qnB")):
                def ev(px, gpx, i=i):
                    for v in range(2):
                        P.add("act" if v == 0 else "dve", (lambda e, v=v: e.activation(
                            out=QA_S[i][v][0][0:64, 0:N], in_=px[0:64, 0:N], func=AF.Copy, scale=0.125)) if v == 0 else
                            (lambda e, v=v: e.tensor_scalar(out=QA_S[i][v][0][0:64, 0:N], in0=px[0:64, 0:N], scalar1=0.125,
                                                            scalar2=None, op0=ALU.mult)),
                            reads=[gpx], writes=[QA_S[i][v][1]])
                fm(nm, ev)
            if cfg.get("STAGE", 99) < 2.61:
                return
            for i, nm in enumerate(("qnC", "qnD")):
                def ev(px, gpx, i=i):
                    P.add("act", lambda e: e.activation(out=QSIB[i][0][:, 0:N], in_=px[0:64, 0:N], func=AF.Copy, scale=0.125),
                          reads=[gpx], writes=[QSIB[i][1]])
                fm(nm, ev)

            if cfg.get("STAGE", 99) < 2.62:
                return

            def ev(px, gpx):
                P.add("act", lambda e: e.activation(out=KA_S[0][0:64, kc0:kc0 + N], in_=px[0:64, 0:N], func=AF.Copy),
                      reads=[gpx], writes=[KA_S[1]])
            fm("ks", ev)
            if cfg.get("STAGE", 99) < 2.63:
                return

            def ev(px, gpx):
                for j in range(0, N, 128):
                    n_ = min(128, N - j)
                    r0 = ((ring_kt + j // 128) % 9) * 128
                    P.add("dve", lambda e, j=j, n_=n_, r0=r0: e.tensor_copy(out=KWR[0][0:64, r0:r0 + n_], in_=px[0:64, j:j + n_]),
                          reads=[gpx], writes=[KWR[1]])
            fm("kw", ev)

            if cfg.get("STAGE", 99) < 2.64:
                return

            def ev(px, gpx):
                P.add("act", lambda e: e.activation(out=KCVC[0][:, kc0:kc0 + N], in_=px[:, 0:N], func=AF.Copy),
                      reads=[gpx], writes=[KCVC[1]])
            fm("kcvc", ev)

            if cfg.get("STAGE", 99) < 2.645:
                return

            def ev(px, gpx):
                P.add("act", lambda e: e.activation(out=GT[0][:, 0:N], in_=px[0:8, 0:N], func=AF.Sigmoid),
                      reads=[gpx], writes=[GT[1]])
            fm("g", ev)

            if cfg.get("STAGE", 99) < 2.7:
                return
            for si, (rows, c0) in enumerate(subs):
                pa, gpa = nxt("x", PS_X)
                pb, gpb = nxt("x", PS_X)
                for k in range(8):
                    P.add("pe", lambda e, pa=pa, k=k, rows=rows, c0=c0: e.matmul(
                        pa[0:rows, 0:512], lhsT=HTt[:, k, c0:c0 + rows], rhs=WTM[0][:, k, 0:512],
                        start=(k == 0), stop=(k == 7)), reads=[WTM[1], gHT], writes=[gpa])
                for k in range(8):
                    P.add("pe", lambda e, pb=pb, k=k, rows=rows, c0=c0: e.matmul(
                        pb[0:rows, 0:128], lhsT=HTt[:, k, c0:c0 + rows], rhs=WTM[0][:, k, 512:640],
                        start=(k == 0), stop=(k == 7)), reads=[WTM[1], gHT], writes=[gpb])
                P.add("act", lambda e, pa=pa, rows=rows: e.activation(out=OTM[0][0:rows, 0:512], in_=pa[0:rows, 0:512], func=AF.Copy),
                      reads=[gpa], writes=[OTM[1]])
                P.add("dve", lambda e, pb=pb, rows=rows: e.tensor_copy(out=OTM[0][0:rows, 512:640], in_=pb[0:rows, 0:128]),
                      reads=[gpb], writes=[OTM[1]])
                if cfg.get("STAGE", 99) < 2.8:
                    continue
                kt = (q0 + c0) // 128
                r_ = (q0 + c0) % 128
                assert r_ == 0
                for i in range(2):
                    P.add("pool", lambda e, i=i, kt=kt, rows=rows: e.tensor_copy(
                        out=VA_M[i][0][0:rows, kt, 0:64], in_=OTM[0][0:rows, 64 * i:64 * i + 64]),
                        reads=[OTM[1]], writes=[VA_M[i][1]])
                P.add("pool", lambda e, kt=kt, rows=rows: e.tensor_copy(
                    out=VA_S[0][0:rows, kt, 0:64], in_=OTM[0][0:rows, 448:512]), reads=[OTM[1]], writes=[VA_S[1]])
                rk = (ring_kt + c0 // 128) % 9
                P.add("pool", lambda e, rk=rk, rows=rows: e.tensor_copy(
                    out=VWR[0][0:rows, rk, 0:64], in_=OTM[0][0:rows, 576:640]), reads=[OTM[1]], writes=[VWR[1]])
                if cfg.get("STAGE", 99) >= 2.9:
                    om_dst(si, rows, c0)

        def compress(c):
            for kvi in range(2):
                lo = 64 * kvi
                px, gpx = nxt("x", PS_X)
                for l in range(32):
                    s0 = 2048 * c + l
                    P.add("pe", lambda e, px=px, l=l, lo=lo, s0=s0: e.matmul(
                        px[:, 0:128], lhsT=W1[0][lo:lo + 64, l, :], rhs=KCVC[0][lo:lo + 64, s0:s0 + 2033:16],
                        start=(l == 0), stop=(l == 31)), reads=[W1[1], KCVC[1]], writes=[gpx])
                P.add("act", lambda e, px=px, kvi=kvi: e.activation(
                    out=SG[0][:, 0:128], in_=px[:, 0:128], func=AF.Sigmoid, bias=PEB[0][:, kvi:kvi + 1]),
                    reads=[gpx, PEB[1]], writes=[SG[1]])
                P.add("dve", lambda e, px=px, kvi=kvi: e.scalar_tensor_tensor(
                    out=AKV[0][:, 128 * kvi:128 * kvi + 128], in0=px[:, 0:128], scalar=PEB[0][:, kvi:kvi + 1], in1=SG[0][:, 0:128],
                    op0=ALU.add, op1=ALU.mult), reads=[gpx, PEB[1], SG[1]], writes=[AKV[1]])
            px, gpx = nxt("x", PS_X)
            P.add("pe", lambda e, px=px: e.matmul(px[0:64, 0:128], lhsT=W2[0][:, 0:64], rhs=AKV[0][:, 0:128], start=True, stop=True),
                  reads=[W2[1], AKV[1]], writes=[gpx])
            P.add("dve", lambda e, px=px: e.tensor_copy(out=KCMP[0][:, 128 * c:128 * c + 128], in_=px[0:64, 0:128]),
                  reads=[gpx], writes=[KCMP[1]])
            px, gpx = nxt("x", PS_X)
            P.add("pe", lambda e, px=px: e.matmul(px[:, 0:64], lhsT=AKV[0][:, 128:256], rhs=W2[0][:, 64:128], start=True, stop=True),
                  reads=[W2[1], AKV[1]], writes=[gpx])
            P.add("dve", lambda e, px=px: e.tensor_copy(out=VC[0][:, c, 0:64], in_=px[:, 0:64]), reads=[gpx], writes=[VC[1]])

        def attend(QT, gQ, qrows, KT, gK, krows, VT, gV, N, tiles, hk, farcol, on_pt=None, first=True, last=True, po=None):
            if po is None:
                po = nxt("o", PS_O)
            pO, gO = po
            nt = len(tiles)
            pts = {}

            def emit_S(ti):
                kc, vs, kind, arg, qlo = tiles[ti]
                pS, gS = nxt("s", PS_S)
                two = kind in ("hk", "cm")
                P.add("pe", lambda e, pS=pS, kc=kc, qlo=qlo, two=two: e.matmul(
                    pS[:, qlo:N], lhsT=KT[0:krows, kc:kc + 128], rhs=QT[0:qrows, qlo:N], start=True, stop=not two),
                    reads=[gK, gQ], writes=[gS])
                if kind == "hk":
                    c0 = arg + 384 + qlo
                    P.add("pe", lambda e, pS=pS, qlo=qlo, c0=c0: e.matmul(
                        pS[:, qlo:N], lhsT=JB[:], rhs=HK[hk][0][:, c0:c0 + N - qlo], start=False, stop=True),
                        reads=[gJB, HK[hk][1]], writes=[gS])
                elif kind == "cm":
                    P.add("pe", lambda e, pS=pS, qlo=qlo, arg=arg: e.matmul(
                        pS[:, qlo:N], lhsT=IDB[:], rhs=CMASK[:, arg, qlo:N], start=False, stop=True),
                        reads=[gIDB, gCMASK], writes=[gS])
                pt, gpt = nxt("pt", PTB)
                if kind == "far":
                    P.add("act", lambda e, pS=pS, pt=pt, qlo=qlo: e.activation(
                        out=pt[:, qlo:N], in_=pS[:, qlo:N], func=AF.Exp, bias=FARB[0][:, farcol:farcol + 1]),
                        reads=[gS, FARB[1]], writes=[gpt])
                else:
                    P.add("act", lambda e, pS=pS, pt=pt, qlo=qlo: e.activation(
                        out=pt[:, qlo:N], in_=pS[:, qlo:N], func=AF.Exp), reads=[gS], writes=[gpt])
                pts[ti] = (pt, gpt)

            def emit_PV(ti):
                kc, vs, kind, arg, qlo = tiles[ti]
                pt, gpt = pts.pop(ti)
                if VT is not None:
                    P.add("pe", lambda e, pt=pt, vs=vs, qlo=qlo, ti=ti: e.matmul(
                        pO[0:65, qlo:N], lhsT=VT[:, vs, 0:65], rhs=pt[:, qlo:N], start=(first and ti == 0), stop=(last and ti == nt - 1)),
                        reads=[gV, gpt], writes=[gO])
                if on_pt is not None:
                    on_pt(ti, pt, gpt)

            if nt > 0:
                emit_S(0)
            for ti in range(nt):
                if ti + 1 < nt:
                    emit_S(ti + 1)
                emit_PV(ti)
            return po

        def finish(po, N, zi, coef_row, acc_first, acc_last, head_slot, q0, ACC=None):
            ACC = ACC or ACCS[0]
            pO, gO = po
            P.add("dve", lambda e: e.tensor_scalar(out=RS[0][64:65, 0:N], in0=pO[64:65, 0:N], scalar1=1e-30, scalar2=None, op0=ALU.max),
                  reads=[gO], writes=[RS[1]])
            P.add("dve", lambda e: e.reciprocal(out=RS[0][64:65, 0:N], in_=RS[0][64:65, 0:N]), reads=[RS[1]], writes=[RS[1]])
            pb, gpb = nxt("x", PS_X)
            P.add("pe", lambda e: e.matmul(pb[0:64, 0:N], lhsT=ONES[64:65, 0:64], rhs=RS[0][64:65, 0:N], start=True, stop=True),
                  reads=[gONES, RS[1]], writes=[gpb])
            P.add("dve", lambda e: e.tensor_tensor(out=BCZ[0][:, 0:N], in0=pb[0:64, 0:N], in1=ZT[zi][0][:, 0:N], op=ALU.mult),
                  reads=[gpb, ZT[zi][1]], writes=[BCZ[1]])
            if coef_row is not None:
                pg, gpg = nxt("x", PS_X)
                P.add("pe", lambda e: e.matmul(pg[0:64, 0:N], lhsT=SEL6[0:6, coef_row * 64:coef_row * 64 + 64], rhs=GT[0][0:6, 0:N],
                                               start=True, stop=True), reads=[gSEL6, GT[1]], writes=[gpg])
                P.add("dve", lambda e: e.tensor_tensor(out=BCZ[0][:, 0:N], in0=BCZ[0][:, 0:N], in1=pg[0:64, 0:N], op=ALU.mult),
                      reads=[gpg, BCZ[1]], writes=[BCZ[1]])
            if acc_first and acc_last:
                P.add("dve", lambda e: e.tensor_tensor(out=MIXB[0][:, 0:N], in0=pO[0:64, 0:N], in1=BCZ[0][:, 0:N], op=ALU.mult),
                      reads=[gO, BCZ[1]], writes=[MIXB[1]])
            elif acc_first:
                P.add("dve", lambda e: e.tensor_tensor(out=ACC[0][:, 0:N], in0=pO[0:64, 0:N], in1=BCZ[0][:, 0:N], op=ALU.mult),
                      reads=[gO, BCZ[1]], writes=[ACC[1]])
            else:
                P.add("dve", lambda e: e.tensor_tensor(out=TMP[0][:, 0:N], in0=pO[0:64, 0:N], in1=BCZ[0][:, 0:N], op=ALU.mult),
                      reads=[gO, BCZ[1]], writes=[TMP[1]])
                if acc_last:
                    P.add("pool", lambda e: e.tensor_tensor(out=MIXB[0][:, 0:N], in0=ACC[0][:, 0:N], in1=TMP[0][:, 0:N], op=ALU.add),
                          reads=[ACC[1], TMP[1]], writes=[MIXB[1]])
                else:
                    P.add("pool", lambda e: e.tensor_tensor(out=ACC[0][:, 0:N], in0=ACC[0][:, 0:N], in1=TMP[0][:, 0:N], op=ALU.add),
                          reads=[ACC[1], TMP[1]], writes=[ACC[1]])
            if acc_last:
                head_slot(MIXB)

        def key_tiles(N, q0, nkeys_tiles, vslot_fn, win=False):
            tl = []
            for kt in range(nkeys_tiles):
                off = q0 - 128 * kt
                qlo = max(0, -off)
                if qlo >= N:
                    continue
                if win:
                    if off > 512:
                        continue
                    tl.append((None, vslot_fn(kt), "hk", off, qlo, kt))
                else:
                    kind = "hk" if off <= 128 else "far"
                    tl.append((128 * kt, vslot_fn(kt), kind, off, qlo, kt))
            return tl

        def attend_tile(hp, N, q0, subs, mix_dst):
            nkt = (q0 + N + 127) // 128
            for i in range(2):
                KT, gK = KA_M[i]
                nblk = (q0 + N) // 256
                curs = [(q0 + c0) // 256 for (_, c0) in subs]
                nb = max(curs)
                if nb > 0:
                    P.add("dve", lambda e, KT=KT, nb=nb, i=i: e.tensor_reduce(
                        out=KMFF[0][:, 0:nb], in_=KT[0:64, 0:nb * 256].rearrange("p (n k) -> p n k", k=256), axis=AX.X, op=ALU.add),
                        reads=[gK], writes=[KMFF[1]])
                    P.add("dve", lambda e, nb=nb, i=i: e.tensor_copy(out=KMF[i][0][:, 0:nb], in_=KMFF[0][:, 0:nb]),
                          reads=[KMFF[1]], writes=[KMF[i][1]])
                for si, (rows, c0) in enumerate(subs):
                    cur = curs[si]
                    P.add("pool", lambda e, rows=rows: e.memset(SC[0][0:rows, :], NEGS), writes=[SC[1]])
                    if cur > 0:
                        px, gpx = nxt("x", PS_X)
                        P.add("pe", lambda e, px=px, rows=rows, c0=c0, cur=cur, i=i: e.matmul(
                            px[0:rows, 0:cur], lhsT=QA_M[i][0][0:64, c0:c0 + rows], rhs=KMF[i][0][:, 0:cur], start=True, stop=True),
                            reads=[QA_M[i][1], KMF[i][1]], writes=[gpx])
                        P.add("dve", lambda e, px=px, rows=rows, cur=cur: e.tensor_copy(out=SC[0][0:rows, 0:cur], in_=px[0:rows, 0:cur]),
                              reads=[gpx], writes=[SC[1]])
                    P.add("dve", lambda e, rows=rows: e.max(out=M8[0][0:rows, 0:8], in_=SC[0][0:rows, 0:32]), reads=[SC[1]], writes=[M8[1]])
                    P.add("dve", lambda e, rows=rows: e.tensor_scalar(out=THR[0][0:rows, 0:1], in0=M8[0][0:rows, 2:3], scalar1=-1e29,
                                                                      scalar2=None, op0=ALU.max), reads=[M8[1]], writes=[THR[1]])
                    P.add("dve", lambda e, rows=rows: e.tensor_scalar(out=MBP[0][0:rows, 64:96], in0=SC[0][0:rows, 0:32],
                                                                      scalar1=THR[0][0:rows, 0:1], scalar2=-BIG, op0=ALU.is_lt, op1=ALU.mult),
                          reads=[SC[1], THR[1]], writes=[MBP[1]])
                    if cur < 32:
                        P.add("dve", lambda e, rows=rows, cur=cur: e.memset(MBP[0][0:rows, 64 + cur:65 + cur], 0.0), writes=[MBP[1]])
                    pt_, gpt_ = nxt("t", PS_T)
                    P.add("pe", lambda e, pt_=pt_, rows=rows: e.transpose(out=pt_[:, 0:rows], in_=MBP[0][0:rows, 0:128], identity=IDB[0:rows, 0:rows]),
                          reads=[MBP[1], gIDB], writes=[gpt_])
                    P.add("act", lambda e, pt_=pt_, rows=rows, c0=c0, i=i: e.activation(
                        out=QA_M[i][0][64:128, c0:c0 + rows], in_=pt_[64:128, 0:rows], func=AF.Copy), reads=[gpt_], writes=[QA_M[i][1]])
                tl = [(a, b, c_, d, e_) for (a, b, c_, d, e_, _) in key_tiles(N, q0, nkt, lambda kt: kt)]
                po = attend(QA_M[i][0], QA_M[i][1], 128, KT, gK, 128, VA_M[i][0], VA_M[i][1], N, tl, hk=i, farcol=hp * 4 + i)
                finish(po, N, i, None, True, True, lambda mb, i=i: mix_dst(2 * hp + i, mb), q0)

            if N == 512:
                t = q0 // 512
                c = t // 4
                if t % 4 == 0 and t > 0:
                    compress(c - 1)
                compress(c)
                cts = list(range(0, c + 1))
            else:
                for c in range(NCT):
                    compress(c)
                cts = list(range(NCT))
            tq = q0 // 512
            ctl = []
            for c in cts:
                dl = 4 * c - tq
                if dl > 0:
                    continue
                kind, arg = ("cm", -dl) if dl >= -4 else ("none", 0)
                ctl.append((128 * c, c, kind, arg, 0))
            nsub = len(subs)
            heads = [(QA_S[0][0][0], QA_S[0][0][1], True, 0), (QA_S[1][0][0], QA_S[1][0][1], True, 1),
                     (QSIB[0][0], QSIB[0][1], False, 0), (QSIB[1][0], QSIB[1][1], False, 1)]
            ocmp = [None, None]
            for hi, (QT, gQ, own, oi) in enumerate(heads):
                pI = [nxt("x", PS_X), nxt("x", PS_X)]
                state = {"first": [True, True]}

                def on_pt(ti, pt, gpt, pI=pI, state=state):
                    cc = ctl[ti][1]
                    for si, (rows, c0) in enumerate(subs):
                        b = si // 2
                        cb = (si % 2) * 129
                        fst = state["first"][b]
                        state["first"][b] = False
                        P.add("pe", lambda e, b=b, cb=cb, rows=rows, c0=c0, cc=cc, fst=fst, pt=pt: e.matmul(
                            pI[b][0][0:rows, cb:cb + 129], lhsT=pt[:, c0:c0 + rows], rhs=OVX[:, cc, :], start=fst, stop=True,
                            skip_group_check=True),
                            reads=[gpt, gOVX], writes=[pI[b][1]])
                po = attend(QT, gQ, 64, KCMP[0], KCMP[1], 64, VC[0] if own else None, VC[1], N, ctl, hk=0, farcol=0, on_pt=on_pt)
                if own:
                    ocmp[oi] = po
                for si, (rows, c0) in enumerate(subs):
                    b = si // 2
                    cb = (si % 2) * 129
                    P.add("dve", lambda e, b=b, cb=cb, rows=rows: e.tensor_scalar(
                        out=RSI[0][0:rows, 0:1], in0=pI[b][0][0:rows, cb + 128:cb + 129], scalar1=1e-30, scalar2=None, op0=ALU.max),
                        reads=[pI[b][1]], writes=[RSI[1]])
                    P.add("dve", lambda e, rows=rows: e.reciprocal(out=RSI[0][0:rows, 0:1], in_=RSI[0][0:rows, 0:1]),
                          reads=[RSI[1]], writes=[RSI[1]])
                    if hi == 0:
                        P.add("dve", lambda e, b=b, cb=cb, rows=rows, si=si: e.tensor_scalar(
                            out=IMPACC[0][0:rows, si, :], in0=pI[b][0][0:rows, cb:cb + 128], scalar1=RSI[0][0:rows, 0:1], scalar2=None,
                            op0=ALU.mult), reads=[pI[b][1], RSI[1]], writes=[IMPACC[1]])
                    else:
                        P.add("dve", lambda e, b=b, cb=cb, rows=rows, si=si: e.scalar_tensor_tensor(
                            out=IMPACC[0][0:rows, si, :], in0=pI[b][0][0:rows, cb:cb + 128], scalar=RSI[0][0:rows, 0:1],
                            in1=IMPACC[0][0:rows, si, :], op0=ALU.mult, op1=ALU.add), reads=[pI[b][1], RSI[1], IMPACC[1]], writes=[IMPACC[1]])
            rank = 16 if (q0 // 64) < 128 else 15
            for si, (rows, c0) in enumerate(subs):
                cur0 = (q0 + c0) // 64
                p0 = 128 - cur0
                P.add("dve", lambda e, rows=rows, si=si, p0=p0: e.tensor_tensor(
                    out=IMPM[0][0:rows, :], in0=IMPACC[0][0:rows, si, :], in1=PATM[0:rows, p0:p0 + 128], op=ALU.mult),
                    reads=[IMPACC[1], gPATM], writes=[IMPM[1]])
                P.add("dve", lambda e, rows=rows, p0=p0: e.tensor_tensor(
                    out=IMPM[0][0:rows, :], in0=IMPM[0][0:rows, :], in1=PATA[0:rows, p0:p0 + 128], op=ALU.add),
                    reads=[IMPM[1], gPATA], writes=[IMPM[1]])
                P.add("dve", lambda e, rows=rows: e.memset(IMPM[0][0:rows, 0:1], 1e9), writes=[IMPM[1]])
                P.add("dve", lambda e, rows=rows: e.max(out=M8[0][0:rows, 0:8], in_=IMPM[0][0:rows, :]), reads=[IMPM[1]], writes=[M8[1]])
                P.add("dve", lambda e, rows=rows: e.match_replace(out=IMP2[0][0:rows, :], in_to_replace=M8[0][0:rows, 0:8],
                                                                  in_values=IMPM[0][0:rows, :], imm_value=-3e38),
                      reads=[IMPM[1], M8[1]], writes=[IMP2[1]])
                P.add("dve", lambda e, rows=rows: e.max(out=M8[0][0:rows, 8:16], in_=IMP2[0][0:rows, :]), reads=[IMP2[1]], writes=[M8[1]])
                P.add("dve", lambda e, rows=rows: e.tensor_scalar(out=THR[0][0:rows, 1:2], in0=M8[0][0:rows, rank - 1:rank], scalar1=-1e29,
                                                                  scalar2=None, op0=ALU.max), reads=[M8[1]], writes=[THR[1]])
                for v in range(2):
                    P.add("dve", lambda e, rows=rows, v=v: e.tensor_scalar(
                        out=MBP[0][0:rows, 64:128], in0=IMPM[0][0:rows, 64 * v:64 * v + 64],
                        scalar1=THR[0][0:rows, 1:2], scalar2=-BIG, op0=ALU.is_lt, op1=ALU.mult),
                        reads=[IMPM[1], THR[1]], writes=[MBP[1]])
                    pt_, gpt_ = nxt("t", PS_T)
                    P.add("pe", lambda e, pt_=pt_, rows=rows: e.transpose(out=pt_[:, 0:rows], in_=MBP[0][0:rows, 0:128], identity=IDB[0:rows, 0:rows]),
                          reads=[MBP[1], gIDB], writes=[gpt_])
                    for i in range(2):
                        P.add("act" if i == 0 else "dve", (lambda e, pt_=pt_, rows=rows, c0=c0, i=i, v=v: e.activation(
                            out=QA_S[i][v][0][64:128, c0:c0 + rows], in_=pt_[64:128, 0:rows], func=AF.Copy)) if i == 0 else
                            (lambda e, pt_=pt_, rows=rows, c0=c0, i=i, v=v: e.tensor_copy(
                                out=QA_S[i][v][0][64:128, c0:c0 + rows], in_=pt_[64:128, 0:rows])),
                            reads=[gpt_], writes=[QA_S[i][v][1]])
            for i in range(2):
                finish(ocmp[i], N, 2 + i, 3 * i + 0, True, False, None, q0, ACC=ACCS[i])
            for i in range(2):
                ktl = key_tiles(N, q0, nkt, lambda kt: kt)
                lo = [(a, b, c_, d, e_) for (a, b, c_, d, e_, kt) in ktl if kt < 32 or kt * 128 >= max(T, PAST)]
                hi_ = [(a, b, c_, d, e_) for (a, b, c_, d, e_, kt) in ktl if not (kt < 32 or kt * 128 >= max(T, PAST))]
                po = attend(QA_S[i][0][0], QA_S[i][0][1], 128, KA_S[0], KA_S[1], 128, VA_S[0], VA_S[1], N, lo, hk=2 + i,
                            farcol=hp * 4 + 2 + i, first=True, last=(len(hi_) == 0))
                if hi_:
                    attend(QA_S[i][1][0], QA_S[i][1][1], 128, KA_S[0], KA_S[1], 128, VA_S[0], VA_S[1], N, hi_, hk=2 + i,
                           farcol=hp * 4 + 2 + i, first=False, last=True, po=po)
                finish(po, N, 2 + i, 3 * i + 1, False, False, None, q0, ACC=ACCS[i])
                wtl = [((kt % 9) * 128, kt % 9, c_, d, e_) for (a, b, c_, d, e_, kt) in key_tiles(N, q0, nkt, lambda kt: kt, win=True)]
                po = attend(QA_S[i][0][0], QA_S[i][0][1], 64, KWR[0], KWR[1], 64, VWR[0], VWR[1], N, wtl, hk=4 + i, farcol=0)
                finish(po, N, 2 + i, 3 * i + 2, False, True, lambda mb, i=i: mix_dst(8 + 2 * hp + i, mb), q0, ACC=ACCS[i])

        def norm_rows(src_ap, rows, HTcol0, rowsel):
            P.add("sp", lambda e: e.dma_start(out=XT_[0][0:rows, :], in_=src_ap), writes=[XT_[1]], dma=True)
            P.add("act", lambda e: e.activation(out=XN[0][0:rows, :], in_=XT_[0][0:rows, :], func=AF.Square, accum_out=SSQ[0][0:rows, 0:1]),
                  reads=[XT_[1]], writes=[XN[1], SSQ[1]])
            P.add("act", lambda e: e.activation(out=SSQ[0][0:rows, 1:2], in_=SSQ[0][0:rows, 0:1], func=AF.Sqrt, scale=1.0 / D, bias=EPS[0:rows, :]),
                  reads=[SSQ[1], gEPS], writes=[SSQ[1]])
            P.add("dve", lambda e: e.reciprocal(out=SSQ[0][0:rows, 2:3], in_=SSQ[0][0:rows, 1:2]), reads=[SSQ[1]], writes=[SSQ[1]])
            P.add("dve", lambda e: e.tensor_scalar(out=XN[0][0:rows, :], in0=XT_[0][0:rows, :], scalar1=SSQ[0][0:rows, 2:3], scalar2=None, op0=ALU.mult),
                  reads=[XT_[1], SSQ[1]], writes=[XN[1]])

        def transpose_rows(rows, HTcol0, groups):
            for k in range(8):
                pt_, gpt_ = nxt("t", PS_T)
                P.add("pe", lambda e, pt_=pt_, k=k: e.transpose(out=pt_[:, 0:rows], in_=XN[0][0:rows, k * 128:(k + 1) * 128],
                                                                identity=IDB[0:rows, 0:rows]), reads=[XN[1], gIDB], writes=[gpt_])
                for (r0, n_, ar) in groups:
                    P.add("act", lambda e, pt_=pt_, k=k, r0=r0, n_=n_, ar=ar: e.activation(
                        out=HT[0][:, k, HTcol0 + r0:HTcol0 + r0 + n_], in_=pt_[:, r0:r0 + n_], func=AF.Identity,
                        scale=SCT[0][:, k, ar:ar + 1], bias=SHT[0][:, k, ar:ar + 1]), reads=[gpt_, SCT[1], SHT[1]], writes=[HT[1]])

        subsP = [(128, 128 * s) for s in range(4)]

        def mix_dst_prompt(q0, N):
            def f(head, mb):
                P.add("sp", lambda e: e.dma_start(out=mixd_d[head, :, q0:q0 + N], in_=mb[0][:, 0:N]), reads=[mb[1]], writes=[gMIXD], dma=True)
            return f

        gMIXD = G()
        STAGE = cfg.get("STAGE", 99)
        for hp in range(NHP if STAGE >= 4 else (1 if STAGE >= 2 else 0)):
            load_hp(hp)
            for t in range(NT):
                q0 = 512 * t
                for s in range(4):
                    if STAGE >= 2.2:
                        norm_rows(x_d[q0 + 128 * s:q0 + 128 * s + 128, :], 128, 128 * s, None)
                    if STAGE >= 2.3:
                        transpose_rows(128, 128 * s, [(0, 128, 0)])
                if STAGE < 2.4:
                    continue

                def om_dst(si, rows, c0, hp=hp, q0=q0):
                    P.add("sp", lambda e: e.dma_start(out=om_d[hp, q0 + c0:q0 + c0 + rows, :], in_=OTM[0][0:rows, 0:256]), reads=[OTM[1]], dma=True)
                    P.add("sp", lambda e: e.dma_start(out=on_d[hp, q0 + c0:q0 + c0 + rows, :], in_=OTM[0][0:rows, 256:512]), reads=[OTM[1]], dma=True)
                    P.add("sp", lambda e: e.dma_start(out=ow_d[hp, q0 + c0:q0 + c0 + rows, :], in_=OTM[0][0:rows, 512:640]), reads=[OTM[1]], dma=True)
                project_tile(hp, 512, q0, None, subsP, om_dst, None, None, ring_kt=4 * t)
                if STAGE >= 3:
                    attend_tile(hp, 512, q0, subsP, mix_dst_prompt(q0, 512))

        if NS > 0 and STAGE >= 5:
            NTOK = NS * 4
            PTI = sb("pti", [128, NPG], I32)
            IDX = sb("idx", [128, NPG], I32)
            STG = [sb("stg%d" % i, [128, 512], BF16) for i in range(2)]
            norm_rows(xs_d, NTOK, 0, None)
            transpose_rows(NTOK, 0, [(4 * s, 4, 1 + s) for s in range(NS)])
            for s in range(NS):
                P.add("sp", lambda e, s=s: e.dma_start(out=owin_d[s], in_=win_d[s, 4:WB, :]), dma=True)
            HTS = sb("hts", [128, 8, 16], BF16)
            P.add("dve", lambda e: e.tensor_copy(out=HTS[0][:, :, 0:NTOK], in_=HT[0][:, :, 0:NTOK]), reads=[HT[1]], writes=[HTS[1]])
            for s in range(NS):
                P.add("sp", lambda e, s=s: e.dma_start(out=PTI[0][:], in_=ptab_d[s].partition_broadcast(128)), writes=[PTI[1]], dma=True)
                P.add("dve", lambda e: e.tensor_scalar(out=IDX[0][:], in0=PTI[0][:], scalar1=128, scalar2=IOP[:, 0:1], op0=ALU.mult, op1=ALU.add),
                      reads=[PTI[1], gIOP], writes=[IDX[1]])
                for hp in range(NHP):
                    load_hp(hp)
                    kv = hp // 2
                    for pg in range(NPG):
                        sg, gsg = STG[pg % 2]
                        P.add("pool", lambda e, sg=sg, pg=pg, hp=hp: e.indirect_dma_start(
                            out=sg[:, 0:256], out_offset=None,
                            in_=cm_d[hp],
                            in_offset=bass.IndirectOffsetOnAxis(ap=IDX[0][:, pg:pg + 1], axis=0)),
                            reads=[IDX[1]], writes=[gsg], dma=True)
                        P.add("pool", lambda e, sg=sg, pg=pg, kv=kv: e.indirect_dma_start(
                            out=sg[:, 256:512], out_offset=None,
                            in_=cn_d[kv],
                            in_offset=bass.IndirectOffsetOnAxis(ap=IDX[0][:, pg:pg + 1], axis=0)),
                            reads=[IDX[1]], writes=[gsg], dma=True)
                        kc0 = 128 * pg
                        for i in range(2):
                            pt_, gpt_ = nxt("t", PS_T)
                            P.add("pe", lambda e, pt_=pt_, sg=sg, i=i: e.transpose(out=pt_[0:64, 0:128], in_=sg[:, 64 * i:64 * i + 64], identity=IDB[:]),
                                  reads=[gsg, gIDB], writes=[gpt_])
                            P.add("act" if i == 0 else "dve", (lambda e, pt_=pt_, i=i, kc0=kc0: e.activation(
                                out=KA_M[i][0][0:64, kc0:kc0 + 128], in_=pt_[0:64, 0:128], func=AF.Copy)) if i == 0 else
                                (lambda e, pt_=pt_, i=i, kc0=kc0: e.tensor_copy(out=KA_M[i][0][0:64, kc0:kc0 + 128], in_=pt_[0:64, 0:128])),
                                reads=[gpt_], writes=[KA_M[i][1]])
                            P.add("pool", lambda e, sg=sg, i=i, pg=pg: e.tensor_copy(out=VA_M[i][0][:, pg, 0:64], in_=sg[:, 128 + 64 * i:192 + 64 * i]),
                                  reads=[gsg], writes=[VA_M[i][1]])
                        pt_, gpt_ = nxt("t", PS_T)
                        P.add("pe", lambda e, pt_=pt_, sg=sg: e.transpose(out=pt_[:, 0:128], in_=sg[:, 256:384], identity=IDB[:]),
                              reads=[gsg, gIDB], writes=[gpt_])
                        P.add("act", lambda e, pt_=pt_, kc0=kc0: e.activation(out=KCVC[0][:, kc0:kc0 + 128], in_=pt_[:, 0:128], func=AF.Copy),
                              reads=[gpt_], writes=[KCVC[1]])
                        pt_, gpt_ = nxt("t", PS_T)
                        P.add("pe", lambda e, pt_=pt_, sg=sg: e.transpose(out=pt_[0:64, 0:128], in_=sg[:, 384:448], identity=IDB[:]),
                              reads=[gsg, gIDB], writes=[gpt_])
                        P.add("dve", lambda e, pt_=pt_, kc0=kc0: e.tensor_copy(out=KA_S[0][0:64, kc0:kc0 + 128], in_=pt_[0:64, 0:128]),
                              reads=[gpt_], writes=[KA_S[1]])
                        P.add("pool", lambda e, sg=sg, pg=pg: e.tensor_copy(out=VA_S[0][:, pg, 0:64], in_=sg[:, 448:512]),
                              reads=[gsg], writes=[VA_S[1]])
                    for wt in range(WB // 128):
                        sg, gsg = STG[wt % 2]
                        kt = (PAST - WB) // 128 + wt
                        P.add("pool", lambda e, sg=sg, s=s, wt=wt, kv=kv: e.dma_start(
                            out=sg[:, 0:128].rearrange("p (f c) -> p f c", f=2),
                            in_=win_d[s, 128 * wt:128 * wt + 128, :].rearrange("r (f h d) -> r f h d", f=2, h=2)[:, :, kv, :]),
                            writes=[gsg], dma=True)
                        pt_, gpt_ = nxt("t", PS_T)
                        P.add("pe", lambda e, pt_=pt_, sg=sg: e.transpose(out=pt_[0:64, 0:128], in_=sg[:, 0:64], identity=IDB[:]),
                              reads=[gsg, gIDB], writes=[gpt_])
                        r0 = (kt % 9) * 128
                        P.add("dve", lambda e, pt_=pt_, r0=r0: e.tensor_copy(out=KWR[0][0:64, r0:r0 + 128], in_=pt_[0:64, 0:128]),
                              reads=[gpt_], writes=[KWR[1]])
                        P.add("pool", lambda e, sg=sg, kt=kt: e.tensor_copy(out=VWR[0][:, kt % 9, 0:64], in_=sg[:, 64:128]),
                              reads=[gsg], writes=[VWR[1]])
                    ktn = PAST // 128
                    for (t_, g_) in (KA_M[0], KA_M[1], KA_S, KCVC):
                        P.add("pool", lambda e, t_=t_: e.memset(t_[0:64, PAST:PAST + 128], 0.0), writes=[g_])
                    P.add("pool", lambda e: e.memset(KCVC[0][64:128, PAST:PAST + 128], 0.0), writes=[KCVC[1]])
                    P.add("pool", lambda e: e.memset(KWR[0][0:64, (ktn % 9) * 128:(ktn % 9) * 128 + 128], 0.0), writes=[KWR[1]])
                    for (t_, g_) in (VA_M[0], VA_M[1], VA_S):
                        P.add("pool", lambda e, t_=t_: e.memset(t_[:, ktn, 0:64], 0.0), writes=[g_])
                    P.add("pool", lambda e: e.memset(VWR[0][:, ktn % 9, 0:64], 0.0), writes=[VWR[1]])
                    P.add("dve", lambda e, s=s: e.tensor_copy(out=HT[0][:, :, 0:4], in_=HTS[0][:, :, 4 * s:4 * s + 4]), reads=[HTS[1]], writes=[HT[1]])

                    def om_dst(si, rows, c0, hp=hp, s=s):
                        P.add("sp", lambda e: e.dma_start(out=oms_d[hp, 4 * s:4 * s + 4, :], in_=OTM[0][0:4, 0:256]), reads=[OTM[1]], dma=True)
                        P.add("sp", lambda e: e.dma_start(out=ons_d[hp, 4 * s:4 * s + 4, :], in_=OTM[0][0:4, 256:512]), reads=[OTM[1]], dma=True)
                        P.add("sp", lambda e: e.dma_start(out=ows_d[hp, 4 * s:4 * s + 4, :], in_=OTM[0][0:4, 512:640]), reads=[OTM[1]], dma=True)
                    project_tile(hp, 4, PAST, None, [(4, 0)], om_dst, None, None, ring_kt=PAST // 128)

                    def mix_dst(head, mb, s=s):
                        P.add("pool", lambda e: e.tensor_copy(out=MIXS[0][:, head, 4 * s:4 * s + 4], in_=mb[0][:, 0:4]), reads=[mb[1]], writes=[MIXS[1]])
                    attend_tile(hp, 4, PAST, [(4, 0)], mix_dst)

        P.flush()
        stA.close()
        cur[0] = st
        P.barrier()
        GATEP = sb("gatep", [128, D], F32)
        GATES = sb("gates", [16, D], F32)
        FG = sb("fg", [128, D], F32)
        gate_tiles.update({"p": GATEP, "s": GATES, "f": FG})
        if STAGE >= 6:
            adaln("gate")
        XT_ = sb("xt2", [128, D], F32)
        SSQ = sb("ssq2", [128, 4], F32)
        WOUT = sb("wout", [64, 16, D], BF16)
        MIXL = sb("mixl", [64, 16, 512], BF16)
        YP = sb("yp", [128, D], F32)
        if STAGE >= 6:
            P.add("pool", lambda e: e.dma_start(out=WOUT[0][:], in_=wout_d), writes=[WOUT[1]], dma=True)

        def outproj(rows, lhs_fn, x_ap, gate_t, y_ap, extra_reads):
            P.add("sp", lambda e: e.dma_start(out=XT_[0][0:rows, :], in_=x_ap), writes=[XT_[1]], dma=True)
            for oc in range(2):
                px, gpx = nxt("x", PS_X)
                for h in range(16):
                    P.add("pe", lambda e, px=px, h=h, oc=oc: e.matmul(
                        px[0:rows, 0:512], lhsT=lhs_fn(h), rhs=WOUT[0][:, h, oc * 512:(oc + 1) * 512], start=(h == 0), stop=(h == 15)),
                        reads=[WOUT[1]] + extra_reads, writes=[gpx])
                P.add("dve", lambda e, px=px, oc=oc: e.tensor_tensor(
                    out=YP[0][0:rows, oc * 512:(oc + 1) * 512], in0=px[0:rows, 0:512], in1=gate_t[0][0:rows, oc * 512:(oc + 1) * 512], op=ALU.mult),
                    reads=[gpx, gate_t[1]], writes=[YP[1]])
            P.add("pool", lambda e: e.tensor_tensor(out=YP[0][0:rows, :], in0=YP[0][0:rows, :], in1=XT_[0][0:rows, :], op=ALU.add),
                  reads=[YP[1], XT_[1]], writes=[YP[1]])
            P.add("act", lambda e: e.activation(out=XT_[0][0:rows, :], in_=YP[0][0:rows, :], func=AF.Square, accum_out=SSQ[0][0:rows, 0:1]),
                  reads=[YP[1]], writes=[XT_[1], SSQ[1]])
            P.add("act", lambda e: e.activation(out=SSQ[0][0:rows, 1:2], in_=SSQ[0][0:rows, 0:1], func=AF.Sqrt, scale=1.0 / D, bias=EPS[0:rows, :]),
                  reads=[SSQ[1], gEPS], writes=[SSQ[1]])
            P.add("dve", lambda e: e.reciprocal(out=SSQ[0][0:rows, 2:3], in_=SSQ[0][0:rows, 1:2]), reads=[SSQ[1]], writes=[SSQ[1]])
            P.add("dve", lambda e: e.scalar_tensor_tensor(out=YP[0][0:rows, :], in0=YP[0][0:rows, :], scalar=SSQ[0][0:rows, 2:3],
                                                          in1=FG[0][0:rows, :], op0=ALU.mult, op1=ALU.mult),
                  reads=[YP[1], SSQ[1], FG[1]], writes=[YP[1]])
            P.add("sp", lambda e: e.dma_start(out=y_ap, in_=YP[0][0:rows, :]), reads=[YP[1]], dma=True)

        if NHP == 4 and STAGE >= 6:
            for t in range(NT):
                q0 = 512 * t
                P.add("sp", lambda e, q0=q0: e.dma_start(out=MIXL[0][:], in_=mixd_d[:, :, q0:q0 + 512].rearrange("h p n -> p h n")),
                      reads=[gMIXD], writes=[MIXL[1]], dma=True)
                for s in range(4):
                    outproj(128, lambda h, s=s: MIXL[0][:, h, 128 * s:128 * s + 128], x_d[q0 + 128 * s:q0 + 128 * s + 128, :],
                            GATEP, y_d[q0 + 128 * s:q0 + 128 * s + 128, :], [MIXL[1]])

        if NS > 0 and NHP == 4 and STAGE >= 6:
            outproj(NS * 4, lambda h: MIXS[0][:, h, 0:NS * 4], xs_d, GATES, ys_d, [MIXS[1]])

        P.emit_all(st)
    return nc


def make_cfg(T, PAST, NS, NHP, NPHYS):
    return {"T": T, "PAST": PAST, "NS": NS, "NHP": NHP, "TK": ((max(T, PAST) + 2047) // 2048) * 2048 + 128, "NPHYS": NPHYS, "WB": min(512, PAST)}


def make_in_maps(inp, cfg, ncores):
    NS, NHP = cfg["NS"], cfg["NHP"]
    f32 = lambda a: np.ascontiguousarray(np.asarray(a, dtype=np.float32))
    w_in = f32(inp["w_in"])[0]
    rb = f32(inp["rel_bias"])
    shared = {}
    shared["w_ada"] = f32(inp["w_ada"])[0]
    shared["b_ada"] = f32(inp["b_ada"])[0]
    shared["gain"] = f32(inp["norm_gain"])[0]
    shared["fgain"] = f32(inp["final_gain"])
    shared["wfm"] = np.stack([w_in[:, fm_cols(hp)] for hp in range(NHP)])
    shared["wtm"] = np.stack([w_in[:, tm_cols(hp)] for hp in range(NHP)])
    w1k = f32(inp["cmp_k_w1"])[0].reshape(32, 64, 128).transpose(1, 0, 2)
    w1v = f32(inp["cmp_v_w1"])[0].reshape(32, 64, 128).transpose(1, 0, 2)
    shared["w1"] = np.ascontiguousarray(np.concatenate([w1k, w1v], 0))
    shared["w2"] = np.ascontiguousarray(np.concatenate([f32(inp["cmp_k_w2"])[0], f32(inp["cmp_v_w2"])[0]], 1))
    pe = f32(inp["cmp_pe"])[0]
    shared["pet"] = np.ascontiguousarray(np.concatenate([pe[0].T, pe[1].T], 0))
    shared["wout"] = np.ascontiguousarray(f32(inp["w_out"])[0].reshape(16, 64, D).transpose(1, 0, 2))
    ts, t31 = [], []
    for hp in range(NHP):
        hs = [2 * hp, 2 * hp + 1, 8 + 2 * hp, 9 + 2 * hp, 8 + 2 * hp, 9 + 2 * hp]
        ts.append(rb[:, hs])
        t31.append(rb[31, hs[:4]])
    shared["tabsel"] = np.ascontiguousarray(np.concatenate(ts, 1))
    shared["tab31"] = np.ascontiguousarray(np.concatenate(t31, 0))
    cmk = f32(inp["cache_moba_kv"])[0]
    nph = cmk.shape[0]
    for hp in range(4):
        shared["cache_m%d" % hp] = np.ascontiguousarray(cmk[:, :, :, 2 * hp:2 * hp + 2, :].reshape(nph * 128, 256))
    cnk = f32(inp["cache_nsa_kv"])[0]
    for kv in range(2):
        shared["cache_n%d" % kv] = np.ascontiguousarray(cnk[:, :, :, kv, :].reshape(nph * 128, 256))
    for k_, v_ in host_consts(cfg).items():
        shared["c_" + k_] = v_
    xp = f32(inp["x_prompt"])
    xs = f32(inp["x_sample"])
    cp = f32(inp["c_prompt"])
    cs = f32(inp["c_sample"])
    win = f32(inp["state_nsa_win"])[0]
    pt = np.asarray(inp["page_table"]).astype(np.int32)
    maps = []
    for c in range(ncores):
        b = c % xp.shape[0]
        m = dict(shared)
        m["x"] = xp[b]
        m["xs"] = np.ascontiguousarray(xs[NS * c:NS * c + NS].reshape(NS * 4, D))
        cT = np.zeros((D, 144), np.float32)
        cT[:, 0:128] = cp[b][:, None]
        for s in range(NS):
            cT[:, 128 + 4 * s:132 + 4 * s] = cs[NS * c + s][:, None]
        m["cT"] = cT
        m["win"] = np.ascontiguousarray(win[NS * c:NS * c + NS].reshape(NS, win.shape[1], 256))
        m["ptab"] = np.ascontiguousarray(pt[NS * c:NS * c + NS])
        maps.append(m)
    return maps


def assemble(res, cfg, B, ncores):
    T, NS, NHP, WB = cfg["T"], cfg["NS"], cfg["NHP"], cfg["WB"]
    DB = NS * ncores
    y_p = np.zeros((B, T, D), np.float32)
    y_s = np.zeros((DB, 4, D), np.float32)
    mkp = np.zeros((1, B, T, 2, 8, 64), np.float32)
    mks = np.zeros((1, DB, 4, 2, 8, 64), np.float32)
    nkp = np.zeros((1, B, T, 4, 2, 64), np.float32)
    nks = np.zeros((1, DB, 4, 4, 2, 64), np.float32)
    wp = np.zeros((1, B, min(512, T), 2, 2, 64), np.float32)
    ws = np.zeros((1, DB, WB, 2, 2, 64), np.float32)
    for c in range(ncores):
        r = res[c]
        if c < B:
            b = c
            y_p[b] = r["y"]
            for hp in range(NHP):
                om = np.asarray(r["om"][hp])
                mkp[0, b, :, 1, 2 * hp:2 * hp + 2, :] = om[:, 0:128].reshape(T, 2, 64)
                mkp[0, b, :, 0, 2 * hp:2 * hp + 2, :] = om[:, 128:256].reshape(T, 2, 64)
            for kv in range(2):
                on = np.asarray(r["on"][2 * kv])
                ow = np.asarray(r["ow"][2 * kv])
                for f in range(4):
                    nkp[0, b, :, f, kv, :] = on[:, 64 * f:64 * f + 64]
                for f in range(2):
                    wp[0, b, :, f, kv, :] = ow[T - wp.shape[2]:, 64 * f:64 * f + 64]
        sl = slice(NS * c, NS * c + NS)
        y_s[sl] = np.asarray(r["ys"]).reshape(NS, 4, D)
        for hp in range(NHP):
            om = np.asarray(r["oms"][hp]).reshape(NS, 4, 256)
            mks[0, sl, :, 1, 2 * hp:2 * hp + 2, :] = om[:, :, 0:128].reshape(NS, 4, 2, 64)
            mks[0, sl, :, 0, 2 * hp:2 * hp + 2, :] = om[:, :, 128:256].reshape(NS, 4, 2, 64)
        ws[0, sl, 0:WB - 4] = np.asarray(r["owin"]).reshape(NS, WB - 4, 2, 2, 64)
        for kv in range(2):
            on = np.asarray(r["ons"][2 * kv]).reshape(NS, 4, 256)
            ow = np.asarray(r["ows"][2 * kv]).reshape(NS, 4, 128)
            for f in range(4):
                nks[0, sl, :, f, kv, :] = on[:, :, 64 * f:64 * f + 64]
            for f in range(2):
                ws[0, sl, WB - 4:, f, kv, :] = ow[:, :, 64 * f:64 * f + 64]
    return (y_p, y_s, mkp, mks, nkp, nks, wp, ws)


def kernel(**inputs):
    cfg = make_cfg(8192, 8192, 4, 4, 2560)
    nc = build(dict(cfg))
    maps = make_in_maps(inputs, cfg, 8)
    res = run_bass_kernel_spmd(nc, maps, core_ids=list(range(8)))
    return assemble(res.results, cfg, 2, 8)
```

```python
import contextlib
import math
import numpy as np
import ml_dtypes
import concourse.bass as bass
import concourse.mybir as mybir
from concourse.bass_utils import run_bass_kernel_spmd

F32 = mybir.dt.float32
BF16 = mybir.dt.bfloat16
I32 = mybir.dt.int32
AF = mybir.ActivationFunctionType
ALU = mybir.AluOpType
AX = mybir.AxisListType
NPBF = ml_dtypes.bfloat16

BIG = 30000.0
NEGS = -1e30
GL = 1664
HW = 1408
D = 1024


class G:
    __slots__ = ("w", "r", "excl")

    def __init__(self, excl=False):
        self.w = None
        self.r = []
        self.excl = excl


class Op:
    __slots__ = ("eng", "emit", "deps", "inc", "val", "dma", "slot", "dval", "prev")

    def __init__(self, eng, emit, dma):
        self.eng = eng
        self.emit = emit
        self.dma = dma
        self.deps = []
        self.inc = False
        self.val = 0
        self.slot = None
        self.dval = 0
        self.prev = 0


SERIAL = [True]
NSLOT = {"sp": 12, "act": 2, "pool": 12}
ENGS = ("pe", "act", "dve", "pool", "sp")


class Prog:
    def __init__(self, nc):
        self.nc = nc
        self.ops = {e: [] for e in ENGS}
        self.rr = {e: 0 for e in NSLOT}
        self.sv = {e: [0] * n for e, n in NSLOT.items()}
        self.pend = {e: [] for e in ENGS}
        self.lastdma = {}

    def barrier(self):
        deps = []
        for e in ("pe", "act", "dve", "pool"):
            for op in reversed(self.ops[e]):
                if not op.dma:
                    deps.append(op)
                    break
        deps += list(self.lastdma.values())
        for e in ENGS:
            self.pend[e] = list(deps)

    def add(self, eng, emit, reads=(), writes=(), dma=False):
        op = Op(eng, emit, dma)
        deps = []
        seen = set()

        def push(d):
            if d is None or id(d) in seen:
                return
            seen.add(id(d))
            if d.eng == "pe" and eng == "pe" and not d.dma and not dma:
                return
            deps.append(d)

        for g in reads:
            push(g.w)
            if g.excl:
                for r in g.r:
                    push(r)
        for g in writes:
            push(g.w)
            for r in g.r:
                push(r)
        if eng in ("act", "dve", "pool") and not dma and SERIAL[0]:
            for prev in reversed(self.ops[eng]):
                if not prev.dma:
                    if id(prev) not in seen:
                        seen.add(id(prev))
                        deps.append(prev)
                    break
        if self.pend[eng]:
            for d in self.pend[eng]:
                if d is not None and id(d) not in seen:
                    seen.add(id(d))
                    deps.append(d)
            self.pend[eng] = []
        op.deps = deps
        for d in deps:
            d.inc = True
        for g in reads:
            g.r.append(op)
        for g in writes:
            g.w = op
            g.r = []
        if dma:
            i = self.rr[eng]
            self.rr[eng] = (i + 1) % NSLOT[eng]
            op.slot = i
            op.prev = self.sv[eng][i]
            self.sv[eng][i] += 16
            op.dval = self.sv[eng][i]
            self.lastdma[(eng, i)] = op
        self.ops[eng].append(op)
        return op

    def setup(self, stack):
        nc = self.nc
        self.stack = stack
        self.csem = {e: [] for e in ("pe", "act", "dve", "pool")}
        self.dsem = {e: [stack.enter_context(nc.semaphore("d_%s%d" % (e, i))) for i in range(n)]
                     for e, n in NSLOT.items()}
        self.cursor = {e: 0 for e in ENGS}
        self.cval = {e: 0 for e in ENGS}
        self.waited = {e: {} for e in ENGS}

    def flush(self, final=False):
        nc = self.nc
        csem, dsem = self.csem, self.dsem
        ops = self.ops
        sv = self.sv
        EP = 12000
        for e in ("pe", "act", "dve", "pool"):
            for op in ops[e][self.cursor[e]:]:
                if not op.dma:
                    self.cval[e] += 1
                    op.val = self.cval[e]

        def csem_of(eng, val):
            ep = (val - 1) // EP
            lst = csem[eng]
            while len(lst) <= ep:
                lst.append(self.stack.enter_context(nc.semaphore("c_%s%d" % (eng, len(lst)))))
            return lst[ep], val - ep * EP, ("c", eng, ep)

        def sig(d):
            if d.dma:
                return dsem[d.eng][d.slot], d.dval, ("d", d.eng, d.slot)
            return csem_of(d.eng, d.val)

        def run(engname, e, fin=False):
            waited = self.waited[engname]

            def w(sem, val, key):
                if waited.get(key, 0) < val:
                    e.wait_ge(sem, val)
                    waited[key] = val

            for op in ops[engname][self.cursor[engname]:]:
                for d in op.deps:
                    s, v, k = sig(d)
                    w(s, v, k)
                if op.dma and op.prev > 0:
                    w(dsem[engname][op.slot], op.prev, ("d", engname, op.slot))
                ins = op.emit(e)
                if op.dma:
                    ins.then_inc(dsem[engname][op.slot], 16)
                else:
                    ins.then_inc(csem_of(engname, op.val)[0], 1)
            self.cursor[engname] = len(ops[engname])
            if fin:
                for qe, n in NSLOT.items():
                    for i in range(n):
                        if sv[qe][i] > 0:
                            w(dsem[qe][i], sv[qe][i], ("d", qe, i))

        with nc.Block() as block:
            @block.sync
            def _(e):
                run("sp", e, fin=final)

            @block.scalar
            def _(e):
                run("act", e)

            @block.vector
            def _(e):
                run("dve", e)

            @block.gpsimd
            def _(e):
                run("pool", e)

            @block.tensor
            def _(e):
                run("pe", e)

    def emit_all(self, stack):
        self.flush(final=True)


def t5_bucket_np(rel):
    n = np.maximum(rel, 0)
    nf = np.maximum(n, 1).astype(np.float32)
    large = 16 + (np.log(nf / np.float32(16)) / np.float32(math.log(8.0)) * np.float32(16)).astype(np.int32)
    return np.where(n < 16, n, np.minimum(large, 31))


def host_consts(cfg):
    T, PAST = cfg["T"], cfg["PAST"]
    TK = cfg["TK"]
    c = {}
    c["idb"] = np.eye(128, dtype=np.float32).astype(NPBF)
    c["jb"] = np.eye(128, dtype=np.float32)[::-1].copy().astype(NPBF)
    c["ones"] = np.ones((128, 64), np.float32)
    sel6 = np.zeros((6, 6 * 64), np.float32)
    for r in range(6):
        sel6[r, r * 64:(r + 1) * 64] = 1.0
    c["sel6"] = sel6
    k = np.arange(TK)
    kmax = max(T, PAST)
    indm = np.zeros((64, TK), np.float32)
    inds = np.zeros((64, TK), np.float32)
    for r in range(32):
        indm[r] = ((k // 256) == r) & (k < kmax)
    for r in range(64):
        inds[r] = (((k // 64) % 64) == r) & (k < kmax)
    c["indm"] = indm.astype(NPBF)
    c["inds"] = inds.astype(NPBF)
    ni = np.arange(128)[:, None]
    qi = np.arange(512)[None, :]
    cm = np.zeros((128, 5, 512), np.float32)
    for j, dl in enumerate([0, -1, -2, -3, -4]):
        cm[:, j, :] = np.where(qi - 16 * ni >= 512 * dl + 31, 0.0, -BIG)
    c["cmask"] = cm.astype(NPBF)
    ovx = np.zeros((128, 4, 129), np.float32)
    for cc in range(4):
        for n_ in range(128):
            n = 128 * cc + n_
            for s in (n // 4, (n + 1) // 4 if n % 4 == 3 else -1):
                if 0 <= s < 128:
                    ovx[n_, cc, s] = 1.0
        ovx[:, cc, 128] = 1.0
    c["ovx"] = ovx.astype(NPBF)
    pm = np.ones((128, 256), np.float32)
    pa = np.zeros((128, 256), np.float32)
    for p in range(128):
        cur = 0 if p < 64 else 1
        for x in range(256):
            j = x - 128
            if j > cur:
                pm[p, x] = 0.0
                pa[p, x] = NEGS
            elif j == cur or j == cur - 1:
                pm[p, x] = 0.0
                pa[p, x] = 1e9
    c["patm"] = pm
    c["pata"] = pa
    m = np.arange(GL)
    rel = m - 511
    oh = np.zeros((32, GL), np.float32)
    b = t5_bucket_np(rel)
    for i in range(GL):
        if rel[i] >= 0:
            oh[b[i], i] = 1.0
    c["oh"] = oh
    addm = np.zeros((6, GL), np.float32)
    addm[:, rel < 0] = -BIG
    addm[4:6, rel >= 512] = -BIG
    c["addm"] = addm
    c["iop"] = np.arange(128, dtype=np.int32).reshape(128, 1)
    return c


FMCH = {}
_o = 0
for _n, _w in [("qmA", 64), ("qmB", 64), ("kmA", 64), ("kmB", 64), ("zmA", 64), ("zmB", 64),
               ("qnA", 64), ("qnB", 64), ("znA", 64), ("znB", 64), ("ks", 64), ("kw", 64),
               ("kcvc", 128), ("g", 8), ("qnC", 64), ("qnD", 64)]:
    FMCH[_n] = (_o, _w)
    _o += _w
FMC = _o
TMC = 640


def proj_offsets():
    sizes = [512] * 4 + [512] + [128] * 6 + [24, 512]
    offs = np.concatenate([[0], np.cumsum(sizes)])
    names = ["q_m", "k_m", "v_m", "z_m", "q_n", "kc", "vc", "ks", "vs", "kw", "vw", "g_n", "z_n"]
    return {n: int(o) for n, o in zip(names, offs)}


def fm_cols(hp):
    po = proj_offsets()
    A, B = 2 * hp, 2 * hp + 1
    kv = hp // 2
    sib = hp ^ 1
    C, Dh = 2 * sib, 2 * sib + 1
    r64 = lambda base, h: list(range(base + 64 * h, base + 64 * h + 64))
    cols = []
    cols += r64(po["q_m"], A) + r64(po["q_m"], B) + r64(po["k_m"], A) + r64(po["k_m"], B)
    cols += r64(po["z_m"], A) + r64(po["z_m"], B)
    cols += r64(po["q_n"], A) + r64(po["q_n"], B) + r64(po["z_n"], A) + r64(po["z_n"], B)
    cols += r64(po["ks"], kv) + r64(po["kw"], kv)
    cols += r64(po["kc"], kv) + r64(po["vc"], kv)
    cols += list(range(po["g_n"] + 6 * hp, po["g_n"] + 6 * hp + 6)) + [po["g_n"], po["g_n"]]
    cols += r64(po["q_n"], C) + r64(po["q_n"], Dh)
    assert len(cols) == FMC
    return cols


def tm_cols(hp):
    po = proj_offsets()
    A, B = 2 * hp, 2 * hp + 1
    kv = hp // 2
    r64 = lambda base, h: list(range(base + 64 * h, base + 64 * h + 64))
    cols = r64(po["v_m"], A) + r64(po["v_m"], B) + r64(po["k_m"], A) + r64(po["k_m"], B)
    cols += r64(po["kc"], kv) + r64(po["vc"], kv) + r64(po["ks"], kv) + r64(po["vs"], kv)
    cols += r64(po["kw"], kv) + r64(po["vw"], kv)
    assert len(cols) == TMC
    return cols


def build(cfg):
    T, PAST, NS, NHP = cfg["T"], cfg["PAST"], cfg["NS"], cfg["NHP"]
    TK = cfg["TK"]
    NTK = TK // 128
    NPHYS = cfg["NPHYS"]
    NPG = PAST // 128
    NT = T // 512
    WB = cfg["WB"]
    nc = bass.Bass("TRN2", target_bir_lowering=False)
    P = Prog(nc)

    def din(name, shape, dt=F32):
        return nc.dram_tensor(name, list(shape), dt, kind="ExternalInput")

    def dout(name, shape, dt=F32):
        return nc.dram_tensor(name, list(shape), dt, kind="ExternalOutput")

    x_d = din("x", [T, D]).ap()
    xs_d = din("xs", [NS * 4, D]).ap()
    cT_d = din("cT", [D, 144]).ap()
    wada_d = din("w_ada", [D, 3 * D]).ap()
    bada_d = din("b_ada", [3 * D]).ap()
    gain_d = din("gain", [D]).ap()
    fgain_d = din("fgain", [D]).ap()
    wfm_d = din("wfm", [NHP, D, FMC]).ap()
    wtm_d = din("wtm", [NHP, D, TMC]).ap()
    w1_d = din("w1", [128, 32, 128]).ap()
    w2_d = din("w2", [128, 128]).ap()
    pet_d = din("pet", [128, 32]).ap()
    wout_d = din("wout", [64, 16, D]).ap()
    tabsel_d = din("tabsel", [32, NHP * 6]).ap()
    tab31_d = din("tab31", [NHP * 4]).ap()
    cm_d = [din("cache_m%d" % i, [NPHYS * 128, 256]).ap() for i in range(4)]
    cn_d = [din("cache_n%d" % i, [NPHYS * 128, 256]).ap() for i in range(2)]
    win_d = din("win", [NS, WB, 256]).ap()
    ptab_d = din("ptab", [NS, NPG], I32).ap()
    hc = host_consts(cfg)
    cdram = {}
    for k_, v_ in hc.items():
        dt_ = BF16 if v_.dtype == NPBF else (I32 if v_.dtype == np.int32 else F32)
        cdram[k_] = din("c_" + k_, v_.shape, dt_).ap()

    y_d = dout("y", [T, D]).ap()
    ys_d = dout("ys", [NS * 4, D]).ap()
    om_d = dout("om", [NHP, T, 256]).ap()
    on_d = dout("on", [NHP, T, 256]).ap()
    ow_d = dout("ow", [NHP, T, 128]).ap()
    oms_d = dout("oms", [NHP, NS * 4, 256]).ap()
    ons_d = dout("ons", [NHP, NS * 4, 256]).ap()
    ows_d = dout("ows", [NHP, NS * 4, 128]).ap()
    owin_d = dout("owin", [NS, WB - 4, 256]).ap()
    gd_h = nc.dram_tensor("gd", [NHP * 6, GL], BF16, kind="Internal")
    gd_d = gd_h.ap()
    mixd_d = nc.dram_tensor("mixd", [16, 64, T], BF16, kind="Internal").ap()

    with contextlib.ExitStack() as st:
        cur = [st]
        P.setup(st)

        def sb(name, shape, dt):
            return cur[0].enter_context(nc.sbuf_tensor("s_" + name, list(shape), dt)), G()

        def psum(name, shape, dt):
            return st.enter_context(nc.psum_tensor(name, list(shape), dt)), G(excl=True)

        PS_S = [psum("ps_s%d" % i, [128, 512], F32) for i in range(2)]
        PS_O = [psum("ps_o%d" % i, [128, 512], F32) for i in range(2)]
        PS_X = [psum("ps_x%d" % i, [128, 512], F32) for i in range(2)]
        PS_T = [psum("ps_t%d" % i, [128, 1024], BF16) for i in range(2)]
        cnt = {"s": 0, "o": 0, "x": 0, "t": 0, "pt": 0}

        def nxt(kind, lst):
            i = cnt[kind]
            cnt[kind] = (i + 1) % len(lst)
            return lst[i]

        def cload(name, shape, dt, eng="sp"):
            t, g = sb("k_" + name, shape, dt)
            P.add(eng, lambda e: e.dma_start(out=t[:], in_=cdram[name]), writes=[g], dma=True)
            return t, g

        IDB, gIDB = cload("idb", [128, 128], BF16)
        JB, gJB = cload("jb", [128, 128], BF16)
        ONES, gONES = cload("ones", [128, 64], F32)
        SEL6, gSEL6 = cload("sel6", [6, 384], F32)
        CMASK, gCMASK = cload("cmask", [128, 5, 512], BF16)
        OVX, gOVX = cload("ovx", [128, 4, 129], BF16)
        PATM, gPATM = cload("patm", [128, 256], F32)
        PATA, gPATA = cload("pata", [128, 256], F32)
        IOP, gIOP = cload("iop", [128, 1], I32)
        EPS, gEPS = sb("eps", [128, 1], F32)
        P.add("dve", lambda e: e.memset(EPS[:], 1e-6), writes=[gEPS])

        MIXS = sb("mixs", [64, 16, 16], BF16)
        stA = contextlib.ExitStack()
        cur[0] = stA
        KA_M = [sb("ka_m%d" % i, [128, TK], BF16) for i in range(2)]
        VA_M = [sb("va_m%d" % i, [128, NTK, 65], BF16) for i in range(2)]
        KA_S = sb("ka_s", [128, TK], BF16)
        VA_S = sb("va_s", [128, NTK, 65], BF16)
        KWR = sb("kwr", [128, 9 * 128], BF16)
        VWR = sb("vwr", [128, 9, 65], BF16)
        KCVC = sb("kcvc", [128, TK], BF16)
        NCT = max(1, (max(T, PAST) + 2047) // 2048)
        KCMP = sb("kcmp", [64, NCT * 128], BF16)
        VC = sb("vc", [128, NCT, 65], BF16)

        def init_resident():
            for (t, g) in KA_M:
                P.add("pool", lambda e, t=t: e.memset(t[:], 0.0), writes=[g])
                P.add("sp", lambda e, t=t: e.dma_start(out=t[64:128, :], in_=cdram["indm"]), writes=[g], dma=True)
            for (t, g) in VA_M + [VA_S, VWR, VC]:
                P.add("pool", lambda e, t=t: e.memset(t[:], 0.0), writes=[g])
                P.add("pool", lambda e, t=t: e.memset(t[:, :, 64:65], 1.0), writes=[g])
            t, g = KA_S
            P.add("pool", lambda e: e.memset(KA_S[0][:], 0.0), writes=[g])
            P.add("sp", lambda e: e.dma_start(out=KA_S[0][64:128, :], in_=cdram["inds"]), writes=[g], dma=True)
            for (t, g) in (KWR, KCVC, KCMP):
                P.add("pool", lambda e, t=t: e.memset(t[:], 0.0), writes=[g])

        init_resident()

        WFM = sb("wfm", [128, 8, FMC], BF16)
        WTM = sb("wtm", [128, 8, TMC], BF16)
        W1 = sb("w1", [128, 32, 128], BF16)
        W2 = sb("w2", [128, 128], BF16)
        PET = sb("pet", [128, 32], BF16)
        PEB = sb("peb", [128, 2], F32)
        HKW = [1024, 1024, 1024, 1024, HW, HW]
        HK = [sb("hk%d" % i, [128, HKW[i]], BF16) for i in range(6)]
        FARB = sb("farb", [128, NHP * 4], F32)
        TAB = sb("tab", [32, NHP * 6], F32)
        P.add("pool", lambda e: e.dma_start(out=W1[0][:], in_=w1_d), writes=[W1[1]], dma=True)
        P.add("pool", lambda e: e.dma_start(out=W2[0][:], in_=w2_d), writes=[W2[1]], dma=True)
        P.add("pool", lambda e: e.dma_start(out=PET[0][:], in_=pet_d), writes=[PET[1]], dma=True)
        P.add("sp", lambda e: e.dma_start(out=FARB[0][:], in_=tab31_d.partition_broadcast(128)), writes=[FARB[1]], dma=True)
        P.add("sp", lambda e: e.dma_start(out=TAB[0][:], in_=tabsel_d), writes=[TAB[1]], dma=True)

        with contextlib.ExitStack() as st2:
            OH = st2.enter_context(nc.sbuf_tensor("oh", [32, GL], F32)); gOH = G()
            ADDM = st2.enter_context(nc.sbuf_tensor("addm", [6, GL], F32)); gADDM = G()
            GV = st2.enter_context(nc.sbuf_tensor("gv", [6, GL], BF16)); gGV = G()
            P.add("sp", lambda e: e.dma_start(out=OH[:], in_=cdram["oh"]), writes=[gOH], dma=True)
            P.add("sp", lambda e: e.dma_start(out=ADDM[:], in_=cdram["addm"]), writes=[gADDM], dma=True)
            for hp in range(NHP):
                for c0 in range(0, GL, 512):
                    w_ = min(512, GL - c0)
                    px, gpx = nxt("x", PS_X)
                    P.add("pe", lambda e, px=px, c0=c0, w_=w_, hp=hp: e.matmul(
                        px[0:6, 0:w_], lhsT=TAB[0][:, hp * 6:hp * 6 + 6], rhs=OH[:, c0:c0 + w_], start=True, stop=True),
                        reads=[TAB[1], gOH], writes=[gpx])
                    P.add("dve", lambda e, px=px, c0=c0, w_=w_: e.tensor_tensor(
                        out=GV[:, c0:c0 + w_], in0=px[0:6, 0:w_], in1=ADDM[:, c0:c0 + w_], op=ALU.add),
                        reads=[gpx, gADDM], writes=[gGV])
                gGD = G()
                P.add("sp", lambda e, hp=hp: e.dma_start(out=gd_d[hp * 6:hp * 6 + 6, :], in_=GV[:]),
                      reads=[gGV], writes=[gGD], dma=True)
                cfg.setdefault("_ggd", []).append(gGD)
            P.flush()
        gGDs = cfg.pop("_ggd")
        P.barrier()

        SHT = sb("sht", [128, 8, 8], F32)
        SCT = sb("sct", [128, 8, 8], F32)
        gate_tiles = {}

        def adaln(part):
            with contextlib.ExitStack() as st2:
                CT = st2.enter_context(nc.sbuf_tensor("ct_" + part, [128, 8, 144], F32)); gCT = G()
                WA = st2.enter_context(nc.sbuf_tensor("wa_" + part, [128, 8, 512], F32)); gWA = G()
                BAT = st2.enter_context(nc.sbuf_tensor("bat_" + part, [128, 24], F32)); gBAT = G()
                GNT = st2.enter_context(nc.sbuf_tensor("gnt_" + part, [128, 8], F32)); gGNT = G()
                P.add("sp", lambda e: e.dma_start(out=CT[:], in_=cT_d.rearrange("(k p) n -> p k n", p=128)), writes=[gCT], dma=True)
                P.add("sp", lambda e: e.dma_start(out=BAT[:], in_=bada_d.rearrange("(k p) -> p k", p=128), allow_slow_non_contiguous=True), writes=[gBAT], dma=True)
                P.add("sp", lambda e: e.dma_start(out=GNT[:], in_=gain_d.rearrange("(k p) -> p k", p=128), allow_slow_non_contiguous=True), writes=[gGNT], dma=True)
                if part == "gate":
                    BAB = st2.enter_context(nc.sbuf_tensor("bab", [128, D], F32)); gBAB = G()
                    GATEP, GATES, FG = gate_tiles["p"], gate_tiles["s"], gate_tiles["f"]
                    P.add("sp", lambda e: e.dma_start(out=BAB[:], in_=bada_d[2 * D:3 * D].partition_broadcast(128)), writes=[gBAB], dma=True)
                    P.add("sp", lambda e: e.dma_start(out=FG[0][:], in_=fgain_d.partition_broadcast(128)), writes=[FG[1]], dma=True)
                for j in (range(4) if part == "fm" else range(4, 6)):
                    P.add("sp", lambda e, j=j: e.dma_start(
                        out=WA[:], in_=wada_d[:, j * 512:(j + 1) * 512].rearrange("(k p) n -> p k n", p=128)),
                        writes=[gWA], dma=True)
                    if j < 4:
                        for f in range(4):
                            fc = j * 4 + f
                            px, gpx = nxt("x", PS_X)
                            for k in range(8):
                                P.add("pe", lambda e, px=px, k=k, f=f: e.matmul(
                                    px[:, 0:144], lhsT=WA[:, k, f * 128:(f + 1) * 128], rhs=CT[:, k, :],
                                    start=(k == 0), stop=(k == 7)), reads=[gWA, gCT], writes=[gpx])
                            dst = SHT if fc < 8 else SCT
                            fcc = fc % 8
                            for (dc, sc0, sc1, stp) in ((0, 0, 1, 1), (1, 128, 144, 4)):
                                ncol = 1 if dc == 0 else 4
                                if fc < 8:
                                    P.add("dve", lambda e, px=px, fc=fc, fcc=fcc, dc=dc, sc0=sc0, sc1=sc1, stp=stp, ncol=ncol: e.tensor_scalar(
                                        out=SHT[0][:, fcc, dc:dc + ncol], in0=px[:, sc0:sc1:stp], scalar1=BAT[:, fc:fc + 1], scalar2=None, op0=ALU.add),
                                        reads=[gpx, gBAT], writes=[SHT[1]])
                                else:
                                    P.add("dve", lambda e, px=px, fc=fc, fcc=fcc, dc=dc, sc0=sc0, sc1=sc1, stp=stp, ncol=ncol: e.tensor_scalar(
                                        out=SCT[0][:, fcc, dc:dc + ncol], in0=px[:, sc0:sc1:stp], scalar1=BAT[:, fc:fc + 1], scalar2=1.0,
                                        op0=ALU.add, op1=ALU.add), reads=[gpx, gBAT], writes=[SCT[1]])
                            if fc >= 8:
                                P.add("dve", lambda e, fcc=fcc: e.tensor_scalar(
                                    out=SCT[0][:, fcc, 0:5], in0=SCT[0][:, fcc, 0:5], scalar1=GNT[:, fcc:fcc + 1], scalar2=None,
                                    op0=ALU.mult), reads=[gGNT, SCT[1]], writes=[SCT[1]])
                    else:
                        oc = j - 4
                        px, gpx = nxt("x", PS_X)
                        for k in range(8):
                            P.add("pe", lambda e, px=px, k=k: e.matmul(
                                px[:, 0:512], lhsT=CT[:, k, 0:128], rhs=WA[:, k, :], start=(k == 0), stop=(k == 7)),
                                reads=[gWA, gCT], writes=[gpx])
                        P.add("dve", lambda e, px=px, oc=oc: e.tensor_tensor(
                            out=GATEP[0][:, oc * 512:(oc + 1) * 512], in0=px[:, 0:512], in1=BAB[:, oc * 512:(oc + 1) * 512], op=ALU.add),
                            reads=[gpx, gBAB], writes=[GATEP[1]])
                        px, gpx = nxt("x", PS_X)
                        for k in range(8):
                            P.add("pe", lambda e, px=px, k=k: e.matmul(
                                px[0:16, 0:512], lhsT=CT[:, k, 128:144], rhs=WA[:, k, :], start=(k == 0), stop=(k == 7)),
                                reads=[gWA, gCT], writes=[gpx])
                        P.add("dve", lambda e, px=px, oc=oc: e.tensor_tensor(
                            out=GATES[0][:, oc * 512:(oc + 1) * 512], in0=px[0:16, 0:512], in1=BAB[0:16, oc * 512:(oc + 1) * 512], op=ALU.add),
                            reads=[gpx, gBAB], writes=[GATES[1]])
                P.flush()
            P.barrier()

        adaln("fm")

        XT_ = sb("xt", [128, D], F32)
        XN = sb("xn", [128, D], BF16)
        SSQ = sb("ssq", [128, 4], F32)
        HT = sb("ht", [128, 8, 512], BF16)
        QA_M = [sb("qa_m%d" % i, [128, 512], BF16) for i in range(2)]
        QA_S = [[sb("qa_s%d%d" % (i, v), [128, 512], BF16) for v in range(2)] for i in range(2)]
        QSIB = [sb("qsib%d" % i, [64, 512], BF16) for i in range(2)]
        ZT = [sb("zt%d" % i, [64, 512], BF16) for i in range(4)]
        GT = sb("gt", [8, 512], F32)
        KMF = [sb("kmf%d" % i, [64, 40], BF16) for i in range(2)]
        KMFF = sb("kmff", [64, 40], F32)
        PTB = [sb("ptb%d" % i, [128, 512], BF16) for i in range(2)]
        OTM = sb("otm", [128, TMC], F32)
        IMPACC = sb("impacc", [128, 4, 128], F32)
        SC = sb("sc", [128, 40], F32)
        M8 = sb("m8", [128, 16], F32)
        THR = sb("thr", [128, 2], F32)
        IMPM = sb("impm", [128, 128], F32)
        IMP2 = sb("imp2", [128, 128], F32)
        MBP = sb("mbp", [128, 256], BF16)
        RS = sb("rs", [128, 512], F32)
        RSI = sb("rsi", [128, 2], F32)
        BCZ = sb("bcz", [64, 512], F32)
        TMP = sb("tmp", [64, 512], F32)
        ACCS = [sb("acc%d" % i, [64, 512], F32) for i in range(2)]
        MIXB = sb("mixb", [64, 512], BF16)
        AKV = sb("akv", [128, 256], BF16)
        SG = sb("sg", [128, 512], F32)
        for (t, g) in (MBP,):
            P.add("pool", lambda e, t=t: e.memset(t[:], 0.0), writes=[g])
        for lst in (QA_M, QA_S[0], QA_S[1]):
            for (t, g) in lst:
                P.add("pool", lambda e, t=t: e.memset(t[:], 0.0), writes=[g])

        def load_hp(hp):
            P.add("pool", lambda e: e.dma_start(out=WFM[0][:], in_=wfm_d[hp].rearrange("(k p) n -> p k n", p=128)),
                  writes=[WFM[1]], dma=True)
            P.add("pool", lambda e: e.dma_start(out=WTM[0][:], in_=wtm_d[hp].rearrange("(k p) n -> p k n", p=128)),
                  writes=[WTM[1]], dma=True)
            for v in range(6):
                src = bass.AP(gd_h, (hp * 6 + v) * GL, [[1, 128], [1, HKW[v]]])
                P.add("sp", lambda e, v=v, src=src: e.dma_start(out=HK[v][0][:], in_=src),
                      reads=[gGDs[hp]], writes=[HK[v][1]], dma=True)

        def peb_compute():
            for kvi in range(2):
                px, gpx = nxt("x", PS_X)
                lo = 64 * kvi
                for l in range(32):
                    P.add("pe", lambda e, px=px, l=l, lo=lo: e.matmul(
                        px[:, 0:1], lhsT=W1[0][lo:lo + 64, l, :], rhs=PET[0][lo:lo + 64, l:l + 1],
                        start=(l == 0), stop=(l == 31)), reads=[W1[1], PET[1]], writes=[gpx])
                P.add("dve", lambda e, px=px, kvi=kvi: e.tensor_copy(out=PEB[0][:, kvi:kvi + 1], in_=px[:, 0:1]),
                      reads=[gpx], writes=[PEB[1]])

        peb_compute()

        def project_tile(hp, N, q0, ht_ready, subs, om_dst, on_dst, ow_dst, ring_kt):
            HTt, gHT = HT

            def fm(name, evac):
                off, w_ = FMCH[name]
                w_ = max(w_, 64)
                px, gpx = nxt("x", PS_X)
                for k in range(8):
                    P.add("pe", lambda e, px=px, k=k, off=off, w_=w_: e.matmul(
                        px[0:w_, 0:N], lhsT=WFM[0][:, k, off:off + w_], rhs=HTt[:, k, 0:N],
                        start=(k == 0), stop=(k == 7)), reads=[WFM[1], gHT], writes=[gpx])
                evac(px, gpx)

            kc0 = q0
            for i, nm in enumerate(("qmA", "qmB")):
                def ev(px, gpx, i=i):
                    P.add("act", lambda e: e.activation(out=QA_M[i][0][0:64, 0:N], in_=px[0:64, 0:N], func=AF.Copy, scale=0.125),
                          reads=[gpx], writes=[QA_M[i][1]])
                fm(nm, ev)
            for i, nm in enumerate(("kmA", "kmB")):
                def ev(px, gpx, i=i):
                    P.add("act", lambda e: e.activation(out=KA_M[i][0][0:64, kc0:kc0 + N], in_=px[0:64, 0:N], func=AF.Copy),
                          reads=[gpx], writes=[KA_M[i][1]])
                fm(nm, ev)
            if cfg.get("STAGE", 99) < 2.45:
                return
            for i, nm in enumerate(("zmA", "zmB", "znA", "znB")):
                def ev(px, gpx, i=i):
                    if cfg.get("STAGE", 99) >= 2.47:
                        P.add("act", lambda e: e.activation(out=SG[0][0:64, 0:N], in_=px[0:64, 0:N], func=AF.Sigmoid),
                              reads=[gpx], writes=[SG[1]])
                    if cfg.get("STAGE", 99) >= 2.49:
                        if cfg.get("VAR", 0) == 1:
                            P.add("dve", lambda e: e.tensor_tensor(out=TMP[0][:, 0:N], in0=px[0:64, 0:N], in1=SG[0][0:64, 0:N], op=ALU.mult),
                                  reads=[gpx, SG[1]], writes=[TMP[1]])
                        elif cfg.get("VAR", 0) == 2:
                            P.add("dve", lambda e: e.tensor_copy(out=TMP[0][:, 0:N], in_=px[0:64, 0:N]), reads=[gpx], writes=[TMP[1]])
                            P.add("dve", lambda e: e.tensor_tensor(out=ZT[i][0][:, 0:N], in0=TMP[0][:, 0:N], in1=SG[0][0:64, 0:N], op=ALU.mult),
                                  reads=[TMP[1], SG[1]], writes=[ZT[i][1]])
                        elif cfg.get("VAR", 0) == 3:
                            P.add("dve", lambda e: e.tensor_tensor(out=ZT[i][0][:, 0:N], in0=px[0:64, 0:N], in1=SG[0][0:64, 0:N], op=ALU.mult),
                                  reads=[gpx, SG[1]], writes=[ZT[i][1], TMP[1]])
                        elif cfg.get("VAR", 0) == 4:
                            P.add("dve", lambda e: e.tensor_tensor(out=ZT[0][0][:, 0:N], in0=px[0:64, 0:N], in1=SG[0][0:64, 0:N], op=ALU.mult),
                                  reads=[gpx, SG[1]], writes=[ZT[0][1]])
                        else:
                            P.add("dve", lambda e: e.tensor_tensor(out=ZT[i][0][:, 0:N], in0=px[0:64, 0:N], in1=SG[0][0:64, 0:N], op=ALU.mult),
                                  reads=[gpx, SG[1]], writes=[ZT[i][1]])
                fm(nm, ev)
            if cfg.get("STAGE", 99) < 2.6:
                return
            for i, nm in enumerate(("qnA", "qnB")):
                def ev(px, gpx, i=i):
                    for v in range(2):
                        P.add("act" if v == 0 else "dve", (lambda e, v=v: e.activation(
                            out=QA_S[i][v][0][0:64, 0:N], in_=px[0:64, 0:N], func=AF.Copy, scale=0.125)) if v == 0 else
                            (lambda e, v=v: e.tensor_scalar(out=QA_S[i][v][0][0:64, 0:N], in0=px[0:64, 0:N], scalar1=0.125,
                                                            scalar2=None, op0=ALU.mult)),
                            reads=[gpx], writes=[QA_S[i][v][1]])
                fm(nm, ev)
            if cfg.get("STAGE", 99) < 2.61:
                return
            for i, nm in enumerate(("qnC", "qnD")):
                def ev(px, gpx, i=i):
                    P.add("act", lambda e: e.activation(out=QSIB[i][0][:, 0:N], in_=px[0:64, 0:N], func=AF.Copy, scale=0.125),
                          reads=[gpx], writes=[QSIB[i][1]])
                fm(nm, ev)

            if cfg.get("STAGE", 99) < 2.62:
                return

            def ev(px, gpx):
                P.add("act", lambda e: e.activation(out=KA_S[0][0:64, kc0:kc0 + N], in_=px[0:64, 0:N], func=AF.Copy),
                      reads=[gpx], writes=[KA_S[1]])
            fm("ks", ev)
            if cfg.get("STAGE", 99) < 2.63:
                return

            def ev(px, gpx):
                for j in range(0, N, 128):
                    n_ = min(128, N - j)
                    r0 = ((ring_kt + j // 128) % 9) * 128
                    P.add("dve", lambda e, j=j, n_=n_, r0=r0: e.tensor_copy(out=KWR[0][0:64, r0:r0 + n_], in_=px[0:64, j:j + n_]),
                          reads=[gpx], writes=[KWR[1]])
            fm("kw", ev)

            if cfg.get("STAGE", 99) < 2.64:
                return

            def ev(px, gpx):
                P.add("act", lambda e: e.activation(out=KCVC[0][:, kc0:kc0 + N], in_=px[:, 0:N], func=AF.Copy),
                      reads=[gpx], writes=[KCVC[1]])
            fm("kcvc", ev)

            if cfg.get("STAGE", 99) < 2.645:
                return

            def ev(px, gpx):
                P.add("act", lambda e: e.activation(out=GT[0][:, 0:N], in_=px[0:8, 0:N], func=AF.Sigmoid),
                      reads=[gpx], writes=[GT[1]])
            fm("g", ev)

            if cfg.get("STAGE", 99) < 2.7:
                return
            for si, (rows, c0) in enumerate(subs):
                pa, gpa = nxt("x", PS_X)
                pb, gpb = nxt("x", PS_X)
                for k in range(8):
                    P.add("pe", lambda e, pa=pa, k=k, rows=rows, c0=c0: e.matmul(
                        pa[0:rows, 0:512], lhsT=HTt[:, k, c0:c0 + rows], rhs=WTM[0][:, k, 0:512],
                        start=(k == 0), stop=(k == 7)), reads=[WTM[1], gHT], writes=[gpa])
                for k in range(8):
                    P.add("pe", lambda e, pb=pb, k=k, rows=rows, c0=c0: e.matmul(
                        pb[0:rows, 0:128], lhsT=HTt[:, k, c0:c0 + rows], rhs=WTM[0][:, k, 512:640],
                        start=(k == 0), stop=(k == 7)), reads=[WTM[1], gHT], writes=[gpb])
                P.add("act", lambda e, pa=pa, rows=rows: e.activation(out=OTM[0][0:rows, 0:512], in_=pa[0:rows, 0:512], func=AF.Copy),
                      reads=[gpa], writes=[OTM[1]])
                P.add("dve", lambda e, pb=pb, rows=rows: e.tensor_copy(out=OTM[0][0:rows, 512:640], in_=pb[0:rows, 0:128]),
                      reads=[gpb], writes=[OTM[1]])
                if cfg.get("STAGE", 99) < 2.8:
                    continue
                kt = (q0 + c0) // 128
                r_ = (q0 + c0) % 128
                assert r_ == 0
                for i in range(2):
                    P.add("pool", lambda e, i=i, kt=kt, rows=rows: e.tensor_copy(
                        out=VA_M[i][0][0:rows, kt, 0:64], in_=OTM[0][0:rows, 64 * i:64 * i + 64]),
                        reads=[OTM[1]], writes=[VA_M[i][1]])
                P.add("pool", lambda e, kt=kt, rows=rows: e.tensor_copy(
                    out=VA_S[0][0:rows, kt, 0:64], in_=OTM[0][0:rows, 448:512]), reads=[OTM[1]], writes=[VA_S[1]])
                rk = (ring_kt + c0 // 128) % 9
                P.add("pool", lambda e, rk=rk, rows=rows: e.tensor_copy(
                    out=VWR[0][0:rows, rk, 0:64], in_=OTM[0][0:rows, 576:640]), reads=[OTM[1]], writes=[VWR[1]])
                if cfg.get("STAGE", 99) >= 2.9:
                    om_dst(si, rows, c0)

        def compress(c):
            for kvi in range(2):
                lo = 64 * kvi
                px, gpx = nxt("x", PS_X)
                for l in range(32):
                    s0 = 2048 * c + l
                    P.add("pe", lambda e, px=px, l=l, lo=lo, s0=s0: e.matmul(
                        px[:, 0:128], lhsT=W1[0][lo:lo + 64, l, :], rhs=KCVC[0][lo:lo + 64, s0:s0 + 2033:16],
                        start=(l == 0), stop=(l == 31)), reads=[W1[1], KCVC[1]], writes=[gpx])
                P.add("act", lambda e, px=px, kvi=kvi: e.activation(
                    out=SG[0][:, 0:128], in_=px[:, 0:128], func=AF.Sigmoid, bias=PEB[0][:, kvi:kvi + 1]),
                    reads=[gpx, PEB[1]], writes=[SG[1]])
                P.add("dve", lambda e, px=px, kvi=kvi: e.scalar_tensor_tensor(
                    out=AKV[0][:, 128 * kvi:128 * kvi + 128], in0=px[:, 0:128], scalar=PEB[0][:, kvi:kvi + 1], in1=SG[0][:, 0:128],
                    op0=ALU.add, op1=ALU.mult), reads=[gpx, PEB[1], SG[1]], writes=[AKV[1]])
            px, gpx = nxt("x", PS_X)
            P.add("pe", lambda e, px=px: e.matmul(px[0:64, 0:128], lhsT=W2[0][:, 0:64], rhs=AKV[0][:, 0:128], start=True, stop=True),
                  reads=[W2[1], AKV[1]], writes=[gpx])
            P.add("dve", lambda e, px=px: e.tensor_copy(out=KCMP[0][:, 128 * c:128 * c + 128], in_=px[0:64, 0:128]),
                  reads=[gpx], writes=[KCMP[1]])
            px, gpx = nxt("x", PS_X)
            P.add("pe", lambda e, px=px: e.matmul(px[:, 0:64], lhsT=AKV[0][:, 128:256], rhs=W2[0][:, 64:128], start=True, stop=True),
                  reads=[W2[1], AKV[1]], writes=[gpx])
            P.add("dve", lambda e, px=px: e.tensor_copy(out=VC[0][:, c, 0:64], in_=px[:, 0:64]), reads=[gpx], writes=[VC[1]])

        def attend(QT, gQ, qrows, KT, gK, krows, VT, gV, N, tiles, hk, farcol, on_pt=None, first=True, last=True, po=None):
            if po is None:
                po = nxt("o", PS_O)
            pO, gO = po
            nt = len(tiles)
            pts = {}

            def emit_S(ti):
                kc, vs, kind, arg, qlo = tiles[ti]
                pS, gS = nxt("s", PS_S)
                two = kind in ("hk", "cm")
                P.add("pe", lambda e, pS=pS, kc=kc, qlo=qlo, two=two: e.matmul(
                    pS[:, qlo:N], lhsT=KT[0:krows, kc:kc + 128], rhs=QT[0:qrows, qlo:N], start=True, stop=not two),
                    reads=[gK, gQ], writes=[gS])
                if kind == "hk":
                    c0 = arg + 384 + qlo
                    P.add("pe", lambda e, pS=pS, qlo=qlo, c0=c0: e.matmul(
                        pS[:, qlo:N], lhsT=JB[:], rhs=HK[hk][0][:, c0:c0 + N - qlo], start=False, stop=True),
                        reads=[gJB, HK[hk][1]], writes=[gS])
                elif kind == "cm":
                    P.add("pe", lambda e, pS=pS, qlo=qlo, arg=arg: e.matmul(
                        pS[:, qlo:N], lhsT=IDB[:], rhs=CMASK[:, arg, qlo:N], start=False, stop=True),
                        reads=[gIDB, gCMASK], writes=[gS])
                pt, gpt = nxt("pt", PTB)
                if kind == "far":
                    P.add("act", lambda e, pS=pS, pt=pt, qlo=qlo: e.activation(
                        out=pt[:, qlo:N], in_=pS[:, qlo:N], func=AF.Exp, bias=FARB[0][:, farcol:farcol + 1]),
                        reads=[gS, FARB[1]], writes=[gpt])
                else:
                    P.add("act", lambda e, pS=pS, pt=pt, qlo=qlo: e.activation(
                        out=pt[:, qlo:N], in_=pS[:, qlo:N], func=AF.Exp), reads=[gS], writes=[gpt])
                pts[ti] = (pt, gpt)

            def emit_PV(ti):
                kc, vs, kind, arg, qlo = tiles[ti]
                pt, gpt = pts.pop(ti)
                if VT is not None:
                    P.add("pe", lambda e, pt=pt, vs=vs, qlo=qlo, ti=ti: e.matmul(
                        pO[0:65, qlo:N], lhsT=VT[:, vs, 0:65], rhs=pt[:, qlo:N], start=(first and ti == 0), stop=(last and ti == nt - 1)),
                        reads=[gV, gpt], writes=[gO])
                if on_pt is not None:
                    on_pt(ti, pt, gpt)

            GS = 1 if N > 64 else min(8, 512 // N)
            if GS > 1:
                groups = []
                for ti, (kc, vs, kind, arg, qlo) in enumerate(tiles):
                    simple = kind in ("far", "none") and qlo == 0
                    if simple and groups and groups[-1][0] == kind and len(groups[-1][1]) < GS:
                        groups[-1][1].append(ti)
                    else:
                        groups.append((kind if simple else "single", [ti]))
                gpts = {}

                def emit_SG(gi):
                    kind, tis = groups[gi]
                    if kind == "single":
                        emit_S(tis[0])
                        return
                    pS, gS = nxt("s", PS_S)
                    for j, ti in enumerate(tis):
                        kc = tiles[ti][0]
                        P.add("pe", lambda e, pS=pS, kc=kc, j=j: e.matmul(
                            pS[:, j * N:(j + 1) * N], lhsT=KT[0:krows, kc:kc + 128], rhs=QT[0:qrows, 0:N], start=True, stop=True,
                            skip_group_check=True), reads=[gK, gQ], writes=[gS])
                    pt, gpt = nxt("pt", PTB)
                    w_ = len(tis) * N
                    if kind == "far":
                        P.add("act", lambda e, pS=pS, pt=pt, w_=w_: e.activation(
                            out=pt[:, 0:w_], in_=pS[:, 0:w_], func=AF.Exp, bias=FARB[0][:, farcol:farcol + 1]),
                            reads=[gS, FARB[1]], writes=[gpt])
                    else:
                        P.add("act", lambda e, pS=pS, pt=pt, w_=w_: e.activation(
                            out=pt[:, 0:w_], in_=pS[:, 0:w_], func=AF.Exp), reads=[gS], writes=[gpt])
                    gpts[gi] = (pt, gpt)

                def emit_PVG(gi):
                    kind, tis = groups[gi]
                    if kind == "single":
                        emit_PV(tis[0])
                        return
                    pt, gpt = gpts.pop(gi)
                    for j, ti in enumerate(tis):
                        vs = tiles[ti][1]
                        ptv = pt[:, j * N:(j + 1) * N]
                        if VT is not None:
                            P.add("pe", lambda e, ptv=ptv, vs=vs, ti=ti: e.matmul(
                                pO[0:65, 0:N], lhsT=VT[:, vs, 0:65], rhs=ptv, start=(first and ti == 0), stop=(last and ti == nt - 1)),
                                reads=[gV, gpt], writes=[gO])
                        if on_pt is not None:
                            on_pt(ti, ptv, gpt)

                ng = len(groups)
                if ng > 0:
                    emit_SG(0)
                for gi in range(ng):
                    if gi + 1 < ng:
                        emit_SG(gi + 1)
                    emit_PVG(gi)
                return po
            if nt > 0:
                emit_S(0)
            for ti in range(nt):
                if ti + 1 < nt:
                    emit_S(ti + 1)
                emit_PV(ti)
            return po

        def finish(po, N, zi, coef_row, acc_first, acc_last, head_slot, q0, ACC=None):
            ACC = ACC or ACCS[0]
            pO, gO = po
            P.add("dve", lambda e: e.tensor_scalar(out=RS[0][64:65, 0:N], in0=pO[64:65, 0:N], scalar1=1e-30, scalar2=None, op0=ALU.max),
                  reads=[gO], writes=[RS[1]])
            P.add("dve", lambda e: e.reciprocal(out=RS[0][64:65, 0:N], in_=RS[0][64:65, 0:N]), reads=[RS[1]], writes=[RS[1]])
            pb, gpb = nxt("x", PS_X)
            P.add("pe", lambda e: e.matmul(pb[0:64, 0:N], lhsT=ONES[64:65, 0:64], rhs=RS[0][64:65, 0:N], start=True, stop=True),
                  reads=[gONES, RS[1]], writes=[gpb])
            P.add("dve", lambda e: e.tensor_tensor(out=BCZ[0][:, 0:N], in0=pb[0:64, 0:N], in1=ZT[zi][0][:, 0:N], op=ALU.mult),
                  reads=[gpb, ZT[zi][1]], writes=[BCZ[1]])
            if coef_row is not None:
                pg, gpg = nxt("x", PS_X)
                P.add("pe", lambda e: e.matmul(pg[0:64, 0:N], lhsT=SEL6[0:6, coef_row * 64:coef_row * 64 + 64], rhs=GT[0][0:6, 0:N],
                                               start=True, stop=True), reads=[gSEL6, GT[1]], writes=[gpg])
                P.add("dve", lambda e: e.tensor_tensor(out=BCZ[0][:, 0:N], in0=BCZ[0][:, 0:N], in1=pg[0:64, 0:N], op=ALU.mult),
                      reads=[gpg, BCZ[1]], writes=[BCZ[1]])
            if acc_first and acc_last:
                P.add("dve", lambda e: e.tensor_tensor(out=MIXB[0][:, 0:N], in0=pO[0:64, 0:N], in1=BCZ[0][:, 0:N], op=ALU.mult),
                      reads=[gO, BCZ[1]], writes=[MIXB[1]])
            elif acc_first:
                P.add("dve", lambda e: e.tensor_tensor(out=ACC[0][:, 0:N], in0=pO[0:64, 0:N], in1=BCZ[0][:, 0:N], op=ALU.mult),
                      reads=[gO, BCZ[1]], writes=[ACC[1]])
            else:
                P.add("dve", lambda e: e.tensor_tensor(out=TMP[0][:, 0:N], in0=pO[0:64, 0:N], in1=BCZ[0][:, 0:N], op=ALU.mult),
                      reads=[gO, BCZ[1]], writes=[TMP[1]])
                if acc_last:
                    P.add("pool", lambda e: e.tensor_tensor(out=MIXB[0][:, 0:N], in0=ACC[0][:, 0:N], in1=TMP[0][:, 0:N], op=ALU.add),
                          reads=[ACC[1], TMP[1]], writes=[MIXB[1]])
                else:
                    P.add("pool", lambda e: e.tensor_tensor(out=ACC[0][:, 0:N], in0=ACC[0][:, 0:N], in1=TMP[0][:, 0:N], op=ALU.add),
                          reads=[ACC[1], TMP[1]], writes=[ACC[1]])
            if acc_last:
                head_slot(MIXB)

        def key_tiles(N, q0, nkeys_tiles, vslot_fn, win=False):
            tl = []
            for kt in range(nkeys_tiles):
                off = q0 - 128 * kt
                qlo = max(0, -off)
                if qlo >= N:
                    continue
                if win:
                    if off > 512:
                        continue
                    tl.append((None, vslot_fn(kt), "hk", off, qlo, kt))
                else:
                    kind = "hk" if off <= 128 else "far"
                    tl.append((128 * kt, vslot_fn(kt), kind, off, qlo, kt))
            return tl

        def attend_tile(hp, N, q0, subs, mix_dst):
            nkt = (q0 + N + 127) // 128
            for i in range(2):
                KT, gK = KA_M[i]
                nblk = (q0 + N) // 256
                curs = [(q0 + c0) // 256 for (_, c0) in subs]
                nb = max(curs)
                if nb > 0:
                    P.add("dve", lambda e, KT=KT, nb=nb, i=i: e.tensor_reduce(
                        out=KMFF[0][:, 0:nb], in_=KT[0:64, 0:nb * 256].rearrange("p (n k) -> p n k", k=256), axis=AX.X, op=ALU.add),
                        reads=[gK], writes=[KMFF[1]])
                    P.add("dve", lambda e, nb=nb, i=i: e.tensor_copy(out=KMF[i][0][:, 0:nb], in_=KMFF[0][:, 0:nb]),
                          reads=[KMFF[1]], writes=[KMF[i][1]])
                for si, (rows, c0) in enumerate(subs):
                    cur = curs[si]
                    P.add("pool", lambda e, rows=rows: e.memset(SC[0][0:rows, :], NEGS), writes=[SC[1]])
                    if cur > 0:
                        px, gpx = nxt("x", PS_X)
                        P.add("pe", lambda e, px=px, rows=rows, c0=c0, cur=cur, i=i: e.matmul(
                            px[0:rows, 0:cur], lhsT=QA_M[i][0][0:64, c0:c0 + rows], rhs=KMF[i][0][:, 0:cur], start=True, stop=True),
                            reads=[QA_M[i][1], KMF[i][1]], writes=[gpx])
                        P.add("dve", lambda e, px=px, rows=rows, cur=cur: e.tensor_copy(out=SC[0][0:rows, 0:cur], in_=px[0:rows, 0:cur]),
                              reads=[gpx], writes=[SC[1]])
                    P.add("dve", lambda e, rows=rows: e.max(out=M8[0][0:rows, 0:8], in_=SC[0][0:rows, 0:32]), reads=[SC[1]], writes=[M8[1]])
                    P.add("dve", lambda e, rows=rows: e.tensor_scalar(out=THR[0][0:rows, 0:1], in0=M8[0][0:rows, 2:3], scalar1=-1e29,
                                                                      scalar2=None, op0=ALU.max), reads=[M8[1]], writes=[THR[1]])
                    P.add("dve", lambda e, rows=rows: e.tensor_scalar(out=MBP[0][0:rows, 64:96], in0=SC[0][0:rows, 0:32],
                                                                      scalar1=THR[0][0:rows, 0:1], scalar2=-BIG, op0=ALU.is_lt, op1=ALU.mult),
                          reads=[SC[1], THR[1]], writes=[MBP[1]])
                    if cur < 32:
                        P.add("dve", lambda e, rows=rows, cur=cur: e.memset(MBP[0][0:rows, 64 + cur:65 + cur], 0.0), writes=[MBP[1]])
                    pt_, gpt_ = nxt("t", PS_T)
                    P.add("pe", lambda e, pt_=pt_, rows=rows: e.transpose(out=pt_[:, 0:rows], in_=MBP[0][0:rows, 0:128], identity=IDB[0:rows, 0:rows]),
                          reads=[MBP[1], gIDB], writes=[gpt_])
                    P.add("act", lambda e, pt_=pt_, rows=rows, c0=c0, i=i: e.activation(
                        out=QA_M[i][0][64:128, c0:c0 + rows], in_=pt_[64:128, 0:rows], func=AF.Copy), reads=[gpt_], writes=[QA_M[i][1]])
                tl = [(a, b, c_, d, e_) for (a, b, c_, d, e_, _) in key_tiles(N, q0, nkt, lambda kt: kt)]
                po = attend(QA_M[i][0], QA_M[i][1], 128, KT, gK, 128, VA_M[i][0], VA_M[i][1], N, tl, hk=i, farcol=hp * 4 + i)
                finish(po, N, i, None, True, True, lambda mb, i=i: mix_dst(2 * hp + i, mb), q0)

            if N == 512:
                t = q0 // 512
                c = t // 4
                if t % 4 == 0 and t > 0:
                    compress(c - 1)
                compress(c)
                cts = list(range(0, c + 1))
            else:
                for c in range(NCT):
                    compress(c)
                cts = list(range(NCT))
            tq = q0 // 512
            ctl = []
            for c in cts:
                dl = 4 * c - tq
                if dl > 0:
                    continue
                kind, arg = ("cm", -dl) if dl >= -4 else ("none", 0)
                ctl.append((128 * c, c, kind, arg, 0))
            nsub = len(subs)
            heads = [(QA_S[0][0][0], QA_S[0][0][1], True, 0), (QA_S[1][0][0], QA_S[1][0][1], True, 1),
                     (QSIB[0][0], QSIB[0][1], False, 0), (QSIB[1][0], QSIB[1][1], False, 1)]
            ocmp = [None, None]
            for hi, (QT, gQ, own, oi) in enumerate(heads):
                pI = [nxt("x", PS_X), nxt("x", PS_X)]
                state = {"first": [True, True]}

                def on_pt(ti, pt, gpt, pI=pI, state=state):
                    cc = ctl[ti][1]
                    for si, (rows, c0) in enumerate(subs):
                        b = si // 2
                        cb = (si % 2) * 129
                        fst = state["first"][b]
                        state["first"][b] = False
                        P.add("pe", lambda e, b=b, cb=cb, rows=rows, c0=c0, cc=cc, fst=fst, pt=pt: e.matmul(
                            pI[b][0][0:rows, cb:cb + 129], lhsT=pt[:, c0:c0 + rows], rhs=OVX[:, cc, :], start=fst, stop=True,
                            skip_group_check=True),
                            reads=[gpt, gOVX], writes=[pI[b][1]])
                po = attend(QT, gQ, 64, KCMP[0], KCMP[1], 64, VC[0] if own else None, VC[1], N, ctl, hk=0, farcol=0, on_pt=on_pt)
                if own:
                    ocmp[oi] = po
                for si, (rows, c0) in enumerate(subs):
                    b = si // 2
                    cb = (si % 2) * 129
                    P.add("dve", lambda e, b=b, cb=cb, rows=rows: e.tensor_scalar(
                        out=RSI[0][0:rows, 0:1], in0=pI[b][0][0:rows, cb + 128:cb + 129], scalar1=1e-30, scalar2=None, op0=ALU.max),
                        reads=[pI[b][1]], writes=[RSI[1]])
                    P.add("dve", lambda e, rows=rows: e.reciprocal(out=RSI[0][0:rows, 0:1], in_=RSI[0][0:rows, 0:1]),
                          reads=[RSI[1]], writes=[RSI[1]])
                    if hi == 0:
                        P.add("dve", lambda e, b=b, cb=cb, rows=rows, si=si: e.tensor_scalar(
                            out=IMPACC[0][0:rows, si, :], in0=pI[b][0][0:rows, cb:cb + 128], scalar1=RSI[0][0:rows, 0:1], scalar2=None,
                            op0=ALU.mult), reads=[pI[b][1], RSI[1]], writes=[IMPACC[1]])
                    else:
                        P.add("dve", lambda e, b=b, cb=cb, rows=rows, si=si: e.scalar_tensor_tensor(
                            out=IMPACC[0][0:rows, si, :], in0=pI[b][0][0:rows, cb:cb + 128], scalar=RSI[0][0:rows, 0:1],
                            in1=IMPACC[0][0:rows, si, :], op0=ALU.mult, op1=ALU.add), reads=[pI[b][1], RSI[1], IMPACC[1]], writes=[IMPACC[1]])
            rank = 16 if (q0 // 64) < 128 else 15
            for si, (rows, c0) in enumerate(subs):
                cur0 = (q0 + c0) // 64
                p0 = 128 - cur0
                P.add("dve", lambda e, rows=rows, si=si, p0=p0: e.tensor_tensor(
                    out=IMPM[0][0:rows, :], in0=IMPACC[0][0:rows, si, :], in1=PATM[0:rows, p0:p0 + 128], op=ALU.mult),
                    reads=[IMPACC[1], gPATM], writes=[IMPM[1]])
                P.add("dve", lambda e, rows=rows, p0=p0: e.tensor_tensor(
                    out=IMPM[0][0:rows, :], in0=IMPM[0][0:rows, :], in1=PATA[0:rows, p0:p0 + 128], op=ALU.add),
                    reads=[IMPM[1], gPATA], writes=[IMPM[1]])
                P.add("dve", lambda e, rows=rows: e.memset(IMPM[0][0:rows, 0:1], 1e9), writes=[IMPM[1]])
                P.add("dve", lambda e, rows=rows: e.max(out=M8[0][0:rows, 0:8], in_=IMPM[0][0:rows, :]), reads=[IMPM[1]], writes=[M8[1]])
                P.add("dve", lambda e, rows=rows: e.match_replace(out=IMP2[0][0:rows, :], in_to_replace=M8[0][0:rows, 0:8],
                                                                  in_values=IMPM[0][0:rows, :], imm_value=-3e38),
                      reads=[IMPM[1], M8[1]], writes=[IMP2[1]])
                P.add("dve", lambda e, rows=rows: e.max(out=M8[0][0:rows, 8:16], in_=IMP2[0][0:rows, :]), reads=[IMP2[1]], writes=[M8[1]])
                P.add("dve", lambda e, rows=rows: e.tensor_scalar(out=THR[0][0:rows, 1:2], in0=M8[0][0:rows, rank - 1:rank], scalar1=-1e29,
                                                                  scalar2=None, op0=ALU.max), reads=[M8[1]], writes=[THR[1]])
                for v in range(2):
                    P.add("dve", lambda e, rows=rows, v=v: e.tensor_scalar(
                        out=MBP[0][0:rows, 64:128], in0=IMPM[0][0:rows, 64 * v:64 * v + 64],
                        scalar1=THR[0][0:rows, 1:2], scalar2=-BIG, op0=ALU.is_lt, op1=ALU.mult),
                        reads=[IMPM[1], THR[1]], writes=[MBP[1]])
                    pt_, gpt_ = nxt("t", PS_T)
                    P.add("pe", lambda e, pt_=pt_, rows=rows: e.transpose(out=pt_[:, 0:rows], in_=MBP[0][0:rows, 0:128], identity=IDB[0:rows, 0:rows]),
                          reads=[MBP[1], gIDB], writes=[gpt_])
                    for i in range(2):
                        P.add("act" if i == 0 else "dve", (lambda e, pt_=pt_, rows=rows, c0=c0, i=i, v=v: e.activation(
                            out=QA_S[i][v][0][64:128, c0:c0 + rows], in_=pt_[64:128, 0:rows], func=AF.Copy)) if i == 0 else
                            (lambda e, pt_=pt_, rows=rows, c0=c0, i=i, v=v: e.tensor_copy(
                                out=QA_S[i][v][0][64:128, c0:c0 + rows], in_=pt_[64:128, 0:rows])),
                            reads=[gpt_], writes=[QA_S[i][v][1]])
            for i in range(2):
                finish(ocmp[i], N, 2 + i, 3 * i + 0, True, False, None, q0, ACC=ACCS[i])
            for i in range(2):
                ktl = key_tiles(N, q0, nkt, lambda kt: kt)
                lo = [(a, b, c_, d, e_) for (a, b, c_, d, e_, kt) in ktl if kt < 32 or kt * 128 >= max(T, PAST)]
                hi_ = [(a, b, c_, d, e_) for (a, b, c_, d, e_, kt) in ktl if not (kt < 32 or kt * 128 >= max(T, PAST))]
                po = attend(QA_S[i][0][0], QA_S[i][0][1], 128, KA_S[0], KA_S[1], 128, VA_S[0], VA_S[1], N, lo, hk=2 + i,
                            farcol=hp * 4 + 2 + i, first=True, last=(len(hi_) == 0))
                if hi_:
                    attend(QA_S[i][1][0], QA_S[i][1][1], 128, KA_S[0], KA_S[1], 128, VA_S[0], VA_S[1], N, hi_, hk=2 + i,
                           farcol=hp * 4 + 2 + i, first=False, last=True, po=po)
                finish(po, N, 2 + i, 3 * i + 1, False, False, None, q0, ACC=ACCS[i])
                wtl = [((kt % 9) * 128, kt % 9, c_, d, e_) for (a, b, c_, d, e_, kt) in key_tiles(N, q0, nkt, lambda kt: kt, win=True)]
                po = attend(QA_S[i][0][0], QA_S[i][0][1], 64, KWR[0], KWR[1], 64, VWR[0], VWR[1], N, wtl, hk=4 + i, farcol=0)
                finish(po, N, 2 + i, 3 * i + 2, False, True, lambda mb, i=i: mix_dst(8 + 2 * hp + i, mb), q0, ACC=ACCS[i])

        def norm_rows(src_ap, rows, HTcol0, rowsel):
            P.add("sp", lambda e: e.dma_start(out=XT_[0][0:rows, :], in_=src_ap), writes=[XT_[1]], dma=True)
            P.add("act", lambda e: e.activation(out=XN[0][0:rows, :], in_=XT_[0][0:rows, :], func=AF.Square, accum_out=SSQ[0][0:rows, 0:1]),
                  reads=[XT_[1]], writes=[XN[1], SSQ[1]])
            P.add("act", lambda e: e.activation(out=SSQ[0][0:rows, 1:2], in_=SSQ[0][0:rows, 0:1], func=AF.Sqrt, scale=1.0 / D, bias=EPS[0:rows, :]),
                  reads=[SSQ[1], gEPS], writes=[SSQ[1]])
            P.add("dve", lambda e: e.reciprocal(out=SSQ[0][0:rows, 2:3], in_=SSQ[0][0:rows, 1:2]), reads=[SSQ[1]], writes=[SSQ[1]])
            P.add("dve", lambda e: e.tensor_scalar(out=XN[0][0:rows, :], in0=XT_[0][0:rows, :], scalar1=SSQ[0][0:rows, 2:3], scalar2=None, op0=ALU.mult),
                  reads=[XT_[1], SSQ[1]], writes=[XN[1]])

        def transpose_rows(rows, HTcol0, groups):
            for k in range(8):
                pt_, gpt_ = nxt("t", PS_T)
                P.add("pe", lambda e, pt_=pt_, k=k: e.transpose(out=pt_[:, 0:rows], in_=XN[0][0:rows, k * 128:(k + 1) * 128],
                                                                identity=IDB[0:rows, 0:rows]), reads=[XN[1], gIDB], writes=[gpt_])
                for (r0, n_, ar) in groups:
                    P.add("act", lambda e, pt_=pt_, k=k, r0=r0, n_=n_, ar=ar: e.activation(
                        out=HT[0][:, k, HTcol0 + r0:HTcol0 + r0 + n_], in_=pt_[:, r0:r0 + n_], func=AF.Identity,
                        scale=SCT[0][:, k, ar:ar + 1], bias=SHT[0][:, k, ar:ar + 1]), reads=[gpt_, SCT[1], SHT[1]], writes=[HT[1]])

        subsP = [(128, 128 * s) for s in range(4)]

        def mix_dst_prompt(q0, N):
            def f(head, mb):
                P.add("sp", lambda e: e.dma_start(out=mixd_d[head, :, q0:q0 + N], in_=mb[0][:, 0:N]), reads=[mb[1]], writes=[gMIXD], dma=True)
            return f

        gMIXD = G()
        STAGE = cfg.get("STAGE", 99)
        for hp in range(NHP if STAGE >= 4 else (1 if STAGE >= 2 else 0)):
            load_hp(hp)
            for t in range(NT):
                q0 = 512 * t
                for s in range(4):
                    if STAGE >= 2.2:
                        norm_rows(x_d[q0 + 128 * s:q0 + 128 * s + 128, :], 128, 128 * s, None)
                    if STAGE >= 2.3:
                        transpose_rows(128, 128 * s, [(0, 128, 0)])
                if STAGE < 2.4:
                    continue

                def om_dst(si, rows, c0, hp=hp, q0=q0):
                    P.add("sp", lambda e: e.dma_start(out=om_d[hp, q0 + c0:q0 + c0 + rows, :], in_=OTM[0][0:rows, 0:256]), reads=[OTM[1]], dma=True)
                    P.add("sp", lambda e: e.dma_start(out=on_d[hp, q0 + c0:q0 + c0 + rows, :], in_=OTM[0][0:rows, 256:512]), reads=[OTM[1]], dma=True)
                    P.add("sp", lambda e: e.dma_start(out=ow_d[hp, q0 + c0:q0 + c0 + rows, :], in_=OTM[0][0:rows, 512:640]), reads=[OTM[1]], dma=True)
                project_tile(hp, 512, q0, None, subsP, om_dst, None, None, ring_kt=4 * t)
                if STAGE >= 3:
                    attend_tile(hp, 512, q0, subsP, mix_dst_prompt(q0, 512))

        if NS > 0 and STAGE >= 5:
            NTOK = NS * 4
            PTI = sb("pti", [128, NPG], I32)
            IDX = sb("idx", [128, NPG], I32)
            STG = [sb("stg%d" % i, [128, 512], BF16) for i in range(2)]
            norm_rows(xs_d, NTOK, 0, None)
            transpose_rows(NTOK, 0, [(4 * s, 4, 1 + s) for s in range(NS)])
            for s in range(NS):
                P.add("sp", lambda e, s=s: e.dma_start(out=owin_d[s], in_=win_d[s, 4:WB, :]), dma=True)
            HTS = sb("hts", [128, 8, 16], BF16)
            P.add("dve", lambda e: e.tensor_copy(out=HTS[0][:, :, 0:NTOK], in_=HT[0][:, :, 0:NTOK]), reads=[HT[1]], writes=[HTS[1]])
            for s in range(NS):
                P.add("sp", lambda e, s=s: e.dma_start(out=PTI[0][:], in_=ptab_d[s].partition_broadcast(128)), writes=[PTI[1]], dma=True)
                P.add("dve", lambda e: e.tensor_scalar(out=IDX[0][:], in0=PTI[0][:], scalar1=128, scalar2=IOP[:, 0:1], op0=ALU.mult, op1=ALU.add),
                      reads=[PTI[1], gIOP], writes=[IDX[1]])
                for hp in range(NHP):
                    load_hp(hp)
                    kv = hp // 2
                    for pg in range(NPG):
                        sg, gsg = STG[pg % 2]
                        P.add("pool", lambda e, sg=sg, pg=pg, hp=hp: e.indirect_dma_start(
                            out=sg[:, 0:256], out_offset=None,
                            in_=cm_d[hp],
                            in_offset=bass.IndirectOffsetOnAxis(ap=IDX[0][:, pg:pg + 1], axis=0)),
                            reads=[IDX[1]], writes=[gsg], dma=True)
                        P.add("pool", lambda e, sg=sg, pg=pg, kv=kv: e.indirect_dma_start(
                            out=sg[:, 256:512], out_offset=None,
                            in_=cn_d[kv],
                            in_offset=bass.IndirectOffsetOnAxis(ap=IDX[0][:, pg:pg + 1], axis=0)),
                            reads=[IDX[1]], writes=[gsg], dma=True)
                        kc0 = 128 * pg
                        for i in range(2):
                            pt_, gpt_ = nxt("t", PS_T)
                            P.add("pe", lambda e, pt_=pt_, sg=sg, i=i: e.transpose(out=pt_[0:64, 0:128], in_=sg[:, 64 * i:64 * i + 64], identity=IDB[:]),
                                  reads=[gsg, gIDB], writes=[gpt_])
                            P.add("act" if i == 0 else "dve", (lambda e, pt_=pt_, i=i, kc0=kc0: e.activation(
                                out=KA_M[i][0][0:64, kc0:kc0 + 128], in_=pt_[0:64, 0:128], func=AF.Copy)) if i == 0 else
                                (lambda e, pt_=pt_, i=i, kc0=kc0: e.tensor_copy(out=KA_M[i][0][0:64, kc0:kc0 + 128], in_=pt_[0:64, 0:128])),
                                reads=[gpt_], writes=[KA_M[i][1]])
                            P.add("pool", lambda e, sg=sg, i=i, pg=pg: e.tensor_copy(out=VA_M[i][0][:, pg, 0:64], in_=sg[:, 128 + 64 * i:192 + 64 * i]),
                                  reads=[gsg], writes=[VA_M[i][1]])
                        pt_, gpt_ = nxt("t", PS_T)
                        P.add("pe", lambda e, pt_=pt_, sg=sg: e.transpose(out=pt_[:, 0:128], in_=sg[:, 256:384], identity=IDB[:]),
                              reads=[gsg, gIDB], writes=[gpt_])
                        P.add("act", lambda e, pt_=pt_, kc0=kc0: e.activation(out=KCVC[0][:, kc0:kc0 + 128], in_=pt_[:, 0:128], func=AF.Copy),
                              reads=[gpt_], writes=[KCVC[1]])
                        pt_, gpt_ = nxt("t", PS_T)
                        P.add("pe", lambda e, pt_=pt_, sg=sg: e.transpose(out=pt_[0:64, 0:128], in_=sg[:, 384:448], identity=IDB[:]),
                              reads=[gsg, gIDB], writes=[gpt_])
                        P.add("dve", lambda e, pt_=pt_, kc0=kc0: e.tensor_copy(out=KA_S[0][0:64, kc0:kc0 + 128], in_=pt_[0:64, 0:128]),
                              reads=[gpt_], writes=[KA_S[1]])
                        P.add("pool", lambda e, sg=sg, pg=pg: e.tensor_copy(out=VA_S[0][:, pg, 0:64], in_=sg[:, 448:512]),
                              reads=[gsg], writes=[VA_S[1]])
                    for wt in range(WB // 128):
                        sg, gsg = STG[wt % 2]
                        kt = (PAST - WB) // 128 + wt
                        P.add("pool", lambda e, sg=sg, s=s, wt=wt, kv=kv: e.dma_start(
                            out=sg[:, 0:128].rearrange("p (f c) -> p f c", f=2),
                            in_=win_d[s, 128 * wt:128 * wt + 128, :].rearrange("r (f h d) -> r f h d", f=2, h=2)[:, :, kv, :]),
                            writes=[gsg], dma=True)
                        pt_, gpt_ = nxt("t", PS_T)
                        P.add("pe", lambda e, pt_=pt_, sg=sg: e.transpose(out=pt_[0:64, 0:128], in_=sg[:, 0:64], identity=IDB[:]),
                              reads=[gsg, gIDB], writes=[gpt_])
                        r0 = (kt % 9) * 128
                        P.add("dve", lambda e, pt_=pt_, r0=r0: e.tensor_copy(out=KWR[0][0:64, r0:r0 + 128], in_=pt_[0:64, 0:128]),
                              reads=[gpt_], writes=[KWR[1]])
                        P.add("pool", lambda e, sg=sg, kt=kt: e.tensor_copy(out=VWR[0][:, kt % 9, 0:64], in_=sg[:, 64:128]),
                              reads=[gsg], writes=[VWR[1]])
                    ktn = PAST // 128
                    for (t_, g_) in (KA_M[0], KA_M[1], KA_S, KCVC):
                        P.add("pool", lambda e, t_=t_: e.memset(t_[0:64, PAST:PAST + 128], 0.0), writes=[g_])
                    P.add("pool", lambda e: e.memset(KCVC[0][64:128, PAST:PAST + 128], 0.0), writes=[KCVC[1]])
                    P.add("pool", lambda e: e.memset(KWR[0][0:64, (ktn % 9) * 128:(ktn % 9) * 128 + 128], 0.0), writes=[KWR[1]])
                    for (t_, g_) in (VA_M[0], VA_M[1], VA_S):
                        P.add("pool", lambda e, t_=t_: e.memset(t_[:, ktn, 0:64], 0.0), writes=[g_])
                    P.add("pool", lambda e: e.memset(VWR[0][:, ktn % 9, 0:64], 0.0), writes=[VWR[1]])
                    P.add("dve", lambda e, s=s: e.tensor_copy(out=HT[0][:, :, 0:4], in_=HTS[0][:, :, 4 * s:4 * s + 4]), reads=[HTS[1]], writes=[HT[1]])

                    def om_dst(si, rows, c0, hp=hp, s=s):
                        P.add("sp", lambda e: e.dma_start(out=oms_d[hp, 4 * s:4 * s + 4, :], in_=OTM[0][0:4, 0:256]), reads=[OTM[1]], dma=True)
                        P.add("sp", lambda e: e.dma_start(out=ons_d[hp, 4 * s:4 * s + 4, :], in_=OTM[0][0:4, 256:512]), reads=[OTM[1]], dma=True)
                        P.add("sp", lambda e: e.dma_start(out=ows_d[hp, 4 * s:4 * s + 4, :], in_=OTM[0][0:4, 512:640]), reads=[OTM[1]], dma=True)
                    project_tile(hp, 4, PAST, None, [(4, 0)], om_dst, None, None, ring_kt=PAST // 128)

                    def mix_dst(head, mb, s=s):
                        P.add("pool", lambda e: e.tensor_copy(out=MIXS[0][:, head, 4 * s:4 * s + 4], in_=mb[0][:, 0:4]), reads=[mb[1]], writes=[MIXS[1]])
                    attend_tile(hp, 4, PAST, [(4, 0)], mix_dst)

        P.flush()
        stA.close()
        cur[0] = st
        P.barrier()
        GATEP = sb("gatep", [128, D], F32)
        GATES = sb("gates", [16, D], F32)
        FG = sb("fg", [128, D], F32)
        gate_tiles.update({"p": GATEP, "s": GATES, "f": FG})
        if STAGE >= 6:
            adaln("gate")
        XT_ = sb("xt2", [128, D], F32)
        SSQ = sb("ssq2", [128, 4], F32)
        WOUT = sb("wout", [64, 16, D], BF16)
        MIXL = sb("mixl", [64, 16, 512], BF16)
        YP = sb("yp", [128, D], F32)
        if STAGE >= 6:
            P.add("pool", lambda e: e.dma_start(out=WOUT[0][:], in_=wout_d), writes=[WOUT[1]], dma=True)

        def outproj(rows, lhs_fn, x_ap, gate_t, y_ap, extra_reads):
            P.add("sp", lambda e: e.dma_start(out=XT_[0][0:rows, :], in_=x_ap), writes=[XT_[1]], dma=True)
            for oc in range(2):
                px, gpx = nxt("x", PS_X)
                for h in range(16):
                    P.add("pe", lambda e, px=px, h=h, oc=oc: e.matmul(
                        px[0:rows, 0:512], lhsT=lhs_fn(h), rhs=WOUT[0][:, h, oc * 512:(oc + 1) * 512], start=(h == 0), stop=(h == 15)),
                        reads=[WOUT[1]] + extra_reads, writes=[gpx])
                P.add("dve", lambda e, px=px, oc=oc: e.tensor_tensor(
                    out=YP[0][0:rows, oc * 512:(oc + 1) * 512], in0=px[0:rows, 0:512], in1=gate_t[0][0:rows, oc * 512:(oc + 1) * 512], op=ALU.mult),
                    reads=[gpx, gate_t[1]], writes=[YP[1]])
            P.add("pool", lambda e: e.tensor_tensor(out=YP[0][0:rows, :], in0=YP[0][0:rows, :], in1=XT_[0][0:rows, :], op=ALU.add),
                  reads=[YP[1], XT_[1]], writes=[YP[1]])
            P.add("act", lambda e: e.activation(out=XT_[0][0:rows, :], in_=YP[0][0:rows, :], func=AF.Square, accum_out=SSQ[0][0:rows, 0:1]),
                  reads=[YP[1]], writes=[XT_[1], SSQ[1]])
            P.add("act", lambda e: e.activation(out=SSQ[0][0:rows, 1:2], in_=SSQ[0][0:rows, 0:1], func=AF.Sqrt, scale=1.0 / D, bias=EPS[0:rows, :]),
                  reads=[SSQ[1], gEPS], writes=[SSQ[1]])
            P.add("dve", lambda e: e.reciprocal(out=SSQ[0][0:rows, 2:3], in_=SSQ[0][0:rows, 1:2]), reads=[SSQ[1]], writes=[SSQ[1]])
            P.add("dve", lambda e: e.scalar_tensor_tensor(out=YP[0][0:rows, :], in0=YP[0][0:rows, :], scalar=SSQ[0][0:rows, 2:3],
                                                          in1=FG[0][0:rows, :], op0=ALU.mult, op1=ALU.mult),
                  reads=[YP[1], SSQ[1], FG[1]], writes=[YP[1]])
            P.add("sp", lambda e: e.dma_start(out=y_ap, in_=YP[0][0:rows, :]), reads=[YP[1]], dma=True)

        if NHP == 4 and STAGE >= 6:
            for t in range(NT):
                q0 = 512 * t
                P.add("sp", lambda e, q0=q0: e.dma_start(out=MIXL[0][:], in_=mixd_d[:, :, q0:q0 + 512].rearrange("h p n -> p h n")),
                      reads=[gMIXD], writes=[MIXL[1]], dma=True)
                for s in range(4):
                    outproj(128, lambda h, s=s: MIXL[0][:, h, 128 * s:128 * s + 128], x_d[q0 + 128 * s:q0 + 128 * s + 128, :],
                            GATEP, y_d[q0 + 128 * s:q0 + 128 * s + 128, :], [MIXL[1]])

        if NS > 0 and NHP == 4 and STAGE >= 6:
            outproj(NS * 4, lambda h: MIXS[0][:, h, 0:NS * 4], xs_d, GATES, ys_d, [MIXS[1]])

        P.emit_all(st)
    return nc


def make_cfg(T, PAST, NS, NHP, NPHYS):
    return {"T": T, "PAST": PAST, "NS": NS, "NHP": NHP, "TK": ((max(T, PAST) + 2047) // 2048) * 2048 + 128, "NPHYS": NPHYS, "WB": min(512, PAST)}


def make_in_maps(inp, cfg, ncores):
    NS, NHP = cfg["NS"], cfg["NHP"]
    f32 = lambda a: np.ascontiguousarray(np.asarray(a, dtype=np.float32))
    w_in = f32(inp["w_in"])[0]
    rb = f32(inp["rel_bias"])
    shared = {}
    shared["w_ada"] = f32(inp["w_ada"])[0]
    shared["b_ada"] = f32(inp["b_ada"])[0]
    shared["gain"] = f32(inp["norm_gain"])[0]
    shared["fgain"] = f32(inp["final_gain"])
    shared["wfm"] = np.stack([w_in[:, fm_cols(hp)] for hp in range(NHP)])
    shared["wtm"] = np.stack([w_in[:, tm_cols(hp)] for hp in range(NHP)])
    w1k = f32(inp["cmp_k_w1"])[0].reshape(32, 64, 128).transpose(1, 0, 2)
    w1v = f32(inp["cmp_v_w1"])[0].reshape(32, 64, 128).transpose(1, 0, 2)
    shared["w1"] = np.ascontiguousarray(np.concatenate([w1k, w1v], 0))
    shared["w2"] = np.ascontiguousarray(np.concatenate([f32(inp["cmp_k_w2"])[0], f32(inp["cmp_v_w2"])[0]], 1))
    pe = f32(inp["cmp_pe"])[0]
    shared["pet"] = np.ascontiguousarray(np.concatenate([pe[0].T, pe[1].T], 0))
    shared["wout"] = np.ascontiguousarray(f32(inp["w_out"])[0].reshape(16, 64, D).transpose(1, 0, 2))
    ts, t31 = [], []
    for hp in range(NHP):
        hs = [2 * hp, 2 * hp + 1, 8 + 2 * hp, 9 + 2 * hp, 8 + 2 * hp, 9 + 2 * hp]
        ts.append(rb[:, hs])
        t31.append(rb[31, hs[:4]])
    shared["tabsel"] = np.ascontiguousarray(np.concatenate(ts, 1))
    shared["tab31"] = np.ascontiguousarray(np.concatenate(t31, 0))
    cmk = f32(inp["cache_moba_kv"])[0]
    nph = cmk.shape[0]
    for hp in range(4):
        shared["cache_m%d" % hp] = np.ascontiguousarray(cmk[:, :, :, 2 * hp:2 * hp + 2, :].reshape(nph * 128, 256))
    cnk = f32(inp["cache_nsa_kv"])[0]
    for kv in range(2):
        shared["cache_n%d" % kv] = np.ascontiguousarray(cnk[:, :, :, kv, :].reshape(nph * 128, 256))
    for k_, v_ in host_consts(cfg).items():
        shared["c_" + k_] = v_
    xp = f32(inp["x_prompt"])
    xs = f32(inp["x_sample"])
    cp = f32(inp["c_prompt"])
    cs = f32(inp["c_sample"])
    win = f32(inp["state_nsa_win"])[0]
    pt = np.asarray(inp["page_table"]).astype(np.int32)
    maps = []
    for c in range(ncores):
        b = c % xp.shape[0]
        m = dict(shared)
        m["x"] = xp[b]
        m["xs"] = np.ascontiguousarray(xs[NS * c:NS * c + NS].reshape(NS * 4, D))
        cT = np.zeros((D, 144), np.float32)
        cT[:, 0:128] = cp[b][:, None]
        for s in range(NS):
            cT[:, 128 + 4 * s:132 + 4 * s] = cs[NS * c + s][:, None]
        m["cT"] = cT
        m["win"] = np.ascontiguousarray(win[NS * c:NS * c + NS].reshape(NS, win.shape[1], 256))
        m["ptab"] = np.ascontiguousarray(pt[NS * c:NS * c + NS])
        maps.append(m)
    return maps


def assemble(res, cfg, B, ncores):
    T, NS, NHP, WB = cfg["T"], cfg["NS"], cfg["NHP"], cfg["WB"]
    DB = NS * ncores
    y_p = np.zeros((B, T, D), np.float32)
    y_s = np.zeros((DB, 4, D), np.float32)
    mkp = np.zeros((1, B, T, 2, 8, 64), np.float32)
    mks = np.zeros((1, DB, 4, 2, 8, 64), np.float32)
    nkp = np.zeros((1, B, T, 4, 2, 64), np.float32)
    nks = np.zeros((1, DB, 4, 4, 2, 64), np.float32)
    wp = np.zeros((1, B, min(512, T), 2, 2, 64), np.float32)
    ws = np.zeros((1, DB, WB, 2, 2, 64), np.float32)
    for c in range(ncores):
        r = res[c]
        if c < B:
            b = c
            y_p[b] = r["y"]
            for hp in range(NHP):
                om = np.asarray(r["om"][hp])
                mkp[0, b, :, 1, 2 * hp:2 * hp + 2, :] = om[:, 0:128].reshape(T, 2, 64)
                mkp[0, b, :, 0, 2 * hp:2 * hp + 2, :] = om[:, 128:256].reshape(T, 2, 64)
            for kv in range(2):
                on = np.asarray(r["on"][2 * kv])
                ow = np.asarray(r["ow"][2 * kv])
                for f in range(4):
                    nkp[0, b, :, f, kv, :] = on[:, 64 * f:64 * f + 64]
                for f in range(2):
                    wp[0, b, :, f, kv, :] = ow[T - wp.shape[2]:, 64 * f:64 * f + 64]
        sl = slice(NS * c, NS * c + NS)
        y_s[sl] = np.asarray(r["ys"]).reshape(NS, 4, D)
        for hp in range(NHP):
            om = np.asarray(r["oms"][hp]).reshape(NS, 4, 256)
            mks[0, sl, :, 1, 2 * hp:2 * hp + 2, :] = om[:, :, 0:128].reshape(NS, 4, 2, 64)
            mks[0, sl, :, 0, 2 * hp:2 * hp + 2, :] = om[:, :, 128:256].reshape(NS, 4, 2, 64)
        ws[0, sl, 0:WB - 4] = np.asarray(r["owin"]).reshape(NS, WB - 4, 2, 2, 64)
        for kv in range(2):
            on = np.asarray(r["ons"][2 * kv]).reshape(NS, 4, 256)
            ow = np.asarray(r["ows"][2 * kv]).reshape(NS, 4, 128)
            for f in range(4):
                nks[0, sl, :, f, kv, :] = on[:, :, 64 * f:64 * f + 64]
            for f in range(2):
                ws[0, sl, WB - 4:, f, kv, :] = ow[:, :, 64 * f:64 * f + 64]
    return (y_p, y_s, mkp, mks, nkp, nks, wp, ws)


def kernel(**inputs):
    cfg = make_cfg(8192, 8192, 4, 4, 2560)
    nc = build(dict(cfg))
    maps = make_in_maps(inputs, cfg, 8)
    res = run_bass_kernel_spmd(nc, maps, core_ids=list(range(8)))
    return assemble(res.results, cfg, 2, 8)
```

```python
import contextlib
import math
import numpy as np
import ml_dtypes
import concourse.bass as bass
import concourse.mybir as mybir
from concourse.bass_utils import run_bass_kernel_spmd

F32 = mybir.dt.float32
BF16 = mybir.dt.bfloat16
I32 = mybir.dt.int32
AF = mybir.ActivationFunctionType
ALU = mybir.AluOpType
AX = mybir.AxisListType
NPBF = ml_dtypes.bfloat16

BIG = 30000.0
NEGS = -1e30
GL = 1664
HW = 1408
D = 1024


class G:
    __slots__ = ("w", "r", "excl")

    def __init__(self, excl=False):
        self.w = None
        self.r = []
        self.excl = excl


class Op:
    __slots__ = ("eng", "emit", "deps", "inc", "val", "dma", "slot", "dval", "prev")

    def __init__(self, eng, emit, dma):
        self.eng = eng
        self.emit = emit
        self.dma = dma
        self.deps = []
        self.inc = False
        self.val = 0
        self.slot = None
        self.dval = 0
        self.prev = 0


SERIAL = [False]
NSLOT = {"sp": 12, "act": 2, "pool": 12}
ENGS = ("pe", "act", "dve", "pool", "sp")


class Prog:
    def __init__(self, nc):
        self.nc = nc
        self.ops = {e: [] for e in ENGS}
        self.rr = {e: 0 for e in NSLOT}
        self.sv = {e: [0] * n for e, n in NSLOT.items()}
        self.pend = {e: [] for e in ENGS}
        self.lastdma = {}

    def barrier(self):
        deps = []
        for e in ("pe", "act", "dve", "pool"):
            for op in reversed(self.ops[e]):
                if not op.dma:
                    deps.append(op)
                    break
        deps += list(self.lastdma.values())
        for e in ENGS:
            self.pend[e] = list(deps)

    def add(self, eng, emit, reads=(), writes=(), dma=False):
        op = Op(eng, emit, dma)
        deps = []
        seen = set()

        def push(d):
            if d is None or id(d) in seen:
                return
            seen.add(id(d))
            if d.eng == "pe" and eng == "pe" and not d.dma and not dma:
                return
            deps.append(d)

        for g in reads:
            push(g.w)
            if g.excl:
                for r in g.r:
                    push(r)
        for g in writes:
            push(g.w)
            for r in g.r:
                push(r)
        if eng in ("act", "dve", "pool") and not dma and SERIAL[0]:
            for prev in reversed(self.ops[eng]):
                if not prev.dma:
                    if id(prev) not in seen:
                        seen.add(id(prev))
                        deps.append(prev)
                    break
        if self.pend[eng]:
            for d in self.pend[eng]:
                if d is not None and id(d) not in seen:
                    seen.add(id(d))
                    deps.append(d)
            self.pend[eng] = []
        op.deps = deps
        for d in deps:
            d.inc = True
        for g in reads:
            g.r.append(op)
        for g in writes:
            g.w = op
            g.r = []
        if dma:
            i = self.rr[eng]
            self.rr[eng] = (i + 1) % NSLOT[eng]
            op.slot = i
            op.prev = self.sv[eng][i]
            self.sv[eng][i] += 16
            op.dval = self.sv[eng][i]
            self.lastdma[(eng, i)] = op
        self.ops[eng].append(op)
        return op

    def setup(self, stack):
        nc = self.nc
        self.stack = stack
        self.csem = {e: [] for e in ("pe", "act", "dve", "pool")}
        self.dsem = {e: [stack.enter_context(nc.semaphore("d_%s%d" % (e, i))) for i in range(n)]
                     for e, n in NSLOT.items()}
        self.cursor = {e: 0 for e in ENGS}
        self.cval = {e: 0 for e in ENGS}
        self.waited = {e: {} for e in ENGS}

    def flush(self, final=False):
        nc = self.nc
        csem, dsem = self.csem, self.dsem
        ops = self.ops
        sv = self.sv
        EP = 12000
        for e in ("pe", "act", "dve", "pool"):
            for op in ops[e][self.cursor[e]:]:
                if not op.dma:
                    self.cval[e] += 1
                    op.val = self.cval[e]

        def csem_of(eng, val):
            ep = (val - 1) // EP
            lst = csem[eng]
            while len(lst) <= ep:
                lst.append(self.stack.enter_context(nc.semaphore("c_%s%d" % (eng, len(lst)))))
            return lst[ep], val - ep * EP, ("c", eng, ep)

        def sig(d):
            if d.dma:
                return dsem[d.eng][d.slot], d.dval, ("d", d.eng, d.slot)
            return csem_of(d.eng, d.val)

        def run(engname, e, fin=False):
            waited = self.waited[engname]

            def w(sem, val, key):
                if waited.get(key, 0) < val:
                    e.wait_ge(sem, val)
                    waited[key] = val

            for op in ops[engname][self.cursor[engname]:]:
                for d in op.deps:
                    s, v, k = sig(d)
                    w(s, v, k)
                if op.dma and op.prev > 0:
                    w(dsem[engname][op.slot], op.prev, ("d", engname, op.slot))
                ins = op.emit(e)
                if op.dma:
                    ins.then_inc(dsem[engname][op.slot], 16)
                else:
                    ins.then_inc(csem_of(engname, op.val)[0], 1)
            self.cursor[engname] = len(ops[engname])
            if fin:
                for qe, n in NSLOT.items():
                    for i in range(n):
                        if sv[qe][i] > 0:
                            w(dsem[qe][i], sv[qe][i], ("d", qe, i))

        with nc.Block() as block:
            @block.sync
            def _(e):
                run("sp", e, fin=final)

            @block.scalar
            def _(e):
                run("act", e)

            @block.vector
            def _(e):
                run("dve", e)

            @block.gpsimd
            def _(e):
                run("pool", e)

            @block.tensor
            def _(e):
                run("pe", e)

    def emit_all(self, stack):
        self.flush(final=True)


def t5_bucket_np(rel):
    n = np.maximum(rel, 0)
    nf = np.maximum(n, 1).astype(np.float32)
    large = 16 + (np.log(nf / np.float32(16)) / np.float32(math.log(8.0)) * np.float32(16)).astype(np.int32)
    return np.where(n < 16, n, np.minimum(large, 31))


def host_consts(cfg):
    T, PAST = cfg["T"], cfg["PAST"]
    TK = cfg["TK"]
    c = {}
    c["idb"] = np.eye(128, dtype=np.float32).astype(NPBF)
    c["jb"] = np.eye(128, dtype=np.float32)[::-1].copy().astype(NPBF)
    c["ones"] = np.ones((128, 64), np.float32)
    sel6 = np.zeros((6, 6 * 64), np.float32)
    for r in range(6):
        sel6[r, r * 64:(r + 1) * 64] = 1.0
    c["sel6"] = sel6
    k = np.arange(TK)
    kmax = max(T, PAST)
    indm = np.zeros((64, TK), np.float32)
    inds = np.zeros((64, TK), np.float32)
    for r in range(32):
        indm[r] = ((k // 256) == r) & (k < kmax)
    for r in range(64):
        inds[r] = (((k // 64) % 64) == r) & (k < kmax)
    c["indm"] = indm.astype(NPBF)
    c["inds"] = inds.astype(NPBF)
    ni = np.arange(128)[:, None]
    qi = np.arange(512)[None, :]
    cm = np.zeros((128, 5, 512), np.float32)
    for j, dl in enumerate([0, -1, -2, -3, -4]):
        cm[:, j, :] = np.where(qi - 16 * ni >= 512 * dl + 31, 0.0, -BIG)
    c["cmask"] = cm.astype(NPBF)
    ovx = np.zeros((128, 4, 129), np.float32)
    for cc in range(4):
        for n_ in range(128):
            n = 128 * cc + n_
            for s in (n // 4, (n + 1) // 4 if n % 4 == 3 else -1):
                if 0 <= s < 128:
                    ovx[n_, cc, s] = 1.0
        ovx[:, cc, 128] = 1.0
    c["ovx"] = ovx.astype(NPBF)
    pm = np.ones((128, 256), np.float32)
    pa = np.zeros((128, 256), np.float32)
    for p in range(128):
        cur = 0 if p < 64 else 1
        for x in range(256):
            j = x - 128
            if j > cur:
                pm[p, x] = 0.0
                pa[p, x] = NEGS
            elif j == cur or j == cur - 1:
                pm[p, x] = 0.0
                pa[p, x] = 1e9
    c["patm"] = pm
    c["pata"] = pa
    m = np.arange(GL)
    rel = m - 511
    oh = np.zeros((32, GL), np.float32)
    b = t5_bucket_np(rel)
    for i in range(GL):
        if rel[i] >= 0:
            oh[b[i], i] = 1.0
    c["oh"] = oh
    addm = np.zeros((6, GL), np.float32)
    addm[:, rel < 0] = -BIG
    addm[4:6, rel >= 512] = -BIG
    c["addm"] = addm
    c["iop"] = np.arange(128, dtype=np.int32).reshape(128, 1)
    return c


FMCH = {}
_o = 0
for _n, _w in [("qmA", 64), ("qmB", 64), ("kmA", 64), ("kmB", 64), ("zmA", 64), ("zmB", 64),
               ("qnA", 64), ("qnB", 64), ("znA", 64), ("znB", 64), ("ks", 64), ("kw", 64),
               ("kcvc", 128), ("g", 8), ("qnC", 64), ("qnD", 64)]:
    FMCH[_n] = (_o, _w)
    _o += _w
FMC = _o
TMC = 640


def proj_offsets():
    sizes = [512] * 4 + [512] + [128] * 6 + [24, 512]
    offs = np.concatenate([[0], np.cumsum(sizes)])
    names = ["q_m", "k_m", "v_m", "z_m", "q_n", "kc", "vc", "ks", "vs", "kw", "vw", "g_n", "z_n"]
    return {n: int(o) for n, o in zip(names, offs)}


def fm_cols(hp):
    po = proj_offsets()
    A, B = 2 * hp, 2 * hp + 1
    kv = hp // 2
    sib = hp ^ 1
    C, Dh = 2 * sib, 2 * sib + 1
    r64 = lambda base, h: list(range(base + 64 * h, base + 64 * h + 64))
    cols = []
    cols += r64(po["q_m"], A) + r64(po["q_m"], B) + r64(po["k_m"], A) + r64(po["k_m"], B)
    cols += r64(po["z_m"], A) + r64(po["z_m"], B)
    cols += r64(po["q_n"], A) + r64(po["q_n"], B) + r64(po["z_n"], A) + r64(po["z_n"], B)
    cols += r64(po["ks"], kv) + r64(po["kw"], kv)
    cols += r64(po["kc"], kv) + r64(po["vc"], kv)
    cols += list(range(po["g_n"] + 6 * hp, po["g_n"] + 6 * hp + 6)) + [po["g_n"], po["g_n"]]
    cols += r64(po["q_n"], C) + r64(po["q_n"], Dh)
    assert len(cols) == FMC
    return cols


def tm_cols(hp):
    po = proj_offsets()
    A, B = 2 * hp, 2 * hp + 1
    kv = hp // 2
    r64 = lambda base, h: list(range(base + 64 * h, base + 64 * h + 64))
    cols = r64(po["v_m"], A) + r64(po["v_m"], B) + r64(po["k_m"], A) + r64(po["k_m"], B)
    cols += r64(po["kc"], kv) + r64(po["vc"], kv) + r64(po["ks"], kv) + r64(po["vs"], kv)
    cols += r64(po["kw"], kv) + r64(po["vw"], kv)
    assert len(cols) == TMC
    return cols


def build(cfg):
    T, PAST, NS, NHP = cfg["T"], cfg["PAST"], cfg["NS"], cfg["NHP"]
    TK = cfg["TK"]
    NTK = TK // 128
    NPHYS = cfg["NPHYS"]
    NPG = PAST // 128
    NT = T // 512
    WB = cfg["WB"]
    nc = bass.Bass("TRN2", target_bir_lowering=False)
    P = Prog(nc)

    def din(name, shape, dt=F32):
        return nc.dram_tensor(name, list(shape), dt, kind="ExternalInput")

    def dout(name, shape, dt=F32):
        return nc.dram_tensor(name, list(shape), dt, kind="ExternalOutput")

    x_d = din("x", [T, D]).ap()
    xs_d = din("xs", [NS * 4, D]).ap()
    cT_d = din("cT", [D, 144]).ap()
    wada_d = din("w_ada", [D, 3 * D]).ap()
    bada_d = din("b_ada", [3 * D]).ap()
    gain_d = din("gain", [D]).ap()
    fgain_d = din("fgain", [D]).ap()
    wfm_d = din("wfm", [NHP, D, FMC]).ap()
    wtm_d = din("wtm", [NHP, D, TMC]).ap()
    w1_d = din("w1", [128, 32, 128]).ap()
    w2_d = din("w2", [128, 128]).ap()
    pet_d = din("pet", [128, 32]).ap()
    wout_d = din("wout", [64, 16, D]).ap()
    tabsel_d = din("tabsel", [32, NHP * 6]).ap()
    tab31_d = din("tab31", [NHP * 4]).ap()
    cm_d = [din("cache_m%d" % i, [NPHYS * 128, 256]).ap() for i in range(4)]
    cn_d = [din("cache_n%d" % i, [NPHYS * 128, 256]).ap() for i in range(2)]
    win_d = din("win", [NS, WB, 256]).ap()
    ptab_d = din("ptab", [NS, NPG], I32).ap()
    hc = host_consts(cfg)
    cdram = {}
    for k_, v_ in hc.items():
        dt_ = BF16 if v_.dtype == NPBF else (I32 if v_.dtype == np.int32 else F32)
        cdram[k_] = din("c_" + k_, v_.shape, dt_).ap()

    y_d = dout("y", [T, D]).ap()
    ys_d = dout("ys", [NS * 4, D]).ap()
    om_d = dout("om", [NHP, T, 256]).ap()
    on_d = dout("on", [NHP, T, 256]).ap()
    ow_d = dout("ow", [NHP, T, 128]).ap()
    oms_d = dout("oms", [NHP, NS * 4, 256]).ap()
    ons_d = dout("ons", [NHP, NS * 4, 256]).ap()
    ows_d = dout("ows", [NHP, NS * 4, 128]).ap()
    owin_d = dout("owin", [NS, WB - 4, 256]).ap()
    gd_h = nc.dram_tensor("gd", [NHP * 6, GL], BF16, kind="Internal")
    gd_d = gd_h.ap()
    mixd_d = nc.dram_tensor("mixd", [16, 64, T], BF16, kind="Internal").ap()

    with contextlib.ExitStack() as st:
        cur = [st]
        P.setup(st)

        def sb(name, shape, dt):
            return cur[0].enter_context(nc.sbuf_tensor("s_" + name, list(shape), dt)), G()

        def psum(name, shape, dt):
            return st.enter_context(nc.psum_tensor(name, list(shape), dt)), G(excl=True)

        PS_S = [psum("ps_s%d" % i, [128, 512], F32) for i in range(2)]
        PS_O = [psum("ps_o%d" % i, [128, 512], F32) for i in range(2)]
        PS_X = [psum("ps_x%d" % i, [128, 512], F32) for i in range(2)]
        PS_T = [psum("ps_t%d" % i, [128, 1024], BF16) for i in range(2)]
        cnt = {"s": 0, "o": 0, "x": 0, "t": 0, "pt": 0}

        def nxt(kind, lst):
            i = cnt[kind]
            cnt[kind] = (i + 1) % len(lst)
            return lst[i]

        def cload(name, shape, dt, eng="sp"):
            t, g = sb("k_" + name, shape, dt)
            P.add(eng, lambda e: e.dma_start(out=t[:], in_=cdram[name]), writes=[g], dma=True)
            return t, g

        IDB, gIDB = cload("idb", [128, 128], BF16)
        JB, gJB = cload("jb", [128, 128], BF16)
        ONES, gONES = cload("ones", [128, 64], F32)
        SEL6, gSEL6 = cload("sel6", [6, 384], F32)
        CMASK, gCMASK = cload("cmask", [128, 5, 512], BF16)
        OVX, gOVX = cload("ovx", [128, 4, 129], BF16)
        PATM, gPATM = cload("patm", [128, 256], F32)
        PATA, gPATA = cload("pata", [128, 256], F32)
        IOP, gIOP = cload("iop", [128, 1], I32)
        EPS, gEPS = sb("eps", [128, 1], F32)
        P.add("dve", lambda e: e.memset(EPS[:], 1e-6), writes=[gEPS])

        MIXS = sb("mixs", [64, 16, 16], BF16)
        stA = contextlib.ExitStack()
        cur[0] = stA
        KA_M = [sb("ka_m%d" % i, [128, TK], BF16) for i in range(2)]
        VA_M = [sb("va_m%d" % i, [128, NTK, 65], BF16) for i in range(2)]
        KA_S = sb("ka_s", [128, TK], BF16)
        VA_S = sb("va_s", [128, NTK, 65], BF16)
        KWR = sb("kwr", [128, 9 * 128], BF16)
        VWR = sb("vwr", [128, 9, 65], BF16)
        KCVC = sb("kcvc", [128, TK], BF16)
        NCT = max(1, (max(T, PAST) + 2047) // 2048)
        KCMP = sb("kcmp", [64, NCT * 128], BF16)
        VC = sb("vc", [128, NCT, 65], BF16)

        def init_resident():
            for (t, g) in KA_M:
                P.add("pool", lambda e, t=t: e.memset(t[:], 0.0), writes=[g])
                P.add("sp", lambda e, t=t: e.dma_start(out=t[64:128, :], in_=cdram["indm"]), writes=[g], dma=True)
            for (t, g) in VA_M + [VA_S, VWR, VC]:
                P.add("pool", lambda e, t=t: e.memset(t[:], 0.0), writes=[g])
                P.add("pool", lambda e, t=t: e.memset(t[:, :, 64:65], 1.0), writes=[g])
            t, g = KA_S
            P.add("pool", lambda e: e.memset(KA_S[0][:], 0.0), writes=[g])
            P.add("sp", lambda e: e.dma_start(out=KA_S[0][64:128, :], in_=cdram["inds"]), writes=[g], dma=True)
            for (t, g) in (KWR, KCVC, KCMP):
                P.add("pool", lambda e, t=t: e.memset(t[:], 0.0), writes=[g])

        init_resident()

        WFM = sb("wfm", [128, 8, FMC], BF16)
        WTM = sb("wtm", [128, 8, TMC], BF16)
        W1 = sb("w1", [128, 32, 128], BF16)
        W2 = sb("w2", [128, 128], BF16)
        PET = sb("pet", [128, 32], BF16)
        PEB = sb("peb", [128, 2], F32)
        HKW = [1024, 1024, 1024, 1024, HW, HW]
        HK = [sb("hk%d" % i, [128, HKW[i]], BF16) for i in range(6)]
        FARB = sb("farb", [128, NHP * 4], F32)
        TAB = sb("tab", [32, NHP * 6], F32)
        P.add("pool", lambda e: e.dma_start(out=W1[0][:], in_=w1_d), writes=[W1[1]], dma=True)
        P.add("pool", lambda e: e.dma_start(out=W2[0][:], in_=w2_d), writes=[W2[1]], dma=True)
        P.add("pool", lambda e: e.dma_start(out=PET[0][:], in_=pet_d), writes=[PET[1]], dma=True)
        P.add("sp", lambda e: e.dma_start(out=FARB[0][:], in_=tab31_d.partition_broadcast(128)), writes=[FARB[1]], dma=True)
        P.add("sp", lambda e: e.dma_start(out=TAB[0][:], in_=tabsel_d), writes=[TAB[1]], dma=True)

        with contextlib.ExitStack() as st2:
            OH = st2.enter_context(nc.sbuf_tensor("oh", [32, GL], F32)); gOH = G()
            ADDM = st2.enter_context(nc.sbuf_tensor("addm", [6, GL], F32)); gADDM = G()
            GV = st2.enter_context(nc.sbuf_tensor("gv", [6, GL], BF16)); gGV = G()
            P.add("sp", lambda e: e.dma_start(out=OH[:], in_=cdram["oh"]), writes=[gOH], dma=True)
            P.add("sp", lambda e: e.dma_start(out=ADDM[:], in_=cdram["addm"]), writes=[gADDM], dma=True)
            for hp in range(NHP):
                for c0 in range(0, GL, 512):
                    w_ = min(512, GL - c0)
                    px, gpx = nxt("x", PS_X)
                    P.add("pe", lambda e, px=px, c0=c0, w_=w_, hp=hp: e.matmul(
                        px[0:6, 0:w_], lhsT=TAB[0][:, hp * 6:hp * 6 + 6], rhs=OH[:, c0:c0 + w_], start=True, stop=True),
                        reads=[TAB[1], gOH], writes=[gpx])
                    P.add("dve", lambda e, px=px, c0=c0, w_=w_: e.tensor_tensor(
                        out=GV[:, c0:c0 + w_], in0=px[0:6, 0:w_], in1=ADDM[:, c0:c0 + w_], op=ALU.add),
                        reads=[gpx, gADDM], writes=[gGV])
                gGD = G()
                P.add("sp", lambda e, hp=hp: e.dma_start(out=gd_d[hp * 6:hp * 6 + 6, :], in_=GV[:]),
                      reads=[gGV], writes=[gGD], dma=True)
                cfg.setdefault("_ggd", []).append(gGD)
            P.flush()
        gGDs = cfg.pop("_ggd")
        P.barrier()

        SHT = sb("sht", [128, 8, 8], F32)
        SCT = sb("sct", [128, 8, 8], F32)
        gate_tiles = {}

        def adaln(part):
            with contextlib.ExitStack() as st2:
                CT = st2.enter_context(nc.sbuf_tensor("ct_" + part, [128, 8, 144], F32)); gCT = G()
                WA = st2.enter_context(nc.sbuf_tensor("wa_" + part, [128, 8, 512], F32)); gWA = G()
                BAT = st2.enter_context(nc.sbuf_tensor("bat_" + part, [128, 24], F32)); gBAT = G()
                GNT = st2.enter_context(nc.sbuf_tensor("gnt_" + part, [128, 8], F32)); gGNT = G()
                P.add("sp", lambda e: e.dma_start(out=CT[:], in_=cT_d.rearrange("(k p) n -> p k n", p=128)), writes=[gCT], dma=True)
                P.add("sp", lambda e: e.dma_start(out=BAT[:], in_=bada_d.rearrange("(k p) -> p k", p=128), allow_slow_non_contiguous=True), writes=[gBAT], dma=True)
                P.add("sp", lambda e: e.dma_start(out=GNT[:], in_=gain_d.rearrange("(k p) -> p k", p=128), allow_slow_non_contiguous=True), writes=[gGNT], dma=True)
                if part == "gate":
                    BAB = st2.enter_context(nc.sbuf_tensor("bab", [128, D], F32)); gBAB = G()
                    GATEP, GATES, FG = gate_tiles["p"], gate_tiles["s"], gate_tiles["f"]
                    P.add("sp", lambda e: e.dma_start(out=BAB[:], in_=bada_d[2 * D:3 * D].partition_broadcast(128)), writes=[gBAB], dma=True)
                    P.add("sp", lambda e: e.dma_start(out=FG[0][:], in_=fgain_d.partition_broadcast(128)), writes=[FG[1]], dma=True)
                for j in (range(4) if part == "fm" else range(4, 6)):
                    P.add("sp", lambda e, j=j: e.dma_start(
                        out=WA[:], in_=wada_d[:, j * 512:(j + 1) * 512].rearrange("(k p) n -> p k n", p=128)),
                        writes=[gWA], dma=True)
                    if j < 4:
                        for f in range(4):
                            fc = j * 4 + f
                            px, gpx = nxt("x", PS_X)
                            for k in range(8):
                                P.add("pe", lambda e, px=px, k=k, f=f: e.matmul(
                                    px[:, 0:144], lhsT=WA[:, k, f * 128:(f + 1) * 128], rhs=CT[:, k, :],
                                    start=(k == 0), stop=(k == 7)), reads=[gWA, gCT], writes=[gpx])
                            dst = SHT if fc < 8 else SCT
                            fcc = fc % 8
                            for (dc, sc0, sc1, stp) in ((0, 0, 1, 1), (1, 128, 144, 4)):
                                ncol = 1 if dc == 0 else 4
                                if fc < 8:
                                    P.add("dve", lambda e, px=px, fc=fc, fcc=fcc, dc=dc, sc0=sc0, sc1=sc1, stp=stp, ncol=ncol: e.tensor_scalar(
                                        out=SHT[0][:, fcc, dc:dc + ncol], in0=px[:, sc0:sc1:stp], scalar1=BAT[:, fc:fc + 1], scalar2=None, op0=ALU.add),
                                        reads=[gpx, gBAT], writes=[SHT[1]])
                                else:
                                    P.add("dve", lambda e, px=px, fc=fc, fcc=fcc, dc=dc, sc0=sc0, sc1=sc1, stp=stp, ncol=ncol: e.tensor_scalar(
                                        out=SCT[0][:, fcc, dc:dc + ncol], in0=px[:, sc0:sc1:stp], scalar1=BAT[:, fc:fc + 1], scalar2=1.0,
                                        op0=ALU.add, op1=ALU.add), reads=[gpx, gBAT], writes=[SCT[1]])
                            if fc >= 8:
                                P.add("dve", lambda e, fcc=fcc: e.tensor_scalar(
                                    out=SCT[0][:, fcc, 0:5], in0=SCT[0][:, fcc, 0:5], scalar1=GNT[:, fcc:fcc + 1], scalar2=None,
                                    op0=ALU.mult), reads=[gGNT, SCT[1]], writes=[SCT[1]])
                    else:
                        oc = j - 4
                        px, gpx = nxt("x", PS_X)
                        for k in range(8):
                            P.add("pe", lambda e, px=px, k=k: e.matmul(
                                px[:, 0:512], lhsT=CT[:, k, 0:128], rhs=WA[:, k, :], start=(k == 0), stop=(k == 7)),
                                reads=[gWA, gCT], writes=[gpx])
                        P.add("dve", lambda e, px=px, oc=oc: e.tensor_tensor(
                            out=GATEP[0][:, oc * 512:(oc + 1) * 512], in0=px[:, 0:512], in1=BAB[:, oc * 512:(oc + 1) * 512], op=ALU.add),
                            reads=[gpx, gBAB], writes=[GATEP[1]])
                        px, gpx = nxt("x", PS_X)
                        for k in range(8):
                            P.add("pe", lambda e, px=px, k=k: e.matmul(
                                px[0:16, 0:512], lhsT=CT[:, k, 128:144], rhs=WA[:, k, :], start=(k == 0), stop=(k == 7)),
                                reads=[gWA, gCT], writes=[gpx])
                        P.add("dve", lambda e, px=px, oc=oc: e.tensor_tensor(
                            out=GATES[0][:, oc * 512:(oc + 1) * 512], in0=px[0:16, 0:512], in1=BAB[0:16, oc * 512:(oc + 1) * 512], op=ALU.add),
                            reads=[gpx, gBAB], writes=[GATES[1]])
                P.flush()
            P.barrier()

        adaln("fm")

        XT_ = sb("xt", [128, D], F32)
        XN = sb("xn", [128, D], BF16)
        SSQ = sb("ssq", [128, 4], F32)
        HT = sb("ht", [128, 8, 512], BF16)
        QA_M = [sb("qa_m%d" % i, [128, 512], BF16) for i in range(2)]
        QA_S = [[sb("qa_s%d%d" % (i, v), [128, 512], BF16) for v in range(2)] for i in range(2)]
        QSIB = [sb("qsib%d" % i, [64, 512], BF16) for i in range(2)]
        ZT = [sb("zt%d" % i, [64, 512], BF16) for i in range(4)]
        GT = sb("gt", [8, 512], F32)
        KMF = [sb("kmf%d" % i, [64, 40], BF16) for i in range(2)]
        KMFF = sb("kmff", [64, 40], F32)
        PTB = [sb("ptb%d" % i, [128, 512], BF16) for i in range(2)]
        OTM = sb("otm", [128, TMC], F32)
        IMPACC = sb("impacc", [128, 4, 128], F32)
        SC = sb("sc", [128, 40], F32)
        M8 = sb("m8", [128, 16], F32)
        THR = sb("thr", [128, 2], F32)
        IMPM = sb("impm", [128, 128], F32)
        IMP2 = sb("imp2", [128, 128], F32)
        MBP = sb("mbp", [128, 256], BF16)
        RS = sb("rs", [128, 512], F32)
        RSI = sb("rsi", [128, 2], F32)
        BCZ = sb("bcz", [64, 512], F32)
        TMP = sb("tmp", [64, 512], F32)
        ACCS = [sb("acc%d" % i, [64, 512], F32) for i in range(2)]
        MIXB = sb("mixb", [64, 512], BF16)
        AKV = sb("akv", [128, 256], BF16)
        SG = sb("sg", [128, 512], F32)
        for (t, g) in (MBP,):
            P.add("pool", lambda e, t=t: e.memset(t[:], 0.0), writes=[g])
        for lst in (QA_M, QA_S[0], QA_S[1]):
            for (t, g) in lst:
                P.add("pool", lambda e, t=t: e.memset(t[:], 0.0), writes=[g])

        def load_hp(hp):
            P.add("pool", lambda e: e.dma_start(out=WFM[0][:], in_=wfm_d[hp].rearrange("(k p) n -> p k n", p=128)),
                  writes=[WFM[1]], dma=True)
            P.add("pool", lambda e: e.dma_start(out=WTM[0][:], in_=wtm_d[hp].rearrange("(k p) n -> p k n", p=128)),
                  writes=[WTM[1]], dma=True)
            for v in range(6):
                src = bass.AP(gd_h, (hp * 6 + v) * GL, [[1, 128], [1, HKW[v]]])
                P.add("sp", lambda e, v=v, src=src: e.dma_start(out=HK[v][0][:], in_=src),
                      reads=[gGDs[hp]], writes=[HK[v][1]], dma=True)

        def peb_compute():
            for kvi in range(2):
                px, gpx = nxt("x", PS_X)
                lo = 64 * kvi
                for l in range(32):
                    P.add("pe", lambda e, px=px, l=l, lo=lo: e.matmul(
                        px[:, 0:1], lhsT=W1[0][lo:lo + 64, l, :], rhs=PET[0][lo:lo + 64, l:l + 1],
                        start=(l == 0), stop=(l == 31)), reads=[W1[1], PET[1]], writes=[gpx])
                P.add("dve", lambda e, px=px, kvi=kvi: e.tensor_copy(out=PEB[0][:, kvi:kvi + 1], in_=px[:, 0:1]),
                      reads=[gpx], writes=[PEB[1]])

        peb_compute()

        def project_tile(hp, N, q0, ht_ready, subs, om_dst, on_dst, ow_dst, ring_kt):
            HTt, gHT = HT

            def fm(name, evac):
                off, w_ = FMCH[name]
                w_ = max(w_, 64)
                px, gpx = nxt("x", PS_X)
                for k in range(8):
                    P.add("pe", lambda e, px=px, k=k, off=off, w_=w_: e.matmul(
                        px[0:w_, 0:N], lhsT=WFM[0][:, k, off:off + w_], rhs=HTt[:, k, 0:N],
                        start=(k == 0), stop=(k == 7)), reads=[WFM[1], gHT], writes=[gpx])
                evac(px, gpx)

            kc0 = q0
            for i, nm in enumerate(("qmA", "qmB")):
                def ev(px, gpx, i=i):
                    P.add("act", lambda e: e.activation(out=QA_M[i][0][0:64, 0:N], in_=px[0:64, 0:N], func=AF.Copy, scale=0.125),
                          reads=[gpx], writes=[QA_M[i][1]])
                fm(nm, ev)
            for i, nm in enumerate(("kmA", "kmB")):
                def ev(px, gpx, i=i):
                    P.add("act", lambda e: e.activation(out=KA_M[i][0][0:64, kc0:kc0 + N], in_=px[0:64, 0:N], func=AF.Copy),
                          reads=[gpx], writes=[KA_M[i][1]])
                fm(nm, ev)
            if cfg.get("STAGE", 99) < 2.45:
                return
            for i, nm in enumerate(("zmA", "zmB", "znA", "znB")):
                def ev(px, gpx, i=i):
                    if cfg.get("STAGE", 99) >= 2.47:
                        P.add("act", lambda e: e.activation(out=SG[0][0:64, 0:N], in_=px[0:64, 0:N], func=AF.Sigmoid),
                              reads=[gpx], writes=[SG[1]])
                    if cfg.get("STAGE", 99) >= 2.49:
                        if cfg.get("VAR", 0) == 1:
                            P.add("dve", lambda e: e.tensor_tensor(out=TMP[0][:, 0:N], in0=px[0:64, 0:N], in1=SG[0][0:64, 0:N], op=ALU.mult),
                                  reads=[gpx, SG[1]], writes=[TMP[1]])
                        elif cfg.get("VAR", 0) == 2:
                            P.add("dve", lambda e: e.tensor_copy(out=TMP[0][:, 0:N], in_=px[0:64, 0:N]), reads=[gpx], writes=[TMP[1]])
                            P.add("dve", lambda e: e.tensor_tensor(out=ZT[i][0][:, 0:N], in0=TMP[0][:, 0:N], in1=SG[0][0:64, 0:N], op=ALU.mult),
                                  reads=[TMP[1], SG[1]], writes=[ZT[i][1]])
                        elif cfg.get("VAR", 0) == 3:
                            P.add("dve", lambda e: e.tensor_tensor(out=ZT[i][0][:, 0:N], in0=px[0:64, 0:N], in1=SG[0][0:64, 0:N], op=ALU.mult),
                                  reads=[gpx, SG[1]], writes=[ZT[i][1], TMP[1]])
                        elif cfg.get("VAR", 0) == 4:
                            P.add("dve", lambda e: e.tensor_tensor(out=ZT[0][0][:, 0:N], in0=px[0:64, 0:N], in1=SG[0][0:64, 0:N], op=ALU.mult),
                                  reads=[gpx, SG[1]], writes=[ZT[0][1]])
                        else:
                            P.add("dve", lambda e: e.tensor_tensor(out=ZT[i][0][:, 0:N], in0=px[0:64, 0:N], in1=SG[0][0:64, 0:N], op=ALU.mult),
                                  reads=[gpx, SG[1]], writes=[ZT[i][1]])
                fm(nm, ev)
            if cfg.get("STAGE", 99) < 2.6:
                return
            for i, nm in enumerate(("qnA", "qnB")):
                def ev(px, gpx, i=i):
                    for v in range(2):
                        P.add("act" if v == 0 else "dve", (lambda e, v=v: e.activation(
                            out=QA_S[i][v][0][0:64, 0:N], in_=px[0:64, 0:N], func=AF.Copy, scale=0.125)) if v == 0 else
                            (lambda e, v=v: e.tensor_scalar(out=QA_S[i][v][0][0:64, 0:N], in0=px[0:64, 0:N], scalar1=0.125,
                                                            scalar2=None, op0=ALU.mult)),
                            reads=[gpx], writes=[QA_S[i][v][1]])
                fm(nm, ev)
            if cfg.get("STAGE", 99) < 2.61:
                return
            for i, nm in enumerate(("qnC", "qnD")):
                def ev(px, gpx, i=i):
                    P.add("act", lambda e: e.activation(out=QSIB[i][0][:, 0:N], in_=px[0:64, 0:N], func=AF.Copy, scale=0.125),
                          reads=[gpx], writes=[QSIB[i][1]])
                fm(nm, ev)

            if cfg.get("STAGE", 99) < 2.62:
                return

            def ev(px, gpx):
                P.add("act", lambda e: e.activation(out=KA_S[0][0:64, kc0:kc0 + N], in_=px[0:64, 0:N], func=AF.Copy),
                      reads=[gpx], writes=[KA_S[1]])
            fm("ks", ev)
            if cfg.get("STAGE", 99) < 2.63:
                return

            def ev(px, gpx):
                for j in range(0, N, 128):
                    n_ = min(128, N - j)
                    r0 = ((ring_kt + j // 128) % 9) * 128
                    P.add("dve", lambda e, j=j, n_=n_, r0=r0: e.tensor_copy(out=KWR[0][0:64, r0:r0 + n_], in_=px[0:64, j:j + n_]),
                          reads=[gpx], writes=[KWR[1]])
            fm("kw", ev)

            if cfg.get("STAGE", 99) < 2.64:
                return

            def ev(px, gpx):
                P.add("act", lambda e: e.activation(out=KCVC[0][:, kc0:kc0 + N], in_=px[:, 0:N], func=AF.Copy),
                      reads=[gpx], writes=[KCVC[1]])
            fm("kcvc", ev)

            if cfg.get("STAGE", 99) < 2.645:
                return

            def ev(px, gpx):
                P.add("act", lambda e: e.activation(out=GT[0][:, 0:N], in_=px[0:8, 0:N], func=AF.Sigmoid),
                      reads=[gpx], writes=[GT[1]])
            fm("g", ev)

            if cfg.get("STAGE", 99) < 2.7:
                return
            for si, (rows, c0) in enumerate(subs):
                pa, gpa = nxt("x", PS_X)
                pb, gpb = nxt("x", PS_X)
                for k in range(8):
                    P.add("pe", lambda e, pa=pa, k=k, rows=rows, c0=c0: e.matmul(
                        pa[0:rows, 0:512], lhsT=HTt[:, k, c0:c0 + rows], rhs=WTM[0][:, k, 0:512],
                        start=(k == 0), stop=(k == 7)), reads=[WTM[1], gHT], writes=[gpa])
                for k in range(8):
                    P.add("pe", lambda e, pb=pb, k=k, rows=rows, c0=c0: e.matmul(
                        pb[0:rows, 0:128], lhsT=HTt[:, k, c0:c0 + rows], rhs=WTM[0][:, k, 512:640],
                        start=(k == 0), stop=(k == 7)), reads=[WTM[1], gHT], writes=[gpb])
                P.add("act", lambda e, pa=pa, rows=rows: e.activation(out=OTM[0][0:rows, 0:512], in_=pa[0:rows, 0:512], func=AF.Copy),
                      reads=[gpa], writes=[OTM[1]])
                P.add("dve", lambda e, pb=pb, rows=rows: e.tensor_copy(out=OTM[0][0:rows, 512:640], in_=pb[0:rows, 0:128]),
                      reads=[gpb], writes=[OTM[1]])
                if cfg.get("STAGE", 99) < 2.8:
                    continue
                kt = (q0 + c0) // 128
                r_ = (q0 + c0) % 128
                assert r_ == 0
                for i in range(2):
                    P.add("pool", lambda e, i=i, kt=kt, rows=rows: e.tensor_copy(
                        out=VA_M[i][0][0:rows, kt, 0:64], in_=OTM[0][0:rows, 64 * i:64 * i + 64]),
                        reads=[OTM[1]], writes=[VA_M[i][1]])
                P.add("pool", lambda e, kt=kt, rows=rows: e.tensor_copy(
                    out=VA_S[0][0:rows, kt, 0:64], in_=OTM[0][0:rows, 448:512]), reads=[OTM[1]], writes=[VA_S[1]])
                rk = (ring_kt + c0 // 128) % 9
                P.add("pool", lambda e, rk=rk, rows=rows: e.tensor_copy(
                    out=VWR[0][0:rows, rk, 0:64], in_=OTM[0][0:rows, 576:640]), reads=[OTM[1]], writes=[VWR[1]])
                if cfg.get("STAGE", 99) >= 2.9:
                    om_dst(si, rows, c0)

        def compress(c):
            for kvi in range(2):
                lo = 64 * kvi
                px, gpx = nxt("x", PS_X)
                for l in range(32):
                    s0 = 2048 * c + l
                    P.add("pe", lambda e, px=px, l=l, lo=lo, s0=s0: e.matmul(
                        px[:, 0:128], lhsT=W1[0][lo:lo + 64, l, :], rhs=KCVC[0][lo:lo + 64, s0:s0 + 2033:16],
                        start=(l == 0), stop=(l == 31)), reads=[W1[1], KCVC[1]], writes=[gpx])
                P.add("act", lambda e, px=px, kvi=kvi: e.activation(
                    out=SG[0][:, 0:128], in_=px[:, 0:128], func=AF.Sigmoid, bias=PEB[0][:, kvi:kvi + 1]),
                    reads=[gpx, PEB[1]], writes=[SG[1]])
                P.add("dve", lambda e, px=px, kvi=kvi: e.scalar_tensor_tensor(
                    out=AKV[0][:, 128 * kvi:128 * kvi + 128], in0=px[:, 0:128], scalar=PEB[0][:, kvi:kvi + 1], in1=SG[0][:, 0:128],
                    op0=ALU.add, op1=ALU.mult), reads=[gpx, PEB[1], SG[1]], writes=[AKV[1]])
            px, gpx = nxt("x", PS_X)
            P.add("pe", lambda e, px=px: e.matmul(px[0:64, 0:128], lhsT=W2[0][:, 0:64], rhs=AKV[0][:, 0:128], start=True, stop=True),
                  reads=[W2[1], AKV[1]], writes=[gpx])
            P.add("dve", lambda e, px=px: e.tensor_copy(out=KCMP[0][:, 128 * c:128 * c + 128], in_=px[0:64, 0:128]),
                  reads=[gpx], writes=[KCMP[1]])
            px, gpx = nxt("x", PS_X)
            P.add("pe", lambda e, px=px: e.matmul(px[:, 0:64], lhsT=AKV[0][:, 128:256], rhs=W2[0][:, 64:128], start=True, stop=True),
                  reads=[W2[1], AKV[1]], writes=[gpx])
            P.add("dve", lambda e, px=px: e.tensor_copy(out=VC[0][:, c, 0:64], in_=px[:, 0:64]), reads=[gpx], writes=[VC[1]])

        def attend(QT, gQ, qrows, KT, gK, krows, VT, gV, N, tiles, hk, farcol, on_pt=None, first=True, last=True, po=None):
            if po is None:
                po = nxt("o", PS_O)
            pO, gO = po
            nt = len(tiles)
            pts = {}

            def emit_S(ti):
                kc, vs, kind, arg, qlo = tiles[ti]
                pS, gS = nxt("s", PS_S)
                two = kind in ("hk", "cm")
                P.add("pe", lambda e, pS=pS, kc=kc, qlo=qlo, two=two: e.matmul(
                    pS[:, qlo:N], lhsT=KT[0:krows, kc:kc + 128], rhs=QT[0:qrows, qlo:N], start=True, stop=not two),
                    reads=[gK, gQ], writes=[gS])
                if kind == "hk":
                    c0 = arg + 384 + qlo
                    P.add("pe", lambda e, pS=pS, qlo=qlo, c0=c0: e.matmul(
                        pS[:, qlo:N], lhsT=JB[:], rhs=HK[hk][0][:, c0:c0 + N - qlo], start=False, stop=True),
                        reads=[gJB, HK[hk][1]], writes=[gS])
                elif kind == "cm":
                    P.add("pe", lambda e, pS=pS, qlo=qlo, arg=arg: e.matmul(
                        pS[:, qlo:N], lhsT=IDB[:], rhs=CMASK[:, arg, qlo:N], start=False, stop=True),
                        reads=[gIDB, gCMASK], writes=[gS])
                pt, gpt = nxt("pt", PTB)
                if kind == "far":
                    P.add("act", lambda e, pS=pS, pt=pt, qlo=qlo: e.activation(
                        out=pt[:, qlo:N], in_=pS[:, qlo:N], func=AF.Exp, bias=FARB[0][:, farcol:farcol + 1]),
                        reads=[gS, FARB[1]], writes=[gpt])
                else:
                    P.add("act", lambda e, pS=pS, pt=pt, qlo=qlo: e.activation(
                        out=pt[:, qlo:N], in_=pS[:, qlo:N], func=AF.Exp), reads=[gS], writes=[gpt])
                pts[ti] = (pt, gpt)

            def emit_PV(ti):
                kc, vs, kind, arg, qlo = tiles[ti]
                pt, gpt = pts.pop(ti)
                if VT is not None:
                    P.add("pe", lambda e, pt=pt, vs=vs, qlo=qlo, ti=ti: e.matmul(
                        pO[0:65, qlo:N], lhsT=VT[:, vs, 0:65], rhs=pt[:, qlo:N], start=(first and ti == 0), stop=(last and ti == nt - 1)),
                        reads=[gV, gpt], writes=[gO])
                if on_pt is not None:
                    on_pt(ti, pt, gpt)

            GS = 1 if N > 64 else min(8, 512 // N)
            if GS > 1:
                groups = []
                for ti, (kc, vs, kind, arg, qlo) in enumerate(tiles):
                    simple = kind in ("far", "none") and qlo == 0
                    if simple and groups and groups[-1][0] == kind and len(groups[-1][1]) < GS:
                        groups[-1][1].append(ti)
                    else:
                        groups.append((kind if simple else "single", [ti]))
                gpts = {}

                def emit_SG(gi):
                    kind, tis = groups[gi]
                    if kind == "single":
                        emit_S(tis[0])
                        return
                    pS, gS = nxt("s", PS_S)
                    for j, ti in enumerate(tis):
                        kc = tiles[ti][0]
                        P.add("pe", lambda e, pS=pS, kc=kc, j=j: e.matmul(
                            pS[:, j * N:(j + 1) * N], lhsT=KT[0:krows, kc:kc + 128], rhs=QT[0:qrows, 0:N], start=True, stop=True,
                            skip_group_check=True), reads=[gK, gQ], writes=[gS])
                    pt, gpt = nxt("pt", PTB)
                    w_ = len(tis) * N
                    if kind == "far":
                        P.add("act", lambda e, pS=pS, pt=pt, w_=w_: e.activation(
                            out=pt[:, 0:w_], in_=pS[:, 0:w_], func=AF.Exp, bias=FARB[0][:, farcol:farcol + 1]),
                            reads=[gS, FARB[1]], writes=[gpt])
                    else:
                        P.add("act", lambda e, pS=pS, pt=pt, w_=w_: e.activation(
                            out=pt[:, 0:w_], in_=pS[:, 0:w_], func=AF.Exp), reads=[gS], writes=[gpt])
                    gpts[gi] = (pt, gpt)

                def emit_PVG(gi):
                    kind, tis = groups[gi]
                    if kind == "single":
                        emit_PV(tis[0])
                        return
                    pt, gpt = gpts.pop(gi)
                    for j, ti in enumerate(tis):
                        vs = tiles[ti][1]
                        ptv = pt[:, j * N:(j + 1) * N]
                        if VT is not None:
                            P.add("pe", lambda e, ptv=ptv, vs=vs, ti=ti: e.matmul(
                                pO[0:65, 0:N], lhsT=VT[:, vs, 0:65], rhs=ptv, start=(first and ti == 0), stop=(last and ti == nt - 1)),
                                reads=[gV, gpt], writes=[gO])
                        if on_pt is not None:
                            on_pt(ti, ptv, gpt)

                ng = len(groups)
                if ng > 0:
                    emit_SG(0)
                for gi in range(ng):
                    if gi + 1 < ng:
                        emit_SG(gi + 1)
                    emit_PVG(gi)
                return po
            if nt > 0:
                emit_S(0)
            for ti in range(nt):
                if ti + 1 < nt:
                    emit_S(ti + 1)
                emit_PV(ti)
            return po

        def finish(po, N, zi, coef_row, acc_first, acc_last, head_slot, q0, ACC=None):
            ACC = ACC or ACCS[0]
            pO, gO = po
            P.add("dve", lambda e: e.tensor_scalar(out=RS[0][64:65, 0:N], in0=pO[64:65, 0:N], scalar1=1e-30, scalar2=None, op0=ALU.max),
                  reads=[gO], writes=[RS[1]])
            P.add("dve", lambda e: e.reciprocal(out=RS[0][64:65, 0:N], in_=RS[0][64:65, 0:N]), reads=[RS[1]], writes=[RS[1]])
            pb, gpb = nxt("x", PS_X)
            P.add("pe", lambda e: e.matmul(pb[0:64, 0:N], lhsT=ONES[64:65, 0:64], rhs=RS[0][64:65, 0:N], start=True, stop=True),
                  reads=[gONES, RS[1]], writes=[gpb])
            P.add("dve", lambda e: e.tensor_tensor(out=BCZ[0][:, 0:N], in0=pb[0:64, 0:N], in1=ZT[zi][0][:, 0:N], op=ALU.mult),
                  reads=[gpb, ZT[zi][1]], writes=[BCZ[1]])
            if coef_row is not None:
                pg, gpg = nxt("x", PS_X)
                P.add("pe", lambda e: e.matmul(pg[0:64, 0:N], lhsT=SEL6[0:6, coef_row * 64:coef_row * 64 + 64], rhs=GT[0][0:6, 0:N],
                                               start=True, stop=True), reads=[gSEL6, GT[1]], writes=[gpg])
                P.add("dve", lambda e: e.tensor_tensor(out=BCZ[0][:, 0:N], in0=BCZ[0][:, 0:N], in1=pg[0:64, 0:N], op=ALU.mult),
                      reads=[gpg, BCZ[1]], writes=[BCZ[1]])
            if acc_first and acc_last:
                P.add("dve", lambda e: e.tensor_tensor(out=MIXB[0][:, 0:N], in0=pO[0:64, 0:N], in1=BCZ[0][:, 0:N], op=ALU.mult),
                      reads=[gO, BCZ[1]], writes=[MIXB[1]])
            elif acc_first:
                P.add("dve", lambda e: e.tensor_tensor(out=ACC[0][:, 0:N], in0=pO[0:64, 0:N], in1=BCZ[0][:, 0:N], op=ALU.mult),
                      reads=[gO, BCZ[1]], writes=[ACC[1]])
            else:
                P.add("dve", lambda e: e.tensor_tensor(out=TMP[0][:, 0:N], in0=pO[0:64, 0:N], in1=BCZ[0][:, 0:N], op=ALU.mult),
                      reads=[gO, BCZ[1]], writes=[TMP[1]])
                if acc_last:
                    P.add("pool", lambda e: e.tensor_tensor(out=MIXB[0][:, 0:N], in0=ACC[0][:, 0:N], in1=TMP[0][:, 0:N], op=ALU.add),
                          reads=[ACC[1], TMP[1]], writes=[MIXB[1]])
                else:
                    P.add("pool", lambda e: e.tensor_tensor(out=ACC[0][:, 0:N], in0=ACC[0][:, 0:N], in1=TMP[0][:, 0:N], op=ALU.add),
                          reads=[ACC[1], TMP[1]], writes=[ACC[1]])
            if acc_last:
                head_slot(MIXB)

        def key_tiles(N, q0, nkeys_tiles, vslot_fn, win=False):
            tl = []
            for kt in range(nkeys_tiles):
                off = q0 - 128 * kt
                qlo = max(0, -off)
                if qlo >= N:
                    continue
                if win:
                    if off > 512:
                        continue
                    tl.append((None, vslot_fn(kt), "hk", off, qlo, kt))
                else:
                    kind = "hk" if off <= 128 else "far"
                    tl.append((128 * kt, vslot_fn(kt), kind, off, qlo, kt))
            return tl

        def attend_tile(hp, N, q0, subs, mix_dst):
            nkt = (q0 + N + 127) // 128
            for i in range(2):
                KT, gK = KA_M[i]
                nblk = (q0 + N) // 256
                curs = [(q0 + c0) // 256 for (_, c0) in subs]
                nb = max(curs)
                if nb > 0:
                    P.add("dve", lambda e, KT=KT, nb=nb, i=i: e.tensor_reduce(
                        out=KMFF[0][:, 0:nb], in_=KT[0:64, 0:nb * 256].rearrange("p (n k) -> p n k", k=256), axis=AX.X, op=ALU.add),
                        reads=[gK], writes=[KMFF[1]])
                    P.add("dve", lambda e, nb=nb, i=i: e.tensor_copy(out=KMF[i][0][:, 0:nb], in_=KMFF[0][:, 0:nb]),
                          reads=[KMFF[1]], writes=[KMF[i][1]])
                for si, (rows, c0) in enumerate(subs):
                    cur = curs[si]
                    P.add("pool", lambda e, rows=rows: e.memset(SC[0][0:rows, :], NEGS), writes=[SC[1]])
                    if cur > 0:
                        px, gpx = nxt("x", PS_X)
                        P.add("pe", lambda e, px=px, rows=rows, c0=c0, cur=cur, i=i: e.matmul(
                            px[0:rows, 0:cur], lhsT=QA_M[i][0][0:64, c0:c0 + rows], rhs=KMF[i][0][:, 0:cur], start=True, stop=True),
                            reads=[QA_M[i][1], KMF[i][1]], writes=[gpx])
                        P.add("dve", lambda e, px=px, rows=rows, cur=cur: e.tensor_copy(out=SC[0][0:rows, 0:cur], in_=px[0:rows, 0:cur]),
                              reads=[gpx], writes=[SC[1]])
                    P.add("dve", lambda e, rows=rows: e.max(out=M8[0][0:rows, 0:8], in_=SC[0][0:rows, 0:32]), reads=[SC[1]], writes=[M8[1]])
                    P.add("dve", lambda e, rows=rows: e.tensor_scalar(out=THR[0][0:rows, 0:1], in0=M8[0][0:rows, 2:3], scalar1=-1e29,
                                                                      scalar2=None, op0=ALU.max), reads=[M8[1]], writes=[THR[1]])
                    P.add("dve", lambda e, rows=rows: e.tensor_scalar(out=MBP[0][0:rows, 64:96], in0=SC[0][0:rows, 0:32],
                                                                      scalar1=THR[0][0:rows, 0:1], scalar2=-BIG, op0=ALU.is_lt, op1=ALU.mult),
                          reads=[SC[1], THR[1]], writes=[MBP[1]])
                    if cur < 32:
                        P.add("dve", lambda e, rows=rows, cur=cur: e.memset(MBP[0][0:rows, 64 + cur:65 + cur], 0.0), writes=[MBP[1]])
                    pt_, gpt_ = nxt("t", PS_T)
                    P.add("pe", lambda e, pt_=pt_, rows=rows: e.transpose(out=pt_[:, 0:rows], in_=MBP[0][0:rows, 0:128], identity=IDB[0:rows, 0:rows]),
                          reads=[MBP[1], gIDB], writes=[gpt_])
                    P.add("act", lambda e, pt_=pt_, rows=rows, c0=c0, i=i: e.activation(
                        out=QA_M[i][0][64:128, c0:c0 + rows], in_=pt_[64:128, 0:rows], func=AF.Copy), reads=[gpt_], writes=[QA_M[i][1]])
                tl = [(a, b, c_, d, e_) for (a, b, c_, d, e_, _) in key_tiles(N, q0, nkt, lambda kt: kt)]
                po = attend(QA_M[i][0], QA_M[i][1], 128, KT, gK, 128, VA_M[i][0], VA_M[i][1], N, tl, hk=i, farcol=hp * 4 + i)
                finish(po, N, i, None, True, True, lambda mb, i=i: mix_dst(2 * hp + i, mb), q0)

            if N == 512:
                t = q0 // 512
                c = t // 4
                if t % 4 == 0 and t > 0:
                    compress(c - 1)
                compress(c)
                cts = list(range(0, c + 1))
            else:
                for c in range(NCT):
                    compress(c)
                cts = list(range(NCT))
            tq = q0 // 512
            ctl = []
            for c in cts:
                dl = 4 * c - tq
                if dl > 0:
                    continue
                kind, arg = ("cm", -dl) if dl >= -4 else ("none", 0)
                ctl.append((128 * c, c, kind, arg, 0))
            nsub = len(subs)
            heads = [(QA_S[0][0][0], QA_S[0][0][1], True, 0), (QA_S[1][0][0], QA_S[1][0][1], True, 1),
                     (QSIB[0][0], QSIB[0][1], False, 0), (QSIB[1][0], QSIB[1][1], False, 1)]
            ocmp = [None, None]
            for hi, (QT, gQ, own, oi) in enumerate(heads):
                pI = [nxt("x", PS_X), nxt("x", PS_X)]
                state = {"first": [True, True]}

                def on_pt(ti, pt, gpt, pI=pI, state=state):
                    cc = ctl[ti][1]
                    for si, (rows, c0) in enumerate(subs):
                        b = si // 2
                        cb = (si % 2) * 129
                        fst = state["first"][b]
                        state["first"][b] = False
                        P.add("pe", lambda e, b=b, cb=cb, rows=rows, c0=c0, cc=cc, fst=fst, pt=pt: e.matmul(
                            pI[b][0][0:rows, cb:cb + 129], lhsT=pt[:, c0:c0 + rows], rhs=OVX[:, cc, :], start=fst, stop=True,
                            skip_group_check=True),
                            reads=[gpt, gOVX], writes=[pI[b][1]])
                po = attend(QT, gQ, 64, KCMP[0], KCMP[1], 64, VC[0] if own else None, VC[1], N, ctl, hk=0, farcol=0, on_pt=on_pt)
                if own:
                    ocmp[oi] = po
                for si, (rows, c0) in enumerate(subs):
                    b = si // 2
                    cb = (si % 2) * 129
                    P.add("dve", lambda e, b=b, cb=cb, rows=rows: e.tensor_scalar(
                        out=RSI[0][0:rows, 0:1], in0=pI[b][0][0:rows, cb + 128:cb + 129], scalar1=1e-30, scalar2=None, op0=ALU.max),
                        reads=[pI[b][1]], writes=[RSI[1]])
                    P.add("dve", lambda e, rows=rows: e.reciprocal(out=RSI[0][0:rows, 0:1], in_=RSI[0][0:rows, 0:1]),
                          reads=[RSI[1]], writes=[RSI[1]])
                    if hi == 0:
                        P.add("dve", lambda e, b=b, cb=cb, rows=rows, si=si: e.tensor_scalar(
                            out=IMPACC[0][0:rows, si, :], in0=pI[b][0][0:rows, cb:cb + 128], scalar1=RSI[0][0:rows, 0:1], scalar2=None,
                            op0=ALU.mult), reads=[pI[b][1], RSI[1]], writes=[IMPACC[1]])
                    else:
                        P.add("dve", lambda e, b=b, cb=cb, rows=rows, si=si: e.scalar_tensor_tensor(
                            out=IMPACC[0][0:rows, si, :], in0=pI[b][0][0:rows, cb:cb + 128], scalar=RSI[0][0:rows, 0:1],
                            in1=IMPACC[0][0:rows, si, :], op0=ALU.mult, op1=ALU.add), reads=[pI[b][1], RSI[1], IMPACC[1]], writes=[IMPACC[1]])
            rank = 16 if (q0 // 64) < 128 else 15
            for si, (rows, c0) in enumerate(subs):
                cur0 = (q0 + c0) // 64
                p0 = 128 - cur0
                P.add("dve", lambda e, rows=rows, si=si, p0=p0: e.tensor_tensor(
                    out=IMPM[0][0:rows, :], in0=IMPACC[0][0:rows, si, :], in1=PATM[0:rows, p0:p0 + 128], op=ALU.mult),
                    reads=[IMPACC[1], gPATM], writes=[IMPM[1]])
                P.add("dve", lambda e, rows=rows, p0=p0: e.tensor_tensor(
                    out=IMPM[0][0:rows, :], in0=IMPM[0][0:rows, :], in1=PATA[0:rows, p0:p0 + 128], op=ALU.add),
                    reads=[IMPM[1], gPATA], writes=[IMPM[1]])
                P.add("dve", lambda e, rows=rows: e.memset(IMPM[0][0:rows, 0:1], 1e9), writes=[IMPM[1]])
                P.add("dve", lambda e, rows=rows: e.max(out=M8[0][0:rows, 0:8], in_=IMPM[0][0:rows, :]), reads=[IMPM[1]], writes=[M8[1]])
                P.add("dve", lambda e, rows=rows: e.match_replace(out=IMP2[0][0:rows, :], in_to_replace=M8[0][0:rows, 0:8],
                                                                  in_values=IMPM[0][0:rows, :], imm_value=-3e38),
                      reads=[IMPM[1], M8[1]], writes=[IMP2[1]])
                P.add("dve", lambda e, rows=rows: e.max(out=M8[0][0:rows, 8:16], in_=IMP2[0][0:rows, :]), reads=[IMP2[1]], writes=[M8[1]])
                P.add("dve", lambda e, rows=rows: e.tensor_scalar(out=THR[0][0:rows, 1:2], in0=M8[0][0:rows, rank - 1:rank], scalar1=-1e29,
                                                                  scalar2=None, op0=ALU.max), reads=[M8[1]], writes=[THR[1]])
                for v in range(2):
                    P.add("dve", lambda e, rows=rows, v=v: e.tensor_scalar(
                        out=MBP[0][0:rows, 64:128], in0=IMPM[0][0:rows, 64 * v:64 * v + 64],
                        scalar1=THR[0][0:rows, 1:2], scalar2=-BIG, op0=ALU.is_lt, op1=ALU.mult),
                        reads=[IMPM[1], THR[1]], writes=[MBP[1]])
                    pt_, gpt_ = nxt("t", PS_T)
                    P.add("pe", lambda e, pt_=pt_, rows=rows: e.transpose(out=pt_[:, 0:rows], in_=MBP[0][0:rows, 0:128], identity=IDB[0:rows, 0:rows]),
                          reads=[MBP[1], gIDB], writes=[gpt_])
                    for i in range(2):
                        P.add("act" if i == 0 else "dve", (lambda e, pt_=pt_, rows=rows, c0=c0, i=i, v=v: e.activation(
                            out=QA_S[i][v][0][64:128, c0:c0 + rows], in_=pt_[64:128, 0:rows], func=AF.Copy)) if i == 0 else
                            (lambda e, pt_=pt_, rows=rows, c0=c0, i=i, v=v: e.tensor_copy(
                                out=QA_S[i][v][0][64:128, c0:c0 + rows], in_=pt_[64:128, 0:rows])),
                            reads=[gpt_], writes=[QA_S[i][v][1]])
            for i in range(2):
                finish(ocmp[i], N, 2 + i, 3 * i + 0, True, False, None, q0, ACC=ACCS[i])
            for i in range(2):
                ktl = key_tiles(N, q0, nkt, lambda kt: kt)
                lo = [(a, b, c_, d, e_) for (a, b, c_, d, e_, kt) in ktl if kt < 32 or kt * 128 >= max(T, PAST)]
                hi_ = [(a, b, c_, d, e_) for (a, b, c_, d, e_, kt) in ktl if not (kt < 32 or kt * 128 >= max(T, PAST))]
                po = attend(QA_S[i][0][0], QA_S[i][0][1], 128, KA_S[0], KA_S[1], 128, VA_S[0], VA_S[1], N, lo, hk=2 + i,
                            farcol=hp * 4 + 2 + i, first=True, last=(len(hi_) == 0))
                if hi_:
                    attend(QA_S[i][1][0], QA_S[i][1][1], 128, KA_S[0], KA_S[1], 128, VA_S[0], VA_S[1], N, hi_, hk=2 + i,
                           farcol=hp * 4 + 2 + i, first=False, last=True, po=po)
                finish(po, N, 2 + i, 3 * i + 1, False, False, None, q0, ACC=ACCS[i])
                wtl = [((kt % 9) * 128, kt % 9, c_, d, e_) for (a, b, c_, d, e_, kt) in key_tiles(N, q0, nkt, lambda kt: kt, win=True)]
                po = attend(QA_S[i][0][0], QA_S[i][0][1], 64, KWR[0], KWR[1], 64, VWR[0], VWR[1], N, wtl, hk=4 + i, farcol=0)
                finish(po, N, 2 + i, 3 * i + 2, False, True, lambda mb, i=i: mix_dst(8 + 2 * hp + i, mb), q0, ACC=ACCS[i])

        def norm_rows(src_ap, rows, HTcol0, rowsel):
            P.add("sp", lambda e: e.dma_start(out=XT_[0][0:rows, :], in_=src_ap), writes=[XT_[1]], dma=True)
            P.add("act", lambda e: e.activation(out=XN[0][0:rows, :], in_=XT_[0][0:rows, :], func=AF.Square, accum_out=SSQ[0][0:rows, 0:1]),
                  reads=[XT_[1]], writes=[XN[1], SSQ[1]])
            P.add("act", lambda e: e.activation(out=SSQ[0][0:rows, 1:2], in_=SSQ[0][0:rows, 0:1], func=AF.Sqrt, scale=1.0 / D, bias=EPS[0:rows, :]),
                  reads=[SSQ[1], gEPS], writes=[SSQ[1]])
            P.add("dve", lambda e: e.reciprocal(out=SSQ[0][0:rows, 2:3], in_=SSQ[0][0:rows, 1:2]), reads=[SSQ[1]], writes=[SSQ[1]])
            P.add("dve", lambda e: e.tensor_scalar(out=XN[0][0:rows, :], in0=XT_[0][0:rows, :], scalar1=SSQ[0][0:rows, 2:3], scalar2=None, op0=ALU.mult),
                  reads=[XT_[1], SSQ[1]], writes=[XN[1]])

        def transpose_rows(rows, HTcol0, groups):
            for k in range(8):
                pt_, gpt_ = nxt("t", PS_T)
                P.add("pe", lambda e, pt_=pt_, k=k: e.transpose(out=pt_[:, 0:rows], in_=XN[0][0:rows, k * 128:(k + 1) * 128],
                                                                identity=IDB[0:rows, 0:rows]), reads=[XN[1], gIDB], writes=[gpt_])
                for (r0, n_, ar) in groups:
                    P.add("act", lambda e, pt_=pt_, k=k, r0=r0, n_=n_, ar=ar: e.activation(
                        out=HT[0][:, k, HTcol0 + r0:HTcol0 + r0 + n_], in_=pt_[:, r0:r0 + n_], func=AF.Identity,
                        scale=SCT[0][:, k, ar:ar + 1], bias=SHT[0][:, k, ar:ar + 1]), reads=[gpt_, SCT[1], SHT[1]], writes=[HT[1]])

        subsP = [(128, 128 * s) for s in range(4)]

        def mix_dst_prompt(q0, N):
            def f(head, mb):
                P.add("sp", lambda e: e.dma_start(out=mixd_d[head, :, q0:q0 + N], in_=mb[0][:, 0:N]), reads=[mb[1]], writes=[gMIXD], dma=True)
            return f

        gMIXD = G()
        STAGE = cfg.get("STAGE", 99)
        for hp in range(NHP if STAGE >= 4 else (1 if STAGE >= 2 else 0)):
            load_hp(hp)
            for t in range(NT):
                q0 = 512 * t
                for s in range(4):
                    if STAGE >= 2.2:
                        norm_rows(x_d[q0 + 128 * s:q0 + 128 * s + 128, :], 128, 128 * s, None)
                    if STAGE >= 2.3:
                        transpose_rows(128, 128 * s, [(0, 128, 0)])
                if STAGE < 2.4:
                    continue

                def om_dst(si, rows, c0, hp=hp, q0=q0):
                    P.add("sp", lambda e: e.dma_start(out=om_d[hp, q0 + c0:q0 + c0 + rows, :], in_=OTM[0][0:rows, 0:256]), reads=[OTM[1]], dma=True)
                    P.add("sp", lambda e: e.dma_start(out=on_d[hp, q0 + c0:q0 + c0 + rows, :], in_=OTM[0][0:rows, 256:512]), reads=[OTM[1]], dma=True)
                    P.add("sp", lambda e: e.dma_start(out=ow_d[hp, q0 + c0:q0 + c0 + rows, :], in_=OTM[0][0:rows, 512:640]), reads=[OTM[1]], dma=True)
                project_tile(hp, 512, q0, None, subsP, om_dst, None, None, ring_kt=4 * t)
                if STAGE >= 3:
                    attend_tile(hp, 512, q0, subsP, mix_dst_prompt(q0, 512))

        if NS > 0 and STAGE >= 5:
            NTOK = NS * 4
            PTI = sb("pti", [128, NPG], I32)
            IDX = sb("idx", [128, NPG], I32)
            STG = [sb("stg%d" % i, [128, 512], BF16) for i in range(2)]
            norm_rows(xs_d, NTOK, 0, None)
            transpose_rows(NTOK, 0, [(4 * s, 4, 1 + s) for s in range(NS)])
            for s in range(NS):
                P.add("sp", lambda e, s=s: e.dma_start(out=owin_d[s], in_=win_d[s, 4:WB, :]), dma=True)
            HTS = sb("hts", [128, 8, 16], BF16)
            P.add("dve", lambda e: e.tensor_copy(out=HTS[0][:, :, 0:NTOK], in_=HT[0][:, :, 0:NTOK]), reads=[HT[1]], writes=[HTS[1]])
            for s in range(NS):
                P.add("sp", lambda e, s=s: e.dma_start(out=PTI[0][:], in_=ptab_d[s].partition_broadcast(128)), writes=[PTI[1]], dma=True)
                P.add("dve", lambda e: e.tensor_scalar(out=IDX[0][:], in0=PTI[0][:], scalar1=128, scalar2=IOP[:, 0:1], op0=ALU.mult, op1=ALU.add),
                      reads=[PTI[1], gIOP], writes=[IDX[1]])
                for hp in range(NHP):
                    load_hp(hp)
                    kv = hp // 2
                    for pg in range(NPG):
                        sg, gsg = STG[pg % 2]
                        P.add("pool", lambda e, sg=sg, pg=pg, hp=hp: e.indirect_dma_start(
                            out=sg[:, 0:256], out_offset=None,
                            in_=cm_d[hp],
                            in_offset=bass.IndirectOffsetOnAxis(ap=IDX[0][:, pg:pg + 1], axis=0)),
                            reads=[IDX[1]], writes=[gsg], dma=True)
                        P.add("pool", lambda e, sg=sg, pg=pg, kv=kv: e.indirect_dma_start(
                            out=sg[:, 256:512], out_offset=None,
                            in_=cn_d[kv],
                            in_offset=bass.IndirectOffsetOnAxis(ap=IDX[0][:, pg:pg + 1], axis=0)),
                            reads=[IDX[1]], writes=[gsg], dma=True)
                        kc0 = 128 * pg
                        for i in range(2):
                            pt_, gpt_ = nxt("t", PS_T)
                            P.add("pe", lambda e, pt_=pt_, sg=sg, i=i: e.transpose(out=pt_[0:64, 0:128], in_=sg[:, 64 * i:64 * i + 64], identity=IDB[:]),
                                  reads=[gsg, gIDB], writes=[gpt_])
                            P.add("act" if i == 0 else "dve", (lambda e, pt_=pt_, i=i, kc0=kc0: e.activation(
                                out=KA_M[i][0][0:64, kc0:kc0 + 128], in_=pt_[0:64, 0:128], func=AF.Copy)) if i == 0 else
                                (lambda e, pt_=pt_, i=i, kc0=kc0: e.tensor_copy(out=KA_M[i][0][0:64, kc0:kc0 + 128], in_=pt_[0:64, 0:128])),
                                reads=[gpt_], writes=[KA_M[i][1]])
                            P.add("pool", lambda e, sg=sg, i=i, pg=pg: e.tensor_copy(out=VA_M[i][0][:, pg, 0:64], in_=sg[:, 128 + 64 * i:192 + 64 * i]),
                                  reads=[gsg], writes=[VA_M[i][1]])
                        pt_, gpt_ = nxt("t", PS_T)
                        P.add("pe", lambda e, pt_=pt_, sg=sg: e.transpose(out=pt_[:, 0:128], in_=sg[:, 256:384], identity=IDB[:]),
                              reads=[gsg, gIDB], writes=[gpt_])
                        P.add("act", lambda e, pt_=pt_, kc0=kc0: e.activation(out=KCVC[0][:, kc0:kc0 + 128], in_=pt_[:, 0:128], func=AF.Copy),
                              reads=[gpt_], writes=[KCVC[1]])
                        pt_, gpt_ = nxt("t", PS_T)
                        P.add("pe", lambda e, pt_=pt_, sg=sg: e.transpose(out=pt_[0:64, 0:128], in_=sg[:, 384:448], identity=IDB[:]),
                              reads=[gsg, gIDB], writes=[gpt_])
                        P.add("dve", lambda e, pt_=pt_, kc0=kc0: e.tensor_copy(out=KA_S[0][0:64, kc0:kc0 + 128], in_=pt_[0:64, 0:128]),
                              reads=[gpt_], writes=[KA_S[1]])
                        P.add("pool", lambda e, sg=sg, pg=pg: e.tensor_copy(out=VA_S[0][:, pg, 0:64], in_=sg[:, 448:512]),
                              reads=[gsg], writes=[VA_S[1]])
                    for wt in range(WB // 128):
                        sg, gsg = STG[wt % 2]
                        kt = (PAST - WB) // 128 + wt
                        P.add("pool", lambda e, sg=sg, s=s, wt=wt, kv=kv: e.dma_start(
                            out=sg[:, 0:128].rearrange("p (f c) -> p f c", f=2),
                            in_=win_d[s, 128 * wt:128 * wt + 128, :].rearrange("r (f h d) -> r f h d", f=2, h=2)[:, :, kv, :]),
                            writes=[gsg], dma=True)
                        pt_, gpt_ = nxt("t", PS_T)
                        P.add("pe", lambda e, pt_=pt_, sg=sg: e.transpose(out=pt_[0:64, 0:128], in_=sg[:, 0:64], identity=IDB[:]),
                              reads=[gsg, gIDB], writes=[gpt_])
                        r0 = (kt % 9) * 128
                        P.add("dve", lambda e, pt_=pt_, r0=r0: e.tensor_copy(out=KWR[0][0:64, r0:r0 + 128], in_=pt_[0:64, 0:128]),
                              reads=[gpt_], writes=[KWR[1]])
                        P.add("pool", lambda e, sg=sg, kt=kt: e.tensor_copy(out=VWR[0][:, kt % 9, 0:64], in_=sg[:, 64:128]),
                              reads=[gsg], writes=[VWR[1]])
                    ktn = PAST // 128
                    for (t_, g_) in (KA_M[0], KA_M[1], KA_S, KCVC):
                        P.add("pool", lambda e, t_=t_: e.memset(t_[0:64, PAST:PAST + 128], 0.0), writes=[g_])
                    P.add("pool", lambda e: e.memset(KCVC[0][64:128, PAST:PAST + 128], 0.0), writes=[KCVC[1]])
                    P.add("pool", lambda e: e.memset(KWR[0][0:64, (ktn % 9) * 128:(ktn % 9) * 128 + 128], 0.0), writes=[KWR[1]])
                    for (t_, g_) in (VA_M[0], VA_M[1], VA_S):
                        P.add("pool", lambda e, t_=t_: e.memset(t_[:, ktn, 0:64], 0.0), writes=[g_])
                    P.add("pool", lambda e: e.memset(VWR[0][:, ktn % 9, 0:64], 0.0), writes=[VWR[1]])
                    P.add("dve", lambda e, s=s: e.tensor_copy(out=HT[0][:, :, 0:4], in_=HTS[0][:, :, 4 * s:4 * s + 4]), reads=[HTS[1]], writes=[HT[1]])

                    def om_dst(si, rows, c0, hp=hp, s=s):
                        P.add("sp", lambda e: e.dma_start(out=oms_d[hp, 4 * s:4 * s + 4, :], in_=OTM[0][0:4, 0:256]), reads=[OTM[1]], dma=True)
                        P.add("sp", lambda e: e.dma_start(out=ons_d[hp, 4 * s:4 * s + 4, :], in_=OTM[0][0:4, 256:512]), reads=[OTM[1]], dma=True)
                        P.add("sp", lambda e: e.dma_start(out=ows_d[hp, 4 * s:4 * s + 4, :], in_=OTM[0][0:4, 512:640]), reads=[OTM[1]], dma=True)
                    project_tile(hp, 4, PAST, None, [(4, 0)], om_dst, None, None, ring_kt=PAST // 128)

                    def mix_dst(head, mb, s=s):
                        P.add("pool", lambda e: e.tensor_copy(out=MIXS[0][:, head, 4 * s:4 * s + 4], in_=mb[0][:, 0:4]), reads=[mb[1]], writes=[MIXS[1]])
                    attend_tile(hp, 4, PAST, [(4, 0)], mix_dst)

        P.flush()
        stA.close()
        cur[0] = st
        P.barrier()
        GATEP = sb("gatep", [128, D], F32)
        GATES = sb("gates", [16, D], F32)
        FG = sb("fg", [128, D], F32)
        gate_tiles.update({"p": GATEP, "s": GATES, "f": FG})
        if STAGE >= 6:
            adaln("gate")
        XT_ = sb("xt2", [128, D], F32)
        SSQ = sb("ssq2", [128, 4], F32)
        WOUT = sb("wout", [64, 16, D], BF16)
        MIXL = sb("mixl", [64, 16, 512], BF16)
        YP = sb("yp", [128, D], F32)
        if STAGE >= 6:
            P.add("pool", lambda e: e.dma_start(out=WOUT[0][:], in_=wout_d), writes=[WOUT[1]], dma=True)

        def outproj(rows, lhs_fn, x_ap, gate_t, y_ap, extra_reads):
            P.add("sp", lambda e: e.dma_start(out=XT_[0][0:rows, :], in_=x_ap), writes=[XT_[1]], dma=True)
            for oc in range(2):
                px, gpx = nxt("x", PS_X)
                for h in range(16):
                    P.add("pe", lambda e, px=px, h=h, oc=oc: e.matmul(
                        px[0:rows, 0:512], lhsT=lhs_fn(h), rhs=WOUT[0][:, h, oc * 512:(oc + 1) * 512], start=(h == 0), stop=(h == 15)),
                        reads=[WOUT[1]] + extra_reads, writes=[gpx])
                P.add("dve", lambda e, px=px, oc=oc: e.tensor_tensor(
                    out=YP[0][0:rows, oc * 512:(oc + 1) * 512], in0=px[0:rows, 0:512], in1=gate_t[0][0:rows, oc * 512:(oc + 1) * 512], op=ALU.mult),
                    reads=[gpx, gate_t[1]], writes=[YP[1]])
            P.add("pool", lambda e: e.tensor_tensor(out=YP[0][0:rows, :], in0=YP[0][0:rows, :], in1=XT_[0][0:rows, :], op=ALU.add),
                  reads=[YP[1], XT_[1]], writes=[YP[1]])
            P.add("act", lambda e: e.activation(out=XT_[0][0:rows, :], in_=YP[0][0:rows, :], func=AF.Square, accum_out=SSQ[0][0:rows, 0:1]),
                  reads=[YP[1]], writes=[XT_[1], SSQ[1]])
            P.add("act", lambda e: e.activation(out=SSQ[0][0:rows, 1:2], in_=SSQ[0][0:rows, 0:1], func=AF.Sqrt, scale=1.0 / D, bias=EPS[0:rows, :]),
                  reads=[SSQ[1], gEPS], writes=[SSQ[1]])
            P.add("dve", lambda e: e.reciprocal(out=SSQ[0][0:rows, 2:3], in_=SSQ[0][0:rows, 1:2]), reads=[SSQ[1]], writes=[SSQ[1]])
            P.add("dve", lambda e: e.scalar_tensor_tensor(out=YP[0][0:rows, :], in0=YP[0][0:rows, :], scalar=SSQ[0][0:rows, 2:3],
                                                          in1=FG[0][0:rows, :], op0=ALU.mult, op1=ALU.mult),
                  reads=[YP[1], SSQ[1], FG[1]], writes=[YP[1]])
            P.add("sp", lambda e: e.dma_start(out=y_ap, in_=YP[0][0:rows, :]), reads=[YP[1]], dma=True)

        if NHP == 4 and STAGE >= 6:
            for t in range(NT):
                q0 = 512 * t
                P.add("sp", lambda e, q0=q0: e.dma_start(out=MIXL[0][:], in_=mixd_d[:, :, q0:q0 + 512].rearrange("h p n -> p h n")),
                      reads=[gMIXD], writes=[MIXL[1]], dma=True)
                for s in range(4):
                    outproj(128, lambda h, s=s: MIXL[0][:, h, 128 * s:128 * s + 128], x_d[q0 + 128 * s:q0 + 128 * s + 128, :],
                            GATEP, y_d[q0 + 128 * s:q0 + 128 * s + 128, :], [MIXL[1]])

        if NS > 0 and NHP == 4 and STAGE >= 6:
            outproj(NS * 4, lambda h: MIXS[0][:, h, 0:NS * 4], xs_d, GATES, ys_d, [MIXS[1]])

        P.emit_all(st)
    return nc


def make_cfg(T, PAST, NS, NHP, NPHYS):
    return {"T": T, "PAST": PAST, "NS": NS, "NHP": NHP, "TK": ((max(T, PAST) + 2047) // 2048) * 2048 + 128, "NPHYS": NPHYS, "WB": min(512, PAST)}


def make_in_maps(inp, cfg, ncores):
    NS, NHP = cfg["NS"], cfg["NHP"]
    f32 = lambda a: np.ascontiguousarray(np.asarray(a, dtype=np.float32))
    w_in = f32(inp["w_in"])[0]
    rb = f32(inp["rel_bias"])
    shared = {}
    shared["w_ada"] = f32(inp["w_ada"])[0]
    shared["b_ada"] = f32(inp["b_ada"])[0]
    shared["gain"] = f32(inp["norm_gain"])[0]
    shared["fgain"] = f32(inp["final_gain"])
    shared["wfm"] = np.stack([w_in[:, fm_cols(hp)] for hp in range(NHP)])
    shared["wtm"] = np.stack([w_in[:, tm_cols(hp)] for hp in range(NHP)])
    w1k = f32(inp["cmp_k_w1"])[0].reshape(32, 64, 128).transpose(1, 0, 2)
    w1v = f32(inp["cmp_v_w1"])[0].reshape(32, 64, 128).transpose(1, 0, 2)
    shared["w1"] = np.ascontiguousarray(np.concatenate([w1k, w1v], 0))
    shared["w2"] = np.ascontiguousarray(np.concatenate([f32(inp["cmp_k_w2"])[0], f32(inp["cmp_v_w2"])[0]], 1))
    pe = f32(inp["cmp_pe"])[0]
    shared["pet"] = np.ascontiguousarray(np.concatenate([pe[0].T, pe[1].T], 0))
    shared["wout"] = np.ascontiguousarray(f32(inp["w_out"])[0].reshape(16, 64, D).transpose(1, 0, 2))
    ts, t31 = [], []
    for hp in range(NHP):
        hs = [2 * hp, 2 * hp + 1, 8 + 2 * hp, 9 + 2 * hp, 8 + 2 * hp, 9 + 2 * hp]
        ts.append(rb[:, hs])
        t31.append(rb[31, hs[:4]])
    shared["tabsel"] = np.ascontiguousarray(np.concatenate(ts, 1))
    shared["tab31"] = np.ascontiguousarray(np.concatenate(t31, 0))
    cmk = f32(inp["cache_moba_kv"])[0]
    nph = cmk.shape[0]
    for hp in range(4):
        shared["cache_m%d" % hp] = np.ascontiguousarray(cmk[:, :, :, 2 * hp:2 * hp + 2, :].reshape(nph * 128, 256))
    cnk = f32(inp["cache_nsa_kv"])[0]
    for kv in range(2):
        shared["cache_n%d" % kv] = np.ascontiguousarray(cnk[:, :, :, kv, :].reshape(nph * 128, 256))
    for k_, v_ in host_consts(cfg).items():
        shared["c_" + k_] = v_
    xp = f32(inp["x_prompt"])
    xs = f32(inp["x_sample"])
    cp = f32(inp["c_prompt"])
    cs = f32(inp["c_sample"])
    win = f32(inp["state_nsa_win"])[0]
    pt = np.asarray(inp["page_table"]).astype(np.int32)
    maps = []
    for c in range(ncores):
        b = c % xp.shape[0]
        m = dict(shared)
        m["x"] = xp[b]
        m["xs"] = np.ascontiguousarray(xs[NS * c:NS * c + NS].reshape(NS * 4, D))
        cT = np.zeros((D, 144), np.float32)
        cT[:, 0:128] = cp[b][:, None]
        for s in range(NS):
            cT[:, 128 + 4 * s:132 + 4 * s] = cs[NS * c + s][:, None]
        m["cT"] = cT
        m["win"] = np.ascontiguousarray(win[NS * c:NS * c + NS].reshape(NS, win.shape[1], 256))
        m["ptab"] = np.ascontiguousarray(pt[NS * c:NS * c + NS])
        maps.append(m)
    return maps


def assemble(res, cfg, B, ncores):
    T, NS, NHP, WB = cfg["T"], cfg["NS"], cfg["NHP"], cfg["WB"]
    DB = NS * ncores
    y_p = np.zeros((B, T, D), np.float32)
    y_s = np.zeros((DB, 4, D), np.float32)
    mkp = np.zeros((1, B, T, 2, 8, 64), np.float32)
    mks = np.zeros((1, DB, 4, 2, 8, 64), np.float32)
    nkp = np.zeros((1, B, T, 4, 2, 64), np.float32)
    nks = np.zeros((1, DB, 4, 4, 2, 64), np.float32)
    wp = np.zeros((1, B, min(512, T), 2, 2, 64), np.float32)
    ws = np.zeros((1, DB, WB, 2, 2, 64), np.float32)
    for c in range(ncores):
        r = res[c]
        if c < B:
            b = c
            y_p[b] = r["y"]
            for hp in range(NHP):
                om = np.asarray(r["om"][hp])
                mkp[0, b, :, 1, 2 * hp:2 * hp + 2, :] = om[:, 0:128].reshape(T, 2, 64)
                mkp[0, b, :, 0, 2 * hp:2 * hp + 2, :] = om[:, 128:256].reshape(T, 2, 64)
            for kv in range(2):
                on = np.asarray(r["on"][2 * kv])
                ow = np.asarray(r["ow"][2 * kv])
                for f in range(4):
                    nkp[0, b, :, f, kv, :] = on[:, 64 * f:64 * f + 64]
                for f in range(2):
                    wp[0, b, :, f, kv, :] = ow[T - wp.shape[2]:, 64 * f:64 * f + 64]
        sl = slice(NS * c, NS * c + NS)
        y_s[sl] = np.asarray(r["ys"]).reshape(NS, 4, D)
        for hp in range(NHP):
            om = np.asarray(r["oms"][hp]).reshape(NS, 4, 256)
            mks[0, sl, :, 1, 2 * hp:2 * hp + 2, :] = om[:, :, 0:128].reshape(NS, 4, 2, 64)
            mks[0, sl, :, 0, 2 * hp:2 * hp + 2, :] = om[:, :, 128:256].reshape(NS, 4, 2, 64)
        ws[0, sl, 0:WB - 4] = np.asarray(r["owin"]).reshape(NS, WB - 4, 2, 2, 64)
        for kv in range(2):
            on = np.asarray(r["ons"][2 * kv]).reshape(NS, 4, 256)
            ow = np.asarray(r["ows"][2 * kv]).reshape(NS, 4, 128)
            for f in range(4):
                nks[0, sl, :, f, kv, :] = on[:, :, 64 * f:64 * f + 64]
            for f in range(2):
                ws[0, sl, WB - 4:, f, kv, :] = ow[:, :, 64 * f:64 * f + 64]
    return (y_p, y_s, mkp, mks, nkp, nks, wp, ws)


def kernel(**inputs):
    cfg = make_cfg(8192, 8192, 4, 4, 2560)
    nc = build(dict(cfg))
    maps = make_in_maps(inputs, cfg, 8)
    res = run_bass_kernel_spmd(nc, maps, core_ids=list(range(8)))
    return assemble(res.results, cfg, 2, 8)
```

```python
import contextlib
import math
import numpy as np
import ml_dtypes
import concourse.bass as bass
import concourse.mybir as mybir
from concourse.bass_utils import run_bass_kernel_spmd

F32 = mybir.dt.float32
BF16 = mybir.dt.bfloat16
I32 = mybir.dt.int32
AF = mybir.ActivationFunctionType
ALU = mybir.AluOpType
AX = mybir.AxisListType
NPBF = ml_dtypes.bfloat16

BIG = 30000.0
NEGS = -1e30
GL = 1664
HW = 1408
D = 1024


class G:
    __slots__ = ("w", "r", "excl")

    def __init__(self, excl=False):
        self.w = None
        self.r = []
        self.excl = excl


class Op:
    __slots__ = ("eng", "emit", "deps", "inc", "val", "dma", "slot", "dval", "prev")

    def __init__(self, eng, emit, dma):
        self.eng = eng
        self.emit = emit
        self.dma = dma
        self.deps = []
        self.inc = False
        self.val = 0
        self.slot = None
        self.dval = 0
        self.prev = 0


SERIAL = [False]
NSLOT = {"sp": 12, "act": 2, "pool": 12}
ENGS = ("pe", "act", "dve", "pool", "sp")


class Prog:
    def __init__(self, nc):
        self.nc = nc
        self.ops = {e: [] for e in ENGS}
        self.rr = {e: 0 for e in NSLOT}
        self.sv = {e: [0] * n for e, n in NSLOT.items()}
        self.pend = {e: [] for e in ENGS}
        self.lastdma = {}

    def barrier(self):
        deps = []
        for e in ("pe", "act", "dve", "pool"):
            for op in reversed(self.ops[e]):
                if not op.dma:
                    deps.append(op)
                    break
        deps += list(self.lastdma.values())
        for e in ENGS:
            self.pend[e] = list(deps)

    def add(self, eng, emit, reads=(), writes=(), dma=False):
        op = Op(eng, emit, dma)
        deps = []
        seen = set()

        def push(d):
            if d is None or id(d) in seen:
                return
            seen.add(id(d))
            if d.eng == "pe" and eng == "pe" and not d.dma and not dma:
                return
            deps.append(d)

        for g in reads:
            push(g.w)
            if g.excl:
                for r in g.r:
                    push(r)
        for g in writes:
            push(g.w)
            for r in g.r:
                push(r)
        if eng in ("act", "dve", "pool") and not dma and SERIAL[0]:
            for prev in reversed(self.ops[eng]):
                if not prev.dma:
                    if id(prev) not in seen:
                        seen.add(id(prev))
                        deps.append(prev)
                    break
        if self.pend[eng]:
            for d in self.pend[eng]:
                if d is not None and id(d) not in seen:
                    seen.add(id(d))
                    deps.append(d)
            self.pend[eng] = []
        op.deps = deps
        for d in deps:
            d.inc = True
        for g in reads:
            g.r.append(op)
        for g in writes:
            g.w = op
            g.r = []
        if dma:
            i = self.rr[eng]
            self.rr[eng] = (i + 1) % NSLOT[eng]
            op.slot = i
            op.prev = self.sv[eng][i]
            self.sv[eng][i] += 16
            op.dval = self.sv[eng][i]
            self.lastdma[(eng, i)] = op
        self.ops[eng].append(op)
        return op

    def setup(self, stack):
        nc = self.nc
        self.stack = stack
        self.csem = {e: [] for e in ("pe", "act", "dve", "pool")}
        self.dsem = {e: [stack.enter_context(nc.semaphore("d_%s%d" % (e, i))) for i in range(n)]
                     for e, n in NSLOT.items()}
        self.cursor = {e: 0 for e in ENGS}
        self.cval = {e: 0 for e in ENGS}
        self.waited = {e: {} for e in ENGS}

    def flush(self, final=False):
        nc = self.nc
        csem, dsem = self.csem, self.dsem
        ops = self.ops
        sv = self.sv
        EP = 12000
        for e in ("pe", "act", "dve", "pool"):
            for op in ops[e][self.cursor[e]:]:
                if not op.dma:
                    self.cval[e] += 1
                    op.val = self.cval[e]

        def csem_of(eng, val):
            ep = (val - 1) // EP
            lst = csem[eng]
            while len(lst) <= ep:
                lst.append(self.stack.enter_context(nc.semaphore("c_%s%d" % (eng, len(lst)))))
            return lst[ep], val - ep * EP, ("c", eng, ep)

        def sig(d):
            if d.dma:
                return dsem[d.eng][d.slot], d.dval, ("d", d.eng, d.slot)
            return csem_of(d.eng, d.val)

        def run(engname, e, fin=False):
            waited = self.waited[engname]

            def w(sem, val, key):
                if waited.get(key, 0) < val:
                    e.wait_ge(sem, val)
                    waited[key] = val

            for op in ops[engname][self.cursor[engname]:]:
                for d in op.deps:
                    s, v, k = sig(d)
                    w(s, v, k)
                if op.dma and op.prev > 0:
                    w(dsem[engname][op.slot], op.prev, ("d", engname, op.slot))
                ins = op.emit(e)
                if op.dma:
                    ins.then_inc(dsem[engname][op.slot], 16)
                else:
                    ins.then_inc(csem_of(engname, op.val)[0], 1)
            self.cursor[engname] = len(ops[engname])
            if fin:
                for qe, n in NSLOT.items():
                    for i in range(n):
                        if sv[qe][i] > 0:
                            w(dsem[qe][i], sv[qe][i], ("d", qe, i))

        with nc.Block() as block:
            @block.sync
            def _(e):
                run("sp", e, fin=final)

            @block.scalar
            def _(e):
                run("act", e)

            @block.vector
            def _(e):
                run("dve", e)

            @block.gpsimd
            def _(e):
                run("pool", e)

            @block.tensor
            def _(e):
                run("pe", e)

    def emit_all(self, stack):
        self.flush(final=True)


def t5_bucket_np(rel):
    n = np.maximum(rel, 0)
    nf = np.maximum(n, 1).astype(np.float32)
    large = 16 + (np.log(nf / np.float32(16)) / np.float32(math.log(8.0)) * np.float32(16)).astype(np.int32)
    return np.where(n < 16, n, np.minimum(large, 31))


def host_consts(cfg):
    T, PAST = cfg["T"], cfg["PAST"]
    TK = cfg["TK"]
    c = {}
    c["idb"] = np.eye(128, dtype=np.float32).astype(NPBF)
    c["jb"] = np.eye(128, dtype=np.float32)[::-1].copy().astype(NPBF)
    c["ones"] = np.ones((128, 64), np.float32)
    sel6 = np.zeros((6, 6 * 64), np.float32)
    for r in range(6):
        sel6[r, r * 64:(r + 1) * 64] = 1.0
    c["sel6"] = sel6
    k = np.arange(TK)
    kmax = max(T, PAST)
    indm = np.zeros((64, TK), np.float32)
    inds = np.zeros((64, TK), np.float32)
    for r in range(32):
        indm[r] = ((k // 256) == r) & (k < kmax)
    for r in range(64):
        inds[r] = (((k // 64) % 64) == r) & (k < kmax)
    c["indm"] = indm.astype(NPBF)
    c["inds"] = inds.astype(NPBF)
    ni = np.arange(128)[:, None]
    qi = np.arange(512)[None, :]
    cm = np.zeros((128, 5, 512), np.float32)
    for j, dl in enumerate([0, -1, -2, -3, -4]):
        cm[:, j, :] = np.where(qi - 16 * ni >= 512 * dl + 31, 0.0, -BIG)
    c["cmask"] = cm.astype(NPBF)
    ovx = np.zeros((128, 4, 129), np.float32)
    for cc in range(4):
        for n_ in range(128):
            n = 128 * cc + n_
            for s in (n // 4, (n + 1) // 4 if n % 4 == 3 else -1):
                if 0 <= s < 128:
                    ovx[n_, cc, s] = 1.0
        ovx[:, cc, 128] = 1.0
    c["ovx"] = ovx.astype(NPBF)
    pm = np.ones((128, 256), np.float32)
    pa = np.zeros((128, 256), np.float32)
    for p in range(128):
        cur = 0 if p < 64 else 1
        for x in range(256):
            j = x - 128
            if j > cur:
                pm[p, x] = 0.0
                pa[p, x] = NEGS
            elif j == cur or j == cur - 1:
                pm[p, x] = 0.0
                pa[p, x] = 1e9
    c["patm"] = pm
    c["pata"] = pa
    m = np.arange(GL)
    rel = m - 511
    oh = np.zeros((32, GL), np.float32)
    b = t5_bucket_np(rel)
    for i in range(GL):
        if rel[i] >= 0:
            oh[b[i], i] = 1.0
    c["oh"] = oh
    addm = np.zeros((6, GL), np.float32)
    addm[:, rel < 0] = -BIG
    addm[4:6, rel >= 512] = -BIG
    c["addm"] = addm
    c["iop"] = np.arange(128, dtype=np.int32).reshape(128, 1)
    return c


FMCH = {}
_o = 0
for _n, _w in [("qmA", 64), ("qmB", 64), ("kmA", 64), ("kmB", 64), ("zmA", 64), ("zmB", 64),
               ("qnA", 64), ("qnB", 64), ("znA", 64), ("znB", 64), ("ks", 64), ("kw", 64),
               ("kcvc", 128), ("g", 8), ("qnC", 64), ("qnD", 64)]:
    FMCH[_n] = (_o, _w)
    _o += _w
FMC = _o
TMC = 640


def proj_offsets():
    sizes = [512] * 4 + [512] + [128] * 6 + [24, 512]
    offs = np.concatenate([[0], np.cumsum(sizes)])
    names = ["q_m", "k_m", "v_m", "z_m", "q_n", "kc", "vc", "ks", "vs", "kw", "vw", "g_n", "z_n"]
    return {n: int(o) for n, o in zip(names, offs)}


def fm_cols(hp):
    po = proj_offsets()
    A, B = 2 * hp, 2 * hp + 1
    kv = hp // 2
    sib = hp ^ 1
    C, Dh = 2 * sib, 2 * sib + 1
    r64 = lambda base, h: list(range(base + 64 * h, base + 64 * h + 64))
    cols = []
    cols += r64(po["q_m"], A) + r64(po["q_m"], B) + r64(po["k_m"], A) + r64(po["k_m"], B)
    cols += r64(po["z_m"], A) + r64(po["z_m"], B)
    cols += r64(po["q_n"], A) + r64(po["q_n"], B) + r64(po["z_n"], A) + r64(po["z_n"], B)
    cols += r64(po["ks"], kv) + r64(po["kw"], kv)
    cols += r64(po["kc"], kv) + r64(po["vc"], kv)
    cols += list(range(po["g_n"] + 6 * hp, po["g_n"] + 6 * hp + 6)) + [po["g_n"], po["g_n"]]
    cols += r64(po["q_n"], C) + r64(po["q_n"], Dh)
    assert len(cols) == FMC
    return cols


def tm_cols(hp):
    po = proj_offsets()
    A, B = 2 * hp, 2 * hp + 1
    kv = hp // 2
    r64 = lambda base, h: list(range(base + 64 * h, base + 64 * h + 64))
    cols = r64(po["v_m"], A) + r64(po["v_m"], B) + r64(po["k_m"], A) + r64(po["k_m"], B)
    cols += r64(po["kc"], kv) + r64(po["vc"], kv) + r64(po["ks"], kv) + r64(po["vs"], kv)
    cols += r64(po["kw"], kv) + r64(po["vw"], kv)
    assert len(cols) == TMC
    return cols


def build(cfg):
    T, PAST, NS, NHP = cfg["T"], cfg["PAST"], cfg["NS"], cfg["NHP"]
    TK = cfg["TK"]
    NTK = TK // 128
    NPHYS = cfg["NPHYS"]
    NPG = PAST // 128
    NT = T // 512
    WB = cfg["WB"]
    nc = bass.Bass("TRN2", target_bir_lowering=False)
    P = Prog(nc)

    def din(name, shape, dt=F32):
        return nc.dram_tensor(name, list(shape), dt, kind="ExternalInput")

    def dout(name, shape, dt=F32):
        return nc.dram_tensor(name, list(shape), dt, kind="ExternalOutput")

    x_d = din("x", [T, D]).ap()
    xs_d = din("xs", [NS * 4, D]).ap()
    cT_d = din("cT", [D, 144]).ap()
    wada_d = din("w_ada", [D, 3 * D]).ap()
    bada_d = din("b_ada", [3 * D]).ap()
    gain_d = din("gain", [D]).ap()
    fgain_d = din("fgain", [D]).ap()
    wfm_d = din("wfm", [NHP, D, FMC]).ap()
    wtm_d = din("wtm", [NHP, D, TMC]).ap()
    w1_d = din("w1", [128, 32, 128]).ap()
    w2_d = din("w2", [128, 128]).ap()
    pet_d = din("pet", [128, 32]).ap()
    wout_d = din("wout", [64, 16, D]).ap()
    tabsel_d = din("tabsel", [32, NHP * 6]).ap()
    tab31_d = din("tab31", [NHP * 4]).ap()
    cm_d = [din("cache_m%d" % i, [NPHYS * 128, 256]).ap() for i in range(4)]
    cn_d = [din("cache_n%d" % i, [NPHYS * 128, 256]).ap() for i in range(2)]
    win_d = din("win", [NS, WB, 256]).ap()
    ptab_d = din("ptab", [NS, NPG], I32).ap()
    hc = host_consts(cfg)
    cdram = {}
    for k_, v_ in hc.items():
        dt_ = BF16 if v_.dtype == NPBF else (I32 if v_.dtype == np.int32 else F32)
        cdram[k_] = din("c_" + k_, v_.shape, dt_).ap()

    y_d = dout("y", [T, D]).ap()
    ys_d = dout("ys", [NS * 4, D]).ap()
    om_d = dout("om", [NHP, T, 256]).ap()
    on_d = dout("on", [NHP, T, 256]).ap()
    ow_d = dout("ow", [NHP, T, 128]).ap()
    oms_d = dout("oms", [NHP, NS * 4, 256]).ap()
    ons_d = dout("ons", [NHP, NS * 4, 256]).ap()
    ows_d = dout("ows", [NHP, NS * 4, 128]).ap()
    owin_d = dout("owin", [NS, WB - 4, 256]).ap()
    gd_h = nc.dram_tensor("gd", [NHP * 6, GL], BF16, kind="Internal")
    gd_d = gd_h.ap()
    mixd_d = nc.dram_tensor("mixd", [16, 64, T], BF16, kind="Internal").ap()

    with contextlib.ExitStack() as st:
        cur = [st]
        P.setup(st)

        def sb(name, shape, dt):
            return cur[0].enter_context(nc.sbuf_tensor("s_" + name, list(shape), dt)), G()

        def psum(name, shape, dt):
            return st.enter_context(nc.psum_tensor(name, list(shape), dt)), G(excl=True)

        PS_S = [psum("ps_s%d" % i, [128, 512], F32) for i in range(2)]
        PS_O = [psum("ps_o%d" % i, [128, 512], F32) for i in range(2)]
        PS_X = [psum("ps_x%d" % i, [128, 512], F32) for i in range(2)]
        PS_T = [psum("ps_t%d" % i, [128, 1024], BF16) for i in range(2)]
        cnt = {"s": 0, "o": 0, "x": 0, "t": 0, "pt": 0}

        def nxt(kind, lst):
            i = cnt[kind]
            cnt[kind] = (i + 1) % len(lst)
            return lst[i]

        def cload(name, shape, dt, eng="sp"):
            t, g = sb("k_" + name, shape, dt)
            P.add(eng, lambda e: e.dma_start(out=t[:], in_=cdram[name]), writes=[g], dma=True)
            return t, g

        IDB, gIDB = cload("idb", [128, 128], BF16)
        JB, gJB = cload("jb", [128, 128], BF16)
        ONES, gONES = cload("ones", [128, 64], F32)
        SEL6, gSEL6 = cload("sel6", [6, 384], F32)
        CMASK, gCMASK = cload("cmask", [128, 5, 512], BF16)
        OVX, gOVX = cload("ovx", [128, 4, 129], BF16)
        PATM, gPATM = cload("patm", [128, 256], F32)
        PATA, gPATA = cload("pata", [128, 256], F32)
        IOP, gIOP = cload("iop", [128, 1], I32)
        EPS, gEPS = sb("eps", [128, 1], F32)
        P.add("dve", lambda e: e.memset(EPS[:], 1e-6), writes=[gEPS])

        MIXS = sb("mixs", [64, 16, 16], BF16)
        stA = contextlib.ExitStack()
        cur[0] = stA
        KA_M = [sb("ka_m%d" % i, [128, TK], BF16) for i in range(2)]
        VA_M = [sb("va_m%d" % i, [128, NTK, 65], BF16) for i in range(2)]
        KA_S = sb("ka_s", [128, TK], BF16)
        VA_S = sb("va_s", [128, NTK, 65], BF16)
        KWR = sb("kwr", [128, 9 * 128], BF16)
        VWR = sb("vwr", [128, 9, 65], BF16)
        KCVC = sb("kcvc", [128, TK], BF16)
        NCT = max(1, (max(T, PAST) + 2047) // 2048)
        KCMP = sb("kcmp", [64, NCT * 128], BF16)
        VC = sb("vc", [128, NCT, 65], BF16)

        def init_resident():
            for (t, g) in KA_M:
                P.add("pool", lambda e, t=t: e.memset(t[:], 0.0), writes=[g])
                P.add("sp", lambda e, t=t: e.dma_start(out=t[64:128, :], in_=cdram["indm"]), writes=[g], dma=True)
            for (t, g) in VA_M + [VA_S, VWR, VC]:
                P.add("pool", lambda e, t=t: e.memset(t[:], 0.0), writes=[g])
                P.add("pool", lambda e, t=t: e.memset(t[:, :, 64:65], 1.0), writes=[g])
            t, g = KA_S
            P.add("pool", lambda e: e.memset(KA_S[0][:], 0.0), writes=[g])
            P.add("sp", lambda e: e.dma_start(out=KA_S[0][64:128, :], in_=cdram["inds"]), writes=[g], dma=True)
            for (t, g) in (KWR, KCVC, KCMP):
                P.add("pool", lambda e, t=t: e.memset(t[:], 0.0), writes=[g])

        init_resident()

        WFM = sb("wfm", [128, 8, FMC], BF16)
        WTM = sb("wtm", [128, 8, TMC], BF16)
        W1 = sb("w1", [128, 32, 128], BF16)
        W2 = sb("w2", [128, 128], BF16)
        PET = sb("pet", [128, 32], BF16)
        PEB = sb("peb", [128, 2], F32)
        HKW = [1024, 1024, 1024, 1024, HW, HW]
        HK = [sb("hk%d" % i, [128, HKW[i]], BF16) for i in range(6)]
        FARB = sb("farb", [128, NHP * 4], F32)
        TAB = sb("tab", [32, NHP * 6], F32)
        P.add("pool", lambda e: e.dma_start(out=W1[0][:], in_=w1_d), writes=[W1[1]], dma=True)
        P.add("pool", lambda e: e.dma_start(out=W2[0][:], in_=w2_d), writes=[W2[1]], dma=True)
        P.add("pool", lambda e: e.dma_start(out=PET[0][:], in_=pet_d), writes=[PET[1]], dma=True)
        P.add("sp", lambda e: e.dma_start(out=FARB[0][:], in_=tab31_d.partition_broadcast(128)), writes=[FARB[1]], dma=True)
        P.add("sp", lambda e: e.dma_start(out=TAB[0][:], in_=tabsel_d), writes=[TAB[1]], dma=True)

        with contextlib.ExitStack() as st2:
            OH = st2.enter_context(nc.sbuf_tensor("oh", [32, GL], F32)); gOH = G()
            ADDM = st2.enter_context(nc.sbuf_tensor("addm", [6, GL], F32)); gADDM = G()
            GV = st2.enter_context(nc.sbuf_tensor("gv", [6, GL], BF16)); gGV = G()
            P.add("sp", lambda e: e.dma_start(out=OH[:], in_=cdram["oh"]), writes=[gOH], dma=True)
            P.add("sp", lambda e: e.dma_start(out=ADDM[:], in_=cdram["addm"]), writes=[gADDM], dma=True)
            for hp in range(NHP):
                for c0 in range(0, GL, 512):
                    w_ = min(512, GL - c0)
                    px, gpx = nxt("x", PS_X)
                    P.add("pe", lambda e, px=px, c0=c0, w_=w_, hp=hp: e.matmul(
                        px[0:6, 0:w_], lhsT=TAB[0][:, hp * 6:hp * 6 + 6], rhs=OH[:, c0:c0 + w_], start=True, stop=True),
                        reads=[TAB[1], gOH], writes=[gpx])
                    P.add("dve", lambda e, px=px, c0=c0, w_=w_: e.tensor_tensor(
                        out=GV[:, c0:c0 + w_], in0=px[0:6, 0:w_], in1=ADDM[:, c0:c0 + w_], op=ALU.add),
                        reads=[gpx, gADDM], writes=[gGV])
                gGD = G()
                P.add("sp", lambda e, hp=hp: e.dma_start(out=gd_d[hp * 6:hp * 6 + 6, :], in_=GV[:]),
                      reads=[gGV], writes=[gGD], dma=True)
                cfg.setdefault("_ggd", []).append(gGD)
            P.flush()
        gGDs = cfg.pop("_ggd")
        P.barrier()

        SHT = sb("sht", [128, 8, 8], F32)
        SCT = sb("sct", [128, 8, 8], F32)
        gate_tiles = {}

        def adaln(part):
            with contextlib.ExitStack() as st2:
                CT = st2.enter_context(nc.sbuf_tensor("ct_" + part, [128, 8, 144], F32)); gCT = G()
                WA = st2.enter_context(nc.sbuf_tensor("wa_" + part, [128, 8, 512], F32)); gWA = G()
                BAT = st2.enter_context(nc.sbuf_tensor("bat_" + part, [128, 24], F32)); gBAT = G()
                GNT = st2.enter_context(nc.sbuf_tensor("gnt_" + part, [128, 8], F32)); gGNT = G()
                P.add("sp", lambda e: e.dma_start(out=CT[:], in_=cT_d.rearrange("(k p) n -> p k n", p=128)), writes=[gCT], dma=True)
                P.add("sp", lambda e: e.dma_start(out=BAT[:], in_=bada_d.rearrange("(k p) -> p k", p=128), allow_slow_non_contiguous=True), writes=[gBAT], dma=True)
                P.add("sp", lambda e: e.dma_start(out=GNT[:], in_=gain_d.rearrange("(k p) -> p k", p=128), allow_slow_non_contiguous=True), writes=[gGNT], dma=True)
                if part == "gate":
                    BAB = st2.enter_context(nc.sbuf_tensor("bab", [128, D], F32)); gBAB = G()
                    GATEP, GATES, FG = gate_tiles["p"], gate_tiles["s"], gate_tiles["f"]
                    P.add("sp", lambda e: e.dma_start(out=BAB[:], in_=bada_d[2 * D:3 * D].partition_broadcast(128)), writes=[gBAB], dma=True)
                    P.add("sp", lambda e: e.dma_start(out=FG[0][:], in_=fgain_d.partition_broadcast(128)), writes=[FG[1]], dma=True)
                for j in (range(4) if part == "fm" else range(4, 6)):
                    P.add("sp", lambda e, j=j: e.dma_start(
                        out=WA[:], in_=wada_d[:, j * 512:(j + 1) * 512].rearrange("(k p) n -> p k n", p=128)),
                        writes=[gWA], dma=True)
                    if j < 4:
                        for f in range(4):
                            fc = j * 4 + f
                            px, gpx = nxt("x", PS_X)
                            for k in range(8):
                                P.add("pe", lambda e, px=px, k=k, f=f: e.matmul(
                                    px[:, 0:144], lhsT=WA[:, k, f * 128:(f + 1) * 128], rhs=CT[:, k, :],
                                    start=(k == 0), stop=(k == 7)), reads=[gWA, gCT], writes=[gpx])
                            dst = SHT if fc < 8 else SCT
                            fcc = fc % 8
                            for (dc, sc0, sc1, stp) in ((0, 0, 1, 1), (1, 128, 144, 4)):
                                ncol = 1 if dc == 0 else 4
                                if fc < 8:
                                    P.add("dve", lambda e, px=px, fc=fc, fcc=fcc, dc=dc, sc0=sc0, sc1=sc1, stp=stp, ncol=ncol: e.tensor_scalar(
                                        out=SHT[0][:, fcc, dc:dc + ncol], in0=px[:, sc0:sc1:stp], scalar1=BAT[:, fc:fc + 1], scalar2=None, op0=ALU.add),
                                        reads=[gpx, gBAT], writes=[SHT[1]])
                                else:
                                    P.add("dve", lambda e, px=px, fc=fc, fcc=fcc, dc=dc, sc0=sc0, sc1=sc1, stp=stp, ncol=ncol: e.tensor_scalar(
                                        out=SCT[0][:, fcc, dc:dc + ncol], in0=px[:, sc0:sc1:stp], scalar1=BAT[:, fc:fc + 1], scalar2=1.0,
                                        op0=ALU.add, op1=ALU.add), reads=[gpx, gBAT], writes=[SCT[1]])
                            if fc >= 8:
                                P.add("dve", lambda e, fcc=fcc: e.tensor_scalar(
                                    out=SCT[0][:, fcc, 0:5], in0=SCT[0][:, fcc, 0:5], scalar1=GNT[:, fcc:fcc + 1], scalar2=None,
                                    op0=ALU.mult), reads=[gGNT, SCT[1]], writes=[SCT[1]])
                    else:
                        oc = j - 4
                        px, gpx = nxt("x", PS_X)
                        for k in range(8):
                            P.add("pe", lambda e, px=px, k=k: e.matmul(
                                px[:, 0:512], lhsT=CT[:, k, 0:128], rhs=WA[:, k, :], start=(k == 0), stop=(k == 7)),
                                reads=[gWA, gCT], writes=[gpx])
                        P.add("dve", lambda e, px=px, oc=oc: e.tensor_tensor(
                            out=GATEP[0][:, oc * 512:(oc + 1) * 512], in0=px[:, 0:512], in1=BAB[:, oc * 512:(oc + 1) * 512], op=ALU.add),
                            reads=[gpx, gBAB], writes=[GATEP[1]])
                        px, gpx = nxt("x", PS_X)
                        for k in range(8):
                            P.add("pe", lambda e, px=px, k=k: e.matmul(
                                px[0:16, 0:512], lhsT=CT[:, k, 128:144], rhs=WA[:, k, :], start=(k == 0), stop=(k == 7)),
                                reads=[gWA, gCT], writes=[gpx])
                        P.add("dve", lambda e, px=px, oc=oc: e.tensor_tensor(
                            out=GATES[0][:, oc * 512:(oc + 1) * 512], in0=px[0:16, 0:512], in1=BAB[0:16, oc * 512:(oc + 1) * 512], op=ALU.add),
                            reads=[gpx, gBAB], writes=[GATES[1]])
                P.flush()
            P.barrier()

        adaln("fm")

        XT_ = sb("xt", [128, D], F32)
        XN = sb("xn", [128, D], BF16)
        SSQ = sb("ssq", [128, 4], F32)
        HT = sb("ht", [128, 8, 512], BF16)
        QA_M = [sb("qa_m%d" % i, [128, 512], BF16) for i in range(2)]
        QA_S = [[sb("qa_s%d%d" % (i, v), [128, 512], BF16) for v in range(2)] for i in range(2)]
        QSIB = [sb("qsib%d" % i, [64, 512], BF16) for i in range(2)]
        ZT = [sb("zt%d" % i, [64, 512], BF16) for i in range(4)]
        GT = sb("gt", [8, 512], F32)
        KMF = [sb("kmf%d" % i, [64, 40], BF16) for i in range(2)]
        KMFF = sb("kmff", [64, 40], F32)
        PTB = [sb("ptb%d" % i, [128, 512], BF16) for i in range(2)]
        OTM = sb("otm", [128, TMC], F32)
        IMPACC = sb("impacc", [128, 4, 128], F32)
        SC = sb("sc", [128, 40], F32)
        M8 = sb("m8", [128, 16], F32)
        THR = sb("thr", [128, 2], F32)
        IMPM = sb("impm", [128, 128], F32)
        IMP2 = sb("imp2", [128, 128], F32)
        MBP = sb("mbp", [128, 256], BF16)
        RSI = sb("rsi", [128, 2], F32)
        BCZ = sb("bcz", [64, 512], F32)
        TMP = sb("tmp", [64, 512], F32)
        ACCS = [sb("acc%d" % i, [64, 512], F32) for i in range(2)]
        MIXB = sb("mixb", [64, 512], BF16)
        AKV = sb("akv", [128, 256], BF16)
        SG = sb("sg", [128, 512], F32)
        RS = SG
        for (t, g) in (MBP,):
            P.add("pool", lambda e, t=t: e.memset(t[:], 0.0), writes=[g])
        for lst in (QA_M, QA_S[0], QA_S[1]):
            for (t, g) in lst:
                P.add("pool", lambda e, t=t: e.memset(t[:], 0.0), writes=[g])

        def load_hp(hp):
            P.add("pool", lambda e: e.dma_start(out=WFM[0][:], in_=wfm_d[hp].rearrange("(k p) n -> p k n", p=128)),
                  writes=[WFM[1]], dma=True)
            P.add("pool", lambda e: e.dma_start(out=WTM[0][:], in_=wtm_d[hp].rearrange("(k p) n -> p k n", p=128)),
                  writes=[WTM[1]], dma=True)
            for v in range(6):
                src = bass.AP(gd_h, (hp * 6 + v) * GL, [[1, 128], [1, HKW[v]]])
                P.add("sp", lambda e, v=v, src=src: e.dma_start(out=HK[v][0][:], in_=src),
                      reads=[gGDs[hp]], writes=[HK[v][1]], dma=True)

        def peb_compute():
            for kvi in range(2):
                px, gpx = nxt("x", PS_X)
                lo = 64 * kvi
                for l in range(32):
                    P.add("pe", lambda e, px=px, l=l, lo=lo: e.matmul(
                        px[:, 0:1], lhsT=W1[0][lo:lo + 64, l, :], rhs=PET[0][lo:lo + 64, l:l + 1],
                        start=(l == 0), stop=(l == 31)), reads=[W1[1], PET[1]], writes=[gpx])
                P.add("dve", lambda e, px=px, kvi=kvi: e.tensor_copy(out=PEB[0][:, kvi:kvi + 1], in_=px[:, 0:1]),
                      reads=[gpx], writes=[PEB[1]])

        peb_compute()

        def project_tile(hp, N, q0, ht_ready, subs, om_dst, on_dst, ow_dst, ring_kt):
            HTt, gHT = HT

            def fm(name, evac):
                off, w_ = FMCH[name]
                w_ = max(w_, 64)
                px, gpx = nxt("x", PS_X)
                for k in range(8):
                    P.add("pe", lambda e, px=px, k=k, off=off, w_=w_: e.matmul(
                        px[0:w_, 0:N], lhsT=WFM[0][:, k, off:off + w_], rhs=HTt[:, k, 0:N],
                        start=(k == 0), stop=(k == 7)), reads=[WFM[1], gHT], writes=[gpx])
                evac(px, gpx)

            kc0 = q0
            for i, nm in enumerate(("qmA", "qmB")):
                def ev(px, gpx, i=i):
                    P.add("act", lambda e: e.activation(out=QA_M[i][0][0:64, 0:N], in_=px[0:64, 0:N], func=AF.Copy, scale=0.125),
                          reads=[gpx], writes=[QA_M[i][1]])
                fm(nm, ev)
            for i, nm in enumerate(("kmA", "kmB")):
                def ev(px, gpx, i=i):
                    P.add("act", lambda e: e.activation(out=KA_M[i][0][0:64, kc0:kc0 + N], in_=px[0:64, 0:N], func=AF.Copy),
                          reads=[gpx], writes=[KA_M[i][1]])
                fm(nm, ev)
            if cfg.get("STAGE", 99) < 2.45:
                return
            for i, nm in enumerate(("zmA", "zmB", "znA", "znB")):
                def ev(px, gpx, i=i):
                    if cfg.get("STAGE", 99) >= 2.47:
                        P.add("act", lambda e: e.activation(out=SG[0][0:64, 0:N], in_=px[0:64, 0:N], func=AF.Sigmoid),
                              reads=[gpx], writes=[SG[1]])
                    if cfg.get("STAGE", 99) >= 2.49:
                        if cfg.get("VAR", 0) == 1:
                            P.add("dve", lambda e: e.tensor_tensor(out=TMP[0][:, 0:N], in0=px[0:64, 0:N], in1=SG[0][0:64, 0:N], op=ALU.mult),
                                  reads=[gpx, SG[1]], writes=[TMP[1]])
                        elif cfg.get("VAR", 0) == 2:
                            P.add("dve", lambda e: e.tensor_copy(out=TMP[0][:, 0:N], in_=px[0:64, 0:N]), reads=[gpx], writes=[TMP[1]])
                            P.add("dve", lambda e: e.tensor_tensor(out=ZT[i][0][:, 0:N], in0=TMP[0][:, 0:N], in1=SG[0][0:64, 0:N], op=ALU.mult),
                                  reads=[TMP[1], SG[1]], writes=[ZT[i][1]])
                        elif cfg.get("VAR", 0) == 3:
                            P.add("dve", lambda e: e.tensor_tensor(out=ZT[i][0][:, 0:N], in0=px[0:64, 0:N], in1=SG[0][0:64, 0:N], op=ALU.mult),
                                  reads=[gpx, SG[1]], writes=[ZT[i][1], TMP[1]])
                        elif cfg.get("VAR", 0) == 4:
                            P.add("dve", lambda e: e.tensor_tensor(out=ZT[0][0][:, 0:N], in0=px[0:64, 0:N], in1=SG[0][0:64, 0:N], op=ALU.mult),
                                  reads=[gpx, SG[1]], writes=[ZT[0][1]])
                        else:
                            P.add("dve", lambda e: e.tensor_tensor(out=ZT[i][0][:, 0:N], in0=px[0:64, 0:N], in1=SG[0][0:64, 0:N], op=ALU.mult),
                                  reads=[gpx, SG[1]], writes=[ZT[i][1]])
                fm(nm, ev)
            if cfg.get("STAGE", 99) < 2.6:
                return
            for i, nm in enumerate(("qnA", "qnB")):
                def ev(px, gpx, i=i):
                    for v in range(2):
                        P.add("act" if v == 0 else "dve", (lambda e, v=v: e.activation(
                            out=QA_S[i][v][0][0:64, 0:N], in_=px[0:64, 0:N], func=AF.Copy, scale=0.125)) if v == 0 else
                            (lambda e, v=v: e.tensor_scalar(out=QA_S[i][v][0][0:64, 0:N], in0=px[0:64, 0:N], scalar1=0.125,
                                                            scalar2=None, op0=ALU.mult)),
                            reads=[gpx], writes=[QA_S[i][v][1]])
                fm(nm, ev)
            if cfg.get("STAGE", 99) < 2.61:
                return
            for i, nm in enumerate(("qnC", "qnD")):
                def ev(px, gpx, i=i):
                    P.add("act", lambda e: e.activation(out=QSIB[i][0][:, 0:N], in_=px[0:64, 0:N], func=AF.Copy, scale=0.125),
                          reads=[gpx], writes=[QSIB[i][1]])
                fm(nm, ev)

            if cfg.get("STAGE", 99) < 2.62:
                return

            def ev(px, gpx):
                P.add("act", lambda e: e.activation(out=KA_S[0][0:64, kc0:kc0 + N], in_=px[0:64, 0:N], func=AF.Copy),
                      reads=[gpx], writes=[KA_S[1]])
            fm("ks", ev)
            if cfg.get("STAGE", 99) < 2.63:
                return

            def ev(px, gpx):
                for j in range(0, N, 128):
                    n_ = min(128, N - j)
                    r0 = ((ring_kt + j // 128) % 9) * 128
                    P.add("dve", lambda e, j=j, n_=n_, r0=r0: e.tensor_copy(out=KWR[0][0:64, r0:r0 + n_], in_=px[0:64, j:j + n_]),
                          reads=[gpx], writes=[KWR[1]])
            fm("kw", ev)

            if cfg.get("STAGE", 99) < 2.64:
                return

            def ev(px, gpx):
                P.add("act", lambda e: e.activation(out=KCVC[0][:, kc0:kc0 + N], in_=px[:, 0:N], func=AF.Copy),
                      reads=[gpx], writes=[KCVC[1]])
            fm("kcvc", ev)

            if cfg.get("STAGE", 99) < 2.645:
                return

            def ev(px, gpx):
                P.add("act", lambda e: e.activation(out=GT[0][:, 0:N], in_=px[0:8, 0:N], func=AF.Sigmoid),
                      reads=[gpx], writes=[GT[1]])
            fm("g", ev)

            if cfg.get("STAGE", 99) < 2.7:
                return
            for si, (rows, c0) in enumerate(subs):
                pa, gpa = nxt("x", PS_X)
                pb, gpb = nxt("x", PS_X)
                for k in range(8):
                    P.add("pe", lambda e, pa=pa, k=k, rows=rows, c0=c0: e.matmul(
                        pa[0:rows, 0:512], lhsT=HTt[:, k, c0:c0 + rows], rhs=WTM[0][:, k, 0:512],
                        start=(k == 0), stop=(k == 7)), reads=[WTM[1], gHT], writes=[gpa])
                for k in range(8):
                    P.add("pe", lambda e, pb=pb, k=k, rows=rows, c0=c0: e.matmul(
                        pb[0:rows, 0:128], lhsT=HTt[:, k, c0:c0 + rows], rhs=WTM[0][:, k, 512:640],
                        start=(k == 0), stop=(k == 7)), reads=[WTM[1], gHT], writes=[gpb])
                P.add("act", lambda e, pa=pa, rows=rows: e.activation(out=OTM[0][0:rows, 0:512], in_=pa[0:rows, 0:512], func=AF.Copy),
                      reads=[gpa], writes=[OTM[1]])
                P.add("dve", lambda e, pb=pb, rows=rows: e.tensor_copy(out=OTM[0][0:rows, 512:640], in_=pb[0:rows, 0:128]),
                      reads=[gpb], writes=[OTM[1]])
                if cfg.get("STAGE", 99) < 2.8:
                    continue
                kt = (q0 + c0) // 128
                r_ = (q0 + c0) % 128
                assert r_ == 0
                for i in range(2):
                    P.add("pool", lambda e, i=i, kt=kt, rows=rows: e.tensor_copy(
                        out=VA_M[i][0][0:rows, kt, 0:64], in_=OTM[0][0:rows, 64 * i:64 * i + 64]),
                        reads=[OTM[1]], writes=[VA_M[i][1]])
                P.add("pool", lambda e, kt=kt, rows=rows: e.tensor_copy(
                    out=VA_S[0][0:rows, kt, 0:64], in_=OTM[0][0:rows, 448:512]), reads=[OTM[1]], writes=[VA_S[1]])
                rk = (ring_kt + c0 // 128) % 9
                P.add("pool", lambda e, rk=rk, rows=rows: e.tensor_copy(
                    out=VWR[0][0:rows, rk, 0:64], in_=OTM[0][0:rows, 576:640]), reads=[OTM[1]], writes=[VWR[1]])
                if cfg.get("STAGE", 99) >= 2.9:
                    om_dst(si, rows, c0)

        def compress(c):
            for kvi in range(2):
                lo = 64 * kvi
                px, gpx = nxt("x", PS_X)
                for l in range(32):
                    s0 = 2048 * c + l
                    P.add("pe", lambda e, px=px, l=l, lo=lo, s0=s0: e.matmul(
                        px[:, 0:128], lhsT=W1[0][lo:lo + 64, l, :], rhs=KCVC[0][lo:lo + 64, s0:s0 + 2033:16],
                        start=(l == 0), stop=(l == 31)), reads=[W1[1], KCVC[1]], writes=[gpx])
                P.add("act", lambda e, px=px, kvi=kvi: e.activation(
                    out=SG[0][:, 0:128], in_=px[:, 0:128], func=AF.Sigmoid, bias=PEB[0][:, kvi:kvi + 1]),
                    reads=[gpx, PEB[1]], writes=[SG[1]])
                P.add("dve", lambda e, px=px, kvi=kvi: e.scalar_tensor_tensor(
                    out=AKV[0][:, 128 * kvi:128 * kvi + 128], in0=px[:, 0:128], scalar=PEB[0][:, kvi:kvi + 1], in1=SG[0][:, 0:128],
                    op0=ALU.add, op1=ALU.mult), reads=[gpx, PEB[1], SG[1]], writes=[AKV[1]])
            px, gpx = nxt("x", PS_X)
            P.add("pe", lambda e, px=px: e.matmul(px[0:64, 0:128], lhsT=W2[0][:, 0:64], rhs=AKV[0][:, 0:128], start=True, stop=True),
                  reads=[W2[1], AKV[1]], writes=[gpx])
            P.add("dve", lambda e, px=px: e.tensor_copy(out=KCMP[0][:, 128 * c:128 * c + 128], in_=px[0:64, 0:128]),
                  reads=[gpx], writes=[KCMP[1]])
            px, gpx = nxt("x", PS_X)
            P.add("pe", lambda e, px=px: e.matmul(px[:, 0:64], lhsT=AKV[0][:, 128:256], rhs=W2[0][:, 64:128], start=True, stop=True),
                  reads=[W2[1], AKV[1]], writes=[gpx])
            P.add("dve", lambda e, px=px: e.tensor_copy(out=VC[0][:, c, 0:64], in_=px[:, 0:64]), reads=[gpx], writes=[VC[1]])

        def attend(QT, gQ, qrows, KT, gK, krows, VT, gV, N, tiles, hk, farcol, on_pt=None, first=True, last=True, po=None):
            if po is None:
                po = nxt("o", PS_O)
            pO, gO = po
            nt = len(tiles)
            pts = {}

            def emit_S(ti):
                kc, vs, kind, arg, qlo = tiles[ti]
                pS, gS = nxt("s", PS_S)
                two = kind in ("hk", "cm")
                P.add("pe", lambda e, pS=pS, kc=kc, qlo=qlo, two=two: e.matmul(
                    pS[:, qlo:N], lhsT=KT[0:krows, kc:kc + 128], rhs=QT[0:qrows, qlo:N], start=True, stop=not two),
                    reads=[gK, gQ], writes=[gS])
                if kind == "hk":
                    c0 = arg + 384 + qlo
                    P.add("pe", lambda e, pS=pS, qlo=qlo, c0=c0: e.matmul(
                        pS[:, qlo:N], lhsT=JB[:], rhs=HK[hk][0][:, c0:c0 + N - qlo], start=False, stop=True),
                        reads=[gJB, HK[hk][1]], writes=[gS])
                elif kind == "cm":
                    P.add("pe", lambda e, pS=pS, qlo=qlo, arg=arg: e.matmul(
                        pS[:, qlo:N], lhsT=IDB[:], rhs=CMASK[:, arg, qlo:N], start=False, stop=True),
                        reads=[gIDB, gCMASK], writes=[gS])
                pt, gpt = nxt("pt", PTB)
                if kind == "far":
                    P.add("act", lambda e, pS=pS, pt=pt, qlo=qlo: e.activation(
                        out=pt[:, qlo:N], in_=pS[:, qlo:N], func=AF.Exp, bias=FARB[0][:, farcol:farcol + 1]),
                        reads=[gS, FARB[1]], writes=[gpt])
                else:
                    P.add("act", lambda e, pS=pS, pt=pt, qlo=qlo: e.activation(
                        out=pt[:, qlo:N], in_=pS[:, qlo:N], func=AF.Exp), reads=[gS], writes=[gpt])
                pts[ti] = (pt, gpt)

            def emit_PV(ti):
                kc, vs, kind, arg, qlo = tiles[ti]
                pt, gpt = pts.pop(ti)
                if VT is not None:
                    P.add("pe", lambda e, pt=pt, vs=vs, qlo=qlo, ti=ti: e.matmul(
                        pO[0:65, qlo:N], lhsT=VT[:, vs, 0:65], rhs=pt[:, qlo:N], start=(first and ti == 0), stop=(last and ti == nt - 1)),
                        reads=[gV, gpt], writes=[gO])
                if on_pt is not None:
                    on_pt(ti, pt, gpt)

            GS = 1 if N > 64 else min(8, 512 // N)
            if GS > 1:
                groups = []
                for ti, (kc, vs, kind, arg, qlo) in enumerate(tiles):
                    simple = kind in ("far", "none") and qlo == 0
                    if simple and groups and groups[-1][0] == kind and len(groups[-1][1]) < GS:
                        groups[-1][1].append(ti)
                    else:
                        groups.append((kind if simple else "single", [ti]))
                gpts = {}

                def emit_SG(gi):
                    kind, tis = groups[gi]
                    if kind == "single":
                        emit_S(tis[0])
                        return
                    pS, gS = nxt("s", PS_S)
                    for j, ti in enumerate(tis):
                        kc = tiles[ti][0]
                        P.add("pe", lambda e, pS=pS, kc=kc, j=j: e.matmul(
                            pS[:, j * N:(j + 1) * N], lhsT=KT[0:krows, kc:kc + 128], rhs=QT[0:qrows, 0:N], start=True, stop=True,
                            skip_group_check=True), reads=[gK, gQ], writes=[gS])
                    pt, gpt = nxt("pt", PTB)
                    w_ = len(tis) * N
                    if kind == "far":
                        P.add("act", lambda e, pS=pS, pt=pt, w_=w_: e.activation(
                            out=pt[:, 0:w_], in_=pS[:, 0:w_], func=AF.Exp, bias=FARB[0][:, farcol:farcol + 1]),
                            reads=[gS, FARB[1]], writes=[gpt])
                    else:
                        P.add("act", lambda e, pS=pS, pt=pt, w_=w_: e.activation(
                            out=pt[:, 0:w_], in_=pS[:, 0:w_], func=AF.Exp), reads=[gS], writes=[gpt])
                    gpts[gi] = (pt, gpt)

                def emit_PVG(gi):
                    kind, tis = groups[gi]
                    if kind == "single":
                        emit_PV(tis[0])
                        return
                    pt, gpt = gpts.pop(gi)
                    for j, ti in enumerate(tis):
                        vs = tiles[ti][1]
                        ptv = pt[:, j * N:(j + 1) * N]
                        if VT is not None:
                            P.add("pe", lambda e, ptv=ptv, vs=vs, ti=ti: e.matmul(
                                pO[0:65, 0:N], lhsT=VT[:, vs, 0:65], rhs=ptv, start=(first and ti == 0), stop=(last and ti == nt - 1)),
                                reads=[gV, gpt], writes=[gO])
                        if on_pt is not None:
                            on_pt(ti, ptv, gpt)

                ng = len(groups)
                if ng > 0:
                    emit_SG(0)
                for gi in range(ng):
                    if gi + 1 < ng:
                        emit_SG(gi + 1)
                    emit_PVG(gi)
                return po
            if nt > 0:
                emit_S(0)
            for ti in range(nt):
                if ti + 1 < nt:
                    emit_S(ti + 1)
                emit_PV(ti)
            return po

        def finish(po, N, zi, coef_row, acc_first, acc_last, head_slot, q0, ACC=None):
            ACC = ACC or ACCS[0]
            pO, gO = po
            P.add("dve", lambda e: e.tensor_scalar(out=RS[0][64:65, 0:N], in0=pO[64:65, 0:N], scalar1=1e-30, scalar2=None, op0=ALU.max),
                  reads=[gO], writes=[RS[1]])
            P.add("dve", lambda e: e.reciprocal(out=RS[0][64:65, 0:N], in_=RS[0][64:65, 0:N]), reads=[RS[1]], writes=[RS[1]])
            pb, gpb = nxt("x", PS_X)
            P.add("pe", lambda e: e.matmul(pb[0:64, 0:N], lhsT=ONES[64:65, 0:64], rhs=RS[0][64:65, 0:N], start=True, stop=True),
                  reads=[gONES, RS[1]], writes=[gpb])
            P.add("dve", lambda e: e.tensor_tensor(out=BCZ[0][:, 0:N], in0=pb[0:64, 0:N], in1=ZT[zi][0][:, 0:N], op=ALU.mult),
                  reads=[gpb, ZT[zi][1]], writes=[BCZ[1]])
            if coef_row is not None:
                pg, gpg = nxt("x", PS_X)
                P.add("pe", lambda e: e.matmul(pg[0:64, 0:N], lhsT=SEL6[0:6, coef_row * 64:coef_row * 64 + 64], rhs=GT[0][0:6, 0:N],
                                               start=True, stop=True), reads=[gSEL6, GT[1]], writes=[gpg])
                P.add("dve", lambda e: e.tensor_tensor(out=BCZ[0][:, 0:N], in0=BCZ[0][:, 0:N], in1=pg[0:64, 0:N], op=ALU.mult),
                      reads=[gpg, BCZ[1]], writes=[BCZ[1]])
            if acc_first and acc_last:
                P.add("dve", lambda e: e.tensor_tensor(out=MIXB[0][:, 0:N], in0=pO[0:64, 0:N], in1=BCZ[0][:, 0:N], op=ALU.mult),
                      reads=[gO, BCZ[1]], writes=[MIXB[1]])
            elif acc_first:
                P.add("dve", lambda e: e.tensor_tensor(out=ACC[0][:, 0:N], in0=pO[0:64, 0:N], in1=BCZ[0][:, 0:N], op=ALU.mult),
                      reads=[gO, BCZ[1]], writes=[ACC[1]])
            else:
                P.add("dve", lambda e: e.tensor_tensor(out=TMP[0][:, 0:N], in0=pO[0:64, 0:N], in1=BCZ[0][:, 0:N], op=ALU.mult),
                      reads=[gO, BCZ[1]], writes=[TMP[1]])
                if acc_last:
                    P.add("pool", lambda e: e.tensor_tensor(out=MIXB[0][:, 0:N], in0=ACC[0][:, 0:N], in1=TMP[0][:, 0:N], op=ALU.add),
                          reads=[ACC[1], TMP[1]], writes=[MIXB[1]])
                else:
                    P.add("pool", lambda e: e.tensor_tensor(out=ACC[0][:, 0:N], in0=ACC[0][:, 0:N], in1=TMP[0][:, 0:N], op=ALU.add),
                          reads=[ACC[1], TMP[1]], writes=[ACC[1]])
            if acc_last:
                head_slot(MIXB)

        def key_tiles(N, q0, nkeys_tiles, vslot_fn, win=False):
            tl = []
            for kt in range(nkeys_tiles):
                off = q0 - 128 * kt
                qlo = max(0, -off)
                if qlo >= N:
                    continue
                if win:
                    if off > 512:
                        continue
                    tl.append((None, vslot_fn(kt), "hk", off, qlo, kt))
                else:
                    kind = "hk" if off <= 128 else "far"
                    tl.append((128 * kt, vslot_fn(kt), kind, off, qlo, kt))
            return tl

        def attend_tile(hp, N, q0, subs, mix_dst):
            nkt = (q0 + N + 127) // 128
            for i in range(2):
                KT, gK = KA_M[i]
                nblk = (q0 + N) // 256
                curs = [(q0 + c0) // 256 for (_, c0) in subs]
                nb = max(curs)
                if nb > 0:
                    P.add("dve", lambda e, KT=KT, nb=nb, i=i: e.tensor_reduce(
                        out=KMFF[0][:, 0:nb], in_=KT[0:64, 0:nb * 256].rearrange("p (n k) -> p n k", k=256), axis=AX.X, op=ALU.add),
                        reads=[gK], writes=[KMFF[1]])
                    P.add("dve", lambda e, nb=nb, i=i: e.tensor_copy(out=KMF[i][0][:, 0:nb], in_=KMFF[0][:, 0:nb]),
                          reads=[KMFF[1]], writes=[KMF[i][1]])
                for si, (rows, c0) in enumerate(subs):
                    cur = curs[si]
                    P.add("pool", lambda e, rows=rows: e.memset(SC[0][0:rows, :], NEGS), writes=[SC[1]])
                    if cur > 0:
                        px, gpx = nxt("x", PS_X)
                        P.add("pe", lambda e, px=px, rows=rows, c0=c0, cur=cur, i=i: e.matmul(
                            px[0:rows, 0:cur], lhsT=QA_M[i][0][0:64, c0:c0 + rows], rhs=KMF[i][0][:, 0:cur], start=True, stop=True),
                            reads=[QA_M[i][1], KMF[i][1]], writes=[gpx])
                        P.add("dve", lambda e, px=px, rows=rows, cur=cur: e.tensor_copy(out=SC[0][0:rows, 0:cur], in_=px[0:rows, 0:cur]),
                              reads=[gpx], writes=[SC[1]])
                    P.add("dve", lambda e, rows=rows: e.max(out=M8[0][0:rows, 0:8], in_=SC[0][0:rows, 0:32]), reads=[SC[1]], writes=[M8[1]])
                    P.add("dve", lambda e, rows=rows: e.tensor_scalar(out=THR[0][0:rows, 0:1], in0=M8[0][0:rows, 2:3], scalar1=-1e29,
                                                                      scalar2=None, op0=ALU.max), reads=[M8[1]], writes=[THR[1]])
                    P.add("dve", lambda e, rows=rows: e.tensor_scalar(out=MBP[0][0:rows, 64:96], in0=SC[0][0:rows, 0:32],
                                                                      scalar1=THR[0][0:rows, 0:1], scalar2=-BIG, op0=ALU.is_lt, op1=ALU.mult),
                          reads=[SC[1], THR[1]], writes=[MBP[1]])
                    if cur < 32:
                        P.add("dve", lambda e, rows=rows, cur=cur: e.memset(MBP[0][0:rows, 64 + cur:65 + cur], 0.0), writes=[MBP[1]])
                    pt_, gpt_ = nxt("t", PS_T)
                    P.add("pe", lambda e, pt_=pt_, rows=rows: e.transpose(out=pt_[:, 0:rows], in_=MBP[0][0:rows, 0:128], identity=IDB[0:rows, 0:rows]),
                          reads=[MBP[1], gIDB], writes=[gpt_])
                    P.add("act", lambda e, pt_=pt_, rows=rows, c0=c0, i=i: e.activation(
                        out=QA_M[i][0][64:128, c0:c0 + rows], in_=pt_[64:128, 0:rows], func=AF.Copy), reads=[gpt_], writes=[QA_M[i][1]])
                tl = [(a, b, c_, d, e_) for (a, b, c_, d, e_, _) in key_tiles(N, q0, nkt, lambda kt: kt)]
                po = attend(QA_M[i][0], QA_M[i][1], 128, KT, gK, 128, VA_M[i][0], VA_M[i][1], N, tl, hk=i, farcol=hp * 4 + i)
                finish(po, N, i, None, True, True, lambda mb, i=i: mix_dst(2 * hp + i, mb), q0)

            if N == 512:
                t = q0 // 512
                c = t // 4
                if t % 4 == 0 and t > 0:
                    compress(c - 1)
                compress(c)
                cts = list(range(0, c + 1))
            else:
                for c in range(NCT):
                    compress(c)
                cts = list(range(NCT))
            tq = q0 // 512
            ctl = []
            for c in cts:
                dl = 4 * c - tq
                if dl > 0:
                    continue
                kind, arg = ("cm", -dl) if dl >= -4 else ("none", 0)
                ctl.append((128 * c, c, kind, arg, 0))
            nsub = len(subs)
            heads = [(QA_S[0][0][0], QA_S[0][0][1], True, 0), (QA_S[1][0][0], QA_S[1][0][1], True, 1),
                     (QSIB[0][0], QSIB[0][1], False, 0), (QSIB[1][0], QSIB[1][1], False, 1)]
            ocmp = [None, None]
            for hi, (QT, gQ, own, oi) in enumerate(heads):
                pI = [nxt("x", PS_X), nxt("x", PS_X)]
                state = {"first": [True, True]}

                def on_pt(ti, pt, gpt, pI=pI, state=state):
                    cc = ctl[ti][1]
                    for si, (rows, c0) in enumerate(subs):
                        b = si // 2
                        cb = (si % 2) * 129
                        fst = state["first"][b]
                        state["first"][b] = False
                        P.add("pe", lambda e, b=b, cb=cb, rows=rows, c0=c0, cc=cc, fst=fst, pt=pt: e.matmul(
                            pI[b][0][0:rows, cb:cb + 129], lhsT=pt[:, c0:c0 + rows], rhs=OVX[:, cc, :], start=fst, stop=True,
                            skip_group_check=True),
                            reads=[gpt, gOVX], writes=[pI[b][1]])
                po = attend(QT, gQ, 64, KCMP[0], KCMP[1], 64, VC[0] if own else None, VC[1], N, ctl, hk=0, farcol=0, on_pt=on_pt)
                if own:
                    ocmp[oi] = po
                for si, (rows, c0) in enumerate(subs):
                    b = si // 2
                    cb = (si % 2) * 129
                    P.add("dve", lambda e, b=b, cb=cb, rows=rows: e.tensor_scalar(
                        out=RSI[0][0:rows, 0:1], in0=pI[b][0][0:rows, cb + 128:cb + 129], scalar1=1e-30, scalar2=None, op0=ALU.max),
                        reads=[pI[b][1]], writes=[RSI[1]])
                    P.add("dve", lambda e, rows=rows: e.reciprocal(out=RSI[0][0:rows, 0:1], in_=RSI[0][0:rows, 0:1]),
                          reads=[RSI[1]], writes=[RSI[1]])
                    if hi == 0:
                        P.add("dve", lambda e, b=b, cb=cb, rows=rows, si=si: e.tensor_scalar(
                            out=IMPACC[0][0:rows, si, :], in0=pI[b][0][0:rows, cb:cb + 128], scalar1=RSI[0][0:rows, 0:1], scalar2=None,
                            op0=ALU.mult), reads=[pI[b][1], RSI[1]], writes=[IMPACC[1]])
                    else:
                        P.add("dve", lambda e, b=b, cb=cb, rows=rows, si=si: e.scalar_tensor_tensor(
                            out=IMPACC[0][0:rows, si, :], in0=pI[b][0][0:rows, cb:cb + 128], scalar=RSI[0][0:rows, 0:1],
                            in1=IMPACC[0][0:rows, si, :], op0=ALU.mult, op1=ALU.add), reads=[pI[b][1], RSI[1], IMPACC[1]], writes=[IMPACC[1]])
            rank = 16 if (q0 // 64) < 128 else 15
            for si, (rows, c0) in enumerate(subs):
                cur0 = (q0 + c0) // 64
                p0 = 128 - cur0
                P.add("dve", lambda e, rows=rows, si=si, p0=p0: e.tensor_tensor(
                    out=IMPM[0][0:rows, :], in0=IMPACC[0][0:rows, si, :], in1=PATM[0:rows, p0:p0 + 128], op=ALU.mult),
                    reads=[IMPACC[1], gPATM], writes=[IMPM[1]])
                P.add("dve", lambda e, rows=rows, p0=p0: e.tensor_tensor(
                    out=IMPM[0][0:rows, :], in0=IMPM[0][0:rows, :], in1=PATA[0:rows, p0:p0 + 128], op=ALU.add),
                    reads=[IMPM[1], gPATA], writes=[IMPM[1]])
                P.add("dve", lambda e, rows=rows: e.memset(IMPM[0][0:rows, 0:1], 1e9), writes=[IMPM[1]])
                P.add("dve", lambda e, rows=rows: e.max(out=M8[0][0:rows, 0:8], in_=IMPM[0][0:rows, :]), reads=[IMPM[1]], writes=[M8[1]])
                P.add("dve", lambda e, rows=rows: e.match_replace(out=IMP2[0][0:rows, :], in_to_replace=M8[0][0:rows, 0:8],
                                                                  in_values=IMPM[0][0:rows, :], imm_value=-3e38),
                      reads=[IMPM[1], M8[1]], writes=[IMP2[1]])
                P.add("dve", lambda e, rows=rows: e.max(out=M8[0][0:rows, 8:16], in_=IMP2[0][0:rows, :]), reads=[IMP2[1]], writes=[M8[1]])
                P.add("dve", lambda e, rows=rows: e.tensor_scalar(out=THR[0][0:rows, 1:2], in0=M8[0][0:rows, rank - 1:rank], scalar1=-1e29,
                                                                  scalar2=None, op0=ALU.max), reads=[M8[1]], writes=[THR[1]])
                for v in range(2):
                    P.add("dve", lambda e, rows=rows, v=v: e.tensor_scalar(
                        out=MBP[0][0:rows, 64:128], in0=IMPM[0][0:rows, 64 * v:64 * v + 64],
                        scalar1=THR[0][0:rows, 1:2], scalar2=-BIG, op0=ALU.is_lt, op1=ALU.mult),
                        reads=[IMPM[1], THR[1]], writes=[MBP[1]])
                    pt_, gpt_ = nxt("t", PS_T)
                    P.add("pe", lambda e, pt_=pt_, rows=rows: e.transpose(out=pt_[:, 0:rows], in_=MBP[0][0:rows, 0:128], identity=IDB[0:rows, 0:rows]),
                          reads=[MBP[1], gIDB], writes=[gpt_])
                    for i in range(2):
                        P.add("act" if i == 0 else "dve", (lambda e, pt_=pt_, rows=rows, c0=c0, i=i, v=v: e.activation(
                            out=QA_S[i][v][0][64:128, c0:c0 + rows], in_=pt_[64:128, 0:rows], func=AF.Copy)) if i == 0 else
                            (lambda e, pt_=pt_, rows=rows, c0=c0, i=i, v=v: e.tensor_copy(
                                out=QA_S[i][v][0][64:128, c0:c0 + rows], in_=pt_[64:128, 0:rows])),
                            reads=[gpt_], writes=[QA_S[i][v][1]])
            for i in range(2):
                finish(ocmp[i], N, 2 + i, 3 * i + 0, True, False, None, q0, ACC=ACCS[i])
            for i in range(2):
                ktl = key_tiles(N, q0, nkt, lambda kt: kt)
                lo = [(a, b, c_, d, e_) for (a, b, c_, d, e_, kt) in ktl if kt < 32 or kt * 128 >= max(T, PAST)]
                hi_ = [(a, b, c_, d, e_) for (a, b, c_, d, e_, kt) in ktl if not (kt < 32 or kt * 128 >= max(T, PAST))]
                po = attend(QA_S[i][0][0], QA_S[i][0][1], 128, KA_S[0], KA_S[1], 128, VA_S[0], VA_S[1], N, lo, hk=2 + i,
                            farcol=hp * 4 + 2 + i, first=True, last=(len(hi_) == 0))
                if hi_:
                    attend(QA_S[i][1][0], QA_S[i][1][1], 128, KA_S[0], KA_S[1], 128, VA_S[0], VA_S[1], N, hi_, hk=2 + i,
                           farcol=hp * 4 + 2 + i, first=False, last=True, po=po)
                finish(po, N, 2 + i, 3 * i + 1, False, False, None, q0, ACC=ACCS[i])
                wtl = [((kt % 9) * 128, kt % 9, c_, d, e_) for (a, b, c_, d, e_, kt) in key_tiles(N, q0, nkt, lambda kt: kt, win=True)]
                po = attend(QA_S[i][0][0], QA_S[i][0][1], 64, KWR[0], KWR[1], 64, VWR[0], VWR[1], N, wtl, hk=4 + i, farcol=0)
                finish(po, N, 2 + i, 3 * i + 2, False, True, lambda mb, i=i: mix_dst(8 + 2 * hp + i, mb), q0, ACC=ACCS[i])

        def norm_rows(src_ap, rows, HTcol0, rowsel):
            P.add("sp", lambda e: e.dma_start(out=XT_[0][0:rows, :], in_=src_ap), writes=[XT_[1]], dma=True)
            P.add("act", lambda e: e.activation(out=XN[0][0:rows, :], in_=XT_[0][0:rows, :], func=AF.Square, accum_out=SSQ[0][0:rows, 0:1]),
                  reads=[XT_[1]], writes=[XN[1], SSQ[1]])
            P.add("act", lambda e: e.activation(out=SSQ[0][0:rows, 1:2], in_=SSQ[0][0:rows, 0:1], func=AF.Sqrt, scale=1.0 / D, bias=EPS[0:rows, :]),
                  reads=[SSQ[1], gEPS], writes=[SSQ[1]])
            P.add("dve", lambda e: e.reciprocal(out=SSQ[0][0:rows, 2:3], in_=SSQ[0][0:rows, 1:2]), reads=[SSQ[1]], writes=[SSQ[1]])
            P.add("dve", lambda e: e.tensor_scalar(out=XN[0][0:rows, :], in0=XT_[0][0:rows, :], scalar1=SSQ[0][0:rows, 2:3], scalar2=None, op0=ALU.mult),
                  reads=[XT_[1], SSQ[1]], writes=[XN[1]])

        def transpose_rows(rows, HTcol0, groups):
            for k in range(8):
                pt_, gpt_ = nxt("t", PS_T)
                P.add("pe", lambda e, pt_=pt_, k=k: e.transpose(out=pt_[:, 0:rows], in_=XN[0][0:rows, k * 128:(k + 1) * 128],
                                                                identity=IDB[0:rows, 0:rows]), reads=[XN[1], gIDB], writes=[gpt_])
                for (r0, n_, ar) in groups:
                    P.add("act", lambda e, pt_=pt_, k=k, r0=r0, n_=n_, ar=ar: e.activation(
                        out=HT[0][:, k, HTcol0 + r0:HTcol0 + r0 + n_], in_=pt_[:, r0:r0 + n_], func=AF.Identity,
                        scale=SCT[0][:, k, ar:ar + 1], bias=SHT[0][:, k, ar:ar + 1]), reads=[gpt_, SCT[1], SHT[1]], writes=[HT[1]])

        subsP = [(128, 128 * s) for s in range(4)]

        def mix_dst_prompt(q0, N):
            def f(head, mb):
                P.add("sp", lambda e: e.dma_start(out=mixd_d[head, :, q0:q0 + N], in_=mb[0][:, 0:N]), reads=[mb[1]], writes=[gMIXD], dma=True)
            return f

        gMIXD = G()
        STAGE = cfg.get("STAGE", 99)
        for hp in range(NHP if STAGE >= 4 else (1 if STAGE >= 2 else 0)):
            load_hp(hp)
            for t in range(NT):
                q0 = 512 * t
                for s in range(4):
                    if STAGE >= 2.2:
                        norm_rows(x_d[q0 + 128 * s:q0 + 128 * s + 128, :], 128, 128 * s, None)
                    if STAGE >= 2.3:
                        transpose_rows(128, 128 * s, [(0, 128, 0)])
                if STAGE < 2.4:
                    continue

                def om_dst(si, rows, c0, hp=hp, q0=q0):
                    P.add("sp", lambda e: e.dma_start(out=om_d[hp, q0 + c0:q0 + c0 + rows, :], in_=OTM[0][0:rows, 0:256]), reads=[OTM[1]], dma=True)
                    P.add("sp", lambda e: e.dma_start(out=on_d[hp, q0 + c0:q0 + c0 + rows, :], in_=OTM[0][0:rows, 256:512]), reads=[OTM[1]], dma=True)
                    P.add("sp", lambda e: e.dma_start(out=ow_d[hp, q0 + c0:q0 + c0 + rows, :], in_=OTM[0][0:rows, 512:640]), reads=[OTM[1]], dma=True)
                project_tile(hp, 512, q0, None, subsP, om_dst, None, None, ring_kt=4 * t)
                if STAGE >= 3:
                    attend_tile(hp, 512, q0, subsP, mix_dst_prompt(q0, 512))

        if NS > 0 and STAGE >= 5:
            NTOK = NS * 4
            PTI = sb("pti", [128, NPG], I32)
            IDX = sb("idx", [128, NPG], I32)
            STG = [sb("stg%d" % i, [128, 512], BF16) for i in range(4)]
            norm_rows(xs_d, NTOK, 0, None)
            transpose_rows(NTOK, 0, [(4 * s, 4, 1 + s) for s in range(NS)])
            for s in range(NS):
                P.add("sp", lambda e, s=s: e.dma_start(out=owin_d[s], in_=win_d[s, 4:WB, :]), dma=True)
            HTS = sb("hts", [128, 8, 16], BF16)
            P.add("dve", lambda e: e.tensor_copy(out=HTS[0][:, :, 0:NTOK], in_=HT[0][:, :, 0:NTOK]), reads=[HT[1]], writes=[HTS[1]])
            for s in range(NS):
                P.add("sp", lambda e, s=s: e.dma_start(out=PTI[0][:], in_=ptab_d[s].partition_broadcast(128)), writes=[PTI[1]], dma=True)
                P.add("dve", lambda e: e.tensor_scalar(out=IDX[0][:], in0=PTI[0][:], scalar1=128, scalar2=IOP[:, 0:1], op0=ALU.mult, op1=ALU.add),
                      reads=[PTI[1], gIOP], writes=[IDX[1]])
                for hp in range(NHP):
                    load_hp(hp)
                    kv = hp // 2
                    for pg in range(NPG):
                        sg, gsg = STG[pg % 4]
                        P.add("pool", lambda e, sg=sg, pg=pg, hp=hp: e.indirect_dma_start(
                            out=sg[:, 0:256], out_offset=None,
                            in_=cm_d[hp],
                            in_offset=bass.IndirectOffsetOnAxis(ap=IDX[0][:, pg:pg + 1], axis=0)),
                            reads=[IDX[1]], writes=[gsg], dma=True)
                        P.add("pool", lambda e, sg=sg, pg=pg, kv=kv: e.indirect_dma_start(
                            out=sg[:, 256:512], out_offset=None,
                            in_=cn_d[kv],
                            in_offset=bass.IndirectOffsetOnAxis(ap=IDX[0][:, pg:pg + 1], axis=0)),
                            reads=[IDX[1]], writes=[gsg], dma=True)
                        kc0 = 128 * pg
                        for i in range(2):
                            pt_, gpt_ = nxt("t", PS_T)
                            P.add("pe", lambda e, pt_=pt_, sg=sg, i=i: e.transpose(out=pt_[0:64, 0:128], in_=sg[:, 64 * i:64 * i + 64], identity=IDB[:]),
                                  reads=[gsg, gIDB], writes=[gpt_])
                            P.add("act" if i == 0 else "dve", (lambda e, pt_=pt_, i=i, kc0=kc0: e.activation(
                                out=KA_M[i][0][0:64, kc0:kc0 + 128], in_=pt_[0:64, 0:128], func=AF.Copy)) if i == 0 else
                                (lambda e, pt_=pt_, i=i, kc0=kc0: e.tensor_copy(out=KA_M[i][0][0:64, kc0:kc0 + 128], in_=pt_[0:64, 0:128])),
                                reads=[gpt_], writes=[KA_M[i][1]])
                            P.add("pool", lambda e, sg=sg, i=i, pg=pg: e.tensor_copy(out=VA_M[i][0][:, pg, 0:64], in_=sg[:, 128 + 64 * i:192 + 64 * i]),
                                  reads=[gsg], writes=[VA_M[i][1]])
                        pt_, gpt_ = nxt("t", PS_T)
                        P.add("pe", lambda e, pt_=pt_, sg=sg: e.transpose(out=pt_[:, 0:128], in_=sg[:, 256:384], identity=IDB[:]),
                              reads=[gsg, gIDB], writes=[gpt_])
                        P.add("act", lambda e, pt_=pt_, kc0=kc0: e.activation(out=KCVC[0][:, kc0:kc0 + 128], in_=pt_[:, 0:128], func=AF.Copy),
                              reads=[gpt_], writes=[KCVC[1]])
                        pt_, gpt_ = nxt("t", PS_T)
                        P.add("pe", lambda e, pt_=pt_, sg=sg: e.transpose(out=pt_[0:64, 0:128], in_=sg[:, 384:448], identity=IDB[:]),
                              reads=[gsg, gIDB], writes=[gpt_])
                        P.add("dve", lambda e, pt_=pt_, kc0=kc0: e.tensor_copy(out=KA_S[0][0:64, kc0:kc0 + 128], in_=pt_[0:64, 0:128]),
                              reads=[gpt_], writes=[KA_S[1]])
                        P.add("pool", lambda e, sg=sg, pg=pg: e.tensor_copy(out=VA_S[0][:, pg, 0:64], in_=sg[:, 448:512]),
                              reads=[gsg], writes=[VA_S[1]])
                    for wt in range(WB // 128):
                        sg, gsg = STG[wt % 2]
                        kt = (PAST - WB) // 128 + wt
                        P.add("pool", lambda e, sg=sg, s=s, wt=wt, kv=kv: e.dma_start(
                            out=sg[:, 0:128].rearrange("p (f c) -> p f c", f=2),
                            in_=win_d[s, 128 * wt:128 * wt + 128, :].rearrange("r (f h d) -> r f h d", f=2, h=2)[:, :, kv, :]),
                            writes=[gsg], dma=True)
                        pt_, gpt_ = nxt("t", PS_T)
                        P.add("pe", lambda e, pt_=pt_, sg=sg: e.transpose(out=pt_[0:64, 0:128], in_=sg[:, 0:64], identity=IDB[:]),
                              reads=[gsg, gIDB], writes=[gpt_])
                        r0 = (kt % 9) * 128
                        P.add("dve", lambda e, pt_=pt_, r0=r0: e.tensor_copy(out=KWR[0][0:64, r0:r0 + 128], in_=pt_[0:64, 0:128]),
                              reads=[gpt_], writes=[KWR[1]])
                        P.add("pool", lambda e, sg=sg, kt=kt: e.tensor_copy(out=VWR[0][:, kt % 9, 0:64], in_=sg[:, 64:128]),
                              reads=[gsg], writes=[VWR[1]])
                    ktn = PAST // 128
                    for (t_, g_) in (KA_M[0], KA_M[1], KA_S, KCVC):
                        P.add("pool", lambda e, t_=t_: e.memset(t_[0:64, PAST:PAST + 128], 0.0), writes=[g_])
                    P.add("pool", lambda e: e.memset(KCVC[0][64:128, PAST:PAST + 128], 0.0), writes=[KCVC[1]])
                    P.add("pool", lambda e: e.memset(KWR[0][0:64, (ktn % 9) * 128:(ktn % 9) * 128 + 128], 0.0), writes=[KWR[1]])
                    for (t_, g_) in (VA_M[0], VA_M[1], VA_S):
                        P.add("pool", lambda e, t_=t_: e.memset(t_[:, ktn, 0:64], 0.0), writes=[g_])
                    P.add("pool", lambda e: e.memset(VWR[0][:, ktn % 9, 0:64], 0.0), writes=[VWR[1]])
                    P.add("dve", lambda e, s=s: e.tensor_copy(out=HT[0][:, :, 0:4], in_=HTS[0][:, :, 4 * s:4 * s + 4]), reads=[HTS[1]], writes=[HT[1]])

                    def om_dst(si, rows, c0, hp=hp, s=s):
                        P.add("sp", lambda e: e.dma_start(out=oms_d[hp, 4 * s:4 * s + 4, :], in_=OTM[0][0:4, 0:256]), reads=[OTM[1]], dma=True)
                        P.add("sp", lambda e: e.dma_start(out=ons_d[hp, 4 * s:4 * s + 4, :], in_=OTM[0][0:4, 256:512]), reads=[OTM[1]], dma=True)
                        P.add("sp", lambda e: e.dma_start(out=ows_d[hp, 4 * s:4 * s + 4, :], in_=OTM[0][0:4, 512:640]), reads=[OTM[1]], dma=True)
                    project_tile(hp, 4, PAST, None, [(4, 0)], om_dst, None, None, ring_kt=PAST // 128)

                    def mix_dst(head, mb, s=s):
                        P.add("pool", lambda e: e.tensor_copy(out=MIXS[0][:, head, 4 * s:4 * s + 4], in_=mb[0][:, 0:4]), reads=[mb[1]], writes=[MIXS[1]])
                    attend_tile(hp, 4, PAST, [(4, 0)], mix_dst)

        P.flush()
        stA.close()
        cur[0] = st
        P.barrier()
        GATEP = sb("gatep", [128, D], F32)
        GATES = sb("gates", [16, D], F32)
        FG = sb("fg", [128, D], F32)
        gate_tiles.update({"p": GATEP, "s": GATES, "f": FG})
        if STAGE >= 6:
            adaln("gate")
        XT_ = sb("xt2", [128, D], F32)
        SSQ = sb("ssq2", [128, 4], F32)
        WOUT = sb("wout", [64, 16, D], BF16)
        MIXL = sb("mixl", [64, 16, 512], BF16)
        YP = sb("yp", [128, D], F32)
        if STAGE >= 6:
            P.add("pool", lambda e: e.dma_start(out=WOUT[0][:], in_=wout_d), writes=[WOUT[1]], dma=True)

        def outproj(rows, lhs_fn, x_ap, gate_t, y_ap, extra_reads):
            P.add("sp", lambda e: e.dma_start(out=XT_[0][0:rows, :], in_=x_ap), writes=[XT_[1]], dma=True)
            for oc in range(2):
                px, gpx = nxt("x", PS_X)
                for h in range(16):
                    P.add("pe", lambda e, px=px, h=h, oc=oc: e.matmul(
                        px[0:rows, 0:512], lhsT=lhs_fn(h), rhs=WOUT[0][:, h, oc * 512:(oc + 1) * 512], start=(h == 0), stop=(h == 15)),
                        reads=[WOUT[1]] + extra_reads, writes=[gpx])
                P.add("dve", lambda e, px=px, oc=oc: e.tensor_tensor(
                    out=YP[0][0:rows, oc * 512:(oc + 1) * 512], in0=px[0:rows, 0:512], in1=gate_t[0][0:rows, oc * 512:(oc + 1) * 512], op=ALU.mult),
                    reads=[gpx, gate_t[1]], writes=[YP[1]])
            P.add("pool", lambda e: e.tensor_tensor(out=YP[0][0:rows, :], in0=YP[0][0:rows, :], in1=XT_[0][0:rows, :], op=ALU.add),
                  reads=[YP[1], XT_[1]], writes=[YP[1]])
            P.add("act", lambda e: e.activation(out=XT_[0][0:rows, :], in_=YP[0][0:rows, :], func=AF.Square, accum_out=SSQ[0][0:rows, 0:1]),
                  reads=[YP[1]], writes=[XT_[1], SSQ[1]])
            P.add("act", lambda e: e.activation(out=SSQ[0][0:rows, 1:2], in_=SSQ[0][0:rows, 0:1], func=AF.Sqrt, scale=1.0 / D, bias=EPS[0:rows, :]),
                  reads=[SSQ[1], gEPS], writes=[SSQ[1]])
            P.add("dve", lambda e: e.reciprocal(out=SSQ[0][0:rows, 2:3], in_=SSQ[0][0:rows, 1:2]), reads=[SSQ[1]], writes=[SSQ[1]])
            P.add("dve", lambda e: e.scalar_tensor_tensor(out=YP[0][0:rows, :], in0=YP[0][0:rows, :], scalar=SSQ[0][0:rows, 2:3],
                                                          in1=FG[0][0:rows, :], op0=ALU.mult, op1=ALU.mult),
                  reads=[YP[1], SSQ[1], FG[1]], writes=[YP[1]])
            P.add("sp", lambda e: e.dma_start(out=y_ap, in_=YP[0][0:rows, :]), reads=[YP[1]], dma=True)

        if NHP == 4 and STAGE >= 6:
            for t in range(NT):
                q0 = 512 * t
                P.add("sp", lambda e, q0=q0: e.dma_start(out=MIXL[0][:], in_=mixd_d[:, :, q0:q0 + 512].rearrange("h p n -> p h n")),
                      reads=[gMIXD], writes=[MIXL[1]], dma=True)
                for s in range(4):
                    outproj(128, lambda h, s=s: MIXL[0][:, h, 128 * s:128 * s + 128], x_d[q0 + 128 * s:q0 + 128 * s + 128, :],
                            GATEP, y_d[q0 + 128 * s:q0 + 128 * s + 128, :], [MIXL[1]])

        if NS > 0 and NHP == 4 and STAGE >= 6:
            outproj(NS * 4, lambda h: MIXS[0][:, h, 0:NS * 4], xs_d, GATES, ys_d, [MIXS[1]])

        P.emit_all(st)
    return nc


def make_cfg(T, PAST, NS, NHP, NPHYS):
    return {"T": T, "PAST": PAST, "NS": NS, "NHP": NHP, "TK": ((max(T, PAST) + 2047) // 2048) * 2048 + 128, "NPHYS": NPHYS, "WB": min(512, PAST)}


def make_in_maps(inp, cfg, ncores):
    NS, NHP = cfg["NS"], cfg["NHP"]
    f32 = lambda a: np.ascontiguousarray(np.asarray(a, dtype=np.float32))
    w_in = f32(inp["w_in"])[0]
    rb = f32(inp["rel_bias"])
    shared = {}
    shared["w_ada"] = f32(inp["w_ada"])[0]
    shared["b_ada"] = f32(inp["b_ada"])[0]
    shared["gain"] = f32(inp["norm_gain"])[0]
    shared["fgain"] = f32(inp["final_gain"])
    shared["wfm"] = np.stack([w_in[:, fm_cols(hp)] for hp in range(NHP)])
    shared["wtm"] = np.stack([w_in[:, tm_cols(hp)] for hp in range(NHP)])
    w1k = f32(inp["cmp_k_w1"])[0].reshape(32, 64, 128).transpose(1, 0, 2)
    w1v = f32(inp["cmp_v_w1"])[0].reshape(32, 64, 128).transpose(1, 0, 2)
    shared["w1"] = np.ascontiguousarray(np.concatenate([w1k, w1v], 0))
    shared["w2"] = np.ascontiguousarray(np.concatenate([f32(inp["cmp_k_w2"])[0], f32(inp["cmp_v_w2"])[0]], 1))
    pe = f32(inp["cmp_pe"])[0]
    shared["pet"] = np.ascontiguousarray(np.concatenate([pe[0].T, pe[1].T], 0))
    shared["wout"] = np.ascontiguousarray(f32(inp["w_out"])[0].reshape(16, 64, D).transpose(1, 0, 2))
    ts, t31 = [], []
    for hp in range(NHP):
        hs = [2 * hp, 2 * hp + 1, 8 + 2 * hp, 9 + 2 * hp, 8 + 2 * hp, 9 + 2 * hp]
        ts.append(rb[:, hs])
        t31.append(rb[31, hs[:4]])
    shared["tabsel"] = np.ascontiguousarray(np.concatenate(ts, 1))
    shared["tab31"] = np.ascontiguousarray(np.concatenate(t31, 0))
    cmk = f32(inp["cache_moba_kv"])[0]
    nph = cmk.shape[0]
    for hp in range(4):
        shared["cache_m%d" % hp] = np.ascontiguousarray(cmk[:, :, :, 2 * hp:2 * hp + 2, :].reshape(nph * 128, 256))
    cnk = f32(inp["cache_nsa_kv"])[0]
    for kv in range(2):
        shared["cache_n%d" % kv] = np.ascontiguousarray(cnk[:, :, :, kv, :].reshape(nph * 128, 256))
    for k_, v_ in host_consts(cfg).items():
        shared["c_" + k_] = v_
    xp = f32(inp["x_prompt"])
    xs = f32(inp["x_sample"])
    cp = f32(inp["c_prompt"])
    cs = f32(inp["c_sample"])
    win = f32(inp["state_nsa_win"])[0]
    pt = np.asarray(inp["page_table"]).astype(np.int32)
    maps = []
    for c in range(ncores):
        b = c % xp.shape[0]
        m = dict(shared)
        m["x"] = xp[b]
        m["xs"] = np.ascontiguousarray(xs[NS * c:NS * c + NS].reshape(NS * 4, D))
        cT = np.zeros((D, 144), np.float32)
        cT[:, 0:128] = cp[b][:, None]
        for s in range(NS):
            cT[:, 128 + 4 * s:132 + 4 * s] = cs[NS * c + s][:, None]
        m["cT"] = cT
        m["win"] = np.ascontiguousarray(win[NS * c:NS * c + NS].reshape(NS, win.shape[1], 256))
        m["ptab"] = np.ascontiguousarray(pt[NS * c:NS * c + NS])
        maps.append(m)
    return maps


def assemble(res, cfg, B, ncores):
    T, NS, NHP, WB = cfg["T"], cfg["NS"], cfg["NHP"], cfg["WB"]
    DB = NS * ncores
    y_p = np.zeros((B, T, D), np.float32)
    y_s = np.zeros((DB, 4, D), np.float32)
    mkp = np.zeros((1, B, T, 2, 8, 64), np.float32)
    mks = np.zeros((1, DB, 4, 2, 8, 64), np.float32)
    nkp = np.zeros((1, B, T, 4, 2, 64), np.float32)
    nks = np.zeros((1, DB, 4, 4, 2, 64), np.float32)
    wp = np.zeros((1, B, min(512, T), 2, 2, 64), np.float32)
    ws = np.zeros((1, DB, WB, 2, 2, 64), np.float32)
    for c in range(ncores):
        r = res[c]
        if c < B:
            b = c
            y_p[b] = r["y"]
            for hp in range(NHP):
                om = np.asarray(r["om"][hp])
                mkp[0, b, :, 1, 2 * hp:2 * hp + 2, :] = om[:, 0:128].reshape(T, 2, 64)
                mkp[0, b, :, 0, 2 * hp:2 * hp + 2, :] = om[:, 128:256].reshape(T, 2, 64)
            for kv in range(2):
                on = np.asarray(r["on"][2 * kv])
                ow = np.asarray(r["ow"][2 * kv])
                for f in range(4):
                    nkp[0, b, :, f, kv, :] = on[:, 64 * f:64 * f + 64]
                for f in range(2):
                    wp[0, b, :, f, kv, :] = ow[T - wp.shape[2]:, 64 * f:64 * f + 64]
        sl = slice(NS * c, NS * c + NS)
        y_s[sl] = np.asarray(r["ys"]).reshape(NS, 4, D)
        for hp in range(NHP):
            om = np.asarray(r["oms"][hp]).reshape(NS, 4, 256)
            mks[0, sl, :, 1, 2 * hp:2 * hp + 2, :] = om[:, :, 0:128].reshape(NS, 4, 2, 64)
            mks[0, sl, :, 0, 2 * hp:2 * hp + 2, :] = om[:, :, 128:256].reshape(NS, 4, 2, 64)
        ws[0, sl, 0:WB - 4] = np.asarray(r["owin"]).reshape(NS, WB - 4, 2, 2, 64)
        for kv in range(2):
            on = np.asarray(r["ons"][2 * kv]).reshape(NS, 4, 256)
            ow = np.asarray(r["ows"][2 * kv]).reshape(NS, 4, 128)
            for f in range(4):
                nks[0, sl, :, f, kv, :] = on[:, :, 64 * f:64 * f + 64]
            for f in range(2):
                ws[0, sl, WB - 4:, f, kv, :] = ow[:, :, 64 * f:64 * f + 64]
    return (y_p, y_s, mkp, mks, nkp, nks, wp, ws)


def kernel(**inputs):
    cfg = make_cfg(8192, 8192, 4, 4, 2560)
    nc = build(dict(cfg))
    maps = make_in_maps(inputs, cfg, 8)
    res = run_bass_kernel_spmd(nc, maps, core_ids=list(range(8)))
    return assemble(res.results, cfg, 2, 8)
```
